# Optimizing a Trainium2 kernel written in Bass

```python
import jax, jax.numpy as jnp
from jax import lax
import numpy as np

D_MODEL = 1024
BATCH = 1
SEQ = 16384
DEPTH = 2

N_MEM = 256
MEM_HEADS = 4
MEM_HEAD_DIM = 64
MEM_WIDTH = MEM_HEADS * MEM_HEAD_DIM
MLA_HEADS = 12
QK_NOPE_DIM = 64
QK_ROPE_DIM = 32
QK_HEAD_DIM = QK_NOPE_DIM + QK_ROPE_DIM
V_HEAD_DIM = 64
Q_LORA_RANK = 384
KV_LORA_RANK = 256
ROPE_THETA = 10000.0
MLA_IN_WIDTH = Q_LORA_RANK + KV_LORA_RANK + QK_ROPE_DIM
MLA_OUT_WIDTH = MLA_HEADS * V_HEAD_DIM
BLOCK_Q = 128
SGU_WIDTH = 768
SGU_GROUPS = 8
SGU_GROUP_DIM = SGU_WIDTH // SGU_GROUPS
CHUNK = 128
SGU_IN_WIDTH = 2 * SGU_WIDTH
MIX_WIDTH = MLA_OUT_WIDTH + MEM_WIDTH
D_FF = -(-8 * D_MODEL // (3 * 256)) * 256
EPS = 1e-6

kernel_name = "hybrid_mla_gmlp_memxattn_swiglu"


def rms_norm(x, g):
    xf = x.astype(jnp.float32)
    y = xf * lax.rsqrt(jnp.mean(xf * xf, axis=-1, keepdims=True) + EPS)
    return (y * g.astype(jnp.float32)).astype(x.dtype)


def layer_norm(x, g, b):
    xf = x.astype(jnp.float32)
    mu = jnp.mean(xf, axis=-1, keepdims=True)
    xc = xf - mu
    y = xc * lax.rsqrt(jnp.mean(xc * xc, axis=-1, keepdims=True) + EPS)
    return (y * g.astype(jnp.float32) + b.astype(jnp.float32)).astype(x.dtype)


def rope_tables(positions):
    inv_freq = ROPE_THETA ** (-jnp.arange(0, QK_ROPE_DIM, 2, dtype=jnp.float32) / QK_ROPE_DIM)
    ang = positions.astype(jnp.float32)[..., None] * inv_freq
    return jnp.cos(ang), jnp.sin(ang)


def apply_rope(x, cos, sin):
    xf = x.astype(jnp.float32)
    c = cos[:, :, None, :]
    s = sin[:, :, None, :]
    x1, x2 = jnp.split(xf, 2, axis=-1)
    out = jnp.concatenate([x1 * c - x2 * s, x2 * c + x1 * s], axis=-1)
    return out.astype(x.dtype)


def causal_attention(q, k, v):
    B, S, H, Dk = q.shape
    Dv = v.shape[-1]
    nb = S // BLOCK_Q
    scale = Dk ** -0.5
    qb = q.reshape(B, nb, BLOCK_Q, H, Dk).transpose(1, 0, 2, 3, 4)
    k_pos = jnp.arange(S)

    def one_block(args):
        i, qi = args
        s = jnp.einsum('bqhd,bkhd->bhqk', qi, k, preferred_element_type=jnp.float32) * scale
        q_pos = i * BLOCK_Q + jnp.arange(BLOCK_Q)
        s = jnp.where(k_pos[None, :] <= q_pos[:, None], s, -jnp.inf)
        p = jax.nn.softmax(s, axis=-1).astype(v.dtype)
        return jnp.einsum('bhqk,bkhd->bqhd', p, v)

    out = lax.map(one_block, (jnp.arange(nb), qb))
    return out.transpose(1, 0, 2, 3, 4).reshape(B, S, H, Dv)


def mla_mixer(z, cos, sin, q_lat_norm, w_uq, kv_lat_norm, w_ukv, q_norm, k_norm):
    B, S, _ = z.shape
    q_lat = z[..., :Q_LORA_RANK]
    kv_lat = z[..., Q_LORA_RANK:Q_LORA_RANK + KV_LORA_RANK]
    k_rope = z[..., Q_LORA_RANK + KV_LORA_RANK:]
    q = jnp.einsum('bsr,rhd->bshd', rms_norm(q_lat, q_lat_norm), w_uq)
    kv = jnp.einsum('bsr,rhd->bshd', rms_norm(kv_lat, kv_lat_norm), w_ukv)
    k_nope, v = kv[..., :QK_NOPE_DIM], kv[..., QK_NOPE_DIM:]
    k = jnp.concatenate(
        [k_nope, jnp.broadcast_to(k_rope[:, :, None, :], (B, S, MLA_HEADS, QK_ROPE_DIM))], axis=-1)
    q = rms_norm(q, q_norm)
    k = rms_norm(k, k_norm)
    q = jnp.concatenate([q[..., :QK_NOPE_DIM], apply_rope(q[..., QK_NOPE_DIM:], cos, sin)], axis=-1)
    k = jnp.concatenate([k[..., :QK_NOPE_DIM], apply_rope(k[..., QK_NOPE_DIM:], cos, sin)], axis=-1)
    o = causal_attention(q, k, v)
    return o.reshape(B, S, MLA_OUT_WIDTH)


def sgu_mixer(z, ln_g, ln_b, w_spatial, b_spatial):
    B, S, _ = z.shape
    z = jax.nn.gelu(z, approximate=False)
    u, v = z[..., :SGU_WIDTH], z[..., SGU_WIDTH:]
    v = layer_norm(v, ln_g, ln_b)
    vc = v.reshape(B, S // CHUNK, CHUNK, SGU_GROUPS, SGU_GROUP_DIM)
    tril = jnp.tril(jnp.ones((CHUNK, CHUNK), dtype=bool))
    w = jnp.where(tril[None], w_spatial, jnp.zeros((), w_spatial.dtype))
    s = jnp.einsum('gts,bcsgd->bctgd', w, vc) + b_spatial.T[None, None, :, :, None]
    return u * s.reshape(B, S, SGU_WIDTH)


def mem_cross_attention(q_cols, mem, mem_norm, w_mem_kv, mq_norm, mk_norm):
    B, S, _ = q_cols.shape
    q = rms_norm(q_cols.reshape(B, S, MEM_HEADS, MEM_HEAD_DIM), mq_norm)
    kv = jnp.einsum('bmd,de->bme', rms_norm(mem, mem_norm), w_mem_kv)
    k = rms_norm(kv[..., :MEM_WIDTH].reshape(B, N_MEM, MEM_HEADS, MEM_HEAD_DIM), mk_norm)
    v = kv[..., MEM_WIDTH:].reshape(B, N_MEM, MEM_HEADS, MEM_HEAD_DIM)
    s = jnp.einsum('bshd,bmhd->bhsm', q, k, preferred_element_type=jnp.float32) * (MEM_HEAD_DIM ** -0.5)
    p = jax.nn.softmax(s, axis=-1).astype(v.dtype)
    o = jnp.einsum('bhsm,bmhd->bshd', p, v)
    return o.reshape(B, S, MEM_WIDTH)


def swiglu(h, w_gate, w_up, w_down):
    return (jax.nn.silu(h @ w_gate) * (h @ w_up)) @ w_down


def setup_inputs(seed: int = 0) -> dict:
    key = jax.random.key(seed)
    ks = iter(jax.random.split(key, 64))

    def nrm(shape, scale):
        return jax.random.normal(next(ks), shape, dtype=jnp.float32) * scale

    def gain(n):
        return 1.0 + 0.02 * jax.random.normal(next(ks), (n,), dtype=jnp.float32)

    d = D_MODEL
    inp = {}
    inp['x'] = nrm((BATCH, SEQ, d), 1.0)
    inp['mem'] = nrm((BATCH, N_MEM, d), 1.0)
    offset = jax.random.randint(next(ks), (BATCH, 1), 0, 4096, dtype=jnp.int32)
    inp['positions'] = offset + jnp.arange(SEQ, dtype=jnp.int32)[None, :]
    inp['l0_attn_norm'] = gain(d)
    inp['l0_w_in'] = nrm((d, MLA_IN_WIDTH + MEM_WIDTH), d ** -0.5)
    inp['l0_q_lat_norm'] = gain(Q_LORA_RANK)
    inp['l0_w_uq'] = nrm((Q_LORA_RANK, MLA_HEADS, QK_HEAD_DIM), Q_LORA_RANK ** -0.5)
    inp['l0_kv_lat_norm'] = gain(KV_LORA_RANK)
    inp['l0_w_ukv'] = nrm((KV_LORA_RANK, MLA_HEADS, QK_NOPE_DIM + V_HEAD_DIM), KV_LORA_RANK ** -0.5)
    inp['l0_q_norm'] = gain(QK_HEAD_DIM)
    inp['l0_k_norm'] = gain(QK_HEAD_DIM)
    inp['l0_mem_norm'] = gain(d)
    inp['l0_w_mem_kv'] = nrm((d, 2 * MEM_WIDTH), d ** -0.5)
    inp['l0_mq_norm'] = gain(MEM_HEAD_DIM)
    inp['l0_mk_norm'] = gain(MEM_HEAD_DIM)
    inp['l0_w_out'] = nrm((MIX_WIDTH, d), MIX_WIDTH ** -0.5)
    inp['l0_ffn_norm'] = gain(d)
    inp['l0_w_gate'] = nrm((d, D_FF), d ** -0.5)
    inp['l0_w_up'] = nrm((d, D_FF), d ** -0.5)
    inp['l0_w_down'] = nrm((D_FF, d), D_FF ** -0.5)
    inp['l1_attn_norm'] = gain(d)
    inp['l1_w_in'] = nrm((d, SGU_IN_WIDTH + MEM_WIDTH), d ** -0.5)
    inp['l1_sgu_ln_g'] = gain(SGU_WIDTH)
    inp['l1_sgu_ln_b'] = nrm((SGU_WIDTH,), 0.02)
    inp['l1_w_spatial'] = nrm((SGU_GROUPS, CHUNK, CHUNK), CHUNK ** -0.5)
    inp['l1_b_spatial'] = 1.0 + nrm((SGU_GROUPS, CHUNK), 0.02)
    inp['l1_mem_norm'] = gain(d)
    inp['l1_w_mem_kv'] = nrm((d, 2 * MEM_WIDTH), d ** -0.5)
    inp['l1_mq_norm'] = gain(MEM_HEAD_DIM)
    inp['l1_mk_norm'] = gain(MEM_HEAD_DIM)
    inp['l1_w_out'] = nrm((MIX_WIDTH, d), MIX_WIDTH ** -0.5)
    inp['l1_ffn_norm'] = gain(d)
    inp['l1_w_gate'] = nrm((d, D_FF), d ** -0.5)
    inp['l1_w_up'] = nrm((d, D_FF), d ** -0.5)
    inp['l1_w_down'] = nrm((D_FF, d), D_FF ** -0.5)
    return inp


def reference(x, mem, positions,
              l0_attn_norm, l0_w_in, l0_q_lat_norm, l0_w_uq, l0_kv_lat_norm, l0_w_ukv,
              l0_q_norm, l0_k_norm, l0_mem_norm, l0_w_mem_kv, l0_mq_norm, l0_mk_norm,
              l0_w_out, l0_ffn_norm, l0_w_gate, l0_w_up, l0_w_down,
              l1_attn_norm, l1_w_in, l1_sgu_ln_g, l1_sgu_ln_b, l1_w_spatial, l1_b_spatial,
              l1_mem_norm, l1_w_mem_kv, l1_mq_norm, l1_mk_norm,
              l1_w_out, l1_ffn_norm, l1_w_gate, l1_w_up, l1_w_down):
    cos, sin = rope_tables(positions)
    layers = [
        dict(attn_norm=l0_attn_norm, w_in=l0_w_in,
             mixer=(l0_q_lat_norm, l0_w_uq, l0_kv_lat_norm, l0_w_ukv, l0_q_norm, l0_k_norm),
             mem=(l0_mem_norm, l0_w_mem_kv, l0_mq_norm, l0_mk_norm),
             w_out=l0_w_out, ffn_norm=l0_ffn_norm, ffn=(l0_w_gate, l0_w_up, l0_w_down)),
        dict(attn_norm=l1_attn_norm, w_in=l1_w_in,
             mixer=(l1_sgu_ln_g, l1_sgu_ln_b, l1_w_spatial, l1_b_spatial),
             mem=(l1_mem_norm, l1_w_mem_kv, l1_mq_norm, l1_mk_norm),
             w_out=l1_w_out, ffn_norm=l1_ffn_norm, ffn=(l1_w_gate, l1_w_up, l1_w_down)),
    ]
    for i in range(DEPTH):
        p = layers[i]
        h = rms_norm(x, p['attn_norm'])
        z = h @ p['w_in']
        if i % 2 == 0:
            y_tok = mla_mixer(z[..., :MLA_IN_WIDTH], cos, sin, *p['mixer'])
        else:
            y_tok = sgu_mixer(z[..., :SGU_IN_WIDTH], *p['mixer'])
        y_mem = mem_cross_attention(z[..., -MEM_WIDTH:], mem, *p['mem'])
        x = x + jnp.concatenate([y_tok, y_mem], axis=-1) @ p['w_out']
        x = x + swiglu(rms_norm(x, p['ffn_norm']), *p['ffn'])
    return x
```

```python
import contextlib
import numpy as np
import ml_dtypes
import concourse.bass as bass
import concourse.mybir as mybir
from concourse.bass_utils import run_bass_kernel_spmd

F32 = mybir.dt.float32
BF16 = mybir.dt.bfloat16
I32 = mybir.dt.int32
AF = mybir.ActivationFunctionType
ALU = mybir.AluOpType
AX = mybir.AxisListType

NCORES = 8
SEQ = 16384
D = 1024
TOWN = 2048
NSLOT = 4
DFF = 2816
NF = 22
EPS = 1e-6
MASKNEG = -30000.0


class Res:
    __slots__ = ("name", "last_w", "readers", "dma_w")

    def __init__(self, name=""):
        self.name = name
        self.last_w = None
        self.readers = []
        self.dma_w = []


class Op:
    __slots__ = ("eng", "fn", "deps", "is_dma", "needed", "token")

    def __init__(self, eng, fn, is_dma):
        self.eng = eng
        self.fn = fn
        self.deps = []
        self.is_dma = is_dma
        self.needed = False
        self.token = None


class Sched:
    ENGS = ("pe", "act", "dve", "pool", "sp")
    NDMA = 24

    def __init__(self, nc):
        self.nc = nc
        self.ops = {e: [] for e in self.ENGS}
        self.all_ops = []
        self.dma_ops = []
        self.cc_ops = []
        self.fence = []

    def _collect(self, op, reads, writes):
        deps = []
        for r in reads:
            if r.last_w is not None:
                deps.append(r.last_w)
            deps.extend(r.dma_w)
        for w in writes:
            if w.last_w is not None:
                deps.append(w.last_w)
            deps.extend(w.readers)
            if not op.is_dma:
                deps.extend(w.dma_w)
        deps.extend(self.fence)
        out = []
        seen = set()
        for d in deps:
            if id(d) in seen or d is op:
                continue
            seen.add(id(d))
            if (not d.is_dma) and (not op.is_dma) and d.eng == "pe" and op.eng == "pe":
                continue
            out.append(d)
        op.deps = out
        for r in reads:
            r.readers.append(op)
        for w in writes:
            if op.is_dma:
                if w.readers:
                    w.dma_w = [op]
                else:
                    w.dma_w.append(op)
                w.readers = []
            else:
                w.last_w = op
                w.dma_w = []
                w.readers = []

    def op(self, eng, fn, reads=(), writes=()):
        o = Op(eng, fn, False)
        self._collect(o, reads, writes)
        self.ops[eng].append(o)
        self.all_ops.append(o)
        return o

    def dma(self, eng, fn, reads=(), writes=()):
        o = Op(eng, fn, True)
        self._collect(o, reads, writes)
        k = len(self.dma_ops)
        if k >= self.NDMA:
            prev = self.dma_ops[k - self.NDMA]
            if prev not in o.deps:
                o.deps.append(prev)
        o.token = ("dma", k % self.NDMA, 16 * (k // self.NDMA + 1))
        o.needed = True
        self.dma_ops.append(o)
        self.ops[eng].append(o)
        self.all_ops.append(o)
        return o

    def cc(self, fn, reads=(), writes=()):
        o = Op("pool", fn, True)
        self._collect(o, reads, writes)
        o.token = ("cc", len(self.cc_ops), 16)
        o.needed = True
        self.cc_ops.append(o)
        self.ops["pool"].append(o)
        self.all_ops.append(o)
        return o

    def barrier(self):
        fence = []
        for e in self.ENGS:
            comp = [o for o in self.ops[e] if not o.is_dma]
            if comp:
                fence.append(comp[-1])
        fence.extend(self.dma_ops[-self.NDMA:])
        self.fence = fence

    def emit(self, final_waits=()):
        nc = self.nc
        for o in self.all_ops:
            for d in o.deps:
                d.needed = True
        for o in final_waits:
            o.needed = True
        for e in self.ENGS:
            cnt = 0
            for o in self.ops[e]:
                if o.is_dma:
                    continue
                if o.needed:
                    cnt += 1
                    o.token = ("eng", e, cnt)
        with contextlib.ExitStack() as st:
            esem = {e: st.enter_context(nc.semaphore(f"s_{e}")) for e in self.ENGS}
            dsem = [st.enter_context(nc.semaphore(f"s_dma{i}")) for i in range(self.NDMA)]
            csem = [st.enter_context(nc.semaphore(f"s_cc{i}")) for i in range(len(self.cc_ops))]
            block = st.enter_context(nc.Block())

            def semof(tok):
                if tok[0] == "eng":
                    return esem[tok[1]], tok[2], ("eng", tok[1])
                if tok[0] == "cc":
                    return csem[tok[1]], tok[2], ("cc", tok[1])
                return dsem[tok[1]], tok[2], ("dma", tok[1])

            def run(e, eh, extra_final=False):
                waited = {}
                for o in self.ops[e]:
                    for d in o.deps:
                        sem, val, key = semof(d.token)
                        if waited.get(key, 0) >= val:
                            continue
                        waited[key] = val
                        eh.wait_ge(sem, val)
                    inst = o.fn(eh)
                    if o.is_dma and o.token[0] == "cc":
                        inst.then_inc(csem[o.token[1]], 16)
                    elif o.is_dma:
                        inst.then_inc(dsem[o.token[1]], 16)
                    elif o.needed:
                        inst.then_inc(esem[e], 1)
                if extra_final:
                    for o in final_waits:
                        sem, val, key = semof(o.token)
                        if waited.get(key, 0) >= val:
                            continue
                        waited[key] = val
                        eh.wait_ge(sem, val)

            @block.tensor
            def _(eh):
                run("pe", eh)

            @block.scalar
            def _(eh):
                run("act", eh)

            @block.vector
            def _(eh):
                run("dve", eh)

            @block.gpsimd
            def _(eh):
                run("pool", eh)

            @block.sync
            def _(eh):
                run("sp", eh, extra_final=True)


ARENA_WORDS = 50 * 1024


class K:
    def __init__(self, mode):
        self.mode = mode
        self.nc = bass.Bass("TRN2", target_bir_lowering=False)
        self.S = Sched(self.nc)
        self.din = {}
        self.dout = {}
        self.off = 0
        self.rot = {}

    def inp(self, name, shape, dt=F32):
        ap = self.nc.dram_tensor(name, list(shape), dt, kind="ExternalInput").ap()
        self.din[name] = ap
        return ap

    def outp(self, name, shape, dt=F32):
        ap = self.nc.dram_tensor(name, list(shape), dt, kind="ExternalOutput").ap()
        self.dout[name] = ap
        return ap

    def alloc(self, cols, dt=F32):
        w = cols if dt in (F32, I32) else (cols + 1) // 2
        assert self.off + w <= ARENA_WORDS, f"arena overflow {self.off + w}"
        ap = self.arena[:, self.off:self.off + w]
        self.off += w
        if dt != F32:
            ap = ap.bitcast(dt)
            if ap.shape[1] != cols:
                ap = ap[:, 0:cols]
        return ap

    def mark(self):
        return self.off

    def release(self, m):
        self.S.barrier()
        self.off = m

    def ps(self, b, n=512):
        return self.psum[:, 512 * b:512 * b + n]

    def ps_bf(self, b):
        return self.psum[:, 512 * b:512 * (b + 1)].bitcast(BF16)

    def MM(self, out, lhsT, rhs, start, stop, rd, wr):
        return self.S.op("pe", lambda e: e.matmul(out, lhsT=lhsT, rhs=rhs, start=start, stop=stop), rd, wr)

    def TR(self, out, in_, rd, wr):
        ident = self.ident
        return self.S.op("pe", lambda e: e.transpose(out=out, in_=in_, identity=ident), rd, wr)

    def ACT(self, out, in_, func, rd, wr, scale=None, bias=None):
        kw = {}
        if scale is not None:
            kw["scale"] = scale
        if bias is not None:
            kw["bias"] = bias
        return self.S.op("act", lambda e: e.activation(out=out, in_=in_, func=func, **kw), rd, wr)

    def TT(self, eng, out, in0, in1, op, rd, wr):
        return self.S.op(eng, lambda e: e.tensor_tensor(out=out, in0=in0, in1=in1, op=op), rd, wr)

    def TS(self, eng, out, in0, s1, s2, op0, op1, rd, wr):
        if op1 is None:
            return self.S.op(eng, lambda e: e.tensor_scalar(out=out, in0=in0, scalar1=s1, scalar2=None, op0=op0), rd, wr)
        return self.S.op(eng, lambda e: e.tensor_scalar(out=out, in0=in0, scalar1=s1, scalar2=s2, op0=op0, op1=op1), rd, wr)

    def STT(self, out, in0, scalar, in1, op0, op1, rd, wr):
        return self.S.op("dve", lambda e: e.scalar_tensor_tensor(out=out, in0=in0, scalar=scalar, in1=in1, op0=op0, op1=op1), rd, wr)

    def CP(self, eng, out, in_, rd, wr):
        if eng == "act":
            return self.S.op("act", lambda e: e.copy(out=out, in_=in_), rd, wr)
        return self.S.op(eng, lambda e: e.tensor_copy(out=out, in_=in_), rd, wr)

    def MEMSET(self, eng, ap, val, wr):
        return self.S.op(eng, lambda e: e.memset(ap, val), (), wr)

    def DMA(self, q, out, in_, rd, wr):
        return self.S.dma(q, lambda e: e.dma_start(out=out, in_=in_), rd, wr)

    def rsqrt(self, out, in_, mul, add, w, rd, wr):
        k = self.rot.get("rs", 0)
        self.rot["rs"] = k + 1
        ta, ra = self.rs_tmp[k % 4]
        tb, rb = self.rs_tmp2[k % 4]
        self.TS("dve", ta[:, 0:w], in_, mul, add, ALU.mult, ALU.add, rd, [ra])
        self.ACT(tb[:, 0:w], ta[:, 0:w], AF.Sqrt, [ra], [rb])
        self.S.op("dve", lambda e: e.reciprocal(out=out, in_=tb[:, 0:w]), [rb], wr)


class Stop(Exception):
    pass


import os
STOP_AT = os.environ.get("K_STOP", "")


def ckpt(k, name):
    if STOP_AT and name == STOP_AT:
        raise Stop()


def chunked(ap, p=128):
    return ap.rearrange("(c p) n -> p c n", p=p)


def build(mode):
    k = K(mode)
    nc, S = k.nc, k.S
    with contextlib.ExitStack() as st:
        k.arena = st.enter_context(nc.sbuf_tensor("arena", [128, ARENA_WORDS], F32))
        k.psum = st.enter_context(nc.psum_tensor("psum", [128, 4096], F32))
        RPS = [Res(f"ps{i}") for i in range(8)]
        k.RPS = RPS

        k.ident = k.alloc(128, BF16)
        R_const = Res("const")
        iop = k.alloc(128)
        ioc = k.alloc(128)
        tmpf = k.alloc(128)
        k.ones_bf = k.alloc(128, BF16)
        k.ones_f = k.alloc(128)
        k.maskneg = k.alloc(128, BF16)
        k.rs_tmp = [(k.alloc(16), Res("rsa")) for _ in range(4)]
        k.rs_tmp2 = [(k.alloc(16), Res("rsb")) for _ in range(4)]
        S.op("pool", lambda e: e.iota(iop, pattern=[[0, 128]], base=0, channel_multiplier=1, allow_small_or_imprecise_dtypes=True), (), [R_const])
        S.op("pool", lambda e: e.iota(ioc, pattern=[[1, 128]], base=0, channel_multiplier=0, allow_small_or_imprecise_dtypes=True), (), [R_const])
        k.TT("dve", tmpf, iop, ioc, ALU.is_equal, [R_const], [R_const])
        k.CP("dve", k.ident, tmpf, [R_const], [R_const])
        k.TT("dve", tmpf, iop, ioc, ALU.is_gt, [R_const], [R_const])
        k.TS("dve", k.maskneg, tmpf, MASKNEG, None, ALU.mult, None, [R_const], [R_const])
        k.MEMSET("dve", k.ones_bf, 1.0, [R_const])
        k.MEMSET("dve", k.ones_f, 1.0, [R_const])
        k.Rc = R_const
        k.iop, k.ioc = iop, ioc

        try:
            if mode == "L1":
                phase_A_layer0(k)
            elif mode == "L2":
                phase_rest(k)
            else:
                build_fused_body(k)
        except Stop:
            pass
        finals = list(k.final_dmas)
        S.emit(final_waits=finals)
    return k


def load_w_bf16(k, dram_ap, nck, cols, res, colsplit=None):
    t = k.alloc(nck * cols, BF16).rearrange("p (c n) -> p c n", c=nck)
    src = chunked(dram_ap)
    step = colsplit or cols
    for ck in range(nck):
        for c0 in range(0, cols, step):
            c1 = min(cols, c0 + step)
            k.DMA("pool", t[:, ck, c0:c1], src[:, ck, c0:c1], (), [res])
    return t


def load_w_scaled_bf16(k, dram_ap, nck, cols, g_ap, res, stage, stage_res):
    t = k.alloc(nck * cols, BF16).rearrange("p (c n) -> p c n", c=nck)
    src = chunked(dram_ap)
    for ck in range(nck):
        k.DMA("sp", stage[:, 0:cols], src[:, ck, :], (), [stage_res])
        k.TS("pool", t[:, ck, :], stage[:, 0:cols], g_ap[:, ck:ck + 1], None, ALU.mult, None, [stage_res, k.Rg], [res])
    return t


def norm_prep_slot(k, xsrc, R_x, g_ap, hgT, R_hg, sq, R_sq, rx_out, R_rx, psb):
    for ck in range(8):
        k.ACT(hgT[:, ck, :], xsrc[:, ck, :], AF.Identity, [R_x, k.Rg], [R_hg], scale=g_ap[:, ck:ck + 1])
    k.ACT(sq, xsrc, AF.Square, [R_x], [R_sq])
    pst = k.ps(psb)
    for bl in range(4):
        for ck in range(8):
            k.MM(pst[:, bl:bl + 1], sq[:, ck, bl * 128:(bl + 1) * 128], k.ones_bf[:, 0:1], ck == 0, ck == 7,
                 [R_sq, k.Rc], [k.RPS[psb]])
    k.rsqrt(rx_out, pst[:, 0:4], 1.0 / D, EPS, 4, [k.RPS[psb]], [R_rx])


def memq_norm(k, zm, R_z, gmq_tile, qmn, R_qmn, tmp, R_tmp, st, R_st):
    zm3 = zm.rearrange("p (h d) -> p h d", h=4)
    t3 = tmp.rearrange("p (h d) -> p h d", h=4)
    k.TT("pool", tmp, zm, zm, ALU.mult, [R_z], [R_tmp])
    S = k.S
    S.op("dve", lambda e: e.tensor_reduce(out=st[:, 0:4], in_=t3, axis=AX.X, op=ALU.add), [R_tmp], [R_st])
    k.rsqrt(st[:, 4:8], st[:, 0:4], 1.0 / 64, EPS, 4, [R_st], [R_st])
    k.TT("dve", t3, zm3, st[:, 4:8].unsqueeze(2).broadcast_to([128, 4, 64]), ALU.mult, [R_z, R_st], [R_tmp])
    k.TT("pool", qmn.rearrange("p (h d) -> p h d", h=4), t3, gmq_tile.unsqueeze(1).broadcast_to([128, 4, 64]), ALU.mult,
         [R_tmp, k.Rg], [R_qmn])


def phase_A_layer0(k):
    S = k.S
    k.final_dmas = []
    xT_d = k.inp("xT", [D, TOWN])
    pos_d = k.inp("pos", [128, 16], I32)
    invf_d = k.inp("invf", [128, 16])
    w_in_d = k.inp("w_in0", [D, 928])
    g_attn_d = k.inp("g_attn0", [128, 8])
    w_uq_d = k.inp("w_uq", [384, 1152])
    g_qlat_d = k.inp("g_qlat", [128, 3])
    w_uk_d = k.inp("w_uk", [256, 768])
    w_uv_d = k.inp("w_uv", [256, 768])
    g_kvlat_d = k.inp("g_kvlat", [128, 2])
    gq_d = k.inp("gq_tile", [128, 96])
    gk_d = k.inp("gk_tile", [128, 96])
    gmq_d = k.inp("g_mq0", [128, 64])
    qT_o = k.outp("qT_o", [96, 12 * TOWN], BF16)
    qmT_o = k.outp("qmT_o", [128, 2 * TOWN], BF16)
    KTn_o = k.outp("KTn_o", [768, TOWN], BF16)
    KTr_o = k.outp("KTr_o", [32, TOWN], BF16)
    Vx_o = k.outp("Vx_o", [12, 128, 16 * 65], BF16)
    rk_o = k.outp("rk_o", [128, 16 * 12])

    k.Rg = Res("gains")
    R_w = Res("weights")
    g_attn = k.alloc(8)
    g_qlat = k.alloc(3)
    g_kvlat = k.alloc(2)
    gq = k.alloc(96)
    gk = k.alloc(96)
    gmq = k.alloc(64)
    invf = k.alloc(16)
    posi = k.alloc(16, I32)
    for t, d in [(g_attn, g_attn_d), (g_qlat, g_qlat_d), (g_kvlat, g_kvlat_d), (gq, gq_d), (gk, gk_d), (gmq, gmq_d), (invf, invf_d)]:
        k.DMA("sp", t, d, (), [k.Rg])
    k.DMA("sp", posi, pos_d, (), [k.Rg])

    Win = load_w_bf16(k, w_in_d, 8, 928, R_w)
    stage = k.alloc(1152)
    R_stage = Res("stage")
    Wuq = load_w_scaled_bf16(k, w_uq_d, 3, 1152, g_qlat, R_w, stage, R_stage)
    Wuk = load_w_scaled_bf16(k, w_uk_d, 2, 768, g_kvlat, R_w, stage, R_stage)
    Wuv = load_w_scaled_bf16(k, w_uv_d, 2, 768, g_kvlat, R_w, stage, R_stage)

    ckpt(k, "weights")
    R_rope = Res("rope")
    posf = k.alloc(16)
    ang = k.alloc(256)
    kq = k.alloc(256)
    ki = k.alloc(256, I32)
    y = k.alloc(256)
    m = k.alloc(256)
    cs2 = k.alloc(16 * 32).rearrange("p (b i) -> p b i", b=16)
    sn2 = k.alloc(16 * 32).rearrange("p (b i) -> p b i", b=16)
    ang3 = ang.rearrange("p (b i) -> p b i", b=16)
    k.CP("dve", posf, posi, [k.Rg], [R_rope])
    k.TT("dve", ang3, posf.unsqueeze(2).broadcast_to([128, 16, 16]), invf.unsqueeze(1).broadcast_to([128, 16, 16]), ALU.mult,
         [R_rope, k.Rg], [R_rope])
    TWO_PI = 2.0 * np.pi
    C1 = 6.28125
    C2 = float(np.float32(TWO_PI - C1))
    k.TS("dve", kq, ang, float(1.0 / TWO_PI), None, ALU.mult, None, [R_rope], [R_rope])
    k.CP("dve", ki, kq, [R_rope], [R_rope])
    k.CP("dve", kq, ki, [R_rope], [R_rope])
    k.STT(y, kq, -C1, ang, ALU.mult, ALU.add, [R_rope], [R_rope])
    k.STT(y, kq, -C2, y, ALU.mult, ALU.add, [R_rope], [R_rope])

    def wrap(t):
        k.TS("dve", m, t, float(np.pi), None, ALU.is_gt, None, [R_rope], [R_rope])
        k.STT(t, m, -TWO_PI, t, ALU.mult, ALU.add, [R_rope], [R_rope])
        k.TS("dve", m, t, float(-np.pi), None, ALU.is_lt, None, [R_rope], [R_rope])
        k.STT(t, m, TWO_PI, t, ALU.mult, ALU.add, [R_rope], [R_rope])

    wrap(y)
    y3 = y.rearrange("p (b i) -> p b i", b=16)
    k.ACT(sn2[:, :, 16:32], y3, AF.Sin, [R_rope], [R_rope])
    k.TS("dve", sn2[:, :, 0:16], sn2[:, :, 16:32], -1.0, None, ALU.mult, None, [R_rope], [R_rope])
    k.TS("dve", y, y, float(np.pi / 2), None, ALU.add, None, [R_rope], [R_rope])
    wrap(y)
    k.ACT(cs2[:, :, 0:16], y3, AF.Sin, [R_rope], [R_rope])
    k.CP("dve", cs2[:, :, 16:32], cs2[:, :, 0:16], [R_rope], [R_rope])

    ckpt(k, "rope")
    Gq = k.alloc(96)
    k.CP("dve", Gq, gq, [k.Rg], [k.Rg])
    k.TT("dve", Gq[:, 0:64], gq[:, 0:64], gk[:, 0:64], ALU.mult, [k.Rg], [k.Rg])

    qT = k.alloc(12 * TOWN, BF16).rearrange("p (h t) -> p h t", h=12)
    qmT = k.alloc(2 * TOWN, BF16).rearrange("p (c t) -> p c t", c=2)
    rk_own = k.alloc(16 * 12).rearrange("p (b h) -> p b h", b=16)
    R_qT = Res("qT")
    R_qmT = Res("qmT")
    R_rk = Res("rk")
    xs0 = k.alloc(8 * 512).rearrange("p (c t) -> p c t", c=8)
    xs = [xs0, xs0]
    R_xs0 = Res("xs0")
    R_xs = [R_xs0, R_xs0]
    hg0 = k.alloc(8 * 512, BF16).rearrange("p (c t) -> p c t", c=8)
    hgT = [hg0, hg0]
    R_hg0 = Res("hg0")
    R_hg = [R_hg0, R_hg0]
    sq = k.alloc(8 * 512, BF16).rearrange("p (c t) -> p c t", c=8)
    R_sq = Res("sq")
    rx = k.alloc(16)
    R_rx = Res("rx")
    ckvT = [k.alloc(2 * 512, BF16).rearrange("p (c t) -> p c t", c=2) for _ in range(2)]
    R_ckvT = [Res("ckvT0"), Res("ckvT1")]
    KTn = [k.alloc(6 * 512, BF16).rearrange("p (c t) -> p c t", c=6) for _ in range(2)]
    R_KTn = [Res("KTn0"), Res("KTn1")]
    KTr = [k.alloc(512, BF16) for _ in range(2)]
    R_KTr = [Res("KTr0"), Res("KTr1")]
    Vx = [k.alloc(12 * 4 * 65, BF16).rearrange("p (h b c) -> p h b c", h=12, b=4) for _ in range(2)]
    R_Vx = [Res("Vx0"), Res("Vx1")]
    for i in range(2):
        k.MEMSET("pool", Vx[i][:, :, :, 64:65], 1.0, [R_Vx[i]])
    z = k.alloc(928)
    R_z = Res("z")
    stt = k.alloc(16)
    R_st = Res("st")
    qln = k.alloc(384, BF16)
    ckvn = k.alloc(256, BF16)
    qmn = k.alloc(256, BF16)
    krr = k.alloc(32, BF16)
    R_tm = Res("tm_bf")
    tmpm = k.alloc(256)
    R_tmpm = Res("tmpm")
    krg = k.alloc(32)
    krt = k.alloc(32)
    kru = k.alloc(32)
    R_kr = Res("kr")
    qlT = k.alloc(3 * 128, BF16).rearrange("p (c t) -> p c t", c=3)
    R_qlT = Res("qlT")
    sqq = k.alloc(1152)
    R_sqq = Res("sqq")
    ssq = k.alloc(32)
    R_ssq = Res("ssq")
    qn = k.alloc(1152)
    R_qn = Res("qn")
    qg = k.alloc(1152)
    R_qg = Res("qg")
    qt = k.alloc(12 * 32)
    qu = k.alloc(12 * 32)
    R_qtu = Res("qtu")
    qfin = k.alloc(12 * 96, BF16).rearrange("p (h d) -> p h d", h=12)
    R_qfin = Res("qfin")
    sqk = k.alloc(768)
    R_sqk = Res("sqk")
    ssk = k.alloc(32)
    R_ssk = Res("ssk")
    RPS = k.RPS

    xT3 = chunked(xT_d)
    for j in range(NSLOT):
        sb = j % 2
        k.DMA("sp", xs[sb], xT3[:, :, j * 512:(j + 1) * 512], (), [R_xs[sb]])
        norm_prep_slot(k, xs[sb], R_xs[sb], g_attn, hgT[sb], R_hg[sb], sq, R_sq, rx[:, 4 * j:4 * j + 4], R_rx, 7)
        ckpt(k, "norm")
        for bl in range(4):
            bg = 4 * j + bl
            tok = slice(bl * 128, (bl + 1) * 128)
            for n, (c0, c1) in enumerate([(0, 512), (512, 928)]):
                for ck in range(8):
                    k.MM(k.ps(n)[:, 0:c1 - c0], hgT[sb][:, ck, tok], Win[:, ck, c0:c1], ck == 0, ck == 7,
                         [R_hg[sb], R_w], [RPS[n]])
                k.ACT(z[:, c0:c1], k.ps(n)[:, 0:c1 - c0], AF.Identity, [RPS[n], R_rx], [R_z], scale=rx[:, bg:bg + 1])
            ckpt(k, "z")
            k.ACT(sqq[:, 0:672], z[:, 0:672], AF.Square, [R_z], [R_sqq])
            k.S.op("dve", lambda e: e.tensor_reduce(out=stt[:, 0:1], in_=sqq[:, 0:384], axis=AX.X, op=ALU.add), [R_sqq], [R_st])
            k.S.op("dve", lambda e: e.tensor_reduce(out=stt[:, 1:2], in_=sqq[:, 384:640], axis=AX.X, op=ALU.add), [R_sqq], [R_st])
            k.S.op("dve", lambda e: e.tensor_reduce(out=stt[:, 2:3], in_=sqq[:, 640:672], axis=AX.X, op=ALU.add), [R_sqq], [R_st])
            k.rsqrt(stt[:, 4:5], stt[:, 0:1], 1.0 / 384, EPS, 1, [R_st], [R_st])
            k.rsqrt(stt[:, 5:6], stt[:, 1:2], 1.0 / 256, EPS, 1, [R_st], [R_st])
            k.TS("dve", qln, z[:, 0:384], stt[:, 4:5], None, ALU.mult, None, [R_z, R_st], [R_tm])
            k.TS("dve", ckvn, z[:, 384:640], stt[:, 5:6], None, ALU.mult, None, [R_z, R_st], [R_tm])
            memq_norm(k, z[:, 672:928], R_z, gmq, qmn, R_tm, tmpm, R_tmpm, stt[:, 8:16], R_st)
            k.TT("pool", krg, z[:, 640:672], gk[:, 64:96], ALU.mult, [R_z, k.Rg], [R_kr])
            k.TT("pool", krt, krg, cs2[:, bg, :], ALU.mult, [R_kr, R_rope], [R_kr])
            k.TT("pool", kru[:, 0:16], krg[:, 16:32], sn2[:, bg, 0:16], ALU.mult, [R_kr, R_rope], [R_kr])
            k.TT("pool", kru[:, 16:32], krg[:, 0:16], sn2[:, bg, 16:32], ALU.mult, [R_kr, R_rope], [R_kr])
            k.TT("pool", krr, krt, kru, ALU.add, [R_kr], [R_tm])
            ckpt(k, "tm")
            pb = k.ps_bf(2)
            for c in range(3):
                k.TR(pb[:, c * 128:(c + 1) * 128], qln[:, c * 128:(c + 1) * 128], [R_tm, k.Rc], [RPS[2]])
            ckpt(k, "tr1")
            for c in range(2):
                k.TR(pb[:, (3 + c) * 128:(4 + c) * 128], ckvn[:, c * 128:(c + 1) * 128], [R_tm, k.Rc], [RPS[2]])
            for c in range(2):
                k.TR(pb[:, (5 + c) * 128:(6 + c) * 128], qmn[:, c * 128:(c + 1) * 128], [R_tm, k.Rc], [RPS[2]])
            ckpt(k, "tr3")
            k.TR(pb[0:32, 7 * 128:8 * 128], krr, [R_tm, k.Rc], [RPS[2]])
            ckpt(k, "tr4")
            k.CP("act", qlT, pb[:, 0:384].rearrange("p (c t) -> p c t", c=3), [RPS[2]], [R_qlT])
            ckpt(k, "cp1")
            k.CP("act", ckvT[sb][:, :, tok], pb[:, 384:640].rearrange("p (c t) -> p c t", c=2), [RPS[2]], [R_ckvT[sb]])
            ckpt(k, "cp2")
            k.CP("act", qmT[:, :, bg * 128:(bg + 1) * 128], pb[:, 640:896].rearrange("p (c t) -> p c t", c=2), [RPS[2]], [R_qmT])
            ckpt(k, "cp3")
            k.CP("act", KTr[sb][0:32, tok], pb[0:32, 896:1024], [RPS[2]], [R_KTr[sb]])
            ckpt(k, "tr")
            for n in range(3):
                for ck in range(3):
                    k.MM(k.ps(3 + n)[:, 0:384], qlT[:, ck, :], Wuq[:, ck, n * 384:(n + 1) * 384], ck == 0, ck == 2,
                         [R_qlT, R_w], [RPS[3 + n]])
                k.ACT(sqq[:, n * 384:(n + 1) * 384], k.ps(3 + n)[:, 0:384], AF.Square, [RPS[3 + n]], [R_sqq])
            k.S.op("dve", lambda e: e.tensor_reduce(out=ssq[:, 0:12], in_=sqq.rearrange("p (h d) -> p h d", h=12), axis=AX.X, op=ALU.add),
                   [R_sqq], [R_ssq])
            k.rsqrt(ssq[:, 16:28], ssq[:, 0:12], 1.0 / 96, EPS, 12, [R_ssq], [R_ssq])
            for n in range(3):
                k.TT("dve", qn[:, n * 384:(n + 1) * 384].rearrange("p (h d) -> p h d", h=4),
                     k.ps(3 + n)[:, 0:384].rearrange("p (h d) -> p h d", h=4),
                     ssq[:, 16 + 4 * n:20 + 4 * n].unsqueeze(2).broadcast_to([128, 4, 96]), ALU.mult,
                     [RPS[3 + n], R_ssq], [R_qn])
            qn3 = qn.rearrange("p (h d) -> p h d", h=12)
            qg3 = qg.rearrange("p (h d) -> p h d", h=12)
            k.TT("pool", qg3, qn3, Gq.unsqueeze(1).broadcast_to([128, 12, 96]), ALU.mult, [R_qn, k.Rg], [R_qg])
            qt3 = qt.rearrange("p (h d) -> p h d", h=12)
            qu3 = qu.rearrange("p (h d) -> p h d", h=12)
            k.TT("pool", qt3, qg3[:, :, 64:96], cs2[:, bg, :].unsqueeze(1).broadcast_to([128, 12, 32]), ALU.mult, [R_qg, R_rope], [R_qtu])
            k.TT("pool", qu3[:, :, 0:16], qg3[:, :, 80:96], sn2[:, bg, 0:16].unsqueeze(1).broadcast_to([128, 12, 16]), ALU.mult,
                 [R_qg, R_rope], [R_qtu])
            k.TT("pool", qu3[:, :, 16:32], qg3[:, :, 64:80], sn2[:, bg, 16:32].unsqueeze(1).broadcast_to([128, 12, 16]), ALU.mult,
                 [R_qg, R_rope], [R_qtu])
            k.TT("dve", qfin[:, :, 64:96], qt3, qu3, ALU.add, [R_qtu], [R_qfin])
            k.CP("dve", qfin[:, :, 0:64], qg3[:, :, 0:64], [R_qg], [R_qfin])
            for h in range(12):
                bk = 6 if h < 8 else 7
                hh = h % 8
                k.TR(k.ps_bf(bk)[0:96, hh * 128:(hh + 1) * 128], qfin[:, h, :], [R_qfin, k.Rc], [RPS[bk]])
            k.CP("act", qT[0:96, 0:8, bg * 128:(bg + 1) * 128], k.ps_bf(6)[0:96, :].rearrange("p (h t) -> p h t", h=8), [RPS[6]], [R_qT])
            k.CP("act", qT[0:96, 8:12, bg * 128:(bg + 1) * 128], k.ps_bf(7)[0:96, 0:512].rearrange("p (h t) -> p h t", h=4), [RPS[7]], [R_qT])
            ckpt(k, "q")
            for n in range(2):
                for ck in range(2):
                    k.MM(k.ps(n)[:, 0:384], ckvT[sb][:, ck, tok], Wuk[:, ck, n * 384:(n + 1) * 384], ck == 0, ck == 1,
                         [R_ckvT[sb], R_w], [RPS[n]])
                k.ACT(sqk[:, n * 384:(n + 1) * 384], k.ps(n)[:, 0:384], AF.Square, [RPS[n]], [R_sqk])
            for n in range(2):
                for ck in range(2):
                    k.MM(k.ps(3 + n)[:, 0:384], ckvT[sb][:, ck, tok], Wuv[:, ck, n * 384:(n + 1) * 384], ck == 0, ck == 1,
                         [R_ckvT[sb], R_w], [RPS[3 + n]])
                k.CP("act", Vx[sb][:, 6 * n:6 * n + 6, bl, 0:64], k.ps(3 + n)[:, 0:384].rearrange("p (h d) -> p h d", h=6),
                     [RPS[3 + n]], [R_Vx[sb]])
            k.S.op("dve", lambda e: e.tensor_reduce(out=ssk[:, 0:12], in_=sqk.rearrange("p (h d) -> p h d", h=12), axis=AX.X, op=ALU.add),
                   [R_sqk], [R_ssk])
            k.TS("dve", ssk[:, 16:28], ssk[:, 0:12], stt[:, 2:3], None, ALU.add, None, [R_ssk, R_st], [R_ssk])
            k.rsqrt(rk_own[:, bg, :], ssk[:, 16:28], 1.0, 96 * EPS, 12, [R_ssk], [R_rk])
        ckpt(k, "blocks")
        for i in range(6):
            for ck in range(2):
                k.MM(k.ps(2), Wuk[:, ck, i * 128:(i + 1) * 128], ckvT[sb][:, ck, :], ck == 0, ck == 1, [R_w, R_ckvT[sb]], [RPS[2]])
            k.CP("act" if i % 2 == 0 else "dve", KTn[sb][:, i, :], k.ps(2), [RPS[2]], [R_KTn[sb]])
        ckpt(k, "ktn")
        KTn_o3 = KTn_o.rearrange("(c p) t -> p c t", p=128)
        k.final_dmas.append(k.DMA("sp", KTn_o3[:, :, j * 512:(j + 1) * 512], KTn[sb], [R_KTn[sb]], ()))
        k.final_dmas.append(k.DMA("sp", KTr_o[:, j * 512:(j + 1) * 512], KTr[sb][0:32, :], [R_KTr[sb]], ()))
        Vx_o4 = Vx_o.rearrange("h p (b c) -> p h b c", b=16)
        for h0 in range(0, 12, 4):
            k.final_dmas.append(k.DMA("sp", Vx_o4[:, h0:h0 + 4, 4 * j:4 * j + 4, :], Vx[sb][:, h0:h0 + 4, :, :], [R_Vx[sb]], ()))
    ckpt(k, "slots")
    k.final_dmas.append(k.DMA("sp", rk_o, rk_own.rearrange("p b h -> p (b h)"), [R_rk], ()))
    k.final_dmas.append(k.DMA("sp", qT_o, qT[0:96].rearrange("p h t -> p (h t)"), [R_qT], ()))
    k.final_dmas.append(k.DMA("sp", qmT_o, qmT.rearrange("p c t -> p (c t)"), [R_qmT], ()))


def mem_kv_prep(k, pre, memT_d, gmem_d, wmkv_d, gmk_d, KmT, R_KmT, Vmx, R_Vmx):
    RPS = k.RPS
    m = k.mark()
    memx = k.alloc(8 * 256).rearrange("p (c t) -> p c t", c=8)
    mg = k.alloc(8 * 256, BF16).rearrange("p (c t) -> p c t", c=8)
    sqm = k.alloc(8 * 256, BF16).rearrange("p (c t) -> p c t", c=8)
    gmem = k.alloc(8)
    gmk = k.alloc(64)
    rmem = k.alloc(4)
    kvm = k.alloc(512)
    kn = k.alloc(256, BF16)
    tmp = k.alloc(256)
    st = k.alloc(8)
    R_a, R_b, R_c, R_d, R_e, R_f, R_w = (Res(pre + n) for n in "abcdefw")
    k.DMA("sp", memx, chunked(memT_d), (), [R_a])
    k.DMA("sp", gmem, gmem_d, (), [k.Rg])
    k.DMA("sp", gmk, gmk_d, (), [k.Rg])
    Wm = load_w_bf16(k, wmkv_d, 8, 512, R_w)
    for ck in range(8):
        k.ACT(mg[:, ck, :], memx[:, ck, :], AF.Identity, [R_a, k.Rg], [R_b], scale=gmem[:, ck:ck + 1])
    k.ACT(sqm, memx, AF.Square, [R_a], [R_c])
    for mb in range(2):
        for ck in range(8):
            k.MM(k.ps(7)[:, mb:mb + 1], sqm[:, ck, mb * 128:(mb + 1) * 128], k.ones_bf[:, 0:1], ck == 0, ck == 7, [R_c, k.Rc], [RPS[7]])
    k.rsqrt(rmem[:, 0:2], k.ps(7)[:, 0:2], 1.0 / D, EPS, 2, [RPS[7]], [R_d])
    k.MEMSET("pool", Vmx[:, :, :, 64:65], 1.0, [R_Vmx])
    for mb in range(2):
        for ck in range(8):
            k.MM(k.ps(0), mg[:, ck, mb * 128:(mb + 1) * 128], Wm[:, ck, :], ck == 0, ck == 7, [R_b, R_w], [RPS[0]])
        k.ACT(kvm, k.ps(0), AF.Identity, [RPS[0], R_d], [R_e], scale=rmem[:, mb:mb + 1])
        memq_norm(k, kvm[:, 0:256], R_e, gmk, kn, R_f, tmp, R_f, st, R_f)
        k.CP("dve", Vmx[:, mb, :, 0:64], kvm[:, 256:512].rearrange("p (h d) -> p h d", h=4), [R_e], [R_Vmx])
        pb = k.ps_bf(2)
        for c in range(2):
            k.TR(pb[:, c * 128:(c + 1) * 128], kn[:, c * 128:(c + 1) * 128], [R_f, k.Rc], [RPS[2]])
        k.CP("act", KmT[:, :, mb * 128:(mb + 1) * 128], pb[:, 0:256].rearrange("p (c t) -> p c t", c=2), [RPS[2]], [R_KmT])
    k.release(m)


class Attn:
    def __init__(self, k, mixT, R_mix):
        self.k = k
        self.mixT = mixT
        self.R_mix = R_mix
        self.P = [k.alloc(1024, BF16) for _ in range(2)]
        self.R_P = [Res("P0"), Res("P1")]
        self.R_S = [Res("S0"), Res("S1")]
        self.rec = k.alloc(512)
        self.R_rec = Res("rec")
        self.bsb = k.alloc(512)
        self.R_bsb = Res("bsb")
        self.ost = [k.alloc(512, BF16) for _ in range(2)]
        self.R_ost = [Res("ost0"), Res("ost1")]
        self.nS = 0
        self.nO = 0

    def sbuf(self):
        i = self.nS % 2
        self.nS += 1
        return i, self.k.psum[:, (4 + 2 * i) * 512:(6 + 2 * i) * 512], self.R_S[i]

    def finalize(self, j, chunk, odd, si=None):
        k = self.k
        acc = k.ps(j)
        Racc = k.RPS[j]
        k.S.op("dve", lambda e: e.reciprocal(out=self.rec[64:65, :], in_=acc[64:65, :]), [Racc], [self.R_rec])
        if si is None:
            i, Sps, RS = self.sbuf()
        else:
            Sps, RS = k.psum[:, (4 + 2 * si) * 512:(6 + 2 * si) * 512], self.R_S[si]
        k.MM(Sps[0:64, 0:512], k.ones_f[64:65, 0:64], self.rec[64:65, :], True, True, [self.R_rec, k.Rc], [RS])
        k.CP("act", self.bsb[0:64, :], Sps[0:64, 0:512], [RS], [self.R_bsb])
        cols = slice(j * 512, (j + 1) * 512)
        if not odd:
            k.TT("dve", self.mixT[0:64, chunk, cols], acc[0:64, :], self.bsb[0:64, :], ALU.mult, [Racc, self.R_bsb], [self.R_mix[chunk][j]])
        else:
            t = self.nO % 2
            self.nO += 1
            k.TT("dve", self.ost[t][0:64, :], acc[0:64, :], self.bsb[0:64, :], ALU.mult, [Racc, self.R_bsb], [self.R_ost[t]])
            k.DMA("pool", self.mixT[64:128, chunk, cols], self.ost[t][0:64, :], [self.R_ost[t]], [self.R_mix[chunk][j]])

    def mem_attention(self, qmT, R_qmT, KmT, R_KmT, Vmx, R_Vmx):
        k = self.k
        for hm in range(4):
            pr = slice((hm % 2) * 64, (hm % 2) * 64 + 64)
            for j in range(4):
                i, Sps, RS = self.sbuf()
                for mb in range(2):
                    k.MM(Sps[:, mb * 512:(mb + 1) * 512], KmT[pr, hm // 2, mb * 128:(mb + 1) * 128], qmT[pr, hm // 2, j * 512:(j + 1) * 512],
                         True, True, [R_KmT, R_qmT], [RS])
                k.ACT(self.P[i], Sps, AF.Exp, [RS], [self.R_P[i]], scale=0.125)
                for mb in range(2):
                    k.MM(k.ps(j)[0:65, :], Vmx[:, mb, hm, :], self.P[i][:, mb * 512:(mb + 1) * 512], mb == 0, mb == 1,
                         [R_Vmx, self.R_P[i]], [k.RPS[j]])
                self.finalize(j, 6 + hm // 2, hm % 2)

    def causal_attention(self, qT, R_qT, rk, R_rk, KTn_w, KTr_w, Vx_w):
        k = self.k
        KT = [k.alloc(4096, BF16) for _ in range(2)]
        Vb = [k.alloc(32 * 65, BF16).rearrange("p (b c) -> p b c", c=65) for _ in range(2)]
        R_KT = [Res("KT0"), Res("KT1")]
        R_V = [Res("V0"), Res("V1")]
        Vx4 = Vx_w.rearrange("h p (b c) -> h p b c", c=65)

        def load(n):
            h, grp = divmod(n, 4)
            b = n % 2
            ts = slice(4096 * grp, 4096 * (grp + 1))
            k.DMA("sp", KT[b][0:64, :], KTn_w[64 * h:64 * h + 64, ts], (), [R_KT[b]])
            k.DMA("sp", KT[b][64:96, :], KTr_w[:, ts], (), [R_KT[b]])
            k.DMA("sp", Vb[b], Vx4[h, :, 32 * grp:32 * grp + 32, :], (), [R_V[b]])

        descs = []
        for n in range(48):
            h, grp = divmod(n, 4)
            b = n % 2
            js = [j for j in range(4) if j >= grp]
            batches = [js[x:x + 2] for x in range(0, len(js), 2)]
            for il in range(8):
                i = 8 * grp + il
                for kb in range(4):
                    for bi, batch in enumerate(batches):
                        last = (il == 7 and kb == 3 and bi == len(batches) - 1)
                        descs.append((n, h, grp, b, il, i, kb, batch, last))

        def emit_qk(dsc):
            n, h, grp, b, il, i, kb, batch, last = dsc
            Bw = 4 * i + kb
            col = (il * 4 + kb) * 128
            lk = KT[b][0:96, col:col + 128]
            si = self.nbatch % 2
            self.nbatch += 1
            Sps, RS = k.psum[:, (4 + 2 * si) * 512:(6 + 2 * si) * 512], self.R_S[si]
            dg = (il == 7 and batch[0] == grp)
            n0 = 128 * kb if dg else 0
            ncols = 512 * len(batch)
            for u, j in enumerate(batch):
                off = n0 if u == 0 else 0
                msk = dg and u == 0
                k.MM(Sps[:, u * 512 + off:(u + 1) * 512], lk, qT[0:96, h, j * 512 + off:(j + 1) * 512], True, not msk,
                     [R_KT[b], R_qT], [RS])
                if msk:
                    k.MM(Sps[:, off:off + 128], k.ident, k.maskneg, False, True, [k.Rc], [RS])
            k.ACT(self.P[si][:, n0:ncols], Sps[:, n0:ncols], AF.Exp, [RS, R_rk], [self.R_P[si]], scale=rk[:, Bw * 12 + h:Bw * 12 + h + 1])
            return si, n0

        def emit_pv(dsc, si, n0):
            n, h, grp, b, il, i, kb, batch, last = dsc
            Bw = 4 * i + kb
            for u, j in enumerate(batch):
                off = n0 if u == 0 else 0
                k.MM(k.ps(j)[0:65, off:512], Vb[b][:, il * 4 + kb, :], self.P[si][:, u * 512 + off:(u + 1) * 512],
                     Bw == 0, (i == 8 * j + 7 and kb == 3), [R_V[b], self.R_P[si]], [k.RPS[j]])
            if last:
                self.finalize(grp, h // 2, h % 2, si=si)
                if n + 2 < 48:
                    load(n + 2)

        self.nbatch = 0
        load(0)
        load(1)
        prev = None
        for dsc in descs:
            cur = emit_qk(dsc)
            if prev is not None:
                emit_pv(prev[0], *prev[1])
            prev = (dsc, cur)
        emit_pv(prev[0], *prev[1])


def wout_residual(k, pre, wout_d, mixT, R_mix, xres, R_x, xsrc):
    RPS = k.RPS
    m = k.mark()
    R_w = Res(pre + "wout")
    Wout = load_w_bf16(k, wout_d, 8, 1024, R_w)
    nb = 0
    for j in range(NSLOT):
        cols = slice(j * 512, (j + 1) * 512)
        if xsrc is not None:
            k.DMA("sp", xres[:, :, cols], xsrc(j), (), [R_x[j]])
        for dch in range(8):
            b = nb % 8
            nb += 1
            for mc in range(8):
                k.MM(k.ps(b), Wout[:, mc, dch * 128:(dch + 1) * 128], mixT[:, mc, cols], mc == 0, mc == 7, [R_w, R_mix[mc][j]], [RPS[b]])
            k.TT("dve", xres[:, dch, cols], k.ps(b), xres[:, dch, cols], ALU.add, [RPS[b], R_x[j]], [R_x[j]])
    k.release(m)


def ffn(k, pre, xres, R_x, gffn_d, wg_d, wu_d, wd_d):
    RPS = k.RPS
    m = k.mark()
    g = k.alloc(8)
    k.DMA("sp", g, gffn_d, (), [k.Rg])
    hT = k.alloc(8 * 1024, BF16).rearrange("p (c t) -> p c t", c=8)
    actT = k.alloc(NF * 1024, BF16).rearrange("p (f t) -> p f t", f=NF)
    sq = k.alloc(8 * 512, BF16).rearrange("p (c t) -> p c t", c=8)
    rt1 = k.alloc(512)
    rt2 = k.alloc(512)
    rbc = k.alloc(512)
    sg = [k.alloc(512) for _ in range(2)]
    Wg_r = [k.alloc(8 * 256, BF16).rearrange("p (c n) -> p c n", c=8) for _ in range(3)]
    Wu_r = [k.alloc(8 * 256, BF16).rearrange("p (c n) -> p c n", c=8) for _ in range(3)]
    Wd_r = [k.alloc(NF * 128, BF16).rearrange("p (f n) -> p f n", f=NF) for _ in range(3)]
    R_hT, R_sq, R_rt, R_rbc = Res(pre + "hT"), Res(pre + "sq"), Res(pre + "rt"), Res(pre + "rbc")
    R_act = [[Res(f"{pre}act{f}_{t}") for t in range(2)] for f in range(NF)]
    R_sg = [Res(pre + "sg0"), Res(pre + "sg1")]
    R_Wgu = [Res(f"{pre}wgu{i}") for i in range(3)]
    R_Wd = [Res(f"{pre}wd{i}") for i in range(3)]
    wg3, wu3 = chunked(wg_d), chunked(wu_d)
    wd3 = chunked(wd_d)
    NFG = NF // 2

    def load_gu(fg):
        s = fg % 3
        k.DMA("pool", Wg_r[s], wg3[:, :, fg * 256:(fg + 1) * 256], (), [R_Wgu[s]])
        k.DMA("pool", Wu_r[s], wu3[:, :, fg * 256:(fg + 1) * 256], (), [R_Wgu[s]])

    def load_d(dch):
        s = dch % 3
        k.DMA("pool", Wd_r[s], wd3[:, :, dch * 128:(dch + 1) * 128], (), [R_Wd[s]])

    for half in range(2):
        hc = half * 1024
        load_gu(0)
        load_gu(1)
        for t in range(2):
            cols = slice(hc + t * 512, hc + (t + 1) * 512)
            j = (hc + t * 512) // 512
            k.ACT(sq, xres[:, :, cols], AF.Square, [R_x[j]], [R_sq])
            for ck in range(8):
                k.MM(k.ps(7), k.ones_bf, sq[:, ck, :], ck == 0, ck == 7, [k.Rc, R_sq], [RPS[7]])
            k.TS("dve", rt1, k.ps(7), 1.0 / D, EPS, ALU.mult, ALU.add, [RPS[7]], [R_rt])
            k.ACT(rt2, rt1, AF.Sqrt, [R_rt], [R_rt])
            k.S.op("dve", lambda e: e.reciprocal(out=rbc, in_=rt2), [R_rt], [R_rbc])
            for ck in range(8):
                k.STT(hT[:, ck, t * 512:(t + 1) * 512], xres[:, ck, cols], g[:, ck:ck + 1], rbc, ALU.mult, ALU.mult,
                      [R_x[j], k.Rg, R_rbc], [R_hT])
        nsg = 0
        for fg in range(NFG):
            if fg + 2 < NFG:
                load_gu(fg + 2)
            if fg == NFG - 2:
                load_d(0)
            if fg == NFG - 1:
                load_d(1)
            s = fg % 3
            for fl in range(2):
                f = 2 * fg + fl
                gb = [0, 1] if f % 2 == 0 else [4, 5]
                ub = [2, 3] if f % 2 == 0 else [6, 7]
                for W_r, banks in ((Wg_r, gb), (Wu_r, ub)):
                    for ck in range(8):
                        for t in range(2):
                            k.MM(k.ps(banks[t]), W_r[s][:, ck, fl * 128:(fl + 1) * 128], hT[:, ck, t * 512:(t + 1) * 512], ck == 0, ck == 7,
                                 [R_Wgu[s], R_hT], [RPS[banks[t]]])
                for t in range(2):
                    q = nsg % 2
                    nsg += 1
                    k.ACT(sg[q], k.ps(gb[t]), AF.Silu, [RPS[gb[t]]], [R_sg[q]])
                    k.TT("dve", actT[:, f, t * 512:(t + 1) * 512], k.ps(ub[t]), sg[q], ALU.mult, [RPS[ub[t]], R_sg[q]], [R_act[f][t]])
        nb = 0
        for dch in range(8):
            if dch + 2 < 8:
                load_d(dch + 2)
            s = dch % 3
            for t in range(2):
                b = nb % 8
                nb += 1
                cols = slice(hc + t * 512, hc + (t + 1) * 512)
                j = (hc + t * 512) // 512
                for f in range(NF):
                    k.MM(k.ps(b), Wd_r[s][:, f, :], actT[:, f, t * 512:(t + 1) * 512], f == 0, f == NF - 1, [R_Wd[s], R_act[f][t]], [RPS[b]])
                k.TT("dve", xres[:, dch, cols], k.ps(b), xres[:, dch, cols], ALU.add, [RPS[b], R_x[j]], [R_x[j]])
    k.release(m)


def sgu_layer1(k, xres, R_x, mixT, R_mix, qmT, R_qmT, d):
    RPS = k.RPS
    m = k.mark()
    R_w = Res("l1w")
    g_attn = k.alloc(8)
    gmq = k.alloc(64)
    lng = k.alloc(768)
    lnb = k.alloc(768)
    bsp = k.alloc(8)
    for t_, d_ in [(g_attn, d["g_attn1"]), (gmq, d["g_mq1"]), (lng, d["ln_g"]), (lnb, d["ln_b"]), (bsp, d["b_sp"])]:
        k.DMA("sp", t_, d_, (), [k.Rg])
    Win = load_w_bf16(k, d["w_in1"], 8, 1792, R_w, colsplit=896)
    wsp = k.alloc(8 * 128, BF16).rearrange("p (g t) -> p g t", g=8)
    wst = k.alloc(8 * 128).rearrange("p (g t) -> p g t", g=8)
    msk = k.alloc(128)
    R_ws = Res("wsp")
    k.DMA("sp", wst, d["w_spT"].rearrange("p (g t) -> p g t", g=8), (), [R_ws])
    k.TT("dve", msk, k.iop, k.ioc, ALU.is_le, [k.Rc], [R_ws])
    k.TT("dve", wsp, wst, msk.unsqueeze(1).broadcast_to([128, 8, 128]), ALU.mult, [R_ws], [R_w])
    hgT = k.alloc(8 * 512, BF16).rearrange("p (c t) -> p c t", c=8)
    sq = k.alloc(8 * 512, BF16).rearrange("p (c t) -> p c t", c=8)
    rx = k.alloc(16)
    R_hg, R_sq, R_rx = Res("l1hg"), Res("l1sq"), Res("l1rx")
    uv = k.alloc(1536)
    zm = k.alloc(256)
    R_uv, R_zm = Res("uv"), Res("zm")
    bst = k.alloc(12)
    mv = k.alloc(4)
    R_bn = Res("bn")
    vt = k.alloc(768)
    vt2 = k.alloc(768)
    R_vt = Res("vt")
    vn = k.alloc(768, BF16)
    R_vn = Res("vn")
    y = k.alloc(768, BF16)
    R_y = Res("y")
    qmn = k.alloc(256, BF16)
    R_qmn = Res("qmn1")
    tmpm = k.alloc(256)
    R_tmpm = Res("tmpm1")
    stt = k.alloc(8)
    R_st = Res("st1")
    for j in range(NSLOT):
        norm_prep_slot(k, xres[:, :, j * 512:(j + 1) * 512], R_x[j], g_attn, hgT, R_hg, sq, R_sq, rx[:, 4 * j:4 * j + 4], R_rx, 7)
        for bl in range(4):
            bg = 4 * j + bl
            tok = slice(bl * 128, (bl + 1) * 128)
            gt = slice(bg * 128, (bg + 1) * 128)
            for n in range(4):
                for ck in range(8):
                    k.MM(k.ps(n)[:, 0:448], hgT[:, ck, tok], Win[:, ck, n * 448:(n + 1) * 448], ck == 0, ck == 7, [R_hg, R_w], [RPS[n]])
            rxa = rx[:, bg:bg + 1]
            for n in range(3):
                k.ACT(uv[:, n * 448:(n + 1) * 448], k.ps(n)[:, 0:448], AF.Gelu, [RPS[n], R_rx], [R_uv], scale=rxa)
            k.ACT(uv[:, 1344:1536], k.ps(3)[:, 0:192], AF.Gelu, [RPS[3], R_rx], [R_uv], scale=rxa)
            k.ACT(zm, k.ps(3)[:, 192:448], AF.Identity, [RPS[3], R_rx], [R_zm], scale=rxa)
            v = uv[:, 768:1536]
            bst3 = bst.rearrange("p (a s) -> p a s", a=2)
            for a in range(2):
                k.S.op("dve", lambda e, a=a: e.bn_stats(out=bst3[:, a, :], in_=v[:, a * 384:(a + 1) * 384]), [R_uv], [R_bn])
            k.S.op("dve", lambda e: e.bn_aggr(out=mv[:, 0:2], in_=bst), [R_bn], [R_bn])
            k.rsqrt(mv[:, 2:3], mv[:, 1:2], 1.0, EPS, 1, [R_bn], [R_bn])
            k.TS("dve", vt, v, mv[:, 0:1], mv[:, 2:3], ALU.subtract, ALU.mult, [R_uv, R_bn], [R_vt])
            k.TT("pool", vt2, vt, lng, ALU.mult, [R_vt, k.Rg], [R_vt])
            k.TT("pool", vn, vt2, lnb, ALU.add, [R_vt, k.Rg], [R_vn])
            for g_ in range(8):
                b = 4 + g_ // 4
                c0 = (g_ % 4) * 96
                k.MM(k.ps(b)[:, c0:c0 + 96], wsp[:, g_, :], vn[:, g_ * 96:(g_ + 1) * 96], True, True, [R_w, R_vn], [RPS[b]])
            for g_ in range(8):
                b = 4 + g_ // 4
                c0 = (g_ % 4) * 96
                k.STT(y[:, g_ * 96:(g_ + 1) * 96], k.ps(b)[:, c0:c0 + 96], bsp[:, g_:g_ + 1], uv[:, g_ * 96:(g_ + 1) * 96], ALU.add, ALU.mult,
                      [RPS[b], k.Rg, R_uv], [R_y])
            memq_norm(k, zm, R_zm, gmq, qmn, R_qmn, tmpm, R_tmpm, stt, R_st)
            pb = k.ps_bf(6)
            for c in range(6):
                k.TR(pb[:, c * 128:(c + 1) * 128], y[:, c * 128:(c + 1) * 128], [R_y, k.Rc], [RPS[6]])
            for c in range(2):
                k.TR(pb[:, (6 + c) * 128:(7 + c) * 128], qmn[:, c * 128:(c + 1) * 128], [R_qmn, k.Rc], [RPS[6]])
            for c in range(6):
                k.CP("act", mixT[:, c, gt], pb[:, c * 128:(c + 1) * 128], [RPS[6]], [R_mix[c][j]])
            k.CP("act", qmT[:, :, gt], pb[:, 768:1024].rearrange("p (c t) -> p c t", c=2), [RPS[6]], [R_qmT])
    k.release(m)


class RopeTab:
    def __init__(self, k, nb):
        self.k = k
        self.nb = nb
        n = nb * 16
        self.posf = k.alloc(nb)
        self.ang = k.alloc(n)
        self.kq = k.alloc(n)
        self.ki = k.alloc(n, I32)
        self.y = k.alloc(n)
        self.m = k.alloc(n)
        self.cs2 = k.alloc(nb * 32).rearrange("p (b i) -> p b i", b=nb)
        self.sn2 = k.alloc(nb * 32).rearrange("p (b i) -> p b i", b=nb)
        self.R = Res("rope")

    def compute(self, posi, R_pos, invf):
        k, nb, R = self.k, self.nb, self.R
        ang, kq, ki, y, m, cs2, sn2 = self.ang, self.kq, self.ki, self.y, self.m, self.cs2, self.sn2
        ang3 = ang.rearrange("p (b i) -> p b i", b=nb)
        k.CP("dve", self.posf, posi, [R_pos], [R])
        k.TT("dve", ang3, self.posf.unsqueeze(2).broadcast_to([128, nb, 16]), invf.unsqueeze(1).broadcast_to([128, nb, 16]), ALU.mult,
             [R, k.Rg], [R])
        TWO_PI = 2.0 * np.pi
        C1 = 6.28125
        C2 = float(np.float32(TWO_PI - C1))
        k.TS("dve", kq, ang, float(1.0 / TWO_PI), None, ALU.mult, None, [R], [R])
        k.CP("dve", ki, kq, [R], [R])
        k.CP("dve", kq, ki, [R], [R])
        k.STT(y, kq, -C1, ang, ALU.mult, ALU.add, [R], [R])
        k.STT(y, kq, -C2, y, ALU.mult, ALU.add, [R], [R])

        def wrap(t):
            k.TS("dve", m, t, float(np.pi), None, ALU.is_gt, None, [R], [R])
            k.STT(t, m, -TWO_PI, t, ALU.mult, ALU.add, [R], [R])
            k.TS("dve", m, t, float(-np.pi), None, ALU.is_lt, None, [R], [R])
            k.STT(t, m, TWO_PI, t, ALU.mult, ALU.add, [R], [R])

        wrap(y)
        y3 = y.rearrange("p (b i) -> p b i", b=nb)
        k.ACT(sn2[:, :, 16:32], y3, AF.Sin, [R], [R])
        k.TS("dve", sn2[:, :, 0:16], sn2[:, :, 16:32], -1.0, None, ALU.mult, None, [R], [R])
        k.TS("dve", y, y, float(np.pi / 2), None, ALU.add, None, [R], [R])
        wrap(y)
        k.ACT(cs2[:, :, 0:16], y3, AF.Sin, [R], [R])
        k.CP("dve", cs2[:, :, 16:32], cs2[:, :, 0:16], [R], [R])


def phase_KV(k, c):
    RPS = k.RPS
    m0 = k.mark()
    R_w = Res("kvw")
    Win = k.alloc(8 * 288, BF16).rearrange("p (c n) -> p c n", c=8)
    w_in3 = chunked(c["w_in0"])
    for ck in range(8):
        k.DMA("pool", Win[:, ck, :], w_in3[:, ck, 384:672], (), [R_w])
    stage = k.alloc(768)
    R_stage = Res("kvstage")
    Wuk = load_w_scaled_bf16(k, c["w_uk"], 2, 768, c["g_kvlat"], R_w, stage, R_stage)
    Wuv = load_w_scaled_bf16(k, c["w_uv"], 2, 768, c["g_kvlat"], R_w, stage, R_stage)
    rt = RopeTab(k, 4)
    xs = k.alloc(8 * 512).rearrange("p (c t) -> p c t", c=8)
    hg = k.alloc(8 * 512, BF16).rearrange("p (c t) -> p c t", c=8)
    sq = k.alloc(8 * 512, BF16).rearrange("p (c t) -> p c t", c=8)
    R_xs, R_hg, R_sq = Res("kxs"), Res("khg"), Res("ksq")
    rx = k.alloc(4)
    valid = k.alloc(4)
    R_rx = Res("krx")
    z = k.alloc(4 * 288).rearrange("p (b n) -> p b n", b=4)
    sqz = k.alloc(4 * 288).rearrange("p (b n) -> p b n", b=4)
    R_z, R_sqz = Res("kz"), Res("ksqz")
    st = k.alloc(16)
    R_st = Res("kst")
    ckvn = k.alloc(4 * 256, BF16).rearrange("p (b n) -> p b n", b=4)
    R_ckvn = Res("kckvn")
    krg = k.alloc(4 * 32).rearrange("p (b n) -> p b n", b=4)
    krt = k.alloc(4 * 32).rearrange("p (b n) -> p b n", b=4)
    kru = k.alloc(4 * 32).rearrange("p (b n) -> p b n", b=4)
    krr = k.alloc(4 * 32, BF16).rearrange("p (b n) -> p b n", b=4)
    R_kr, R_krr = Res("kkr"), Res("kkrr")
    ckvT = k.alloc(2 * 512, BF16).rearrange("p (c t) -> p c t", c=2)
    R_ckvT = Res("kckvT")
    KTr_t = [k.alloc(512, BF16) for _ in range(2)]
    R_KTr = [Res("kKTr0"), Res("kKTr1")]
    KTn_t = [k.alloc(6 * 512, BF16).rearrange("p (c t) -> p c t", c=6) for _ in range(2)]
    R_KTn = [Res("kKTn0"), Res("kKTn1")]
    Vx_t = [k.alloc(12 * 4 * 65, BF16).rearrange("p (h b c) -> p h b c", h=12, b=4) for _ in range(2)]
    R_Vx = [Res("kVx0"), Res("kVx1")]
    sqk = k.alloc(768)
    R_sqk = Res("ksqk")
    ssk = k.alloc(32)
    R_ssk = Res("kssk")
    rk3 = c["rk"].rearrange("p (b h) -> p b h", h=12)
    xw3 = chunked(c["xw"])
    KTn_w3 = c["KTn_w"].rearrange("(c p) t -> p c t", p=128)
    Vx_w4 = c["Vx_w"].rearrange("h p (b c) -> p h b c", c=65)
    gk = c["gk"]
    posw = c["posw"]
    for t in range(32):
        tb = t % 2
        tcols = slice(t * 512, (t + 1) * 512)
        k.DMA("sp", xs, xw3[:, :, tcols], (), [R_xs])
        k.TT("pool", hg, xs, c["g_attn"].unsqueeze(2).broadcast_to([128, 8, 512]), ALU.mult, [R_xs, k.Rg], [R_hg])
        k.ACT(sq, xs, AF.Square, [R_xs], [R_sq])
        rt.compute(posw[:, 4 * t:4 * t + 4], c["R_pos"], c["invf"])
        pst = k.ps(7)
        for bl in range(4):
            for ck in range(8):
                k.MM(pst[:, bl:bl + 1], sq[:, ck, bl * 128:(bl + 1) * 128], k.ones_bf[:, 0:1], ck == 0, ck == 7, [R_sq, k.Rc], [RPS[7]])
        k.TS("dve", valid, pst[:, 0:4], 0.0, None, ALU.is_gt, None, [RPS[7]], [R_rx])
        k.rsqrt(rx, pst[:, 0:4], 1.0 / D, EPS, 4, [RPS[7]], [R_rx])
        for bl in range(4):
            for ck in range(8):
                k.MM(k.ps(bl)[:, 0:288], hg[:, ck, bl * 128:(bl + 1) * 128], Win[:, ck, :], ck == 0, ck == 7, [R_hg, R_w], [RPS[bl]])
            k.ACT(z[:, bl, :], k.ps(bl)[:, 0:288], AF.Identity, [RPS[bl], R_rx], [R_z], scale=rx[:, bl:bl + 1])
        k.ACT(sqz, z, AF.Square, [R_z], [R_sqz])
        k.S.op("dve", lambda e: e.tensor_reduce(out=st[:, 0:4], in_=sqz[:, :, 0:256], axis=AX.X, op=ALU.add), [R_sqz], [R_st])
        k.S.op("dve", lambda e: e.tensor_reduce(out=st[:, 4:8], in_=sqz[:, :, 256:288], axis=AX.X, op=ALU.add), [R_sqz], [R_st])
        k.rsqrt(st[:, 8:12], st[:, 0:4], 1.0 / 256, EPS, 4, [R_st], [R_st])
        k.TT("dve", ckvn, z[:, :, 0:256], st[:, 8:12].unsqueeze(2).broadcast_to([128, 4, 256]), ALU.mult, [R_z, R_st], [R_ckvn])
        k.TT("pool", krg, z[:, :, 256:288], gk[:, 64:96].unsqueeze(1).broadcast_to([128, 4, 32]), ALU.mult, [R_z, k.Rg], [R_kr])
        k.TT("pool", krt, krg, rt.cs2, ALU.mult, [R_kr, rt.R], [R_kr])
        k.TT("pool", kru[:, :, 0:16], krg[:, :, 16:32], rt.sn2[:, :, 0:16], ALU.mult, [R_kr, rt.R], [R_kr])
        k.TT("pool", kru[:, :, 16:32], krg[:, :, 0:16], rt.sn2[:, :, 16:32], ALU.mult, [R_kr, rt.R], [R_kr])
        k.TT("pool", krr, krt, kru, ALU.add, [R_kr], [R_krr])
        pb4, pb5 = k.ps_bf(4), k.ps_bf(5)
        for ck in range(2):
            for bl in range(4):
                k.TR(pb4[:, ck * 512 + bl * 128:ck * 512 + (bl + 1) * 128], ckvn[:, bl, ck * 128:(ck + 1) * 128], [R_ckvn, k.Rc], [RPS[4]])
        for bl in range(4):
            k.TR(pb5[0:32, bl * 128:(bl + 1) * 128], krr[:, bl, :], [R_krr, k.Rc], [RPS[5]])
        k.CP("act", ckvT, pb4.rearrange("p (c t) -> p c t", c=2), [RPS[4]], [R_ckvT])
        k.CP("act", KTr_t[tb][0:32, :], pb5[0:32, 0:512], [RPS[5]], [R_KTr[tb]])
        k.DMA("sp", c["KTr_w"][:, tcols], KTr_t[tb][0:32, :], [R_KTr[tb]], [c["R_KTr_w"]])
        for bl in range(4):
            bk = [0, 1, 2, 3] if bl % 2 == 0 else [4, 5, 6, 7]
            tok = slice(bl * 128, (bl + 1) * 128)
            for n in range(2):
                for ck in range(2):
                    k.MM(k.ps(bk[n])[:, 0:384], ckvT[:, ck, tok], Wuk[:, ck, n * 384:(n + 1) * 384], ck == 0, ck == 1, [R_ckvT, R_w], [RPS[bk[n]]])
                k.ACT(sqk[:, n * 384:(n + 1) * 384], k.ps(bk[n])[:, 0:384], AF.Square, [RPS[bk[n]]], [R_sqk])
            for n in range(2):
                for ck in range(2):
                    k.MM(k.ps(bk[2 + n])[:, 0:384], ckvT[:, ck, tok], Wuv[:, ck, n * 384:(n + 1) * 384], ck == 0, ck == 1, [R_ckvT, R_w], [RPS[bk[2 + n]]])
                k.CP("dve", Vx_t[tb][:, 6 * n:6 * n + 6, bl, 0:64], k.ps(bk[2 + n])[:, 0:384].rearrange("p (h d) -> p h d", h=6),
                     [RPS[bk[2 + n]]], [R_Vx[tb]])
            k.S.op("dve", lambda e: e.tensor_reduce(out=ssk[:, 0:12], in_=sqk.rearrange("p (h d) -> p h d", h=12), axis=AX.X, op=ALU.add),
                   [R_sqk], [R_ssk])
            k.TS("dve", ssk[:, 16:28], ssk[:, 0:12], st[:, 4 + bl:5 + bl], None, ALU.add, None, [R_ssk, R_st], [R_ssk])
            k.rsqrt(rk3[:, 4 * t + bl, :], ssk[:, 16:28], 1.0, 96 * EPS, 12, [R_ssk], [c["R_rk"]])
        k.CP("pool", Vx_t[tb][:, :, :, 64], valid.unsqueeze(1).broadcast_to([128, 12, 4]), [R_rx], [R_Vx[tb]])
        for i in range(6):
            b = (i + 2) % 8
            for ck in range(2):
                k.MM(k.ps(b), Wuk[:, ck, i * 128:(i + 1) * 128], ckvT[:, ck, :], ck == 0, ck == 1, [R_w, R_ckvT], [RPS[b]])
            k.CP("act" if i % 2 == 0 else "dve", KTn_t[tb][:, i, :], k.ps(b), [RPS[b]], [R_KTn[tb]])
        k.DMA("sp", KTn_w3[:, :, tcols], KTn_t[tb], [R_KTn[tb]], [c["R_KTn_w"]])
        for h0 in range(0, 12, 4):
            k.DMA("sp", Vx_w4[:, h0:h0 + 4, 4 * t:4 * t + 4, :], Vx_t[tb][:, h0:h0 + 4, :, :], [R_Vx[tb]], [c["R_Vx_w"]])
    k.release(m0)


def phase_A_fused(k, c):
    RPS = k.RPS
    m0 = k.mark()
    R_w = Res("aw")
    Win = load_w_bf16(k, c["w_in0"], 8, 928, R_w)
    stage = k.alloc(1152)
    R_stage = Res("astage")
    Wuq = load_w_scaled_bf16(k, c["w_uq"], 3, 1152, c["g_qlat"], R_w, stage, R_stage)
    gq, gk, gmq = c["gq"], c["gk"], c["gmq"]
    Gq = k.alloc(96)
    k.CP("dve", Gq, gq, [k.Rg], [k.Rg])
    k.TT("dve", Gq[:, 0:64], gq[:, 0:64], gk[:, 0:64], ALU.mult, [k.Rg], [k.Rg])
    qT, qmT, R_qT, R_qmT = c["qT"], c["qmT"], c["R_qT"], c["R_qmT"]
    rt = RopeTab(k, 4)
    xs = k.alloc(8 * 512).rearrange("p (c t) -> p c t", c=8)
    hgT = k.alloc(8 * 512, BF16).rearrange("p (c t) -> p c t", c=8)
    sq = k.alloc(8 * 512, BF16).rearrange("p (c t) -> p c t", c=8)
    R_xs, R_hg, R_sq = Res("axs"), Res("ahg"), Res("asq")
    rx = k.alloc(4)
    R_rx = Res("arx")
    z = k.alloc(928)
    R_z = Res("az")
    stt = k.alloc(16)
    R_st = Res("ast")
    qln = k.alloc(384, BF16)
    qmn = k.alloc(256, BF16)
    R_tm = Res("atm")
    tmpm = k.alloc(256)
    R_tmpm = Res("atmpm")
    qlT = k.alloc(3 * 128, BF16).rearrange("p (c t) -> p c t", c=3)
    R_qlT = Res("aqlT")
    sqq = k.alloc(1152)
    R_sqq = Res("asqq")
    ssq = k.alloc(32)
    R_ssq = Res("assq")
    qn = k.alloc(1152)
    R_qn = Res("aqn")
    qg = k.alloc(1152)
    R_qg = Res("aqg")
    qt = k.alloc(12 * 32)
    qu = k.alloc(12 * 32)
    R_qtu = Res("aqtu")
    qfin = k.alloc(12 * 96, BF16).rearrange("p (h d) -> p h d", h=12)
    R_qfin = Res("aqfin")
    xw3 = chunked(c["xw"])
    for j in range(NSLOT):
        wt = 8 * j + 7
        k.DMA("sp", xs, xw3[:, :, wt * 512:(wt + 1) * 512], (), [R_xs])
        norm_prep_slot(k, xs, R_xs, c["g_attn"], hgT, R_hg, sq, R_sq, rx, R_rx, 7)
        rt.compute(c["posw"][:, 4 * wt:4 * wt + 4], c["R_pos"], c["invf"])
        for bl in range(4):
            bg = 4 * j + bl
            tok = slice(bl * 128, (bl + 1) * 128)
            for n, (c0, c1) in enumerate([(0, 512), (512, 928)]):
                for ck in range(8):
                    k.MM(k.ps(n)[:, 0:c1 - c0], hgT[:, ck, tok], Win[:, ck, c0:c1], ck == 0, ck == 7, [R_hg, R_w], [RPS[n]])
                k.ACT(z[:, c0:c1], k.ps(n)[:, 0:c1 - c0], AF.Identity, [RPS[n], R_rx], [R_z], scale=rx[:, bl:bl + 1])
            k.ACT(sqq[:, 0:384], z[:, 0:384], AF.Square, [R_z], [R_sqq])
            k.S.op("dve", lambda e: e.tensor_reduce(out=stt[:, 0:1], in_=sqq[:, 0:384], axis=AX.X, op=ALU.add), [R_sqq], [R_st])
            k.rsqrt(stt[:, 4:5], stt[:, 0:1], 1.0 / 384, EPS, 1, [R_st], [R_st])
            k.TS("dve", qln, z[:, 0:384], stt[:, 4:5], None, ALU.mult, None, [R_z, R_st], [R_tm])
            memq_norm(k, z[:, 672:928], R_z, gmq, qmn, R_tm, tmpm, R_tmpm, stt[:, 8:16], R_st)
            pb = k.ps_bf(2)
            for cc in range(3):
                k.TR(pb[:, cc * 128:(cc + 1) * 128], qln[:, cc * 128:(cc + 1) * 128], [R_tm, k.Rc], [RPS[2]])
            for cc in range(2):
                k.TR(pb[:, (3 + cc) * 128:(4 + cc) * 128], qmn[:, cc * 128:(cc + 1) * 128], [R_tm, k.Rc], [RPS[2]])
            k.CP("act", qlT, pb[:, 0:384].rearrange("p (c t) -> p c t", c=3), [RPS[2]], [R_qlT])
            k.CP("act", qmT[:, :, bg * 128:(bg + 1) * 128], pb[:, 384:640].rearrange("p (c t) -> p c t", c=2), [RPS[2]], [R_qmT])
            for n in range(3):
                for ck in range(3):
                    k.MM(k.ps(3 + n)[:, 0:384], qlT[:, ck, :], Wuq[:, ck, n * 384:(n + 1) * 384], ck == 0, ck == 2, [R_qlT, R_w], [RPS[3 + n]])
                k.ACT(sqq[:, n * 384:(n + 1) * 384], k.ps(3 + n)[:, 0:384], AF.Square, [RPS[3 + n]], [R_sqq])
            k.S.op("dve", lambda e: e.tensor_reduce(out=ssq[:, 0:12], in_=sqq.rearrange("p (h d) -> p h d", h=12), axis=AX.X, op=ALU.add),
                   [R_sqq], [R_ssq])
            k.rsqrt(ssq[:, 16:28], ssq[:, 0:12], 1.0 / 96, EPS, 12, [R_ssq], [R_ssq])
            for n in range(3):
                k.TT("dve", qn[:, n * 384:(n + 1) * 384].rearrange("p (h d) -> p h d", h=4),
                     k.ps(3 + n)[:, 0:384].rearrange("p (h d) -> p h d", h=4),
                     ssq[:, 16 + 4 * n:20 + 4 * n].unsqueeze(2).broadcast_to([128, 4, 96]), ALU.mult, [RPS[3 + n], R_ssq], [R_qn])
            qn3 = qn.rearrange("p (h d) -> p h d", h=12)
            qg3 = qg.rearrange("p (h d) -> p h d", h=12)
            k.TT("pool", qg3, qn3, Gq.unsqueeze(1).broadcast_to([128, 12, 96]), ALU.mult, [R_qn, k.Rg], [R_qg])
            qt3 = qt.rearrange("p (h d) -> p h d", h=12)
            qu3 = qu.rearrange("p (h d) -> p h d", h=12)
            k.TT("pool", qt3, qg3[:, :, 64:96], rt.cs2[:, bl, :].unsqueeze(1).broadcast_to([128, 12, 32]), ALU.mult, [R_qg, rt.R], [R_qtu])
            k.TT("pool", qu3[:, :, 0:16], qg3[:, :, 80:96], rt.sn2[:, bl, 0:16].unsqueeze(1).broadcast_to([128, 12, 16]), ALU.mult,
                 [R_qg, rt.R], [R_qtu])
            k.TT("pool", qu3[:, :, 16:32], qg3[:, :, 64:80], rt.sn2[:, bl, 16:32].unsqueeze(1).broadcast_to([128, 12, 16]), ALU.mult,
                 [R_qg, rt.R], [R_qtu])
            k.TT("dve", qfin[:, :, 64:96], qt3, qu3, ALU.add, [R_qtu], [R_qfin])
            k.CP("dve", qfin[:, :, 0:64], qg3[:, :, 0:64], [R_qg], [R_qfin])
            for h in range(12):
                bk = 6 if h < 8 else 7
                hh = h % 8
                k.TR(k.ps_bf(bk)[0:96, hh * 128:(hh + 1) * 128], qfin[:, h, :], [R_qfin, k.Rc], [RPS[bk]])
            k.CP("act", qT[0:96, 0:8, bg * 128:(bg + 1) * 128], k.ps_bf(6)[0:96, :].rearrange("p (h t) -> p h t", h=8), [RPS[6]], [R_qT])
            k.CP("act", qT[0:96, 8:12, bg * 128:(bg + 1) * 128], k.ps_bf(7)[0:96, 0:512].rearrange("p (h t) -> p h t", h=4), [RPS[7]], [R_qT])
    k.release(m0)


XRES_WORDS = 8 * TOWN


def phase_rest(k, fz=None):
    S = k.S
    k.final_dmas = []
    d = {}
    if fz is None:
        k.Rg = Res("gains")
        d["xT"] = k.inp("xT", [D, TOWN])
        KTn_w = k.inp("KTn_w", [768, SEQ], BF16)
        KTr_w = k.inp("KTr_w", [32, SEQ], BF16)
        Vx_w = k.inp("Vx_w", [12, 128, 128 * 65], BF16)
        rk_w = k.inp("rk_w", [128, 128 * 12])
        qT_i = k.inp("qT_i", [96, 12 * TOWN], BF16)
        qmT_i = k.inp("qmT_i", [128, 2 * TOWN], BF16)
        xT3_ = chunked(d["xT"])
        xsrc = lambda j: xT3_[:, :, j * 512:(j + 1) * 512]
    else:
        KTn_w, KTr_w, Vx_w = fz["KTn_w"], fz["KTr_w"], fz["Vx_w"]
        xw3_ = chunked(fz["xw"])
        xsrc = lambda j: xw3_[:, :, (8 * j + 7) * 512:(8 * j + 8) * 512]
    d["memT"] = k.inp("memT", [D, 256])
    for L in (0, 1):
        d[f"g_mem{L}"] = k.inp(f"g_mem{L}", [128, 8])
        d[f"w_mkv{L}"] = k.inp(f"w_mkv{L}", [D, 512])
        d[f"g_mk{L}"] = k.inp(f"g_mk{L}", [128, 64])
        d[f"w_out{L}"] = k.inp(f"w_out{L}", [D, D])
        d[f"g_ffn{L}"] = k.inp(f"g_ffn{L}", [128, 8])
        d[f"w_gate{L}"] = k.inp(f"w_gate{L}", [D, DFF])
        d[f"w_up{L}"] = k.inp(f"w_up{L}", [D, DFF])
        d[f"w_down{L}"] = k.inp(f"w_down{L}", [DFF, D])
    d["g_attn1"] = k.inp("g_attn1", [128, 8])
    d["w_in1"] = k.inp("w_in1", [D, 1792])
    d["ln_g"] = k.inp("ln_g", [128, 768])
    d["ln_b"] = k.inp("ln_b", [128, 768])
    d["w_spT"] = k.inp("w_spT", [128, 8 * 128])
    d["b_sp"] = k.inp("b_sp", [128, 8])
    d["g_mq1"] = k.inp("g_mq1", [128, 64])
    yT_o = k.outp("yT", [D, TOWN])

    xres = k.arena[:, ARENA_WORDS - XRES_WORDS:ARENA_WORDS].rearrange("p (c t) -> p c t", c=8)
    R_x = [Res(f"x{j}") for j in range(NSLOT)]
    LIMIT_FREE = ARENA_WORDS
    LIMIT_X = ARENA_WORDS - XRES_WORDS
    base = k.mark() if fz is None else fz["base0"]

    mixT = k.alloc(8 * TOWN, BF16).rearrange("p (c t) -> p c t", c=8)
    R_mix = [[Res(f"mix{c}_{j}") for j in range(NSLOT)] for c in range(8)]
    KmT = k.alloc(2 * 256, BF16).rearrange("p (c t) -> p c t", c=2)
    Vmx = k.alloc(2 * 4 * 65, BF16).rearrange("p (m h c) -> p m h c", m=2, h=4)
    R_KmT, R_Vmx = Res("KmT"), Res("Vmx")
    mem_kv_prep(k, "m0", d["memT"], d["g_mem0"], d["w_mkv0"], d["g_mk0"], KmT, R_KmT, Vmx, R_Vmx)
    ckpt(k, "memkv0")
    m_at = k.mark()
    if fz is None:
        qT = k.alloc(12 * TOWN, BF16).rearrange("p (h t) -> p h t", h=12)
        qmT = k.alloc(2 * TOWN, BF16).rearrange("p (c t) -> p c t", c=2)
        rk = k.alloc(128 * 12)
        R_qT, R_qmT, R_rk = Res("qT"), Res("qmT"), Res("rk")
        k.DMA("sp", qT[0:96].rearrange("p h t -> p (h t)"), qT_i, (), [R_qT])
        k.DMA("sp", qmT.rearrange("p c t -> p (c t)"), qmT_i, (), [R_qmT])
        k.DMA("sp", rk, rk_w, (), [R_rk])
    else:
        qT, qmT, rk = fz["qT"], fz["qmT"], fz["rk"]
        R_qT, R_qmT, R_rk = fz["R_qT"], fz["R_qmT"], fz["R_rk"]
    at = Attn(k, mixT, R_mix)
    at.mem_attention(qmT, R_qmT, KmT, R_KmT, Vmx, R_Vmx)
    ckpt(k, "memattn0")
    at.causal_attention(qT, R_qT, rk, R_rk, KTn_w, KTr_w, Vx_w)
    assert k.off <= LIMIT_FREE
    ckpt(k, "attn0")
    k.release(m_at)
    wout_residual(k, "l0", d["w_out0"], mixT, R_mix, xres, R_x, xsrc)
    assert k.off <= LIMIT_X
    ckpt(k, "wout0")
    k.release(base)
    ffn(k, "f0", xres, R_x, d["g_ffn0"], d["w_gate0"], d["w_up0"], d["w_down0"])
    ckpt(k, "ffn0")
    mixT = k.alloc(8 * TOWN, BF16).rearrange("p (c t) -> p c t", c=8)
    R_mix = [[Res(f"mixb{c}_{j}") for j in range(NSLOT)] for c in range(8)]
    qmT = k.alloc(2 * TOWN, BF16).rearrange("p (c t) -> p c t", c=2)
    R_qmT = Res("qmT1")
    KmT = k.alloc(2 * 256, BF16).rearrange("p (c t) -> p c t", c=2)
    Vmx = k.alloc(2 * 4 * 65, BF16).rearrange("p (m h c) -> p m h c", m=2, h=4)
    R_KmT, R_Vmx = Res("KmT1"), Res("Vmx1")
    mem_kv_prep(k, "m1", d["memT"], d["g_mem1"], d["w_mkv1"], d["g_mk1"], KmT, R_KmT, Vmx, R_Vmx)
    sgu_layer1(k, xres, R_x, mixT, R_mix, qmT, R_qmT, d)
    ckpt(k, "sgu")
    m_at = k.mark()
    at = Attn(k, mixT, R_mix)
    at.mem_attention(qmT, R_qmT, KmT, R_KmT, Vmx, R_Vmx)
    k.release(m_at)
    wout_residual(k, "l1", d["w_out1"], mixT, R_mix, xres, R_x, None)
    ckpt(k, "wout1")
    k.release(base)
    ffn(k, "f1", xres, R_x, d["g_ffn1"], d["w_gate1"], d["w_up1"], d["w_down1"])
    ckpt(k, "ffn1")
    yT3 = chunked(yT_o)
    for j in range(NSLOT):
        cols = slice(j * 512, (j + 1) * 512)
        k.final_dmas.append(k.DMA("sp", yT3[:, :, cols], xres[:, :, cols], [R_x[j]], ()))


def build_fused_body(k):
    nc = k.nc
    k.final_dmas = []
    k.Rg = Res("gains")
    c = {}
    c["xw"] = k.inp("xw", [D, SEQ])
    posw_d = k.inp("posw", [128, 128], I32)
    invf_d = k.inp("invf", [128, 16])
    c["w_in0"] = k.inp("w_in0", [D, 928])
    c["w_uq"] = k.inp("w_uq", [384, 1152])
    c["w_uk"] = k.inp("w_uk", [256, 768])
    c["w_uv"] = k.inp("w_uv", [256, 768])
    small = {}
    for name, w in [("g_attn0", 8), ("g_qlat", 3), ("g_kvlat", 2), ("gq_tile", 96), ("gk_tile", 96), ("g_mq0", 64)]:
        dd = k.inp(name, [128, w])
        t = k.alloc(w)
        k.DMA("sp", t, dd, (), [k.Rg])
        small[name] = t
    c["g_attn"], c["g_qlat"], c["g_kvlat"] = small["g_attn0"], small["g_qlat"], small["g_kvlat"]
    c["gq"], c["gk"], c["gmq"] = small["gq_tile"], small["gk_tile"], small["g_mq0"]
    c["invf"] = k.alloc(16)
    k.DMA("sp", c["invf"], invf_d, (), [k.Rg])
    c["posw"] = k.alloc(128, I32)
    c["R_pos"] = Res("posw")
    k.DMA("sp", c["posw"], posw_d, (), [c["R_pos"]])
    c["KTn_w"] = nc.dram_tensor("KTn_scr", [768, SEQ], BF16).ap()
    c["KTr_w"] = nc.dram_tensor("KTr_scr", [32, SEQ], BF16).ap()
    c["Vx_w"] = nc.dram_tensor("Vx_scr", [12, 128, 128 * 65], BF16).ap()
    c["R_KTn_w"], c["R_KTr_w"], c["R_Vx_w"] = Res("KTn_w"), Res("KTr_w"), Res("Vx_w")
    c["base0"] = k.mark()
    c["qT"] = k.alloc(12 * TOWN, BF16).rearrange("p (h t) -> p h t", h=12)
    c["qmT"] = k.alloc(2 * TOWN, BF16).rearrange("p (c t) -> p c t", c=2)
    c["rk"] = k.alloc(128 * 12)
    c["R_qT"], c["R_qmT"], c["R_rk"] = Res("qT"), Res("qmT"), Res("rk")
    phase_KV(k, c)
    ckpt(k, "kv")
    phase_A_fused(k, c)
    ckpt(k, "afused")
    phase_rest(k, fz=c)


_CACHE = {}


def _get(mode):
    if mode not in _CACHE:
        _CACHE[mode] = build(mode)
    return _CACHE[mode]


def own_tokens(c):
    idx = []
    for j in range(NSLOT):
        g = 8 * j + c
        idx.append(np.arange(g * 512, (g + 1) * 512))
    return np.concatenate(idx)


def rep128(v):
    return np.ascontiguousarray(np.broadcast_to(np.asarray(v, np.float32)[None, :], (128, len(v))))


def pchunk(v):
    v = np.asarray(v, np.float32)
    return np.ascontiguousarray(v.reshape(-1, 128).T)


def l1_inputs(inp, c):
    tok = own_tokens(c)
    x = inp["x"][0]
    pos = inp["positions"][0][tok].astype(np.int32)
    inv_freq = (10000.0 ** (-np.arange(0, 32, 2, dtype=np.float32) / 32)).astype(np.float32)
    w_ukv = inp["l0_w_ukv"]
    return {
        "xT": np.ascontiguousarray(x[tok].T),
        "pos": np.ascontiguousarray(pos.reshape(16, 128).T),
        "invf": rep128(inv_freq),
        "w_in0": inp["l0_w_in"],
        "g_attn0": pchunk(inp["l0_attn_norm"]),
        "w_uq": np.ascontiguousarray(inp["l0_w_uq"].reshape(384, 1152)),
        "g_qlat": pchunk(inp["l0_q_lat_norm"]),
        "w_uk": np.ascontiguousarray(w_ukv[:, :, 0:64].reshape(256, 768)),
        "w_uv": np.ascontiguousarray(w_ukv[:, :, 64:128].reshape(256, 768)),
        "g_kvlat": pchunk(inp["l0_kv_lat_norm"]),
        "gq_tile": rep128(inp["l0_q_norm"]),
        "gk_tile": rep128(inp["l0_k_norm"]),
        "g_mq0": rep128(inp["l0_mq_norm"]),
    }


def l2_weights(inp):
    w = {"memT": np.ascontiguousarray(inp["mem"][0].T)}
    for L in (0, 1):
        p = f"l{L}_"
        w[f"g_mem{L}"] = pchunk(inp[p + "mem_norm"])
        w[f"w_mkv{L}"] = inp[p + "w_mem_kv"]
        w[f"g_mk{L}"] = rep128(inp[p + "mk_norm"])
        w[f"w_out{L}"] = inp[p + "w_out"]
        w[f"g_ffn{L}"] = pchunk(inp[p + "ffn_norm"])
        w[f"w_gate{L}"] = inp[p + "w_gate"]
        w[f"w_up{L}"] = inp[p + "w_up"]
        w[f"w_down{L}"] = inp[p + "w_down"]
    w["g_attn1"] = pchunk(inp["l1_attn_norm"])
    w["w_in1"] = inp["l1_w_in"]
    w["ln_g"] = rep128(inp["l1_sgu_ln_g"])
    w["ln_b"] = rep128(inp["l1_sgu_ln_b"])
    w["w_spT"] = np.ascontiguousarray(inp["l1_w_spatial"].transpose(2, 0, 1).reshape(128, 8 * 128))
    w["b_sp"] = np.ascontiguousarray(inp["l1_b_spatial"].T)
    w["g_mq1"] = rep128(inp["l1_mq_norm"])
    return {k_: np.ascontiguousarray(np.asarray(v, np.float32)) for k_, v in w.items()}


def gather_payload(res):
    NCH = 7 + 32
    bf = ml_dtypes.bfloat16
    KTn = np.zeros((768, NCH * 512), bf)
    KTr = np.zeros((32, NCH * 512), bf)
    Vx = np.zeros((12, 128, NCH * 4, 65), bf)
    rk = np.zeros((128, NCH * 4, 12), np.float32)
    for r in range(NCORES):
        a = np.asarray(res[r]["KTn_o"])
        b = np.asarray(res[r]["KTr_o"])
        v = np.asarray(res[r]["Vx_o"]).reshape(12, 128, 16, 65)
        q = np.asarray(res[r]["rk_o"]).reshape(128, 16, 12)
        for j in range(NSLOT):
            g = 8 * j + r
            KTn[:, (7 + g) * 512:(8 + g) * 512] = a[:, j * 512:(j + 1) * 512]
            KTr[:, (7 + g) * 512:(8 + g) * 512] = b[:, j * 512:(j + 1) * 512]
            Vx[:, :, (7 + g) * 4:(8 + g) * 4, :] = v[:, :, 4 * j:4 * j + 4, :]
            rk[:, (7 + g) * 4:(8 + g) * 4, :] = q[:, 4 * j:4 * j + 4, :]
    return KTn, KTr, Vx, rk


def kernel_unfused(**inp):
    inp = {k_: np.asarray(v) for k_, v in inp.items()}
    k1 = _get("L1")
    maps = [l1_inputs(inp, c) for c in range(NCORES)]
    r1 = run_bass_kernel_spmd(k1.nc, maps, core_ids=list(range(NCORES))).results
    KTn, KTr, Vx, rk = gather_payload(r1)
    k2 = _get("L2")
    w = l2_weights(inp)
    maps2 = []
    for c in range(NCORES):
        m = dict(w)
        m["xT"] = maps[c]["xT"]
        m["KTn_w"] = np.ascontiguousarray(KTn[:, c * 512:(c + 32) * 512])
        m["KTr_w"] = np.ascontiguousarray(KTr[:, c * 512:(c + 32) * 512])
        m["Vx_w"] = np.ascontiguousarray(Vx[:, :, c * 4:(c + 32) * 4, :]).reshape(12, 128, 128 * 65)
        m["rk_w"] = np.ascontiguousarray(rk[:, c * 4:(c + 32) * 4, :]).reshape(128, 128 * 12)
        m["qT_i"] = np.asarray(r1[c]["qT_o"])
        m["qmT_i"] = np.asarray(r1[c]["qmT_o"])
        maps2.append(m)
    r2 = run_bass_kernel_spmd(k2.nc, maps2, core_ids=list(range(NCORES))).results
    out = np.zeros((1, SEQ, D), np.float32)
    for c in range(NCORES):
        out[0, own_tokens(c), :] = np.asarray(r2[c]["yT"]).T
    return out


def fused_inputs(inp, c, w):
    x = inp["x"][0]
    pos = inp["positions"][0].astype(np.int32)
    xw = np.zeros((D, SEQ), np.float32)
    pw = np.zeros((SEQ,), np.int32)
    g0 = c - 7
    lo = max(0, g0)
    n = (g0 + 32 - lo) * 512
    dst = (lo - g0) * 512
    xw[:, dst:dst + n] = x[lo * 512:lo * 512 + n].T
    pw[dst:dst + n] = pos[lo * 512:lo * 512 + n]
    m = dict(w)
    m["xw"] = xw
    m["posw"] = np.ascontiguousarray(pw.reshape(128, 128).T)
    return m


def kernel(**inp):
    inp = {k_: np.asarray(v) for k_, v in inp.items()}
    kf = _get("fused")
    w = l2_weights(inp)
    l1 = l1_inputs(inp, 0)
    for name in ("invf", "w_in0", "g_attn0", "w_uq", "g_qlat", "w_uk", "w_uv", "g_kvlat", "gq_tile", "gk_tile", "g_mq0"):
        w[name] = l1[name]
    maps = [fused_inputs(inp, c, w) for c in range(NCORES)]
    r = run_bass_kernel_spmd(kf.nc, maps, core_ids=list(range(NCORES))).results
    out = np.zeros((1, SEQ, D), np.float32)
    for c in range(NCORES):
        out[0, own_tokens(c), :] = np.asarray(r[c]["yT"]).T
    return out
```

```python
import contextlib
import numpy as np
import ml_dtypes
import concourse.bass as bass
import concourse.mybir as mybir
from concourse.bass_utils import run_bass_kernel_spmd

F32 = mybir.dt.float32
BF16 = mybir.dt.bfloat16
I32 = mybir.dt.int32
AF = mybir.ActivationFunctionType
ALU = mybir.AluOpType
AX = mybir.AxisListType

NCORES = 8
SEQ = 16384
D = 1024
TOWN = 2048
NSLOT = 4
DFF = 2816
NF = 22
EPS = 1e-6
MASKNEG = -30000.0


class Res:
    __slots__ = ("name", "last_w", "readers", "dma_w")

    def __init__(self, name=""):
        self.name = name
        self.last_w = None
        self.readers = []
        self.dma_w = []


class Op:
    __slots__ = ("eng", "fn", "deps", "is_dma", "needed", "token")

    def __init__(self, eng, fn, is_dma):
        self.eng = eng
        self.fn = fn
        self.deps = []
        self.is_dma = is_dma
        self.needed = False
        self.token = None


class Sched:
    ENGS = ("pe", "act", "dve", "pool", "sp")
    NDMA = 24

    def __init__(self, nc):
        self.nc = nc
        self.ops = {e: [] for e in self.ENGS}
        self.all_ops = []
        self.dma_ops = []
        self.cc_ops = []
        self.fence = []

    def _collect(self, op, reads, writes):
        deps = []
        for r in reads:
            if r.last_w is not None:
                deps.append(r.last_w)
            deps.extend(r.dma_w)
        for w in writes:
            if w.last_w is not None:
                deps.append(w.last_w)
            deps.extend(w.readers)
            if not op.is_dma:
                deps.extend(w.dma_w)
        deps.extend(self.fence)
        out = []
        seen = set()
        for d in deps:
            if id(d) in seen or d is op:
                continue
            seen.add(id(d))
            if (not d.is_dma) and (not op.is_dma) and d.eng == "pe" and op.eng == "pe":
                continue
            out.append(d)
        op.deps = out
        for r in reads:
            r.readers.append(op)
        for w in writes:
            if op.is_dma:
                if w.readers:
                    w.dma_w = [op]
                else:
                    w.dma_w.append(op)
                w.readers = []
            else:
                w.last_w = op
                w.dma_w = []
                w.readers = []

    def op(self, eng, fn, reads=(), writes=()):
        o = Op(eng, fn, False)
        self._collect(o, reads, writes)
        self.ops[eng].append(o)
        self.all_ops.append(o)
        return o

    def dma(self, eng, fn, reads=(), writes=()):
        o = Op(eng, fn, True)
        self._collect(o, reads, writes)
        k = len(self.dma_ops)
        if k >= self.NDMA:
            prev = self.dma_ops[k - self.NDMA]
            if prev not in o.deps:
                o.deps.append(prev)
        o.token = ("dma", k % self.NDMA, 16 * (k // self.NDMA + 1))
        o.needed = True
        self.dma_ops.append(o)
        self.ops[eng].append(o)
        self.all_ops.append(o)
        return o

    def cc(self, fn, reads=(), writes=()):
        o = Op("pool", fn, True)
        self._collect(o, reads, writes)
        o.token = ("cc", len(self.cc_ops), 16)
        o.needed = True
        self.cc_ops.append(o)
        self.ops["pool"].append(o)
        self.all_ops.append(o)
        return o

    def barrier(self):
        fence = []
        for e in self.ENGS:
            comp = [o for o in self.ops[e] if not o.is_dma]
            if comp:
                fence.append(comp[-1])
        fence.extend(self.dma_ops[-self.NDMA:])
        self.fence = fence

    def emit(self, final_waits=()):
        nc = self.nc
        for o in self.all_ops:
            for d in o.deps:
                d.needed = True
        for o in final_waits:
            o.needed = True
        for e in self.ENGS:
            cnt = 0
            for o in self.ops[e]:
                if o.is_dma:
                    continue
                if o.needed:
                    cnt += 1
                    o.token = ("eng", e, cnt)
        with contextlib.ExitStack() as st:
            esem = {e: st.enter_context(nc.semaphore(f"s_{e}")) for e in self.ENGS}
            dsem = [st.enter_context(nc.semaphore(f"s_dma{i}")) for i in range(self.NDMA)]
            csem = [st.enter_context(nc.semaphore(f"s_cc{i}")) for i in range(len(self.cc_ops))]
            block = st.enter_context(nc.Block())

            def semof(tok):
                if tok[0] == "eng":
                    return esem[tok[1]], tok[2], ("eng", tok[1])
                if tok[0] == "cc":
                    return csem[tok[1]], tok[2], ("cc", tok[1])
                return dsem[tok[1]], tok[2], ("dma", tok[1])

            def run(e, eh, extra_final=False):
                waited = {}
                for o in self.ops[e]:
                    for d in o.deps:
                        sem, val, key = semof(d.token)
                        if waited.get(key, 0) >= val:
                            continue
                        waited[key] = val
                        eh.wait_ge(sem, val)
                    inst = o.fn(eh)
                    if o.is_dma and o.token[0] == "cc":
                        inst.then_inc(csem[o.token[1]], 16)
                    elif o.is_dma:
                        inst.then_inc(dsem[o.token[1]], 16)
                    elif o.needed:
                        inst.then_inc(esem[e], 1)
                if extra_final:
                    for o in final_waits:
                        sem, val, key = semof(o.token)
                        if waited.get(key, 0) >= val:
                            continue
                        waited[key] = val
                        eh.wait_ge(sem, val)

            @block.tensor
            def _(eh):
                run("pe", eh)

            @block.scalar
            def _(eh):
                run("act", eh)

            @block.vector
            def _(eh):
                run("dve", eh)

            @block.gpsimd
            def _(eh):
                run("pool", eh)

            @block.sync
            def _(eh):
                run("sp", eh, extra_final=True)


ARENA_WORDS = 53000


class K:
    def __init__(self, mode):
        self.mode = mode
        self.nc = bass.Bass("TRN2", target_bir_lowering=False)
        self.S = Sched(self.nc)
        self.din = {}
        self.dout = {}
        self.off = 0
        self.rot = {}

    def inp(self, name, shape, dt=F32):
        ap = self.nc.dram_tensor(name, list(shape), dt, kind="ExternalInput").ap()
        self.din[name] = ap
        return ap

    def outp(self, name, shape, dt=F32):
        ap = self.nc.dram_tensor(name, list(shape), dt, kind="ExternalOutput").ap()
        self.dout[name] = ap
        return ap

    def alloc(self, cols, dt=F32):
        w = cols if dt in (F32, I32) else (cols + 1) // 2
        assert self.off + w <= ARENA_WORDS, f"arena overflow {self.off + w}"
        ap = self.arena[:, self.off:self.off + w]
        self.off += w
        if dt != F32:
            ap = ap.bitcast(dt)
            if ap.shape[1] != cols:
                ap = ap[:, 0:cols]
        return ap

    def mark(self):
        return self.off

    def release(self, m):
        self.S.barrier()
        self.off = m

    def ps(self, b, n=512):
        return self.psum[:, 512 * b:512 * b + n]

    def ps_bf(self, b):
        return self.psum[:, 512 * b:512 * (b + 1)].bitcast(BF16)

    def MM(self, out, lhsT, rhs, start, stop, rd, wr):
        return self.S.op("pe", lambda e: e.matmul(out, lhsT=lhsT, rhs=rhs, start=start, stop=stop), rd, wr)

    def TR(self, out, in_, rd, wr):
        ident = self.ident
        return self.S.op("pe", lambda e: e.transpose(out=out, in_=in_, identity=ident), rd, wr)

    def ACT(self, out, in_, func, rd, wr, scale=None, bias=None):
        kw = {}
        if scale is not None:
            kw["scale"] = scale
        if bias is not None:
            kw["bias"] = bias
        return self.S.op("act", lambda e: e.activation(out=out, in_=in_, func=func, **kw), rd, wr)

    def TT(self, eng, out, in0, in1, op, rd, wr):
        return self.S.op(eng, lambda e: e.tensor_tensor(out=out, in0=in0, in1=in1, op=op), rd, wr)

    def TS(self, eng, out, in0, s1, s2, op0, op1, rd, wr):
        if op1 is None:
            return self.S.op(eng, lambda e: e.tensor_scalar(out=out, in0=in0, scalar1=s1, scalar2=None, op0=op0), rd, wr)
        return self.S.op(eng, lambda e: e.tensor_scalar(out=out, in0=in0, scalar1=s1, scalar2=s2, op0=op0, op1=op1), rd, wr)

    def STT(self, out, in0, scalar, in1, op0, op1, rd, wr):
        return self.S.op("dve", lambda e: e.scalar_tensor_tensor(out=out, in0=in0, scalar=scalar, in1=in1, op0=op0, op1=op1), rd, wr)

    def CP(self, eng, out, in_, rd, wr):
        if eng == "act":
            return self.S.op("act", lambda e: e.copy(out=out, in_=in_), rd, wr)
        return self.S.op(eng, lambda e: e.tensor_copy(out=out, in_=in_), rd, wr)

    def MEMSET(self, eng, ap, val, wr):
        return self.S.op(eng, lambda e: e.memset(ap, val), (), wr)

    def DMA(self, q, out, in_, rd, wr):
        return self.S.dma(q, lambda e: e.dma_start(out=out, in_=in_), rd, wr)

    def rsqrt(self, out, in_, mul, add, w, rd, wr):
        k = self.rot.get("rs", 0)
        self.rot["rs"] = k + 1
        ta, ra = self.rs_tmp[k % 4]
        tb, rb = self.rs_tmp2[k % 4]
        if getattr(self, "pool_rsqrt", False) and w <= 16:
            self.TS("dve", ta[:, 0:w], in_, mul, add, ALU.mult, ALU.add, rd, [ra])
            self.TT("pool", out, ta[:, 0:w], self.negh[:, 0:w], ALU.pow, [ra, self.Rc], wr)
            return
        self.TS("dve", ta[:, 0:w], in_, mul, add, ALU.mult, ALU.add, rd, [ra])
        self.ACT(tb[:, 0:w], ta[:, 0:w], AF.Sqrt, [ra], [rb])
        self.S.op("dve", lambda e: e.reciprocal(out=out, in_=tb[:, 0:w]), [rb], wr)


class Stop(Exception):
    pass


import os
STOP_AT = os.environ.get("K_STOP", "")


def ckpt(k, name):
    if STOP_AT and name == STOP_AT:
        raise Stop()


def chunked(ap, p=128):
    return ap.rearrange("(c p) n -> p c n", p=p)


def build(mode):
    k = K(mode)
    nc, S = k.nc, k.S
    with contextlib.ExitStack() as st:
        k.arena = st.enter_context(nc.sbuf_tensor("arena", [128, ARENA_WORDS], F32))
        k.psum = st.enter_context(nc.psum_tensor("psum", [128, 4096], F32))
        RPS = [Res(f"ps{i}") for i in range(8)]
        k.RPS = RPS

        k.ident = k.alloc(128, BF16)
        R_const = Res("const")
        iop = k.alloc(128)
        ioc = k.alloc(128)
        tmpf = k.alloc(128)
        k.ones_bf = k.alloc(128, BF16)
        k.ones_f = k.alloc(128)
        k.maskneg = k.alloc(128, BF16)
        k.rs_tmp = [(k.alloc(16), Res("rsa")) for _ in range(4)]
        k.rs_tmp2 = [(k.alloc(16), Res("rsb")) for _ in range(4)]
        S.op("pool", lambda e: e.iota(iop, pattern=[[0, 128]], base=0, channel_multiplier=1, allow_small_or_imprecise_dtypes=True), (), [R_const])
        S.op("pool", lambda e: e.iota(ioc, pattern=[[1, 128]], base=0, channel_multiplier=0, allow_small_or_imprecise_dtypes=True), (), [R_const])
        k.TT("dve", tmpf, iop, ioc, ALU.is_equal, [R_const], [R_const])
        k.CP("dve", k.ident, tmpf, [R_const], [R_const])
        k.TT("dve", tmpf, iop, ioc, ALU.is_gt, [R_const], [R_const])
        k.TS("dve", k.maskneg, tmpf, MASKNEG, None, ALU.mult, None, [R_const], [R_const])
        k.MEMSET("dve", k.ones_bf, 1.0, [R_const])
        k.MEMSET("dve", k.ones_f, 1.0, [R_const])
        k.negh = k.alloc(16)
        k.MEMSET("dve", k.negh, -0.5, [R_const])
        k.pool_rsqrt = (mode == "fused")
        k.Rc = R_const
        k.iop, k.ioc = iop, ioc

        try:
            if mode == "L1":
                phase_A_layer0(k)
            elif mode == "L2":
                phase_rest(k)
            else:
                build_fused_body(k)
        except Stop:
            pass
        finals = list(k.final_dmas)
        S.emit(final_waits=finals)
    return k


def load_w_bf16(k, dram_ap, nck, cols, res, colsplit=None):
    t = k.alloc(nck * cols, BF16).rearrange("p (c n) -> p c n", c=nck)
    src = chunked(dram_ap)
    step = colsplit or cols
    for ck in range(nck):
        for c0 in range(0, cols, step):
            c1 = min(cols, c0 + step)
            k.DMA("pool", t[:, ck, c0:c1], src[:, ck, c0:c1], (), [res])
    return t


def load_w_scaled_bf16(k, dram_ap, nck, cols, g_ap, res, stage, stage_res):
    t = k.alloc(nck * cols, BF16).rearrange("p (c n) -> p c n", c=nck)
    src = chunked(dram_ap)
    for ck in range(nck):
        k.DMA("sp", stage[:, 0:cols], src[:, ck, :], (), [stage_res])
        k.TS("pool", t[:, ck, :], stage[:, 0:cols], g_ap[:, ck:ck + 1], None, ALU.mult, None, [stage_res, k.Rg], [res])
    return t


def norm_prep_slot(k, xsrc, R_x, g_ap, hgT, R_hg, sq, R_sq, rx_out, R_rx, psb):
    for ck in range(8):
        k.ACT(hgT[:, ck, :], xsrc[:, ck, :], AF.Identity, [R_x, k.Rg], [R_hg], scale=g_ap[:, ck:ck + 1])
    k.ACT(sq, xsrc, AF.Square, [R_x], [R_sq])
    pst = k.ps(psb)
    for bl in range(4):
        for ck in range(8):
            k.MM(pst[:, bl:bl + 1], sq[:, ck, bl * 128:(bl + 1) * 128], k.ones_bf[:, 0:1], ck == 0, ck == 7,
                 [R_sq, k.Rc], [k.RPS[psb]])
    k.rsqrt(rx_out, pst[:, 0:4], 1.0 / D, EPS, 4, [k.RPS[psb]], [R_rx])


def memq_norm(k, zm, R_z, gmq_tile, qmn, R_qmn, tmp, R_tmp, st, R_st):
    zm3 = zm.rearrange("p (h d) -> p h d", h=4)
    t3 = tmp.rearrange("p (h d) -> p h d", h=4)
    k.TT("pool", tmp, zm, zm, ALU.mult, [R_z], [R_tmp])
    S = k.S
    S.op("dve", lambda e: e.tensor_reduce(out=st[:, 0:4], in_=t3, axis=AX.X, op=ALU.add), [R_tmp], [R_st])
    k.rsqrt(st[:, 4:8], st[:, 0:4], 1.0 / 64, EPS, 4, [R_st], [R_st])
    k.TT("dve", t3, zm3, st[:, 4:8].unsqueeze(2).broadcast_to([128, 4, 64]), ALU.mult, [R_z, R_st], [R_tmp])
    k.TT("pool", qmn.rearrange("p (h d) -> p h d", h=4), t3, gmq_tile.unsqueeze(1).broadcast_to([128, 4, 64]), ALU.mult,
         [R_tmp, k.Rg], [R_qmn])


def phase_A_layer0(k):
    S = k.S
    k.final_dmas = []
    xT_d = k.inp("xT", [D, TOWN])
    pos_d = k.inp("pos", [128, 16], I32)
    invf_d = k.inp("invf", [128, 16])
    w_in_d = k.inp("w_in0", [D, 928])
    g_attn_d = k.inp("g_attn0", [128, 8])
    w_uq_d = k.inp("w_uq", [384, 1152])
    g_qlat_d = k.inp("g_qlat", [128, 3])
    w_uk_d = k.inp("w_uk", [256, 768])
    w_uv_d = k.inp("w_uv", [256, 768])
    g_kvlat_d = k.inp("g_kvlat", [128, 2])
    gq_d = k.inp("gq_tile", [128, 96])
    gk_d = k.inp("gk_tile", [128, 96])
    gmq_d = k.inp("g_mq0", [128, 64])
    qT_o = k.outp("qT_o", [96, 12 * TOWN], BF16)
    qmT_o = k.outp("qmT_o", [128, 2 * TOWN], BF16)
    KTn_o = k.outp("KTn_o", [768, TOWN], BF16)
    KTr_o = k.outp("KTr_o", [32, TOWN], BF16)
    Vx_o = k.outp("Vx_o", [12, 128, 16 * 65], BF16)
    rk_o = k.outp("rk_o", [128, 16 * 12])

    k.Rg = Res("gains")
    R_w = Res("weights")
    g_attn = k.alloc(8)
    g_qlat = k.alloc(3)
    g_kvlat = k.alloc(2)
    gq = k.alloc(96)
    gk = k.alloc(96)
    gmq = k.alloc(64)
    invf = k.alloc(16)
    posi = k.alloc(16, I32)
    for t, d in [(g_attn, g_attn_d), (g_qlat, g_qlat_d), (g_kvlat, g_kvlat_d), (gq, gq_d), (gk, gk_d), (gmq, gmq_d), (invf, invf_d)]:
        k.DMA("sp", t, d, (), [k.Rg])
    k.DMA("sp", posi, pos_d, (), [k.Rg])

    Win = load_w_bf16(k, w_in_d, 8, 928, R_w)
    stage = k.alloc(1152)
    R_stage = Res("stage")
    Wuq = load_w_scaled_bf16(k, w_uq_d, 3, 1152, g_qlat, R_w, stage, R_stage)
    Wuk = load_w_scaled_bf16(k, w_uk_d, 2, 768, g_kvlat, R_w, stage, R_stage)
    Wuv = load_w_scaled_bf16(k, w_uv_d, 2, 768, g_kvlat, R_w, stage, R_stage)

    ckpt(k, "weights")
    R_rope = Res("rope")
    posf = k.alloc(16)
    ang = k.alloc(256)
    kq = k.alloc(256)
    ki = k.alloc(256, I32)
    y = k.alloc(256)
    m = k.alloc(256)
    cs2 = k.alloc(16 * 32).rearrange("p (b i) -> p b i", b=16)
    sn2 = k.alloc(16 * 32).rearrange("p (b i) -> p b i", b=16)
    ang3 = ang.rearrange("p (b i) -> p b i", b=16)
    k.CP("dve", posf, posi, [k.Rg], [R_rope])
    k.TT("dve", ang3, posf.unsqueeze(2).broadcast_to([128, 16, 16]), invf.unsqueeze(1).broadcast_to([128, 16, 16]), ALU.mult,
         [R_rope, k.Rg], [R_rope])
    TWO_PI = 2.0 * np.pi
    C1 = 6.28125
    C2 = float(np.float32(TWO_PI - C1))
    k.TS("dve", kq, ang, float(1.0 / TWO_PI), None, ALU.mult, None, [R_rope], [R_rope])
    k.CP("dve", ki, kq, [R_rope], [R_rope])
    k.CP("dve", kq, ki, [R_rope], [R_rope])
    k.STT(y, kq, -C1, ang, ALU.mult, ALU.add, [R_rope], [R_rope])
    k.STT(y, kq, -C2, y, ALU.mult, ALU.add, [R_rope], [R_rope])

    def wrap(t):
        k.TS("dve", m, t, float(np.pi), None, ALU.is_gt, None, [R_rope], [R_rope])
        k.STT(t, m, -TWO_PI, t, ALU.mult, ALU.add, [R_rope], [R_rope])
        k.TS("dve", m, t, float(-np.pi), None, ALU.is_lt, None, [R_rope], [R_rope])
        k.STT(t, m, TWO_PI, t, ALU.mult, ALU.add, [R_rope], [R_rope])

    wrap(y)
    y3 = y.rearrange("p (b i) -> p b i", b=16)
    k.ACT(sn2[:, :, 16:32], y3, AF.Sin, [R_rope], [R_rope])
    k.TS("dve", sn2[:, :, 0:16], sn2[:, :, 16:32], -1.0, None, ALU.mult, None, [R_rope], [R_rope])
    k.TS("dve", y, y, float(np.pi / 2), None, ALU.add, None, [R_rope], [R_rope])
    wrap(y)
    k.ACT(cs2[:, :, 0:16], y3, AF.Sin, [R_rope], [R_rope])
    k.CP("dve", cs2[:, :, 16:32], cs2[:, :, 0:16], [R_rope], [R_rope])

    ckpt(k, "rope")
    Gq = k.alloc(96)
    k.CP("dve", Gq, gq, [k.Rg], [k.Rg])
    k.TT("dve", Gq[:, 0:64], gq[:, 0:64], gk[:, 0:64], ALU.mult, [k.Rg], [k.Rg])

    qT = k.alloc(12 * TOWN, BF16).rearrange("p (h t) -> p h t", h=12)
    qmT = k.alloc(2 * TOWN, BF16).rearrange("p (c t) -> p c t", c=2)
    rk_own = k.alloc(16 * 12).rearrange("p (b h) -> p b h", b=16)
    R_qT = Res("qT")
    R_qmT = Res("qmT")
    R_rk = Res("rk")
    xs0 = k.alloc(8 * 512).rearrange("p (c t) -> p c t", c=8)
    xs = [xs0, xs0]
    R_xs0 = Res("xs0")
    R_xs = [R_xs0, R_xs0]
    hg0 = k.alloc(8 * 512, BF16).rearrange("p (c t) -> p c t", c=8)
    hgT = [hg0, hg0]
    R_hg0 = Res("hg0")
    R_hg = [R_hg0, R_hg0]
    sq = k.alloc(8 * 512, BF16).rearrange("p (c t) -> p c t", c=8)
    R_sq = Res("sq")
    rx = k.alloc(16)
    R_rx = Res("rx")
    ckvT = [k.alloc(2 * 512, BF16).rearrange("p (c t) -> p c t", c=2) for _ in range(2)]
    R_ckvT = [Res("ckvT0"), Res("ckvT1")]
    KTn = [k.alloc(6 * 512, BF16).rearrange("p (c t) -> p c t", c=6) for _ in range(2)]
    R_KTn = [Res("KTn0"), Res("KTn1")]
    KTr = [k.alloc(512, BF16) for _ in range(2)]
    R_KTr = [Res("KTr0"), Res("KTr1")]
    Vx = [k.alloc(12 * 4 * 65, BF16).rearrange("p (h b c) -> p h b c", h=12, b=4) for _ in range(2)]
    R_Vx = [Res("Vx0"), Res("Vx1")]
    for i in range(2):
        k.MEMSET("pool", Vx[i][:, :, :, 64:65], 1.0, [R_Vx[i]])
    z = k.alloc(928)
    R_z = Res("z")
    stt = k.alloc(16)
    R_st = Res("st")
    qln = k.alloc(384, BF16)
    ckvn = k.alloc(256, BF16)
    qmn = k.alloc(256, BF16)
    krr = k.alloc(32, BF16)
    R_tm = Res("tm_bf")
    tmpm = k.alloc(256)
    R_tmpm = Res("tmpm")
    krg = k.alloc(32)
    krt = k.alloc(32)
    kru = k.alloc(32)
    R_kr = Res("kr")
    qlT = k.alloc(3 * 128, BF16).rearrange("p (c t) -> p c t", c=3)
    R_qlT = Res("qlT")
    sqq = k.alloc(1152)
    R_sqq = Res("sqq")
    ssq = k.alloc(32)
    R_ssq = Res("ssq")
    qn = k.alloc(1152)
    R_qn = Res("qn")
    qg = k.alloc(1152)
    R_qg = Res("qg")
    qt = k.alloc(12 * 32)
    qu = k.alloc(12 * 32)
    R_qtu = Res("qtu")
    qfin = k.alloc(12 * 96, BF16).rearrange("p (h d) -> p h d", h=12)
    R_qfin = Res("qfin")
    sqk = k.alloc(768)
    R_sqk = Res("sqk")
    ssk = k.alloc(32)
    R_ssk = Res("ssk")
    RPS = k.RPS

    xT3 = chunked(xT_d)
    for j in range(NSLOT):
        sb = j % 2
        k.DMA("sp", xs[sb], xT3[:, :, j * 512:(j + 1) * 512], (), [R_xs[sb]])
        norm_prep_slot(k, xs[sb], R_xs[sb], g_attn, hgT[sb], R_hg[sb], sq, R_sq, rx[:, 4 * j:4 * j + 4], R_rx, 7)
        ckpt(k, "norm")
        for bl in range(4):
            bg = 4 * j + bl
            tok = slice(bl * 128, (bl + 1) * 128)
            for n, (c0, c1) in enumerate([(0, 512), (512, 928)]):
                for ck in range(8):
                    k.MM(k.ps(n)[:, 0:c1 - c0], hgT[sb][:, ck, tok], Win[:, ck, c0:c1], ck == 0, ck == 7,
                         [R_hg[sb], R_w], [RPS[n]])
                k.ACT(z[:, c0:c1], k.ps(n)[:, 0:c1 - c0], AF.Identity, [RPS[n], R_rx], [R_z], scale=rx[:, bg:bg + 1])
            ckpt(k, "z")
            k.ACT(sqq[:, 0:672], z[:, 0:672], AF.Square, [R_z], [R_sqq])
            k.S.op("dve", lambda e: e.tensor_reduce(out=stt[:, 0:1], in_=sqq[:, 0:384], axis=AX.X, op=ALU.add), [R_sqq], [R_st])
            k.S.op("dve", lambda e: e.tensor_reduce(out=stt[:, 1:2], in_=sqq[:, 384:640], axis=AX.X, op=ALU.add), [R_sqq], [R_st])
            k.S.op("dve", lambda e: e.tensor_reduce(out=stt[:, 2:3], in_=sqq[:, 640:672], axis=AX.X, op=ALU.add), [R_sqq], [R_st])
            k.rsqrt(stt[:, 4:5], stt[:, 0:1], 1.0 / 384, EPS, 1, [R_st], [R_st])
            k.rsqrt(stt[:, 5:6], stt[:, 1:2], 1.0 / 256, EPS, 1, [R_st], [R_st])
            k.TS("dve", qln, z[:, 0:384], stt[:, 4:5], None, ALU.mult, None, [R_z, R_st], [R_tm])
            k.TS("dve", ckvn, z[:, 384:640], stt[:, 5:6], None, ALU.mult, None, [R_z, R_st], [R_tm])
            memq_norm(k, z[:, 672:928], R_z, gmq, qmn, R_tm, tmpm, R_tmpm, stt[:, 8:16], R_st)
            k.TT("pool", krg, z[:, 640:672], gk[:, 64:96], ALU.mult, [R_z, k.Rg], [R_kr])
            k.TT("pool", krt, krg, cs2[:, bg, :], ALU.mult, [R_kr, R_rope], [R_kr])
            k.TT("pool", kru[:, 0:16], krg[:, 16:32], sn2[:, bg, 0:16], ALU.mult, [R_kr, R_rope], [R_kr])
            k.TT("pool", kru[:, 16:32], krg[:, 0:16], sn2[:, bg, 16:32], ALU.mult, [R_kr, R_rope], [R_kr])
            k.TT("pool", krr, krt, kru, ALU.add, [R_kr], [R_tm])
            ckpt(k, "tm")
            pb = k.ps_bf(2)
            for c in range(3):
                k.TR(pb[:, c * 128:(c + 1) * 128], qln[:, c * 128:(c + 1) * 128], [R_tm, k.Rc], [RPS[2]])
            ckpt(k, "tr1")
            for c in range(2):
                k.TR(pb[:, (3 + c) * 128:(4 + c) * 128], ckvn[:, c * 128:(c + 1) * 128], [R_tm, k.Rc], [RPS[2]])
            for c in range(2):
                k.TR(pb[:, (5 + c) * 128:(6 + c) * 128], qmn[:, c * 128:(c + 1) * 128], [R_tm, k.Rc], [RPS[2]])
            ckpt(k, "tr3")
            k.TR(pb[0:32, 7 * 128:8 * 128], krr, [R_tm, k.Rc], [RPS[2]])
            ckpt(k, "tr4")
            k.CP("act", qlT, pb[:, 0:384].rearrange("p (c t) -> p c t", c=3), [RPS[2]], [R_qlT])
            ckpt(k, "cp1")
            k.CP("act", ckvT[sb][:, :, tok], pb[:, 384:640].rearrange("p (c t) -> p c t", c=2), [RPS[2]], [R_ckvT[sb]])
            ckpt(k, "cp2")
            k.CP("act", qmT[:, :, bg * 128:(bg + 1) * 128], pb[:, 640:896].rearrange("p (c t) -> p c t", c=2), [RPS[2]], [R_qmT])
            ckpt(k, "cp3")
            k.CP("act", KTr[sb][0:32, tok], pb[0:32, 896:1024], [RPS[2]], [R_KTr[sb]])
            ckpt(k, "tr")
            for n in range(3):
                for ck in range(3):
                    k.MM(k.ps(3 + n)[:, 0:384], qlT[:, ck, :], Wuq[:, ck, n * 384:(n + 1) * 384], ck == 0, ck == 2,
                         [R_qlT, R_w], [RPS[3 + n]])
                k.ACT(sqq[:, n * 384:(n + 1) * 384], k.ps(3 + n)[:, 0:384], AF.Square, [RPS[3 + n]], [R_sqq])
            k.S.op("dve", lambda e: e.tensor_reduce(out=ssq[:, 0:12], in_=sqq.rearrange("p (h d) -> p h d", h=12), axis=AX.X, op=ALU.add),
                   [R_sqq], [R_ssq])
            k.rsqrt(ssq[:, 16:28], ssq[:, 0:12], 1.0 / 96, EPS, 12, [R_ssq], [R_ssq])
            for n in range(3):
                k.TT("dve", qn[:, n * 384:(n + 1) * 384].rearrange("p (h d) -> p h d", h=4),
                     k.ps(3 + n)[:, 0:384].rearrange("p (h d) -> p h d", h=4),
                     ssq[:, 16 + 4 * n:20 + 4 * n].unsqueeze(2).broadcast_to([128, 4, 96]), ALU.mult,
                     [RPS[3 + n], R_ssq], [R_qn])
            qn3 = qn.rearrange("p (h d) -> p h d", h=12)
            qg3 = qg.rearrange("p (h d) -> p h d", h=12)
            k.TT("pool", qg3, qn3, Gq.unsqueeze(1).broadcast_to([128, 12, 96]), ALU.mult, [R_qn, k.Rg], [R_qg])
            qt3 = qt.rearrange("p (h d) -> p h d", h=12)
            qu3 = qu.rearrange("p (h d) -> p h d", h=12)
            k.TT("pool", qt3, qg3[:, :, 64:96], cs2[:, bg, :].unsqueeze(1).broadcast_to([128, 12, 32]), ALU.mult, [R_qg, R_rope], [R_qtu])
            k.TT("pool", qu3[:, :, 0:16], qg3[:, :, 80:96], sn2[:, bg, 0:16].unsqueeze(1).broadcast_to([128, 12, 16]), ALU.mult,
                 [R_qg, R_rope], [R_qtu])
            k.TT("pool", qu3[:, :, 16:32], qg3[:, :, 64:80], sn2[:, bg, 16:32].unsqueeze(1).broadcast_to([128, 12, 16]), ALU.mult,
                 [R_qg, R_rope], [R_qtu])
            k.TT("dve", qfin[:, :, 64:96], qt3, qu3, ALU.add, [R_qtu], [R_qfin])
            k.CP("dve", qfin[:, :, 0:64], qg3[:, :, 0:64], [R_qg], [R_qfin])
            for h in range(12):
                bk = 6 if h < 8 else 7
                hh = h % 8
                k.TR(k.ps_bf(bk)[0:96, hh * 128:(hh + 1) * 128], qfin[:, h, :], [R_qfin, k.Rc], [RPS[bk]])
            k.CP("act", qT[0:96, 0:8, bg * 128:(bg + 1) * 128], k.ps_bf(6)[0:96, :].rearrange("p (h t) -> p h t", h=8), [RPS[6]], [R_qT])
            k.CP("act", qT[0:96, 8:12, bg * 128:(bg + 1) * 128], k.ps_bf(7)[0:96, 0:512].rearrange("p (h t) -> p h t", h=4), [RPS[7]], [R_qT])
            ckpt(k, "q")
            for n in range(2):
                for ck in range(2):
                    k.MM(k.ps(n)[:, 0:384], ckvT[sb][:, ck, tok], Wuk[:, ck, n * 384:(n + 1) * 384], ck == 0, ck == 1,
                         [R_ckvT[sb], R_w], [RPS[n]])
                k.ACT(sqk[:, n * 384:(n + 1) * 384], k.ps(n)[:, 0:384], AF.Square, [RPS[n]], [R_sqk])
            for n in range(2):
                for ck in range(2):
                    k.MM(k.ps(3 + n)[:, 0:384], ckvT[sb][:, ck, tok], Wuv[:, ck, n * 384:(n + 1) * 384], ck == 0, ck == 1,
                         [R_ckvT[sb], R_w], [RPS[3 + n]])
                k.CP("act", Vx[sb][:, 6 * n:6 * n + 6, bl, 0:64], k.ps(3 + n)[:, 0:384].rearrange("p (h d) -> p h d", h=6),
                     [RPS[3 + n]], [R_Vx[sb]])
            k.S.op("dve", lambda e: e.tensor_reduce(out=ssk[:, 0:12], in_=sqk.rearrange("p (h d) -> p h d", h=12), axis=AX.X, op=ALU.add),
                   [R_sqk], [R_ssk])
            k.TS("dve", ssk[:, 16:28], ssk[:, 0:12], stt[:, 2:3], None, ALU.add, None, [R_ssk, R_st], [R_ssk])
            k.rsqrt(rk_own[:, bg, :], ssk[:, 16:28], 1.0, 96 * EPS, 12, [R_ssk], [R_rk])
        ckpt(k, "blocks")
        for i in range(6):
            for ck in range(2):
                k.MM(k.ps(2), Wuk[:, ck, i * 128:(i + 1) * 128], ckvT[sb][:, ck, :], ck == 0, ck == 1, [R_w, R_ckvT[sb]], [RPS[2]])
            k.CP("act" if i % 2 == 0 else "dve", KTn[sb][:, i, :], k.ps(2), [RPS[2]], [R_KTn[sb]])
        ckpt(k, "ktn")
        KTn_o3 = KTn_o.rearrange("(c p) t -> p c t", p=128)
        k.final_dmas.append(k.DMA("sp", KTn_o3[:, :, j * 512:(j + 1) * 512], KTn[sb], [R_KTn[sb]], ()))
        k.final_dmas.append(k.DMA("sp", KTr_o[:, j * 512:(j + 1) * 512], KTr[sb][0:32, :], [R_KTr[sb]], ()))
        Vx_o4 = Vx_o.rearrange("h p (b c) -> p h b c", b=16)
        for h0 in range(0, 12, 4):
            k.final_dmas.append(k.DMA("sp", Vx_o4[:, h0:h0 + 4, 4 * j:4 * j + 4, :], Vx[sb][:, h0:h0 + 4, :, :], [R_Vx[sb]], ()))
    ckpt(k, "slots")
    k.final_dmas.append(k.DMA("sp", rk_o, rk_own.rearrange("p b h -> p (b h)"), [R_rk], ()))
    k.final_dmas.append(k.DMA("sp", qT_o, qT[0:96].rearrange("p h t -> p (h t)"), [R_qT], ()))
    k.final_dmas.append(k.DMA("sp", qmT_o, qmT.rearrange("p c t -> p (c t)"), [R_qmT], ()))


def mem_kv_prep(k, pre, memT_d, gmem_d, wmkv_d, gmk_d, KmT, R_KmT, Vmx, R_Vmx):
    RPS = k.RPS
    m = k.mark()
    memx = k.alloc(8 * 256).rearrange("p (c t) -> p c t", c=8)
    mg = k.alloc(8 * 256, BF16).rearrange("p (c t) -> p c t", c=8)
    sqm = k.alloc(8 * 256, BF16).rearrange("p (c t) -> p c t", c=8)
    gmem = k.alloc(8)
    gmk = k.alloc(64)
    rmem = k.alloc(4)
    kvm = k.alloc(512)
    kn = k.alloc(256, BF16)
    tmp = k.alloc(256)
    st = k.alloc(8)
    R_a, R_b, R_c, R_d, R_e, R_f, R_w = (Res(pre + n) for n in "abcdefw")
    k.DMA("sp", memx, chunked(memT_d), (), [R_a])
    k.DMA("sp", gmem, gmem_d, (), [k.Rg])
    k.DMA("sp", gmk, gmk_d, (), [k.Rg])
    Wm = load_w_bf16(k, wmkv_d, 8, 512, R_w)
    for ck in range(8):
        k.ACT(mg[:, ck, :], memx[:, ck, :], AF.Identity, [R_a, k.Rg], [R_b], scale=gmem[:, ck:ck + 1])
    k.ACT(sqm, memx, AF.Square, [R_a], [R_c])
    for mb in range(2):
        for ck in range(8):
            k.MM(k.ps(7)[:, mb:mb + 1], sqm[:, ck, mb * 128:(mb + 1) * 128], k.ones_bf[:, 0:1], ck == 0, ck == 7, [R_c, k.Rc], [RPS[7]])
    k.rsqrt(rmem[:, 0:2], k.ps(7)[:, 0:2], 1.0 / D, EPS, 2, [RPS[7]], [R_d])
    k.MEMSET("pool", Vmx[:, :, :, 64:65], 1.0, [R_Vmx])
    for mb in range(2):
        for ck in range(8):
            k.MM(k.ps(0), mg[:, ck, mb * 128:(mb + 1) * 128], Wm[:, ck, :], ck == 0, ck == 7, [R_b, R_w], [RPS[0]])
        k.ACT(kvm, k.ps(0), AF.Identity, [RPS[0], R_d], [R_e], scale=rmem[:, mb:mb + 1])
        memq_norm(k, kvm[:, 0:256], R_e, gmk, kn, R_f, tmp, R_f, st, R_f)
        k.CP("dve", Vmx[:, mb, :, 0:64], kvm[:, 256:512].rearrange("p (h d) -> p h d", h=4), [R_e], [R_Vmx])
        pb = k.ps_bf(2)
        for c in range(2):
            k.TR(pb[:, c * 128:(c + 1) * 128], kn[:, c * 128:(c + 1) * 128], [R_f, k.Rc], [RPS[2]])
        k.CP("act", KmT[:, :, mb * 128:(mb + 1) * 128], pb[:, 0:256].rearrange("p (c t) -> p c t", c=2), [RPS[2]], [R_KmT])
    k.release(m)


class Attn:
    def __init__(self, k, mixT, R_mix):
        self.k = k
        self.mixT = mixT
        self.R_mix = R_mix
        self.P = [k.alloc(1024, BF16) for _ in range(2)]
        self.R_P = [Res("P0"), Res("P1")]
        self.R_S = [Res("S0"), Res("S1")]
        self.rec = k.alloc(512)
        self.R_rec = Res("rec")
        self.bsb = k.alloc(512)
        self.R_bsb = Res("bsb")
        self.ost = [k.alloc(512, BF16) for _ in range(2)]
        self.R_ost = [Res("ost0"), Res("ost1")]
        self.nS = 0
        self.nO = 0

    def sbuf(self):
        i = self.nS % 2
        self.nS += 1
        return i, self.k.psum[:, (4 + 2 * i) * 512:(6 + 2 * i) * 512], self.R_S[i]

    def finalize(self, j, chunk, odd, si=None):
        k = self.k
        acc = k.ps(j)
        Racc = k.RPS[j]
        k.CP("act", self.rec[64:65, :], acc[64:65, :], [Racc], [self.R_rec])
        if si is None:
            i, Sps, RS = self.sbuf()
        else:
            Sps, RS = k.psum[:, (4 + 2 * si) * 512:(6 + 2 * si) * 512], self.R_S[si]
        k.MM(Sps[0:64, 0:512], k.ones_f[64:65, 0:64], self.rec[64:65, :], True, True, [self.R_rec, k.Rc], [RS])
        k.S.op("dve", lambda e: e.reciprocal(out=self.bsb[0:64, :], in_=Sps[0:64, 0:512]), [RS], [self.R_bsb])
        cols = slice(j * 512, (j + 1) * 512)
        if not odd:
            k.TT("dve", self.mixT[0:64, chunk, cols], acc[0:64, :], self.bsb[0:64, :], ALU.mult, [Racc, self.R_bsb], [self.R_mix[chunk][j]])
        else:
            t = self.nO % 2
            self.nO += 1
            k.TT("dve", self.ost[t][0:64, :], acc[0:64, :], self.bsb[0:64, :], ALU.mult, [Racc, self.R_bsb], [self.R_ost[t]])
            k.DMA("pool", self.mixT[64:128, chunk, cols], self.ost[t][0:64, :], [self.R_ost[t]], [self.R_mix[chunk][j]])

    def mem_attention(self, qmT, R_qmT, KmT, R_KmT, Vmx, R_Vmx):
        k = self.k
        for hm in range(4):
            pr = slice((hm % 2) * 64, (hm % 2) * 64 + 64)
            for j in range(4):
                i, Sps, RS = self.sbuf()
                for mb in range(2):
                    k.MM(Sps[:, mb * 512:(mb + 1) * 512], KmT[pr, hm // 2, mb * 128:(mb + 1) * 128], qmT[pr, hm // 2, j * 512:(j + 1) * 512],
                         True, True, [R_KmT, R_qmT], [RS])
                k.ACT(self.P[i], Sps, AF.Exp, [RS], [self.R_P[i]], scale=0.125)
                for mb in range(2):
                    k.MM(k.ps(j)[0:65, :], Vmx[:, mb, hm, :], self.P[i][:, mb * 512:(mb + 1) * 512], mb == 0, mb == 1,
                         [R_Vmx, self.R_P[i]], [k.RPS[j]])
                self.finalize(j, 6 + hm // 2, hm % 2)

    def causal_attention(self, qT, R_qT, rk, R_rk, KTn_w, KTr_w, Vx_w):
        k = self.k
        KT = [k.alloc(4096, BF16) for _ in range(2)]
        Vb = [k.alloc(32 * 65, BF16).rearrange("p (b c) -> p b c", c=65) for _ in range(2)]
        R_KT = [Res("KT0"), Res("KT1")]
        R_V = [Res("V0"), Res("V1")]
        Vx4 = Vx_w.rearrange("h p (b c) -> h p b c", c=65)

        def load(n):
            h, grp = divmod(n, 4)
            b = n % 2
            ts = slice(4096 * grp, 4096 * (grp + 1))
            k.DMA("sp", KT[b][0:64, :], KTn_w[64 * h:64 * h + 64, ts], (), [R_KT[b]])
            k.DMA("sp", KT[b][64:96, :], KTr_w[:, ts], (), [R_KT[b]])
            k.DMA("sp", Vb[b], Vx4[h, :, 32 * grp:32 * grp + 32, :], (), [R_V[b]])

        descs = []
        for n in range(48):
            h, grp = divmod(n, 4)
            b = n % 2
            js = [j for j in range(4) if j >= grp]
            batches = [js[x:x + 2] for x in range(0, len(js), 2)]
            for il in range(8):
                i = 8 * grp + il
                for kb in range(4):
                    for bi, batch in enumerate(batches):
                        last = (il == 7 and kb == 3 and bi == len(batches) - 1)
                        descs.append((n, h, grp, b, il, i, kb, batch, last))

        def emit_qk(dsc):
            n, h, grp, b, il, i, kb, batch, last = dsc
            Bw = 4 * i + kb
            col = (il * 4 + kb) * 128
            lk = KT[b][0:96, col:col + 128]
            si = self.nbatch % 2
            self.nbatch += 1
            Sps, RS = k.psum[:, (4 + 2 * si) * 512:(6 + 2 * si) * 512], self.R_S[si]
            dg = (il == 7 and batch[0] == grp)
            n0 = 128 * kb if dg else 0
            ncols = 512 * len(batch)
            for u, j in enumerate(batch):
                off = n0 if u == 0 else 0
                msk = dg and u == 0
                k.MM(Sps[:, u * 512 + off:(u + 1) * 512], lk, qT[0:96, h, j * 512 + off:(j + 1) * 512], True, not msk,
                     [R_KT[b], R_qT], [RS])
                if msk:
                    k.MM(Sps[:, off:off + 128], k.ident, k.maskneg, False, True, [k.Rc], [RS])
            k.ACT(self.P[si][:, n0:ncols], Sps[:, n0:ncols], AF.Exp, [RS, R_rk], [self.R_P[si]], scale=rk[:, Bw * 12 + h:Bw * 12 + h + 1])
            return si, n0

        def emit_pv(dsc, si, n0):
            n, h, grp, b, il, i, kb, batch, last = dsc
            Bw = 4 * i + kb
            for u, j in enumerate(batch):
                off = n0 if u == 0 else 0
                k.MM(k.ps(j)[0:65, off:512], Vb[b][:, il * 4 + kb, :], self.P[si][:, u * 512 + off:(u + 1) * 512],
                     Bw == 0, (i == 8 * j + 7 and kb == 3), [R_V[b], self.R_P[si]], [k.RPS[j]])
            if last:
                self.finalize(grp, h // 2, h % 2, si=si)
                if n + 2 < 48:
                    load(n + 2)

        self.nbatch = 0
        load(0)
        load(1)
        prev = None
        for dsc in descs:
            cur = emit_qk(dsc)
            if prev is not None:
                emit_pv(prev[0], *prev[1])
            prev = (dsc, cur)
        emit_pv(prev[0], *prev[1])


def wout_residual(k, pre, wout_d, mixT, R_mix, xres, R_x, xsrc):
    RPS = k.RPS
    m = k.mark()
    R_w = Res(pre + "wout")
    Wout = load_w_bf16(k, wout_d, 8, 1024, R_w)
    nb = 0
    for j in range(NSLOT):
        cols = slice(j * 512, (j + 1) * 512)
        if xsrc is not None:
            k.DMA("sp", xres[:, :, cols], xsrc(j), (), [R_x[j]])
        for dch in range(8):
            b = nb % 8
            nb += 1
            for mc in range(8):
                k.MM(k.ps(b), Wout[:, mc, dch * 128:(dch + 1) * 128], mixT[:, mc, cols], mc == 0, mc == 7, [R_w, R_mix[mc][j]], [RPS[b]])
            k.TT("dve", xres[:, dch, cols], k.ps(b), xres[:, dch, cols], ALU.add, [RPS[b], R_x[j]], [R_x[j]])
    k.release(m)


def ffn(k, pre, xres, R_x, gffn_d, wg_d, wu_d, wd_d):
    RPS = k.RPS
    m = k.mark()
    g = k.alloc(8)
    k.DMA("sp", g, gffn_d, (), [k.Rg])
    hT = k.alloc(8 * 1024, BF16).rearrange("p (c t) -> p c t", c=8)
    actT = k.alloc(NF * 1024, BF16).rearrange("p (f t) -> p f t", f=NF)
    sq = k.alloc(8 * 512, BF16).rearrange("p (c t) -> p c t", c=8)
    rt1 = k.alloc(512)
    rt2 = k.alloc(512)
    rbc = k.alloc(512)
    sg = [k.alloc(512) for _ in range(2)]
    Wg_r = [k.alloc(8 * 256, BF16).rearrange("p (c n) -> p c n", c=8) for _ in range(3)]
    Wu_r = [k.alloc(8 * 256, BF16).rearrange("p (c n) -> p c n", c=8) for _ in range(3)]
    Wd_r = [k.alloc(NF * 128, BF16).rearrange("p (f n) -> p f n", f=NF) for _ in range(3)]
    R_hT, R_sq, R_rt, R_rbc = Res(pre + "hT"), Res(pre + "sq"), Res(pre + "rt"), Res(pre + "rbc")
    R_act = [[Res(f"{pre}act{f}_{t}") for t in range(2)] for f in range(NF)]
    R_sg = [Res(pre + "sg0"), Res(pre + "sg1")]
    R_Wgu = [Res(f"{pre}wgu{i}") for i in range(3)]
    R_Wd = [Res(f"{pre}wd{i}") for i in range(3)]
    wg3, wu3 = chunked(wg_d), chunked(wu_d)
    wd3 = chunked(wd_d)
    NFG = NF // 2

    def load_gu(fg):
        s = fg % 3
        k.DMA("pool", Wg_r[s], wg3[:, :, fg * 256:(fg + 1) * 256], (), [R_Wgu[s]])
        k.DMA("pool", Wu_r[s], wu3[:, :, fg * 256:(fg + 1) * 256], (), [R_Wgu[s]])

    def load_d(dch):
        s = dch % 3
        k.DMA("pool", Wd_r[s], wd3[:, :, dch * 128:(dch + 1) * 128], (), [R_Wd[s]])

    for half in range(2):
        hc = half * 1024
        load_gu(0)
        load_gu(1)
        for t in range(2):
            cols = slice(hc + t * 512, hc + (t + 1) * 512)
            j = (hc + t * 512) // 512
            k.ACT(sq, xres[:, :, cols], AF.Square, [R_x[j]], [R_sq])
            for ck in range(8):
                k.MM(k.ps(7), k.ones_bf, sq[:, ck, :], ck == 0, ck == 7, [k.Rc, R_sq], [RPS[7]])
            k.TS("dve", rt1, k.ps(7), 1.0 / D, EPS, ALU.mult, ALU.add, [RPS[7]], [R_rt])
            k.ACT(rt2, rt1, AF.Sqrt, [R_rt], [R_rt])
            k.S.op("dve", lambda e: e.reciprocal(out=rbc, in_=rt2), [R_rt], [R_rbc])
            for ck in range(8):
                k.STT(hT[:, ck, t * 512:(t + 1) * 512], xres[:, ck, cols], g[:, ck:ck + 1], rbc, ALU.mult, ALU.mult,
                      [R_x[j], k.Rg, R_rbc], [R_hT])
        nsg = 0
        for fg in range(NFG):
            if fg + 2 < NFG:
                load_gu(fg + 2)
            if fg == NFG - 2:
                load_d(0)
            if fg == NFG - 1:
                load_d(1)
            s = fg % 3
            for fl in range(2):
                f = 2 * fg + fl
                gb = [0, 1] if f % 2 == 0 else [4, 5]
                ub = [2, 3] if f % 2 == 0 else [6, 7]
                for W_r, banks in ((Wg_r, gb), (Wu_r, ub)):
                    for ck in range(8):
                        for t in range(2):
                            k.MM(k.ps(banks[t]), W_r[s][:, ck, fl * 128:(fl + 1) * 128], hT[:, ck, t * 512:(t + 1) * 512], ck == 0, ck == 7,
                                 [R_Wgu[s], R_hT], [RPS[banks[t]]])
                for t in range(2):
                    q = nsg % 2
                    nsg += 1
                    k.ACT(sg[q], k.ps(gb[t]), AF.Silu, [RPS[gb[t]]], [R_sg[q]])
                    k.TT("dve", actT[:, f, t * 512:(t + 1) * 512], k.ps(ub[t]), sg[q], ALU.mult, [RPS[ub[t]], R_sg[q]], [R_act[f][t]])
        nb = 0
        for dch in range(8):
            if dch + 2 < 8:
                load_d(dch + 2)
            s = dch % 3
            for t in range(2):
                b = nb % 8
                nb += 1
                cols = slice(hc + t * 512, hc + (t + 1) * 512)
                j = (hc + t * 512) // 512
                for f in range(NF):
                    k.MM(k.ps(b), Wd_r[s][:, f, :], actT[:, f, t * 512:(t + 1) * 512], f == 0, f == NF - 1, [R_Wd[s], R_act[f][t]], [RPS[b]])
                k.TT("dve", xres[:, dch, cols], k.ps(b), xres[:, dch, cols], ALU.add, [RPS[b], R_x[j]], [R_x[j]])
    k.release(m)


def sgu_layer1(k, xres, R_x, mixT, R_mix, qmT, R_qmT, d):
    RPS = k.RPS
    m = k.mark()
    R_w = Res("l1w")
    g_attn = k.alloc(8)
    gmq = k.alloc(64)
    lng = k.alloc(768)
    lnb = k.alloc(768)
    bsp = k.alloc(8)
    for t_, d_ in [(g_attn, d["g_attn1"]), (gmq, d["g_mq1"]), (lng, d["ln_g"]), (lnb, d["ln_b"]), (bsp, d["b_sp"])]:
        k.DMA("sp", t_, d_, (), [k.Rg])
    Win = load_w_bf16(k, d["w_in1"], 8, 1792, R_w, colsplit=896)
    wsp = k.alloc(8 * 128, BF16).rearrange("p (g t) -> p g t", g=8)
    wst = k.alloc(8 * 128).rearrange("p (g t) -> p g t", g=8)
    msk = k.alloc(128)
    R_ws = Res("wsp")
    k.DMA("sp", wst, d["w_spT"].rearrange("p (g t) -> p g t", g=8), (), [R_ws])
    k.TT("dve", msk, k.iop, k.ioc, ALU.is_le, [k.Rc], [R_ws])
    k.TT("dve", wsp, wst, msk.unsqueeze(1).broadcast_to([128, 8, 128]), ALU.mult, [R_ws], [R_w])
    hgT = k.alloc(8 * 512, BF16).rearrange("p (c t) -> p c t", c=8)
    sq = k.alloc(8 * 512, BF16).rearrange("p (c t) -> p c t", c=8)
    rx = k.alloc(16)
    R_hg, R_sq, R_rx = Res("l1hg"), Res("l1sq"), Res("l1rx")
    uv = k.alloc(1536)
    zm = k.alloc(256)
    R_uv, R_zm = Res("uv"), Res("zm")
    bst = k.alloc(12)
    mv = k.alloc(4)
    R_bn = Res("bn")
    vt = k.alloc(768)
    vt2 = k.alloc(768)
    R_vt = Res("vt")
    vn = k.alloc(768, BF16)
    R_vn = Res("vn")
    y = k.alloc(768, BF16)
    R_y = Res("y")
    qmn = k.alloc(256, BF16)
    R_qmn = Res("qmn1")
    tmpm = k.alloc(256)
    R_tmpm = Res("tmpm1")
    stt = k.alloc(8)
    R_st = Res("st1")
    for j in range(NSLOT):
        norm_prep_slot(k, xres[:, :, j * 512:(j + 1) * 512], R_x[j], g_attn, hgT, R_hg, sq, R_sq, rx[:, 4 * j:4 * j + 4], R_rx, 7)
        for bl in range(4):
            bg = 4 * j + bl
            tok = slice(bl * 128, (bl + 1) * 128)
            gt = slice(bg * 128, (bg + 1) * 128)
            for n in range(4):
                for ck in range(8):
                    k.MM(k.ps(n)[:, 0:448], hgT[:, ck, tok], Win[:, ck, n * 448:(n + 1) * 448], ck == 0, ck == 7, [R_hg, R_w], [RPS[n]])
            rxa = rx[:, bg:bg + 1]
            for n in range(3):
                k.ACT(uv[:, n * 448:(n + 1) * 448], k.ps(n)[:, 0:448], AF.Gelu, [RPS[n], R_rx], [R_uv], scale=rxa)
            k.ACT(uv[:, 1344:1536], k.ps(3)[:, 0:192], AF.Gelu, [RPS[3], R_rx], [R_uv], scale=rxa)
            k.ACT(zm, k.ps(3)[:, 192:448], AF.Identity, [RPS[3], R_rx], [R_zm], scale=rxa)
            v = uv[:, 768:1536]
            bst3 = bst.rearrange("p (a s) -> p a s", a=2)
            for a in range(2):
                k.S.op("dve", lambda e, a=a: e.bn_stats(out=bst3[:, a, :], in_=v[:, a * 384:(a + 1) * 384]), [R_uv], [R_bn])
            k.S.op("dve", lambda e: e.bn_aggr(out=mv[:, 0:2], in_=bst), [R_bn], [R_bn])
            k.rsqrt(mv[:, 2:3], mv[:, 1:2], 1.0, EPS, 1, [R_bn], [R_bn])
            k.TS("dve", vt, v, mv[:, 0:1], mv[:, 2:3], ALU.subtract, ALU.mult, [R_uv, R_bn], [R_vt])
            k.TT("pool", vt2, vt, lng, ALU.mult, [R_vt, k.Rg], [R_vt])
            k.TT("pool", vn, vt2, lnb, ALU.add, [R_vt, k.Rg], [R_vn])
            for g_ in range(8):
                b = 4 + g_ // 4
                c0 = (g_ % 4) * 96
                k.MM(k.ps(b)[:, c0:c0 + 96], wsp[:, g_, :], vn[:, g_ * 96:(g_ + 1) * 96], True, True, [R_w, R_vn], [RPS[b]])
            for g_ in range(8):
                b = 4 + g_ // 4
                c0 = (g_ % 4) * 96
                k.STT(y[:, g_ * 96:(g_ + 1) * 96], k.ps(b)[:, c0:c0 + 96], bsp[:, g_:g_ + 1], uv[:, g_ * 96:(g_ + 1) * 96], ALU.add, ALU.mult,
                      [RPS[b], k.Rg, R_uv], [R_y])
            memq_norm(k, zm, R_zm, gmq, qmn, R_qmn, tmpm, R_tmpm, stt, R_st)
            pb = k.ps_bf(6)
            for c in range(6):
                k.TR(pb[:, c * 128:(c + 1) * 128], y[:, c * 128:(c + 1) * 128], [R_y, k.Rc], [RPS[6]])
            for c in range(2):
                k.TR(pb[:, (6 + c) * 128:(7 + c) * 128], qmn[:, c * 128:(c + 1) * 128], [R_qmn, k.Rc], [RPS[6]])
            for c in range(6):
                k.CP("act", mixT[:, c, gt], pb[:, c * 128:(c + 1) * 128], [RPS[6]], [R_mix[c][j]])
            k.CP("act", qmT[:, :, gt], pb[:, 768:1024].rearrange("p (c t) -> p c t", c=2), [RPS[6]], [R_qmT])
    k.release(m)


class RopeTab:
    def __init__(self, k, nb):
        self.k = k
        self.nb = nb
        n = nb * 16
        self.posf = k.alloc(nb)
        self.ang = k.alloc(n)
        self.kq = k.alloc(n)
        self.ki = k.alloc(n, I32)
        self.y = k.alloc(n)
        self.m = k.alloc(n)
        self.cs2 = k.alloc(nb * 32).rearrange("p (b i) -> p b i", b=nb)
        self.sn2 = k.alloc(nb * 32).rearrange("p (b i) -> p b i", b=nb)
        self.R = Res("rope")

    def compute(self, posi, R_pos, invf):
        k, nb, R = self.k, self.nb, self.R
        ang, kq, ki, y, m, cs2, sn2 = self.ang, self.kq, self.ki, self.y, self.m, self.cs2, self.sn2
        ang3 = ang.rearrange("p (b i) -> p b i", b=nb)
        k.CP("dve", self.posf, posi, [R_pos], [R])
        k.TT("dve", ang3, self.posf.unsqueeze(2).broadcast_to([128, nb, 16]), invf.unsqueeze(1).broadcast_to([128, nb, 16]), ALU.mult,
             [R, k.Rg], [R])
        TWO_PI = 2.0 * np.pi
        C1 = 6.28125
        C2 = float(np.float32(TWO_PI - C1))
        k.TS("dve", kq, ang, float(1.0 / TWO_PI), None, ALU.mult, None, [R], [R])
        k.CP("dve", ki, kq, [R], [R])
        k.CP("dve", kq, ki, [R], [R])
        k.STT(y, kq, -C1, ang, ALU.mult, ALU.add, [R], [R])
        k.STT(y, kq, -C2, y, ALU.mult, ALU.add, [R], [R])

        def wrap(t):
            k.TS("dve", m, t, float(np.pi), None, ALU.is_gt, None, [R], [R])
            k.STT(t, m, -TWO_PI, t, ALU.mult, ALU.add, [R], [R])
            k.TS("dve", m, t, float(-np.pi), None, ALU.is_lt, None, [R], [R])
            k.STT(t, m, TWO_PI, t, ALU.mult, ALU.add, [R], [R])

        wrap(y)
        y3 = y.rearrange("p (b i) -> p b i", b=nb)
        k.ACT(sn2[:, :, 16:32], y3, AF.Sin, [R], [R])
        k.TS("dve", sn2[:, :, 0:16], sn2[:, :, 16:32], -1.0, None, ALU.mult, None, [R], [R])
        k.TS("dve", y, y, float(np.pi / 2), None, ALU.add, None, [R], [R])
        wrap(y)
        k.ACT(cs2[:, :, 0:16], y3, AF.Sin, [R], [R])
        k.CP("dve", cs2[:, :, 16:32], cs2[:, :, 0:16], [R], [R])


def phase_KV(k, c):
    RPS = k.RPS
    m0 = k.mark()
    R_w = Res("kvw")
    Win = k.alloc(8 * 288, BF16).rearrange("p (c n) -> p c n", c=8)
    w_in3 = chunked(c["w_in0"])
    for ck in range(8):
        k.DMA("pool", Win[:, ck, :], w_in3[:, ck, 384:672], (), [R_w])
    stage = k.alloc(768)
    R_stage = Res("kvstage")
    Wuk = load_w_scaled_bf16(k, c["w_uk"], 2, 768, c["g_kvlat"], R_w, stage, R_stage)
    Wuv = load_w_scaled_bf16(k, c["w_uv"], 2, 768, c["g_kvlat"], R_w, stage, R_stage)
    rt = c["rt"]
    xs2 = [k.alloc(8 * 512).rearrange("p (c t) -> p c t", c=8) for _ in range(2)]
    R_xs2 = [Res("kxs0"), Res("kxs1")]
    hg = k.alloc(8 * 512, BF16).rearrange("p (c t) -> p c t", c=8)
    sq = k.alloc(8 * 512, BF16).rearrange("p (c t) -> p c t", c=8)
    R_hg, R_sq = Res("khg"), Res("ksq")
    rx = k.alloc(4)
    valid = k.alloc(4)
    R_rx = Res("krx")
    z = k.alloc(4 * 288).rearrange("p (b n) -> p b n", b=4)
    sqz = k.alloc(4 * 288).rearrange("p (b n) -> p b n", b=4)
    R_z, R_sqz = Res("kz"), Res("ksqz")
    st = k.alloc(16)
    R_st = Res("kst")
    ckvn = k.alloc(4 * 256, BF16).rearrange("p (b n) -> p b n", b=4)
    R_ckvn = Res("kckvn")
    krg = k.alloc(4 * 32).rearrange("p (b n) -> p b n", b=4)
    krt = k.alloc(4 * 32).rearrange("p (b n) -> p b n", b=4)
    kru = k.alloc(4 * 32).rearrange("p (b n) -> p b n", b=4)
    krr = k.alloc(4 * 32, BF16).rearrange("p (b n) -> p b n", b=4)
    R_kr, R_krr = Res("kkr"), Res("kkrr")
    ckvT = k.alloc(2 * 512, BF16).rearrange("p (c t) -> p c t", c=2)
    R_ckvT = Res("kckvT")
    KTr_t = [k.alloc(512, BF16) for _ in range(2)]
    R_KTr = [Res("kKTr0"), Res("kKTr1")]
    KTn_t = [k.alloc(6 * 512, BF16).rearrange("p (c t) -> p c t", c=6) for _ in range(2)]
    R_KTn = [Res("kKTn0"), Res("kKTn1")]
    Vx_t = [k.alloc(12 * 4 * 65, BF16).rearrange("p (h b c) -> p h b c", h=12, b=4) for _ in range(2)]
    R_Vx = [Res("kVx0"), Res("kVx1")]
    sqk = k.alloc(768)
    R_sqk = Res("ksqk")
    ssk = k.alloc(32)
    R_ssk = Res("kssk")
    rk3 = c["rk"].rearrange("p (b h) -> p b h", h=12)
    xw3 = chunked(c["xw"])
    KTn_w3 = c["KTn_w"].rearrange("(c p) t -> p c t", p=128)
    Vx_w4 = c["Vx_w"].rearrange("h p (b c) -> p h b c", c=65)
    gk = c["gk"]
    posw = c["posw"]
    ckvn2 = [ckvn, k.alloc(4 * 256, BF16).rearrange("p (b n) -> p b n", b=4)]
    R_ckvn2 = [R_ckvn, Res("kckvn1")]
    krr2 = [krr, k.alloc(4 * 32, BF16).rearrange("p (b n) -> p b n", b=4)]
    R_krr2 = [R_krr, Res("kkrr1")]
    st2 = [st, k.alloc(16)]
    R_st2 = [R_st, Res("kst1")]
    valid2 = [valid, k.alloc(4)]
    R_val2 = [Res("kval0"), Res("kval1")]

    def xload(t):
        k.DMA("sp", xs2[t % 2], xw3[:, :, t * 512:(t + 1) * 512], (), [R_xs2[t % 2]])

    def s1a(t):
        xs, R_xs = xs2[t % 2], R_xs2[t % 2]
        if t + 1 < 32:
            xload(t + 1)
        k.TT("pool", hg, xs, c["g_attn"].unsqueeze(2).broadcast_to([128, 8, 512]), ALU.mult, [R_xs, k.Rg], [R_hg])
        k.ACT(sq, xs, AF.Square, [R_xs], [R_sq])
        pst = k.ps(3)[:, 508:512]
        for bl in range(4):
            for ck in range(8):
                k.MM(pst[:, bl:bl + 1], sq[:, ck, bl * 128:(bl + 1) * 128], k.ones_bf[:, 0:1], ck == 0, ck == 7, [R_sq, k.Rc], [RPS[3]])
        k.TS("dve", valid2[t % 2], pst, 0.0, None, ALU.is_gt, None, [RPS[3]], [R_val2[t % 2]])
        k.rsqrt(rx, pst, 1.0 / D, EPS, 4, [RPS[3]], [R_rx])

    def s1z(t, bl):
        for ck in range(8):
            k.MM(k.ps(bl)[:, 0:288], hg[:, ck, bl * 128:(bl + 1) * 128], Win[:, ck, :], ck == 0, ck == 7, [R_hg, R_w], [RPS[bl]])
        k.ACT(z[:, bl, :], k.ps(bl)[:, 0:288], AF.Identity, [RPS[bl], R_rx], [R_z], scale=rx[:, bl:bl + 1])

    def s1b(t):
        q = t % 2
        sT = st2[q]
        k.ACT(sqz, z, AF.Square, [R_z], [R_sqz])
        k.S.op("dve", lambda e: e.tensor_reduce(out=sT[:, 0:4], in_=sqz[:, :, 0:256], axis=AX.X, op=ALU.add), [R_sqz], [R_st2[q]])
        k.S.op("dve", lambda e: e.tensor_reduce(out=sT[:, 4:8], in_=sqz[:, :, 256:288], axis=AX.X, op=ALU.add), [R_sqz], [R_st2[q]])
        k.rsqrt(sT[:, 8:12], sT[:, 0:4], 1.0 / 256, EPS, 4, [R_st2[q]], [R_st2[q]])
        k.TT("dve", ckvn2[q], z[:, :, 0:256], sT[:, 8:12].unsqueeze(2).broadcast_to([128, 4, 256]), ALU.mult, [R_z, R_st2[q]], [R_ckvn2[q]])
        k.TT("pool", krg, z[:, :, 256:288], gk[:, 64:96].unsqueeze(1).broadcast_to([128, 4, 32]), ALU.mult, [R_z, k.Rg], [R_kr])
        tb4 = slice(4 * t, 4 * t + 4)
        k.TT("pool", krt, krg, rt.cs2[:, tb4, :], ALU.mult, [R_kr, rt.R], [R_kr])
        k.TT("pool", kru[:, :, 0:16], krg[:, :, 16:32], rt.sn2[:, tb4, 0:16], ALU.mult, [R_kr, rt.R], [R_kr])
        k.TT("pool", kru[:, :, 16:32], krg[:, :, 0:16], rt.sn2[:, tb4, 16:32], ALU.mult, [R_kr, rt.R], [R_kr])
        k.TT("pool", krr2[q], krt, kru, ALU.add, [R_kr], [R_krr2[q]])

    def s2tr(t):
        q = t % 2
        tcols = slice(t * 512, (t + 1) * 512)
        pb4, pb5 = k.ps_bf(4), k.ps_bf(5)
        for ck in range(2):
            for bl in range(4):
                k.TR(pb4[:, ck * 512 + bl * 128:ck * 512 + (bl + 1) * 128], ckvn2[q][:, bl, ck * 128:(ck + 1) * 128], [R_ckvn2[q], k.Rc], [RPS[4]])
        for bl in range(4):
            k.TR(pb5[0:32, bl * 128:(bl + 1) * 128], krr2[q][:, bl, :], [R_krr2[q], k.Rc], [RPS[5]])
        k.CP("act", ckvT, pb4.rearrange("p (c t) -> p c t", c=2), [RPS[4]], [R_ckvT])
        k.CP("act", KTr_t[q][0:32, :], pb5[0:32, 0:512], [RPS[5]], [R_KTr[q]])
        k.DMA("sp", c["KTr_w"][:, tcols], KTr_t[q][0:32, :], [R_KTr[q]], [c["R_KTr_w"]])

    def s2kv(t, bl):
        q = t % 2
        bk = [4, 5, 6, 7]
        tok = slice(bl * 128, (bl + 1) * 128)
        for n in range(2):
            for ck in range(2):
                k.MM(k.ps(bk[n])[:, 0:384], ckvT[:, ck, tok], Wuk[:, ck, n * 384:(n + 1) * 384], ck == 0, ck == 1, [R_ckvT, R_w], [RPS[bk[n]]])
            k.ACT(sqk[:, n * 384:(n + 1) * 384], k.ps(bk[n])[:, 0:384], AF.Square, [RPS[bk[n]]], [R_sqk])
        for n in range(2):
            for ck in range(2):
                k.MM(k.ps(bk[2 + n])[:, 0:384], ckvT[:, ck, tok], Wuv[:, ck, n * 384:(n + 1) * 384], ck == 0, ck == 1, [R_ckvT, R_w], [RPS[bk[2 + n]]])
            k.CP("dve" if n == 0 else "act", Vx_t[q][:, 6 * n:6 * n + 6, bl, 0:64], k.ps(bk[2 + n])[:, 0:384].rearrange("p (h d) -> p h d", h=6),
                 [RPS[bk[2 + n]]], [R_Vx[q]])
        k.S.op("dve", lambda e: e.tensor_reduce(out=ssk[:, 0:12], in_=sqk.rearrange("p (h d) -> p h d", h=12), axis=AX.X, op=ALU.add),
               [R_sqk], [R_ssk])
        k.TS("dve", ssk[:, 16:28], ssk[:, 0:12], st2[q][:, 4 + bl:5 + bl], None, ALU.add, None, [R_ssk, R_st2[q]], [R_ssk])
        k.rsqrt(rk3[:, 4 * t + bl, :], ssk[:, 16:28], 1.0, 96 * EPS, 12, [R_ssk], [c["R_rk"]])

    def s2kt(t):
        q = t % 2
        tcols = slice(t * 512, (t + 1) * 512)
        k.CP("pool", Vx_t[q][:, :, :, 64], valid2[q].unsqueeze(1).broadcast_to([128, 12, 4]), [R_val2[q]], [R_Vx[q]])
        for i in range(6):
            b = 4 + (i % 4)
            for ck in range(2):
                k.MM(k.ps(b), Wuk[:, ck, i * 128:(i + 1) * 128], ckvT[:, ck, :], ck == 0, ck == 1, [R_w, R_ckvT], [RPS[b]])
            k.CP("act" if i % 2 == 0 else "dve", KTn_t[q][:, i, :], k.ps(b), [RPS[b]], [R_KTn[q]])
        k.DMA("sp", KTn_w3[:, :, tcols], KTn_t[q], [R_KTn[q]], [c["R_KTn_w"]])
        for h0 in range(0, 12, 4):
            k.DMA("sp", Vx_w4[:, h0:h0 + 4, 4 * t:4 * t + 4, :], Vx_t[q][:, h0:h0 + 4, :, :], [R_Vx[q]], [c["R_Vx_w"]])

    NT = 32
    xload(0)
    for t in range(NT + 1):
        if t < NT:
            s1a(t)
        if t > 0:
            s2tr(t - 1)
        for bl in range(4):
            if t < NT:
                s1z(t, bl)
            if t > 0:
                s2kv(t - 1, bl)
        if t < NT:
            s1b(t)
        if t > 0:
            s2kt(t - 1)
    k.release(m0)


def phase_A_fused(k, c):
    RPS = k.RPS
    m0 = k.mark()
    R_w = Res("aw")
    Win = load_w_bf16(k, c["w_in0"], 8, 928, R_w)
    stage = k.alloc(1152)
    R_stage = Res("astage")
    Wuq = load_w_scaled_bf16(k, c["w_uq"], 3, 1152, c["g_qlat"], R_w, stage, R_stage)
    gq, gk, gmq = c["gq"], c["gk"], c["gmq"]
    Gq = k.alloc(96)
    k.CP("dve", Gq, gq, [k.Rg], [k.Rg])
    k.TT("dve", Gq[:, 0:64], gq[:, 0:64], gk[:, 0:64], ALU.mult, [k.Rg], [k.Rg])
    qT, qmT, R_qT, R_qmT = c["qT"], c["qmT"], c["R_qT"], c["R_qmT"]
    rt = c["rt"]
    xs = k.alloc(8 * 512).rearrange("p (c t) -> p c t", c=8)
    hgT = k.alloc(8 * 512, BF16).rearrange("p (c t) -> p c t", c=8)
    sq = k.alloc(8 * 512, BF16).rearrange("p (c t) -> p c t", c=8)
    R_xs, R_hg, R_sq = Res("axs"), Res("ahg"), Res("asq")
    rx = k.alloc(4)
    R_rx = Res("arx")
    z = k.alloc(928)
    R_z = Res("az")
    stt = k.alloc(16)
    R_st = Res("ast")
    qln = k.alloc(384, BF16)
    qmn = k.alloc(256, BF16)
    R_tm = Res("atm")
    tmpm = k.alloc(256)
    R_tmpm = Res("atmpm")
    qlT = k.alloc(3 * 128, BF16).rearrange("p (c t) -> p c t", c=3)
    R_qlT = Res("aqlT")
    sqq = k.alloc(1152)
    R_sqq = Res("asqq")
    ssq = k.alloc(32)
    R_ssq = Res("assq")
    qn = k.alloc(1152)
    R_qn = Res("aqn")
    qg = k.alloc(1152)
    R_qg = Res("aqg")
    qt = k.alloc(12 * 32)
    qu = k.alloc(12 * 32)
    R_qtu = Res("aqtu")
    qfin = k.alloc(12 * 96, BF16).rearrange("p (h d) -> p h d", h=12)
    R_qfin = Res("aqfin")
    xw3 = chunked(c["xw"])
    for j in range(NSLOT):
        wt = 8 * j + 7
        k.DMA("sp", xs, xw3[:, :, wt * 512:(wt + 1) * 512], (), [R_xs])
        norm_prep_slot(k, xs, R_xs, c["g_attn"], hgT, R_hg, sq, R_sq, rx, R_rx, 7)
        for bl in range(4):
            bg = 4 * j + bl
            wb = 4 * wt + bl
            tok = slice(bl * 128, (bl + 1) * 128)
            for n, (c0, c1) in enumerate([(0, 512), (512, 928)]):
                for ck in range(8):
                    k.MM(k.ps(n)[:, 0:c1 - c0], hgT[:, ck, tok], Win[:, ck, c0:c1], ck == 0, ck == 7, [R_hg, R_w], [RPS[n]])
                k.ACT(z[:, c0:c1], k.ps(n)[:, 0:c1 - c0], AF.Identity, [RPS[n], R_rx], [R_z], scale=rx[:, bl:bl + 1])
            k.ACT(sqq[:, 0:384], z[:, 0:384], AF.Square, [R_z], [R_sqq])
            k.S.op("dve", lambda e: e.tensor_reduce(out=stt[:, 0:1], in_=sqq[:, 0:384], axis=AX.X, op=ALU.add), [R_sqq], [R_st])
            k.rsqrt(stt[:, 4:5], stt[:, 0:1], 1.0 / 384, EPS, 1, [R_st], [R_st])
            k.TS("dve", qln, z[:, 0:384], stt[:, 4:5], None, ALU.mult, None, [R_z, R_st], [R_tm])
            memq_norm(k, z[:, 672:928], R_z, gmq, qmn, R_tm, tmpm, R_tmpm, stt[:, 8:16], R_st)
            pb = k.ps_bf(2)
            for cc in range(3):
                k.TR(pb[:, cc * 128:(cc + 1) * 128], qln[:, cc * 128:(cc + 1) * 128], [R_tm, k.Rc], [RPS[2]])
            for cc in range(2):
                k.TR(pb[:, (3 + cc) * 128:(4 + cc) * 128], qmn[:, cc * 128:(cc + 1) * 128], [R_tm, k.Rc], [RPS[2]])
            k.CP("act", qlT, pb[:, 0:384].rearrange("p (c t) -> p c t", c=3), [RPS[2]], [R_qlT])
            k.CP("act", qmT[:, :, bg * 128:(bg + 1) * 128], pb[:, 384:640].rearrange("p (c t) -> p c t", c=2), [RPS[2]], [R_qmT])
            for n in range(3):
                for ck in range(3):
                    k.MM(k.ps(3 + n)[:, 0:384], qlT[:, ck, :], Wuq[:, ck, n * 384:(n + 1) * 384], ck == 0, ck == 2, [R_qlT, R_w], [RPS[3 + n]])
                k.ACT(sqq[:, n * 384:(n + 1) * 384], k.ps(3 + n)[:, 0:384], AF.Square, [RPS[3 + n]], [R_sqq])
            k.S.op("dve", lambda e: e.tensor_reduce(out=ssq[:, 0:12], in_=sqq.rearrange("p (h d) -> p h d", h=12), axis=AX.X, op=ALU.add),
                   [R_sqq], [R_ssq])
            k.rsqrt(ssq[:, 16:28], ssq[:, 0:12], 1.0 / 96, EPS, 12, [R_ssq], [R_ssq])
            for n in range(3):
                k.TT("dve", qn[:, n * 384:(n + 1) * 384].rearrange("p (h d) -> p h d", h=4),
                     k.ps(3 + n)[:, 0:384].rearrange("p (h d) -> p h d", h=4),
                     ssq[:, 16 + 4 * n:20 + 4 * n].unsqueeze(2).broadcast_to([128, 4, 96]), ALU.mult, [RPS[3 + n], R_ssq], [R_qn])
            qn3 = qn.rearrange("p (h d) -> p h d", h=12)
            qg3 = qg.rearrange("p (h d) -> p h d", h=12)
            k.TT("pool", qg3, qn3, Gq.unsqueeze(1).broadcast_to([128, 12, 96]), ALU.mult, [R_qn, k.Rg], [R_qg])
            qt3 = qt.rearrange("p (h d) -> p h d", h=12)
            qu3 = qu.rearrange("p (h d) -> p h d", h=12)
            k.TT("pool", qt3, qg3[:, :, 64:96], rt.cs2[:, wb, :].unsqueeze(1).broadcast_to([128, 12, 32]), ALU.mult, [R_qg, rt.R], [R_qtu])
            k.TT("pool", qu3[:, :, 0:16], qg3[:, :, 80:96], rt.sn2[:, wb, 0:16].unsqueeze(1).broadcast_to([128, 12, 16]), ALU.mult,
                 [R_qg, rt.R], [R_qtu])
            k.TT("pool", qu3[:, :, 16:32], qg3[:, :, 64:80], rt.sn2[:, wb, 16:32].unsqueeze(1).broadcast_to([128, 12, 16]), ALU.mult,
                 [R_qg, rt.R], [R_qtu])
            k.TT("dve", qfin[:, :, 64:96], qt3, qu3, ALU.add, [R_qtu], [R_qfin])
            k.CP("dve", qfin[:, :, 0:64], qg3[:, :, 0:64], [R_qg], [R_qfin])
            for h in range(12):
                bk = 6 if h < 8 else 7
                hh = h % 8
                k.TR(k.ps_bf(bk)[0:96, hh * 128:(hh + 1) * 128], qfin[:, h, :], [R_qfin, k.Rc], [RPS[bk]])
            k.CP("act", qT[0:96, 0:8, bg * 128:(bg + 1) * 128], k.ps_bf(6)[0:96, :].rearrange("p (h t) -> p h t", h=8), [RPS[6]], [R_qT])
            k.CP("act", qT[0:96, 8:12, bg * 128:(bg + 1) * 128], k.ps_bf(7)[0:96, 0:512].rearrange("p (h t) -> p h t", h=4), [RPS[7]], [R_qT])
    k.release(m0)


XRES_WORDS = 8 * TOWN


def phase_rest(k, fz=None):
    S = k.S
    k.final_dmas = []
    d = {}
    if fz is None:
        k.Rg = Res("gains")
        d["xT"] = k.inp("xT", [D, TOWN])
        KTn_w = k.inp("KTn_w", [768, SEQ], BF16)
        KTr_w = k.inp("KTr_w", [32, SEQ], BF16)
        Vx_w = k.inp("Vx_w", [12, 128, 128 * 65], BF16)
        rk_w = k.inp("rk_w", [128, 128 * 12])
        qT_i = k.inp("qT_i", [96, 12 * TOWN], BF16)
        qmT_i = k.inp("qmT_i", [128, 2 * TOWN], BF16)
        xT3_ = chunked(d["xT"])
        xsrc = lambda j: xT3_[:, :, j * 512:(j + 1) * 512]
    else:
        KTn_w, KTr_w, Vx_w = fz["KTn_w"], fz["KTr_w"], fz["Vx_w"]
        xw3_ = chunked(fz["xw"])
        xsrc = lambda j: xw3_[:, :, (8 * j + 7) * 512:(8 * j + 8) * 512]
    d["memT"] = k.inp("memT", [D, 256])
    for L in (0, 1):
        d[f"g_mem{L}"] = k.inp(f"g_mem{L}", [128, 8])
        d[f"w_mkv{L}"] = k.inp(f"w_mkv{L}", [D, 512])
        d[f"g_mk{L}"] = k.inp(f"g_mk{L}", [128, 64])
        d[f"w_out{L}"] = k.inp(f"w_out{L}", [D, D])
        d[f"g_ffn{L}"] = k.inp(f"g_ffn{L}", [128, 8])
        d[f"w_gate{L}"] = k.inp(f"w_gate{L}", [D, DFF])
        d[f"w_up{L}"] = k.inp(f"w_up{L}", [D, DFF])
        d[f"w_down{L}"] = k.inp(f"w_down{L}", [DFF, D])
    d["g_attn1"] = k.inp("g_attn1", [128, 8])
    d["w_in1"] = k.inp("w_in1", [D, 1792])
    d["ln_g"] = k.inp("ln_g", [128, 768])
    d["ln_b"] = k.inp("ln_b", [128, 768])
    d["w_spT"] = k.inp("w_spT", [128, 8 * 128])
    d["b_sp"] = k.inp("b_sp", [128, 8])
    d["g_mq1"] = k.inp("g_mq1", [128, 64])
    yT_o = k.outp("yT", [D, TOWN])

    xres = k.arena[:, ARENA_WORDS - XRES_WORDS:ARENA_WORDS].rearrange("p (c t) -> p c t", c=8)
    R_x = [Res(f"x{j}") for j in range(NSLOT)]
    LIMIT_FREE = ARENA_WORDS
    LIMIT_X = ARENA_WORDS - XRES_WORDS
    base = k.mark() if fz is None else fz["base0"]

    mixT = k.alloc(8 * TOWN, BF16).rearrange("p (c t) -> p c t", c=8)
    R_mix = [[Res(f"mix{c}_{j}") for j in range(NSLOT)] for c in range(8)]
    KmT = k.alloc(2 * 256, BF16).rearrange("p (c t) -> p c t", c=2)
    Vmx = k.alloc(2 * 4 * 65, BF16).rearrange("p (m h c) -> p m h c", m=2, h=4)
    R_KmT, R_Vmx = Res("KmT"), Res("Vmx")
    mem_kv_prep(k, "m0", d["memT"], d["g_mem0"], d["w_mkv0"], d["g_mk0"], KmT, R_KmT, Vmx, R_Vmx)
    ckpt(k, "memkv0")
    m_at = k.mark()
    if fz is None:
        qT = k.alloc(12 * TOWN, BF16).rearrange("p (h t) -> p h t", h=12)
        qmT = k.alloc(2 * TOWN, BF16).rearrange("p (c t) -> p c t", c=2)
        rk = k.alloc(128 * 12)
        R_qT, R_qmT, R_rk = Res("qT"), Res("qmT"), Res("rk")
        k.DMA("sp", qT[0:96].rearrange("p h t -> p (h t)"), qT_i, (), [R_qT])
        k.DMA("sp", qmT.rearrange("p c t -> p (c t)"), qmT_i, (), [R_qmT])
        k.DMA("sp", rk, rk_w, (), [R_rk])
    else:
        qT, qmT, rk = fz["qT"], fz["qmT"], fz["rk"]
        R_qT, R_qmT, R_rk = fz["R_qT"], fz["R_qmT"], fz["R_rk"]
    at = Attn(k, mixT, R_mix)
    at.mem_attention(qmT, R_qmT, KmT, R_KmT, Vmx, R_Vmx)
    ckpt(k, "memattn0")
    at.causal_attention(qT, R_qT, rk, R_rk, KTn_w, KTr_w, Vx_w)
    assert k.off <= LIMIT_FREE
    ckpt(k, "attn0")
    k.release(m_at)
    wout_residual(k, "l0", d["w_out0"], mixT, R_mix, xres, R_x, xsrc)
    assert k.off <= LIMIT_X
    ckpt(k, "wout0")
    k.release(base)
    ffn(k, "f0", xres, R_x, d["g_ffn0"], d["w_gate0"], d["w_up0"], d["w_down0"])
    ckpt(k, "ffn0")
    mixT = k.alloc(8 * TOWN, BF16).rearrange("p (c t) -> p c t", c=8)
    R_mix = [[Res(f"mixb{c}_{j}") for j in range(NSLOT)] for c in range(8)]
    qmT = k.alloc(2 * TOWN, BF16).rearrange("p (c t) -> p c t", c=2)
    R_qmT = Res("qmT1")
    KmT = k.alloc(2 * 256, BF16).rearrange("p (c t) -> p c t", c=2)
    Vmx = k.alloc(2 * 4 * 65, BF16).rearrange("p (m h c) -> p m h c", m=2, h=4)
    R_KmT, R_Vmx = Res("KmT1"), Res("Vmx1")
    mem_kv_prep(k, "m1", d["memT"], d["g_mem1"], d["w_mkv1"], d["g_mk1"], KmT, R_KmT, Vmx, R_Vmx)
    sgu_layer1(k, xres, R_x, mixT, R_mix, qmT, R_qmT, d)
    ckpt(k, "sgu")
    m_at = k.mark()
    at = Attn(k, mixT, R_mix)
    at.mem_attention(qmT, R_qmT, KmT, R_KmT, Vmx, R_Vmx)
    k.release(m_at)
    wout_residual(k, "l1", d["w_out1"], mixT, R_mix, xres, R_x, None)
    ckpt(k, "wout1")
    k.release(base)
    ffn(k, "f1", xres, R_x, d["g_ffn1"], d["w_gate1"], d["w_up1"], d["w_down1"])
    ckpt(k, "ffn1")
    yT3 = chunked(yT_o)
    for j in range(NSLOT):
        cols = slice(j * 512, (j + 1) * 512)
        k.final_dmas.append(k.DMA("sp", yT3[:, :, cols], xres[:, :, cols], [R_x[j]], ()))


def build_fused_body(k):
    nc = k.nc
    k.final_dmas = []
    k.Rg = Res("gains")
    c = {}
    c["xw"] = k.inp("xw", [D, SEQ])
    posw_d = k.inp("posw", [128, 128], I32)
    invf_d = k.inp("invf", [128, 16])
    c["w_in0"] = k.inp("w_in0", [D, 928])
    c["w_uq"] = k.inp("w_uq", [384, 1152])
    c["w_uk"] = k.inp("w_uk", [256, 768])
    c["w_uv"] = k.inp("w_uv", [256, 768])
    small = {}
    for name, w in [("g_attn0", 8), ("g_qlat", 3), ("g_kvlat", 2), ("gq_tile", 96), ("gk_tile", 96), ("g_mq0", 64)]:
        dd = k.inp(name, [128, w])
        t = k.alloc(w)
        k.DMA("sp", t, dd, (), [k.Rg])
        small[name] = t
    c["g_attn"], c["g_qlat"], c["g_kvlat"] = small["g_attn0"], small["g_qlat"], small["g_kvlat"]
    c["gq"], c["gk"], c["gmq"] = small["gq_tile"], small["gk_tile"], small["g_mq0"]
    c["invf"] = k.alloc(16)
    k.DMA("sp", c["invf"], invf_d, (), [k.Rg])
    c["posw"] = k.alloc(128, I32)
    c["R_pos"] = Res("posw")
    k.DMA("sp", c["posw"], posw_d, (), [c["R_pos"]])
    c["KTn_w"] = nc.dram_tensor("KTn_scr", [768, SEQ], BF16).ap()
    c["KTr_w"] = nc.dram_tensor("KTr_scr", [32, SEQ], BF16).ap()
    c["Vx_w"] = nc.dram_tensor("Vx_scr", [12, 128, 128 * 65], BF16).ap()
    c["R_KTn_w"], c["R_KTr_w"], c["R_Vx_w"] = Res("KTn_w"), Res("KTr_w"), Res("Vx_w")
    c["base0"] = k.mark()
    c["qT"] = k.alloc(12 * TOWN, BF16).rearrange("p (h t) -> p h t", h=12)
    c["qmT"] = k.alloc(2 * TOWN, BF16).rearrange("p (c t) -> p c t", c=2)
    c["rk"] = k.alloc(128 * 12)
    c["R_qT"], c["R_qmT"], c["R_rk"] = Res("qT"), Res("qmT"), Res("rk")
    m_rt = k.mark()
    cs2 = k.alloc(128 * 32).rearrange("p (b i) -> p b i", b=128)
    sn2 = k.alloc(128 * 32).rearrange("p (b i) -> p b i", b=128)
    m_tmp = k.mark()
    rt = RopeTab(k, 128)
    rt.compute(c["posw"], c["R_pos"], c["invf"])
    k.CP("pool", cs2, rt.cs2, [rt.R], [rt.R])
    k.CP("pool", sn2, rt.sn2, [rt.R], [rt.R])
    k.release(m_tmp)
    rt.cs2, rt.sn2 = cs2, sn2
    c["rt"] = rt
    phase_KV(k, c)
    ckpt(k, "kv")
    phase_A_fused(k, c)
    ckpt(k, "afused")
    k.release(m_rt)
    phase_rest(k, fz=c)


_CACHE = {}


def _get(mode):
    if mode not in _CACHE:
        _CACHE[mode] = build(mode)
    return _CACHE[mode]


def own_tokens(c):
    idx = []
    for j in range(NSLOT):
        g = 8 * j + c
        idx.append(np.arange(g * 512, (g + 1) * 512))
    return np.concatenate(idx)


def rep128(v):
    return np.ascontiguousarray(np.broadcast_to(np.asarray(v, np.float32)[None, :], (128, len(v))))


def pchunk(v):
    v = np.asarray(v, np.float32)
    return np.ascontiguousarray(v.reshape(-1, 128).T)


def l1_inputs(inp, c):
    tok = own_tokens(c)
    x = inp["x"][0]
    pos = inp["positions"][0][tok].astype(np.int32)
    inv_freq = (10000.0 ** (-np.arange(0, 32, 2, dtype=np.float32) / 32)).astype(np.float32)
    w_ukv = inp["l0_w_ukv"]
    return {
        "xT": np.ascontiguousarray(x[tok].T),
        "pos": np.ascontiguousarray(pos.reshape(16, 128).T),
        "invf": rep128(inv_freq),
        "w_in0": inp["l0_w_in"],
        "g_attn0": pchunk(inp["l0_attn_norm"]),
        "w_uq": np.ascontiguousarray(inp["l0_w_uq"].reshape(384, 1152)),
        "g_qlat": pchunk(inp["l0_q_lat_norm"]),
        "w_uk": np.ascontiguousarray(w_ukv[:, :, 0:64].reshape(256, 768)),
        "w_uv": np.ascontiguousarray(w_ukv[:, :, 64:128].reshape(256, 768)),
        "g_kvlat": pchunk(inp["l0_kv_lat_norm"]),
        "gq_tile": rep128(inp["l0_q_norm"]),
        "gk_tile": rep128(inp["l0_k_norm"]),
        "g_mq0": rep128(inp["l0_mq_norm"]),
    }


def l2_weights(inp):
    w = {"memT": np.ascontiguousarray(inp["mem"][0].T)}
    for L in (0, 1):
        p = f"l{L}_"
        w[f"g_mem{L}"] = pchunk(inp[p + "mem_norm"])
        w[f"w_mkv{L}"] = inp[p + "w_mem_kv"]
        w[f"g_mk{L}"] = rep128(inp[p + "mk_norm"])
        w[f"w_out{L}"] = inp[p + "w_out"]
        w[f"g_ffn{L}"] = pchunk(inp[p + "ffn_norm"])
        w[f"w_gate{L}"] = inp[p + "w_gate"]
        w[f"w_up{L}"] = inp[p + "w_up"]
        w[f"w_down{L}"] = inp[p + "w_down"]
    w["g_attn1"] = pchunk(inp["l1_attn_norm"])
    w["w_in1"] = inp["l1_w_in"]
    w["ln_g"] = rep128(inp["l1_sgu_ln_g"])
    w["ln_b"] = rep128(inp["l1_sgu_ln_b"])
    w["w_spT"] = np.ascontiguousarray(inp["l1_w_spatial"].transpose(2, 0, 1).reshape(128, 8 * 128))
    w["b_sp"] = np.ascontiguousarray(inp["l1_b_spatial"].T)
    w["g_mq1"] = rep128(inp["l1_mq_norm"])
    return {k_: np.ascontiguousarray(np.asarray(v, np.float32)) for k_, v in w.items()}


def gather_payload(res):
    NCH = 7 + 32
    bf = ml_dtypes.bfloat16
    KTn = np.zeros((768, NCH * 512), bf)
    KTr = np.zeros((32, NCH * 512), bf)
    Vx = np.zeros((12, 128, NCH * 4, 65), bf)
    rk = np.zeros((128, NCH * 4, 12), np.float32)
    for r in range(NCORES):
        a = np.asarray(res[r]["KTn_o"])
        b = np.asarray(res[r]["KTr_o"])
        v = np.asarray(res[r]["Vx_o"]).reshape(12, 128, 16, 65)
        q = np.asarray(res[r]["rk_o"]).reshape(128, 16, 12)
        for j in range(NSLOT):
            g = 8 * j + r
            KTn[:, (7 + g) * 512:(8 + g) * 512] = a[:, j * 512:(j + 1) * 512]
            KTr[:, (7 + g) * 512:(8 + g) * 512] = b[:, j * 512:(j + 1) * 512]
            Vx[:, :, (7 + g) * 4:(8 + g) * 4, :] = v[:, :, 4 * j:4 * j + 4, :]
            rk[:, (7 + g) * 4:(8 + g) * 4, :] = q[:, 4 * j:4 * j + 4, :]
    return KTn, KTr, Vx, rk


def kernel_unfused(**inp):
    inp = {k_: np.asarray(v) for k_, v in inp.items()}
    k1 = _get("L1")
    maps = [l1_inputs(inp, c) for c in range(NCORES)]
    r1 = run_bass_kernel_spmd(k1.nc, maps, core_ids=list(range(NCORES))).results
    KTn, KTr, Vx, rk = gather_payload(r1)
    k2 = _get("L2")
    w = l2_weights(inp)
    maps2 = []
    for c in range(NCORES):
        m = dict(w)
        m["xT"] = maps[c]["xT"]
        m["KTn_w"] = np.ascontiguousarray(KTn[:, c * 512:(c + 32) * 512])
        m["KTr_w"] = np.ascontiguousarray(KTr[:, c * 512:(c + 32) * 512])
        m["Vx_w"] = np.ascontiguousarray(Vx[:, :, c * 4:(c + 32) * 4, :]).reshape(12, 128, 128 * 65)
        m["rk_w"] = np.ascontiguousarray(rk[:, c * 4:(c + 32) * 4, :]).reshape(128, 128 * 12)
        m["qT_i"] = np.asarray(r1[c]["qT_o"])
        m["qmT_i"] = np.asarray(r1[c]["qmT_o"])
        maps2.append(m)
    r2 = run_bass_kernel_spmd(k2.nc, maps2, core_ids=list(range(NCORES))).results
    out = np.zeros((1, SEQ, D), np.float32)
    for c in range(NCORES):
        out[0, own_tokens(c), :] = np.asarray(r2[c]["yT"]).T
    return out


def fused_inputs(inp, c, w):
    x = inp["x"][0]
    pos = inp["positions"][0].astype(np.int32)
    xw = np.zeros((D, SEQ), np.float32)
    pw = np.zeros((SEQ,), np.int32)
    g0 = c - 7
    lo = max(0, g0)
    n = (g0 + 32 - lo) * 512
    dst = (lo - g0) * 512
    xw[:, dst:dst + n] = x[lo * 512:lo * 512 + n].T
    pw[dst:dst + n] = pos[lo * 512:lo * 512 + n]
    m = dict(w)
    m["xw"] = xw
    m["posw"] = np.ascontiguousarray(pw.reshape(128, 128).T)
    return m


def kernel(**inp):
    inp = {k_: np.asarray(v) for k_, v in inp.items()}
    kf = _get("fused")
    w = l2_weights(inp)
    l1 = l1_inputs(inp, 0)
    for name in ("invf", "w_in0", "g_attn0", "w_uq", "g_qlat", "w_uk", "w_uv", "g_kvlat", "gq_tile", "gk_tile", "g_mq0"):
        w[name] = l1[name]
    maps = [fused_inputs(inp, c, w) for c in range(NCORES)]
    r = run_bass_kernel_spmd(kf.nc, maps, core_ids=list(range(NCORES))).results
    out = np.zeros((1, SEQ, D), np.float32)
    for c in range(NCORES):
        out[0, own_tokens(c), :] = np.asarray(r[c]["yT"]).T
    return out
```

```python
import contextlib
import numpy as np
import ml_dtypes
import concourse.bass as bass
import concourse.mybir as mybir
from concourse.bass_utils import run_bass_kernel_spmd

F32 = mybir.dt.float32
BF16 = mybir.dt.bfloat16
I32 = mybir.dt.int32
AF = mybir.ActivationFunctionType
ALU = mybir.AluOpType
AX = mybir.AxisListType

NCORES = 8
SEQ = 16384
D = 1024
TOWN = 2048
NSLOT = 4
DFF = 2816
NF = 22
EPS = 1e-6
MASKNEG = -30000.0


class Res:
    __slots__ = ("name", "last_w", "readers", "dma_w")

    def __init__(self, name=""):
        self.name = name
        self.last_w = None
        self.readers = []
        self.dma_w = []


class Op:
    __slots__ = ("eng", "fn", "deps", "is_dma", "needed", "token")

    def __init__(self, eng, fn, is_dma):
        self.eng = eng
        self.fn = fn
        self.deps = []
        self.is_dma = is_dma
        self.needed = False
        self.token = None


class Sched:
    ENGS = ("pe", "act", "dve", "pool", "sp")
    NDMA = 24

    def __init__(self, nc):
        self.nc = nc
        self.ops = {e: [] for e in self.ENGS}
        self.all_ops = []
        self.dma_ops = []
        self.cc_ops = []
        self.fence = []

    def _collect(self, op, reads, writes):
        deps = []
        for r in reads:
            if r.last_w is not None:
                deps.append(r.last_w)
            deps.extend(r.dma_w)
        for w in writes:
            if w.last_w is not None:
                deps.append(w.last_w)
            deps.extend(w.readers)
            if not op.is_dma:
                deps.extend(w.dma_w)
        deps.extend(self.fence)
        out = []
        seen = set()
        for d in deps:
            if id(d) in seen or d is op:
                continue
            seen.add(id(d))
            if (not d.is_dma) and (not op.is_dma) and d.eng == "pe" and op.eng == "pe":
                continue
            out.append(d)
        op.deps = out
        for r in reads:
            r.readers.append(op)
        for w in writes:
            if op.is_dma:
                if w.readers:
                    w.dma_w = [op]
                else:
                    w.dma_w.append(op)
                w.readers = []
            else:
                w.last_w = op
                w.dma_w = []
                w.readers = []

    def op(self, eng, fn, reads=(), writes=()):
        o = Op(eng, fn, False)
        self._collect(o, reads, writes)
        self.ops[eng].append(o)
        self.all_ops.append(o)
        return o

    def dma(self, eng, fn, reads=(), writes=()):
        o = Op(eng, fn, True)
        self._collect(o, reads, writes)
        k = len(self.dma_ops)
        if k >= self.NDMA:
            prev = self.dma_ops[k - self.NDMA]
            if prev not in o.deps:
                o.deps.append(prev)
        o.token = ("dma", k % self.NDMA, 16 * (k // self.NDMA + 1))
        o.needed = True
        self.dma_ops.append(o)
        self.ops[eng].append(o)
        self.all_ops.append(o)
        return o

    def cc(self, fn, reads=(), writes=()):
        o = Op("pool", fn, True)
        self._collect(o, reads, writes)
        o.token = ("cc", len(self.cc_ops), 16)
        o.needed = True
        self.cc_ops.append(o)
        self.ops["pool"].append(o)
        self.all_ops.append(o)
        return o

    def barrier(self):
        fence = []
        for e in self.ENGS:
            comp = [o for o in self.ops[e] if not o.is_dma]
            if comp:
                fence.append(comp[-1])
        fence.extend(self.dma_ops[-self.NDMA:])
        self.fence = fence

    def emit(self, final_waits=()):
        nc = self.nc
        for o in self.all_ops:
            for d in o.deps:
                d.needed = True
        for o in final_waits:
            o.needed = True
        for e in self.ENGS:
            cnt = 0
            for o in self.ops[e]:
                if o.is_dma:
                    continue
                if o.needed:
                    cnt += 1
                    o.token = ("eng", e, cnt)
        with contextlib.ExitStack() as st:
            esem = {e: st.enter_context(nc.semaphore(f"s_{e}")) for e in self.ENGS}
            dsem = [st.enter_context(nc.semaphore(f"s_dma{i}")) for i in range(self.NDMA)]
            csem = [st.enter_context(nc.semaphore(f"s_cc{i}")) for i in range(len(self.cc_ops))]
            block = st.enter_context(nc.Block())

            def semof(tok):
                if tok[0] == "eng":
                    return esem[tok[1]], tok[2], ("eng", tok[1])
                if tok[0] == "cc":
                    return csem[tok[1]], tok[2], ("cc", tok[1])
                return dsem[tok[1]], tok[2], ("dma", tok[1])

            def run(e, eh, extra_final=False):
                waited = {}
                for o in self.ops[e]:
                    for d in o.deps:
                        sem, val, key = semof(d.token)
                        if waited.get(key, 0) >= val:
                            continue
                        waited[key] = val
                        eh.wait_ge(sem, val)
                    inst = o.fn(eh)
                    if o.is_dma and o.token[0] == "cc":
                        inst.then_inc(csem[o.token[1]], 16)
                    elif o.is_dma:
                        inst.then_inc(dsem[o.token[1]], 16)
                    elif o.needed:
                        inst.then_inc(esem[e], 1)
                if extra_final:
                    for o in final_waits:
                        sem, val, key = semof(o.token)
                        if waited.get(key, 0) >= val:
                            continue
                        waited[key] = val
                        eh.wait_ge(sem, val)

            @block.tensor
            def _(eh):
                run("pe", eh)

            @block.scalar
            def _(eh):
                run("act", eh)

            @block.vector
            def _(eh):
                run("dve", eh)

            @block.gpsimd
            def _(eh):
                run("pool", eh)

            @block.sync
            def _(eh):
                run("sp", eh, extra_final=True)


ARENA_WORDS = 53000


class K:
    def __init__(self, mode):
        self.mode = mode
        self.nc = bass.Bass("TRN2", target_bir_lowering=False)
        self.S = Sched(self.nc)
        self.din = {}
        self.dout = {}
        self.off = 0
        self.rot = {}

    def inp(self, name, shape, dt=F32):
        ap = self.nc.dram_tensor(name, list(shape), dt, kind="ExternalInput").ap()
        self.din[name] = ap
        return ap

    def outp(self, name, shape, dt=F32):
        ap = self.nc.dram_tensor(name, list(shape), dt, kind="ExternalOutput").ap()
        self.dout[name] = ap
        return ap

    def alloc(self, cols, dt=F32):
        w = cols if dt in (F32, I32) else (cols + 1) // 2
        assert self.off + w <= ARENA_WORDS, f"arena overflow {self.off + w}"
        ap = self.arena[:, self.off:self.off + w]
        self.off += w
        if dt != F32:
            ap = ap.bitcast(dt)
            if ap.shape[1] != cols:
                ap = ap[:, 0:cols]
        return ap

    def mark(self):
        return self.off

    def release(self, m):
        self.S.barrier()
        self.off = m

    def ps(self, b, n=512):
        return self.psum[:, 512 * b:512 * b + n]

    def ps_bf(self, b):
        return self.psum[:, 512 * b:512 * (b + 1)].bitcast(BF16)

    def MM(self, out, lhsT, rhs, start, stop, rd, wr):
        return self.S.op("pe", lambda e: e.matmul(out, lhsT=lhsT, rhs=rhs, start=start, stop=stop), rd, wr)

    def TR(self, out, in_, rd, wr):
        ident = self.ident
        return self.S.op("pe", lambda e: e.transpose(out=out, in_=in_, identity=ident), rd, wr)

    def ACT(self, out, in_, func, rd, wr, scale=None, bias=None):
        kw = {}
        if scale is not None:
            kw["scale"] = scale
        if bias is not None:
            kw["bias"] = bias
        return self.S.op("act", lambda e: e.activation(out=out, in_=in_, func=func, **kw), rd, wr)

    def TT(self, eng, out, in0, in1, op, rd, wr):
        return self.S.op(eng, lambda e: e.tensor_tensor(out=out, in0=in0, in1=in1, op=op), rd, wr)

    def TS(self, eng, out, in0, s1, s2, op0, op1, rd, wr):
        if op1 is None:
            return self.S.op(eng, lambda e: e.tensor_scalar(out=out, in0=in0, scalar1=s1, scalar2=None, op0=op0), rd, wr)
        return self.S.op(eng, lambda e: e.tensor_scalar(out=out, in0=in0, scalar1=s1, scalar2=s2, op0=op0, op1=op1), rd, wr)

    def STT(self, out, in0, scalar, in1, op0, op1, rd, wr):
        return self.S.op("dve", lambda e: e.scalar_tensor_tensor(out=out, in0=in0, scalar=scalar, in1=in1, op0=op0, op1=op1), rd, wr)

    def CP(self, eng, out, in_, rd, wr):
        if eng == "act":
            return self.S.op("act", lambda e: e.copy(out=out, in_=in_), rd, wr)
        return self.S.op(eng, lambda e: e.tensor_copy(out=out, in_=in_), rd, wr)

    def MEMSET(self, eng, ap, val, wr):
        return self.S.op(eng, lambda e: e.memset(ap, val), (), wr)

    def DMA(self, q, out, in_, rd, wr):
        return self.S.dma(q, lambda e: e.dma_start(out=out, in_=in_), rd, wr)

    def rsqrt(self, out, in_, mul, add, w, rd, wr):
        k = self.rot.get("rs", 0)
        self.rot["rs"] = k + 1
        ta, ra = self.rs_tmp[k % 4]
        tb, rb = self.rs_tmp2[k % 4]
        if getattr(self, "pool_rsqrt", False) and w <= 16:
            self.TS("dve", ta[:, 0:w], in_, mul, add, ALU.mult, ALU.add, rd, [ra])
            self.TT("pool", out, ta[:, 0:w], self.negh[:, 0:w], ALU.pow, [ra, self.Rc], wr)
            return
        self.TS("dve", ta[:, 0:w], in_, mul, add, ALU.mult, ALU.add, rd, [ra])
        self.ACT(tb[:, 0:w], ta[:, 0:w], AF.Sqrt, [ra], [rb])
        self.S.op("dve", lambda e: e.reciprocal(out=out, in_=tb[:, 0:w]), [rb], wr)


class Stop(Exception):
    pass


import os
STOP_AT = os.environ.get("K_STOP", "")


def ckpt(k, name):
    if STOP_AT and name == STOP_AT:
        raise Stop()


def chunked(ap, p=128):
    return ap.rearrange("(c p) n -> p c n", p=p)


def build(mode):
    k = K(mode)
    nc, S = k.nc, k.S
    with contextlib.ExitStack() as st:
        k.arena = st.enter_context(nc.sbuf_tensor("arena", [128, ARENA_WORDS], F32))
        k.psum = st.enter_context(nc.psum_tensor("psum", [128, 4096], F32))
        RPS = [Res(f"ps{i}") for i in range(8)]
        k.RPS = RPS

        k.ident = k.alloc(128, BF16)
        R_const = Res("const")
        iop = k.alloc(128)
        ioc = k.alloc(128)
        tmpf = k.alloc(128)
        k.ones_bf = k.alloc(128, BF16)
        k.ones_f = k.alloc(128)
        k.maskneg = k.alloc(128, BF16)
        k.rs_tmp = [(k.alloc(16), Res("rsa")) for _ in range(4)]
        k.rs_tmp2 = [(k.alloc(16), Res("rsb")) for _ in range(4)]
        S.op("pool", lambda e: e.iota(iop, pattern=[[0, 128]], base=0, channel_multiplier=1, allow_small_or_imprecise_dtypes=True), (), [R_const])
        S.op("pool", lambda e: e.iota(ioc, pattern=[[1, 128]], base=0, channel_multiplier=0, allow_small_or_imprecise_dtypes=True), (), [R_const])
        k.TT("dve", tmpf, iop, ioc, ALU.is_equal, [R_const], [R_const])
        k.CP("dve", k.ident, tmpf, [R_const], [R_const])
        k.TT("dve", tmpf, iop, ioc, ALU.is_gt, [R_const], [R_const])
        k.TS("dve", k.maskneg, tmpf, MASKNEG, None, ALU.mult, None, [R_const], [R_const])
        k.MEMSET("dve", k.ones_bf, 1.0, [R_const])
        k.MEMSET("dve", k.ones_f, 1.0, [R_const])
        k.negh = k.alloc(16)
        k.MEMSET("dve", k.negh, -0.5, [R_const])
        k.pool_rsqrt = (mode == "fused")
        k.Rc = R_const
        k.iop, k.ioc = iop, ioc

        try:
            if mode == "L1":
                phase_A_layer0(k)
            elif mode == "L2":
                phase_rest(k)
            else:
                build_fused_body(k)
        except Stop:
            pass
        finals = list(k.final_dmas)
        S.emit(final_waits=finals)
    return k


def load_w_bf16(k, dram_ap, nck, cols, res, colsplit=None):
    t = k.alloc(nck * cols, BF16).rearrange("p (c n) -> p c n", c=nck)
    src = chunked(dram_ap)
    step = colsplit or cols
    for ck in range(nck):
        for c0 in range(0, cols, step):
            c1 = min(cols, c0 + step)
            k.DMA("pool", t[:, ck, c0:c1], src[:, ck, c0:c1], (), [res])
    return t


def load_w_scaled_bf16(k, dram_ap, nck, cols, g_ap, res, stage, stage_res):
    t = k.alloc(nck * cols, BF16).rearrange("p (c n) -> p c n", c=nck)
    src = chunked(dram_ap)
    for ck in range(nck):
        k.DMA("sp", stage[:, 0:cols], src[:, ck, :], (), [stage_res])
        k.TS("pool", t[:, ck, :], stage[:, 0:cols], g_ap[:, ck:ck + 1], None, ALU.mult, None, [stage_res, k.Rg], [res])
    return t


def norm_prep_slot(k, xsrc, R_x, g_ap, hgT, R_hg, sq, R_sq, rx_out, R_rx, psb):
    for ck in range(8):
        k.ACT(hgT[:, ck, :], xsrc[:, ck, :], AF.Identity, [R_x, k.Rg], [R_hg], scale=g_ap[:, ck:ck + 1])
    k.ACT(sq, xsrc, AF.Square, [R_x], [R_sq])
    pst = k.ps(psb)
    for bl in range(4):
        for ck in range(8):
            k.MM(pst[:, bl:bl + 1], sq[:, ck, bl * 128:(bl + 1) * 128], k.ones_bf[:, 0:1], ck == 0, ck == 7,
                 [R_sq, k.Rc], [k.RPS[psb]])
    k.rsqrt(rx_out, pst[:, 0:4], 1.0 / D, EPS, 4, [k.RPS[psb]], [R_rx])


def memq_norm(k, zm, R_z, gmq_tile, qmn, R_qmn, tmp, R_tmp, st, R_st):
    zm3 = zm.rearrange("p (h d) -> p h d", h=4)
    t3 = tmp.rearrange("p (h d) -> p h d", h=4)
    k.TT("pool", tmp, zm, zm, ALU.mult, [R_z], [R_tmp])
    S = k.S
    S.op("dve", lambda e: e.tensor_reduce(out=st[:, 0:4], in_=t3, axis=AX.X, op=ALU.add), [R_tmp], [R_st])
    k.rsqrt(st[:, 4:8], st[:, 0:4], 1.0 / 64, EPS, 4, [R_st], [R_st])
    k.TT("dve", t3, zm3, st[:, 4:8].unsqueeze(2).broadcast_to([128, 4, 64]), ALU.mult, [R_z, R_st], [R_tmp])
    k.TT("pool", qmn.rearrange("p (h d) -> p h d", h=4), t3, gmq_tile.unsqueeze(1).broadcast_to([128, 4, 64]), ALU.mult,
         [R_tmp, k.Rg], [R_qmn])


def phase_A_layer0(k):
    S = k.S
    k.final_dmas = []
    xT_d = k.inp("xT", [D, TOWN])
    pos_d = k.inp("pos", [128, 16], I32)
    invf_d = k.inp("invf", [128, 16])
    w_in_d = k.inp("w_in0", [D, 928])
    g_attn_d = k.inp("g_attn0", [128, 8])
    w_uq_d = k.inp("w_uq", [384, 1152])
    g_qlat_d = k.inp("g_qlat", [128, 3])
    w_uk_d = k.inp("w_uk", [256, 768])
    w_uv_d = k.inp("w_uv", [256, 768])
    g_kvlat_d = k.inp("g_kvlat", [128, 2])
    gq_d = k.inp("gq_tile", [128, 96])
    gk_d = k.inp("gk_tile", [128, 96])
    gmq_d = k.inp("g_mq0", [128, 64])
    qT_o = k.outp("qT_o", [96, 12 * TOWN], BF16)
    qmT_o = k.outp("qmT_o", [128, 2 * TOWN], BF16)
    KTn_o = k.outp("KTn_o", [768, TOWN], BF16)
    KTr_o = k.outp("KTr_o", [32, TOWN], BF16)
    Vx_o = k.outp("Vx_o", [12, 128, 16 * 65], BF16)
    rk_o = k.outp("rk_o", [128, 16 * 12])

    k.Rg = Res("gains")
    R_w = Res("weights")
    g_attn = k.alloc(8)
    g_qlat = k.alloc(3)
    g_kvlat = k.alloc(2)
    gq = k.alloc(96)
    gk = k.alloc(96)
    gmq = k.alloc(64)
    invf = k.alloc(16)
    posi = k.alloc(16, I32)
    for t, d in [(g_attn, g_attn_d), (g_qlat, g_qlat_d), (g_kvlat, g_kvlat_d), (gq, gq_d), (gk, gk_d), (gmq, gmq_d), (invf, invf_d)]:
        k.DMA("sp", t, d, (), [k.Rg])
    k.DMA("sp", posi, pos_d, (), [k.Rg])

    Win = load_w_bf16(k, w_in_d, 8, 928, R_w)
    stage = k.alloc(1152)
    R_stage = Res("stage")
    Wuq = load_w_scaled_bf16(k, w_uq_d, 3, 1152, g_qlat, R_w, stage, R_stage)
    Wuk = load_w_scaled_bf16(k, w_uk_d, 2, 768, g_kvlat, R_w, stage, R_stage)
    Wuv = load_w_scaled_bf16(k, w_uv_d, 2, 768, g_kvlat, R_w, stage, R_stage)

    ckpt(k, "weights")
    R_rope = Res("rope")
    posf = k.alloc(16)
    ang = k.alloc(256)
    kq = k.alloc(256)
    ki = k.alloc(256, I32)
    y = k.alloc(256)
    m = k.alloc(256)
    cs2 = k.alloc(16 * 32).rearrange("p (b i) -> p b i", b=16)
    sn2 = k.alloc(16 * 32).rearrange("p (b i) -> p b i", b=16)
    ang3 = ang.rearrange("p (b i) -> p b i", b=16)
    k.CP("dve", posf, posi, [k.Rg], [R_rope])
    k.TT("dve", ang3, posf.unsqueeze(2).broadcast_to([128, 16, 16]), invf.unsqueeze(1).broadcast_to([128, 16, 16]), ALU.mult,
         [R_rope, k.Rg], [R_rope])
    TWO_PI = 2.0 * np.pi
    C1 = 6.28125
    C2 = float(np.float32(TWO_PI - C1))
    k.TS("dve", kq, ang, float(1.0 / TWO_PI), None, ALU.mult, None, [R_rope], [R_rope])
    k.CP("dve", ki, kq, [R_rope], [R_rope])
    k.CP("dve", kq, ki, [R_rope], [R_rope])
    k.STT(y, kq, -C1, ang, ALU.mult, ALU.add, [R_rope], [R_rope])
    k.STT(y, kq, -C2, y, ALU.mult, ALU.add, [R_rope], [R_rope])

    def wrap(t):
        k.TS("dve", m, t, float(np.pi), None, ALU.is_gt, None, [R_rope], [R_rope])
        k.STT(t, m, -TWO_PI, t, ALU.mult, ALU.add, [R_rope], [R_rope])
        k.TS("dve", m, t, float(-np.pi), None, ALU.is_lt, None, [R_rope], [R_rope])
        k.STT(t, m, TWO_PI, t, ALU.mult, ALU.add, [R_rope], [R_rope])

    wrap(y)
    y3 = y.rearrange("p (b i) -> p b i", b=16)
    k.ACT(sn2[:, :, 16:32], y3, AF.Sin, [R_rope], [R_rope])
    k.TS("dve", sn2[:, :, 0:16], sn2[:, :, 16:32], -1.0, None, ALU.mult, None, [R_rope], [R_rope])
    k.TS("dve", y, y, float(np.pi / 2), None, ALU.add, None, [R_rope], [R_rope])
    wrap(y)
    k.ACT(cs2[:, :, 0:16], y3, AF.Sin, [R_rope], [R_rope])
    k.CP("dve", cs2[:, :, 16:32], cs2[:, :, 0:16], [R_rope], [R_rope])

    ckpt(k, "rope")
    Gq = k.alloc(96)
    k.CP("dve", Gq, gq, [k.Rg], [k.Rg])
    k.TT("dve", Gq[:, 0:64], gq[:, 0:64], gk[:, 0:64], ALU.mult, [k.Rg], [k.Rg])

    qT = k.alloc(12 * TOWN, BF16).rearrange("p (h t) -> p h t", h=12)
    qmT = k.alloc(2 * TOWN, BF16).rearrange("p (c t) -> p c t", c=2)
    rk_own = k.alloc(16 * 12).rearrange("p (b h) -> p b h", b=16)
    R_qT = Res("qT")
    R_qmT = Res("qmT")
    R_rk = Res("rk")
    xs0 = k.alloc(8 * 512).rearrange("p (c t) -> p c t", c=8)
    xs = [xs0, xs0]
    R_xs0 = Res("xs0")
    R_xs = [R_xs0, R_xs0]
    hg0 = k.alloc(8 * 512, BF16).rearrange("p (c t) -> p c t", c=8)
    hgT = [hg0, hg0]
    R_hg0 = Res("hg0")
    R_hg = [R_hg0, R_hg0]
    sq = k.alloc(8 * 512, BF16).rearrange("p (c t) -> p c t", c=8)
    R_sq = Res("sq")
    rx = k.alloc(16)
    R_rx = Res("rx")
    ckvT = [k.alloc(2 * 512, BF16).rearrange("p (c t) -> p c t", c=2) for _ in range(2)]
    R_ckvT = [Res("ckvT0"), Res("ckvT1")]
    KTn = [k.alloc(6 * 512, BF16).rearrange("p (c t) -> p c t", c=6) for _ in range(2)]
    R_KTn = [Res("KTn0"), Res("KTn1")]
    KTr = [k.alloc(512, BF16) for _ in range(2)]
    R_KTr = [Res("KTr0"), Res("KTr1")]
    Vx = [k.alloc(12 * 4 * 65, BF16).rearrange("p (h b c) -> p h b c", h=12, b=4) for _ in range(2)]
    R_Vx = [Res("Vx0"), Res("Vx1")]
    for i in range(2):
        k.MEMSET("pool", Vx[i][:, :, :, 64:65], 1.0, [R_Vx[i]])
    z = k.alloc(928)
    R_z = Res("z")
    stt = k.alloc(16)
    R_st = Res("st")
    qln = k.alloc(384, BF16)
    ckvn = k.alloc(256, BF16)
    qmn = k.alloc(256, BF16)
    krr = k.alloc(32, BF16)
    R_tm = Res("tm_bf")
    tmpm = k.alloc(256)
    R_tmpm = Res("tmpm")
    krg = k.alloc(32)
    krt = k.alloc(32)
    kru = k.alloc(32)
    R_kr = Res("kr")
    qlT = k.alloc(3 * 128, BF16).rearrange("p (c t) -> p c t", c=3)
    R_qlT = Res("qlT")
    sqq = k.alloc(1152)
    R_sqq = Res("sqq")
    ssq = k.alloc(32)
    R_ssq = Res("ssq")
    qn = k.alloc(1152)
    R_qn = Res("qn")
    qg = k.alloc(1152)
    R_qg = Res("qg")
    qt = k.alloc(12 * 32)
    qu = k.alloc(12 * 32)
    R_qtu = Res("qtu")
    qfin = k.alloc(12 * 96, BF16).rearrange("p (h d) -> p h d", h=12)
    R_qfin = Res("qfin")
    sqk = k.alloc(768)
    R_sqk = Res("sqk")
    ssk = k.alloc(32)
    R_ssk = Res("ssk")
    RPS = k.RPS

    xT3 = chunked(xT_d)
    for j in range(NSLOT):
        sb = j % 2
        k.DMA("sp", xs[sb], xT3[:, :, j * 512:(j + 1) * 512], (), [R_xs[sb]])
        norm_prep_slot(k, xs[sb], R_xs[sb], g_attn, hgT[sb], R_hg[sb], sq, R_sq, rx[:, 4 * j:4 * j + 4], R_rx, 7)
        ckpt(k, "norm")
        for bl in range(4):
            bg = 4 * j + bl
            tok = slice(bl * 128, (bl + 1) * 128)
            for n, (c0, c1) in enumerate([(0, 512), (512, 928)]):
                for ck in range(8):
                    k.MM(k.ps(n)[:, 0:c1 - c0], hgT[sb][:, ck, tok], Win[:, ck, c0:c1], ck == 0, ck == 7,
                         [R_hg[sb], R_w], [RPS[n]])
                k.ACT(z[:, c0:c1], k.ps(n)[:, 0:c1 - c0], AF.Identity, [RPS[n], R_rx], [R_z], scale=rx[:, bg:bg + 1])
            ckpt(k, "z")
            k.ACT(sqq[:, 0:672], z[:, 0:672], AF.Square, [R_z], [R_sqq])
            k.S.op("dve", lambda e: e.tensor_reduce(out=stt[:, 0:1], in_=sqq[:, 0:384], axis=AX.X, op=ALU.add), [R_sqq], [R_st])
            k.S.op("dve", lambda e: e.tensor_reduce(out=stt[:, 1:2], in_=sqq[:, 384:640], axis=AX.X, op=ALU.add), [R_sqq], [R_st])
            k.S.op("dve", lambda e: e.tensor_reduce(out=stt[:, 2:3], in_=sqq[:, 640:672], axis=AX.X, op=ALU.add), [R_sqq], [R_st])
            k.rsqrt(stt[:, 4:5], stt[:, 0:1], 1.0 / 384, EPS, 1, [R_st], [R_st])
            k.rsqrt(stt[:, 5:6], stt[:, 1:2], 1.0 / 256, EPS, 1, [R_st], [R_st])
            k.TS("dve", qln, z[:, 0:384], stt[:, 4:5], None, ALU.mult, None, [R_z, R_st], [R_tm])
            k.TS("dve", ckvn, z[:, 384:640], stt[:, 5:6], None, ALU.mult, None, [R_z, R_st], [R_tm])
            memq_norm(k, z[:, 672:928], R_z, gmq, qmn, R_tm, tmpm, R_tmpm, stt[:, 8:16], R_st)
            k.TT("pool", krg, z[:, 640:672], gk[:, 64:96], ALU.mult, [R_z, k.Rg], [R_kr])
            k.TT("pool", krt, krg, cs2[:, bg, :], ALU.mult, [R_kr, R_rope], [R_kr])
            k.TT("pool", kru[:, 0:16], krg[:, 16:32], sn2[:, bg, 0:16], ALU.mult, [R_kr, R_rope], [R_kr])
            k.TT("pool", kru[:, 16:32], krg[:, 0:16], sn2[:, bg, 16:32], ALU.mult, [R_kr, R_rope], [R_kr])
            k.TT("pool", krr, krt, kru, ALU.add, [R_kr], [R_tm])
            ckpt(k, "tm")
            pb = k.ps_bf(2)
            for c in range(3):
                k.TR(pb[:, c * 128:(c + 1) * 128], qln[:, c * 128:(c + 1) * 128], [R_tm, k.Rc], [RPS[2]])
            ckpt(k, "tr1")
            for c in range(2):
                k.TR(pb[:, (3 + c) * 128:(4 + c) * 128], ckvn[:, c * 128:(c + 1) * 128], [R_tm, k.Rc], [RPS[2]])
            for c in range(2):
                k.TR(pb[:, (5 + c) * 128:(6 + c) * 128], qmn[:, c * 128:(c + 1) * 128], [R_tm, k.Rc], [RPS[2]])
            ckpt(k, "tr3")
            k.TR(pb[0:32, 7 * 128:8 * 128], krr, [R_tm, k.Rc], [RPS[2]])
            ckpt(k, "tr4")
            k.CP("act", qlT, pb[:, 0:384].rearrange("p (c t) -> p c t", c=3), [RPS[2]], [R_qlT])
            ckpt(k, "cp1")
            k.CP("act", ckvT[sb][:, :, tok], pb[:, 384:640].rearrange("p (c t) -> p c t", c=2), [RPS[2]], [R_ckvT[sb]])
            ckpt(k, "cp2")
            k.CP("act", qmT[:, :, bg * 128:(bg + 1) * 128], pb[:, 640:896].rearrange("p (c t) -> p c t", c=2), [RPS[2]], [R_qmT])
            ckpt(k, "cp3")
            k.CP("act", KTr[sb][0:32, tok], pb[0:32, 896:1024], [RPS[2]], [R_KTr[sb]])
            ckpt(k, "tr")
            for n in range(3):
                for ck in range(3):
                    k.MM(k.ps(3 + n)[:, 0:384], qlT[:, ck, :], Wuq[:, ck, n * 384:(n + 1) * 384], ck == 0, ck == 2,
                         [R_qlT, R_w], [RPS[3 + n]])
                k.ACT(sqq[:, n * 384:(n + 1) * 384], k.ps(3 + n)[:, 0:384], AF.Square, [RPS[3 + n]], [R_sqq])
            k.S.op("dve", lambda e: e.tensor_reduce(out=ssq[:, 0:12], in_=sqq.rearrange("p (h d) -> p h d", h=12), axis=AX.X, op=ALU.add),
                   [R_sqq], [R_ssq])
            k.rsqrt(ssq[:, 16:28], ssq[:, 0:12], 1.0 / 96, EPS, 12, [R_ssq], [R_ssq])
            for n in range(3):
                k.TT("dve", qn[:, n * 384:(n + 1) * 384].rearrange("p (h d) -> p h d", h=4),
                     k.ps(3 + n)[:, 0:384].rearrange("p (h d) -> p h d", h=4),
                     ssq[:, 16 + 4 * n:20 + 4 * n].unsqueeze(2).broadcast_to([128, 4, 96]), ALU.mult,
                     [RPS[3 + n], R_ssq], [R_qn])
            qn3 = qn.rearrange("p (h d) -> p h d", h=12)
            qg3 = qg.rearrange("p (h d) -> p h d", h=12)
            k.TT("pool", qg3, qn3, Gq.unsqueeze(1).broadcast_to([128, 12, 96]), ALU.mult, [R_qn, k.Rg], [R_qg])
            qt3 = qt.rearrange("p (h d) -> p h d", h=12)
            qu3 = qu.rearrange("p (h d) -> p h d", h=12)
            k.TT("pool", qt3, qg3[:, :, 64:96], cs2[:, bg, :].unsqueeze(1).broadcast_to([128, 12, 32]), ALU.mult, [R_qg, R_rope], [R_qtu])
            k.TT("pool", qu3[:, :, 0:16], qg3[:, :, 80:96], sn2[:, bg, 0:16].unsqueeze(1).broadcast_to([128, 12, 16]), ALU.mult,
                 [R_qg, R_rope], [R_qtu])
            k.TT("pool", qu3[:, :, 16:32], qg3[:, :, 64:80], sn2[:, bg, 16:32].unsqueeze(1).broadcast_to([128, 12, 16]), ALU.mult,
                 [R_qg, R_rope], [R_qtu])
            k.TT("dve", qfin[:, :, 64:96], qt3, qu3, ALU.add, [R_qtu], [R_qfin])
            k.CP("dve", qfin[:, :, 0:64], qg3[:, :, 0:64], [R_qg], [R_qfin])
            for h in range(12):
                bk = 6 if h < 8 else 7
                hh = h % 8
                k.TR(k.ps_bf(bk)[0:96, hh * 128:(hh + 1) * 128], qfin[:, h, :], [R_qfin, k.Rc], [RPS[bk]])
            k.CP("act", qT[0:96, 0:8, bg * 128:(bg + 1) * 128], k.ps_bf(6)[0:96, :].rearrange("p (h t) -> p h t", h=8), [RPS[6]], [R_qT])
            k.CP("act", qT[0:96, 8:12, bg * 128:(bg + 1) * 128], k.ps_bf(7)[0:96, 0:512].rearrange("p (h t) -> p h t", h=4), [RPS[7]], [R_qT])
            ckpt(k, "q")
            for n in range(2):
                for ck in range(2):
                    k.MM(k.ps(n)[:, 0:384], ckvT[sb][:, ck, tok], Wuk[:, ck, n * 384:(n + 1) * 384], ck == 0, ck == 1,
                         [R_ckvT[sb], R_w], [RPS[n]])
                k.ACT(sqk[:, n * 384:(n + 1) * 384], k.ps(n)[:, 0:384], AF.Square, [RPS[n]], [R_sqk])
            for n in range(2):
                for ck in range(2):
                    k.MM(k.ps(3 + n)[:, 0:384], ckvT[sb][:, ck, tok], Wuv[:, ck, n * 384:(n + 1) * 384], ck == 0, ck == 1,
                         [R_ckvT[sb], R_w], [RPS[3 + n]])
                k.CP("act", Vx[sb][:, 6 * n:6 * n + 6, bl, 0:64], k.ps(3 + n)[:, 0:384].rearrange("p (h d) -> p h d", h=6),
                     [RPS[3 + n]], [R_Vx[sb]])
            k.S.op("dve", lambda e: e.tensor_reduce(out=ssk[:, 0:12], in_=sqk.rearrange("p (h d) -> p h d", h=12), axis=AX.X, op=ALU.add),
                   [R_sqk], [R_ssk])
            k.TS("dve", ssk[:, 16:28], ssk[:, 0:12], stt[:, 2:3], None, ALU.add, None, [R_ssk, R_st], [R_ssk])
            k.rsqrt(rk_own[:, bg, :], ssk[:, 16:28], 1.0, 96 * EPS, 12, [R_ssk], [R_rk])
        ckpt(k, "blocks")
        for i in range(6):
            for ck in range(2):
                k.MM(k.ps(2), Wuk[:, ck, i * 128:(i + 1) * 128], ckvT[sb][:, ck, :], ck == 0, ck == 1, [R_w, R_ckvT[sb]], [RPS[2]])
            k.CP("act" if i % 2 == 0 else "dve", KTn[sb][:, i, :], k.ps(2), [RPS[2]], [R_KTn[sb]])
        ckpt(k, "ktn")
        KTn_o3 = KTn_o.rearrange("(c p) t -> p c t", p=128)
        k.final_dmas.append(k.DMA("sp", KTn_o3[:, :, j * 512:(j + 1) * 512], KTn[sb], [R_KTn[sb]], ()))
        k.final_dmas.append(k.DMA("sp", KTr_o[:, j * 512:(j + 1) * 512], KTr[sb][0:32, :], [R_KTr[sb]], ()))
        Vx_o4 = Vx_o.rearrange("h p (b c) -> p h b c", b=16)
        for h0 in range(0, 12, 4):
            k.final_dmas.append(k.DMA("sp", Vx_o4[:, h0:h0 + 4, 4 * j:4 * j + 4, :], Vx[sb][:, h0:h0 + 4, :, :], [R_Vx[sb]], ()))
    ckpt(k, "slots")
    k.final_dmas.append(k.DMA("sp", rk_o, rk_own.rearrange("p b h -> p (b h)"), [R_rk], ()))
    k.final_dmas.append(k.DMA("sp", qT_o, qT[0:96].rearrange("p h t -> p (h t)"), [R_qT], ()))
    k.final_dmas.append(k.DMA("sp", qmT_o, qmT.rearrange("p c t -> p (c t)"), [R_qmT], ()))


def mem_kv_prep(k, pre, memT_d, gmem_d, wmkv_d, gmk_d, KmT, R_KmT, Vmx, R_Vmx):
    RPS = k.RPS
    m = k.mark()
    memx = k.alloc(8 * 256).rearrange("p (c t) -> p c t", c=8)
    mg = k.alloc(8 * 256, BF16).rearrange("p (c t) -> p c t", c=8)
    sqm = k.alloc(8 * 256, BF16).rearrange("p (c t) -> p c t", c=8)
    gmem = k.alloc(8)
    gmk = k.alloc(64)
    rmem = k.alloc(4)
    kvm = k.alloc(512)
    kn = k.alloc(256, BF16)
    tmp = k.alloc(256)
    st = k.alloc(8)
    R_a, R_b, R_c, R_d, R_e, R_f, R_w = (Res(pre + n) for n in "abcdefw")
    k.DMA("sp", memx, chunked(memT_d), (), [R_a])
    k.DMA("sp", gmem, gmem_d, (), [k.Rg])
    k.DMA("sp", gmk, gmk_d, (), [k.Rg])
    Wm = load_w_bf16(k, wmkv_d, 8, 512, R_w)
    for ck in range(8):
        k.ACT(mg[:, ck, :], memx[:, ck, :], AF.Identity, [R_a, k.Rg], [R_b], scale=gmem[:, ck:ck + 1])
    k.ACT(sqm, memx, AF.Square, [R_a], [R_c])
    for mb in range(2):
        for ck in range(8):
            k.MM(k.ps(7)[:, mb:mb + 1], sqm[:, ck, mb * 128:(mb + 1) * 128], k.ones_bf[:, 0:1], ck == 0, ck == 7, [R_c, k.Rc], [RPS[7]])
    k.rsqrt(rmem[:, 0:2], k.ps(7)[:, 0:2], 1.0 / D, EPS, 2, [RPS[7]], [R_d])
    k.MEMSET("pool", Vmx[:, :, :, 64:65], 1.0, [R_Vmx])
    for mb in range(2):
        for ck in range(8):
            k.MM(k.ps(0), mg[:, ck, mb * 128:(mb + 1) * 128], Wm[:, ck, :], ck == 0, ck == 7, [R_b, R_w], [RPS[0]])
        k.ACT(kvm, k.ps(0), AF.Identity, [RPS[0], R_d], [R_e], scale=rmem[:, mb:mb + 1])
        memq_norm(k, kvm[:, 0:256], R_e, gmk, kn, R_f, tmp, R_f, st, R_f)
        k.CP("dve", Vmx[:, mb, :, 0:64], kvm[:, 256:512].rearrange("p (h d) -> p h d", h=4), [R_e], [R_Vmx])
        pb = k.ps_bf(2)
        for c in range(2):
            k.TR(pb[:, c * 128:(c + 1) * 128], kn[:, c * 128:(c + 1) * 128], [R_f, k.Rc], [RPS[2]])
        k.CP("act", KmT[:, :, mb * 128:(mb + 1) * 128], pb[:, 0:256].rearrange("p (c t) -> p c t", c=2), [RPS[2]], [R_KmT])
    k.release(m)


class Attn:
    def __init__(self, k, mixT, R_mix):
        self.k = k
        self.mixT = mixT
        self.R_mix = R_mix
        self.P = [k.alloc(1024, BF16) for _ in range(3)]
        self.R_P = [Res("P0"), Res("P1"), Res("P2")]
        self.R_S = [Res("S0"), Res("S1")]
        self.rec = k.alloc(512)
        self.R_rec = Res("rec")
        self.bsb = k.alloc(512)
        self.R_bsb = Res("bsb")
        self.ost = [k.alloc(512, BF16) for _ in range(2)]
        self.R_ost = [Res("ost0"), Res("ost1")]
        self.nS = 0
        self.nO = 0

    def sbuf(self):
        i = self.nS % 2
        self.nS += 1
        return i, self.k.psum[:, (4 + 2 * i) * 512:(6 + 2 * i) * 512], self.R_S[i]

    def finalize(self, j, chunk, odd, si=None):
        k = self.k
        acc = k.ps(j)
        Racc = k.RPS[j]
        k.CP("act", self.rec[64:65, :], acc[64:65, :], [Racc], [self.R_rec])
        if si is None:
            i, Sps, RS = self.sbuf()
        else:
            Sps, RS = k.psum[:, (4 + 2 * si) * 512:(6 + 2 * si) * 512], self.R_S[si]
        k.MM(Sps[0:64, 0:512], k.ones_f[64:65, 0:64], self.rec[64:65, :], True, True, [self.R_rec, k.Rc], [RS])
        k.S.op("dve", lambda e: e.reciprocal(out=self.bsb[0:64, :], in_=Sps[0:64, 0:512]), [RS], [self.R_bsb])
        cols = slice(j * 512, (j + 1) * 512)
        if not odd:
            k.TT("dve", self.mixT[0:64, chunk, cols], acc[0:64, :], self.bsb[0:64, :], ALU.mult, [Racc, self.R_bsb], [self.R_mix[chunk][j]])
        else:
            t = self.nO % 2
            self.nO += 1
            k.TT("dve", self.ost[t][0:64, :], acc[0:64, :], self.bsb[0:64, :], ALU.mult, [Racc, self.R_bsb], [self.R_ost[t]])
            k.DMA("pool", self.mixT[64:128, chunk, cols], self.ost[t][0:64, :], [self.R_ost[t]], [self.R_mix[chunk][j]])

    def mem_attention(self, qmT, R_qmT, KmT, R_KmT, Vmx, R_Vmx):
        k = self.k
        for hm in range(4):
            pr = slice((hm % 2) * 64, (hm % 2) * 64 + 64)
            for j in range(4):
                i, Sps, RS = self.sbuf()
                for mb in range(2):
                    k.MM(Sps[:, mb * 512:(mb + 1) * 512], KmT[pr, hm // 2, mb * 128:(mb + 1) * 128], qmT[pr, hm // 2, j * 512:(j + 1) * 512],
                         True, True, [R_KmT, R_qmT], [RS])
                k.ACT(self.P[i], Sps, AF.Exp, [RS], [self.R_P[i]], scale=0.125)
                for mb in range(2):
                    k.MM(k.ps(j)[0:65, :], Vmx[:, mb, hm, :], self.P[i][:, mb * 512:(mb + 1) * 512], mb == 0, mb == 1,
                         [R_Vmx, self.R_P[i]], [k.RPS[j]])
                self.finalize(j, 6 + hm // 2, hm % 2)

    def causal_attention(self, qT, R_qT, rk, R_rk, KTn_w, KTr_w, Vx_w):
        k = self.k
        KT = [k.alloc(4096, BF16) for _ in range(2)]
        Vb = [k.alloc(32 * 128, BF16).rearrange("p (b c) -> p b c", c=128) for _ in range(2)]
        R_KT = [Res("KT0"), Res("KT1")]
        R_V = [Res("V0"), Res("V1")]
        for i_ in range(2):
            k.MEMSET("pool", Vb[i_], 0.0, [R_V[i_]])
        Vx4 = Vx_w.rearrange("h p (b c) -> h p b c", c=65)

        def load(n):
            h, grp = divmod(n, 4)
            b = n % 2
            ts = slice(4096 * grp, 4096 * (grp + 1))
            k.DMA("sp", KT[b][0:64, :], KTn_w[64 * h:64 * h + 64, ts], (), [R_KT[b]])
            k.DMA("sp", KT[b][64:96, :], KTr_w[:, ts], (), [R_KT[b]])
            k.DMA("sp", Vb[b][:, :, 0:65], Vx4[h, :, 32 * grp:32 * grp + 32, :], (), [R_V[b]])

        descs = []
        for n in range(48):
            h, grp = divmod(n, 4)
            b = n % 2
            js = [j for j in range(4) if j >= grp]
            batches = [js[x:x + 2] for x in range(0, len(js), 2)]
            for il in range(8):
                i = 8 * grp + il
                for kb in range(4):
                    for bi, batch in enumerate(batches):
                        last = (il == 7 and kb == 3 and bi == len(batches) - 1)
                        descs.append((n, h, grp, b, il, i, kb, batch, last))

        def emit_qk(dsc):
            n, h, grp, b, il, i, kb, batch, last = dsc
            Bw = 4 * i + kb
            col = (il * 4 + kb) * 128
            lk = KT[b][0:96, col:col + 128]
            si = self.nbatch % 2
            pi = self.nbatch % 3
            self.nbatch += 1
            Sps, RS = k.psum[:, (4 + 2 * si) * 512:(6 + 2 * si) * 512], self.R_S[si]
            dg = (il == 7 and batch[0] == grp)
            n0 = 128 * kb if dg else 0
            ncols = 512 * len(batch)
            for u, j in enumerate(batch):
                off = n0 if u == 0 else 0
                msk = dg and u == 0
                k.MM(Sps[:, u * 512 + off:(u + 1) * 512], lk, qT[0:96, h, j * 512 + off:(j + 1) * 512], True, not msk,
                     [R_KT[b], R_qT], [RS])
                if msk:
                    k.MM(Sps[:, off:off + 128], k.ident, k.maskneg, False, True, [k.Rc], [RS])
            k.ACT(self.P[pi][:, n0:ncols], Sps[:, n0:ncols], AF.Exp, [RS, R_rk], [self.R_P[pi]], scale=rk[:, Bw * 12 + h:Bw * 12 + h + 1])
            return si, n0, pi

        def emit_pv(dsc, si, n0, pi):
            n, h, grp, b, il, i, kb, batch, last = dsc
            Bw = 4 * i + kb
            for u, j in enumerate(batch):
                off = n0 if u == 0 else 0
                k.MM(k.ps(j)[:, off:512], Vb[b][:, il * 4 + kb, :], self.P[pi][:, u * 512 + off:(u + 1) * 512],
                     Bw == 0, (i == 8 * j + 7 and kb == 3), [R_V[b], self.R_P[pi]], [k.RPS[j]])
            if last:
                self.finalize(grp, h // 2, h % 2, si=si)
                if n + 2 < 48:
                    load(n + 2)

        self.nbatch = 0
        load(0)
        load(1)
        pend = []
        for dsc in descs:
            cur = emit_qk(dsc)
            pend.append((dsc, cur))
            if len(pend) > 2:
                d0, c0 = pend.pop(0)
                emit_pv(d0, *c0)
        for d0, c0 in pend:
            emit_pv(d0, *c0)


def wout_residual(k, pre, wout_d, mixT, R_mix, xres, R_x, xsrc):
    RPS = k.RPS
    m = k.mark()
    R_w = Res(pre + "wout")
    Wout = load_w_bf16(k, wout_d, 8, 1024, R_w)
    nb = 0
    for j in range(NSLOT):
        cols = slice(j * 512, (j + 1) * 512)
        if xsrc is not None:
            k.DMA("sp", xres[:, :, cols], xsrc(j), (), [R_x[j]])
        for dch in range(8):
            b = nb % 8
            nb += 1
            for mc in range(8):
                k.MM(k.ps(b), Wout[:, mc, dch * 128:(dch + 1) * 128], mixT[:, mc, cols], mc == 0, mc == 7, [R_w, R_mix[mc][j]], [RPS[b]])
            k.TT("dve", xres[:, dch, cols], k.ps(b), xres[:, dch, cols], ALU.add, [RPS[b], R_x[j]], [R_x[j]])
    k.release(m)


def ffn(k, pre, xres, R_x, gffn_d, wg_d, wu_d, wd_d):
    RPS = k.RPS
    m = k.mark()
    g = k.alloc(8)
    k.DMA("sp", g, gffn_d, (), [k.Rg])
    hT = k.alloc(8 * 1024, BF16).rearrange("p (c t) -> p c t", c=8)
    actT = k.alloc(NF * 1024, BF16).rearrange("p (f t) -> p f t", f=NF)
    sq = k.alloc(8 * 512, BF16).rearrange("p (c t) -> p c t", c=8)
    rt1 = k.alloc(512)
    rt2 = k.alloc(512)
    rbc = k.alloc(512)
    sg = [k.alloc(512) for _ in range(2)]
    Wg_r = [k.alloc(8 * 256, BF16).rearrange("p (c n) -> p c n", c=8) for _ in range(3)]
    Wu_r = [k.alloc(8 * 256, BF16).rearrange("p (c n) -> p c n", c=8) for _ in range(3)]
    Wd_r = [k.alloc(NF * 128, BF16).rearrange("p (f n) -> p f n", f=NF) for _ in range(3)]
    R_hT, R_sq, R_rt, R_rbc = Res(pre + "hT"), Res(pre + "sq"), Res(pre + "rt"), Res(pre + "rbc")
    R_act = [[Res(f"{pre}act{f}_{t}") for t in range(2)] for f in range(NF)]
    R_sg = [Res(pre + "sg0"), Res(pre + "sg1")]
    R_Wgu = [Res(f"{pre}wgu{i}") for i in range(3)]
    R_Wd = [Res(f"{pre}wd{i}") for i in range(3)]
    wg3, wu3 = chunked(wg_d), chunked(wu_d)
    wd3 = chunked(wd_d)
    NFG = NF // 2

    def load_gu(fg):
        s = fg % 3
        k.DMA("pool", Wg_r[s], wg3[:, :, fg * 256:(fg + 1) * 256], (), [R_Wgu[s]])
        k.DMA("pool", Wu_r[s], wu3[:, :, fg * 256:(fg + 1) * 256], (), [R_Wgu[s]])

    def load_d(dch):
        s = dch % 3
        k.DMA("pool", Wd_r[s], wd3[:, :, dch * 128:(dch + 1) * 128], (), [R_Wd[s]])

    for half in range(2):
        hc = half * 1024
        load_gu(0)
        load_gu(1)
        for t in range(2):
            cols = slice(hc + t * 512, hc + (t + 1) * 512)
            j = (hc + t * 512) // 512
            k.ACT(sq, xres[:, :, cols], AF.Square, [R_x[j]], [R_sq])
            for ck in range(8):
                k.MM(k.ps(7), k.ones_bf, sq[:, ck, :], ck == 0, ck == 7, [k.Rc, R_sq], [RPS[7]])
            k.TS("dve", rt1, k.ps(7), 1.0 / D, EPS, ALU.mult, ALU.add, [RPS[7]], [R_rt])
            k.ACT(rt2, rt1, AF.Sqrt, [R_rt], [R_rt])
            k.S.op("dve", lambda e: e.reciprocal(out=rbc, in_=rt2), [R_rt], [R_rbc])
            for ck in range(8):
                k.STT(hT[:, ck, t * 512:(t + 1) * 512], xres[:, ck, cols], g[:, ck:ck + 1], rbc, ALU.mult, ALU.mult,
                      [R_x[j], k.Rg, R_rbc], [R_hT])
        nsg = 0
        for fg in range(NFG):
            if fg + 2 < NFG:
                load_gu(fg + 2)
            if fg == NFG - 2:
                load_d(0)
            if fg == NFG - 1:
                load_d(1)
            s = fg % 3
            for fl in range(2):
                f = 2 * fg + fl
                gb = [0, 1] if f % 2 == 0 else [4, 5]
                ub = [2, 3] if f % 2 == 0 else [6, 7]
                for W_r, banks in ((Wg_r, gb), (Wu_r, ub)):
                    for ck in range(8):
                        for t in range(2):
                            k.MM(k.ps(banks[t]), W_r[s][:, ck, fl * 128:(fl + 1) * 128], hT[:, ck, t * 512:(t + 1) * 512], ck == 0, ck == 7,
                                 [R_Wgu[s], R_hT], [RPS[banks[t]]])
                for t in range(2):
                    q = nsg % 2
                    nsg += 1
                    k.ACT(sg[q], k.ps(gb[t]), AF.Silu, [RPS[gb[t]]], [R_sg[q]])
                    k.TT("dve", actT[:, f, t * 512:(t + 1) * 512], k.ps(ub[t]), sg[q], ALU.mult, [RPS[ub[t]], R_sg[q]], [R_act[f][t]])
        nb = 0
        for dch in range(8):
            if dch + 2 < 8:
                load_d(dch + 2)
            s = dch % 3
            for t in range(2):
                b = nb % 8
                nb += 1
                cols = slice(hc + t * 512, hc + (t + 1) * 512)
                j = (hc + t * 512) // 512
                for f in range(NF):
                    k.MM(k.ps(b), Wd_r[s][:, f, :], actT[:, f, t * 512:(t + 1) * 512], f == 0, f == NF - 1, [R_Wd[s], R_act[f][t]], [RPS[b]])
                k.TT("dve", xres[:, dch, cols], k.ps(b), xres[:, dch, cols], ALU.add, [RPS[b], R_x[j]], [R_x[j]])
    k.release(m)


def sgu_layer1(k, xres, R_x, mixT, R_mix, qmT, R_qmT, d):
    RPS = k.RPS
    m = k.mark()
    R_w = Res("l1w")
    g_attn = k.alloc(8)
    gmq = k.alloc(64)
    lng = k.alloc(768)
    lnb = k.alloc(768)
    bsp = k.alloc(8)
    for t_, d_ in [(g_attn, d["g_attn1"]), (gmq, d["g_mq1"]), (lng, d["ln_g"]), (lnb, d["ln_b"]), (bsp, d["b_sp"])]:
        k.DMA("sp", t_, d_, (), [k.Rg])
    Win = load_w_bf16(k, d["w_in1"], 8, 1792, R_w, colsplit=896)
    wsp = k.alloc(8 * 128, BF16).rearrange("p (g t) -> p g t", g=8)
    wst = k.alloc(8 * 128).rearrange("p (g t) -> p g t", g=8)
    msk = k.alloc(128)
    R_ws = Res("wsp")
    k.DMA("sp", wst, d["w_spT"].rearrange("p (g t) -> p g t", g=8), (), [R_ws])
    k.TT("dve", msk, k.iop, k.ioc, ALU.is_le, [k.Rc], [R_ws])
    k.TT("dve", wsp, wst, msk.unsqueeze(1).broadcast_to([128, 8, 128]), ALU.mult, [R_ws], [R_w])
    hgT = k.alloc(8 * 512, BF16).rearrange("p (c t) -> p c t", c=8)
    sq = k.alloc(8 * 512, BF16).rearrange("p (c t) -> p c t", c=8)
    rx = k.alloc(16)
    R_hg, R_sq, R_rx = Res("l1hg"), Res("l1sq"), Res("l1rx")
    uv = k.alloc(1536)
    zm = k.alloc(256)
    R_uv, R_zm = Res("uv"), Res("zm")
    bst = k.alloc(12)
    mv = k.alloc(4)
    R_bn = Res("bn")
    vt = k.alloc(768)
    vt2 = k.alloc(768)
    R_vt = Res("vt")
    vn = k.alloc(768, BF16)
    R_vn = Res("vn")
    y = k.alloc(768, BF16)
    R_y = Res("y")
    qmn = k.alloc(256, BF16)
    R_qmn = Res("qmn1")
    tmpm = k.alloc(256)
    R_tmpm = Res("tmpm1")
    stt = k.alloc(8)
    R_st = Res("st1")
    for j in range(NSLOT):
        norm_prep_slot(k, xres[:, :, j * 512:(j + 1) * 512], R_x[j], g_attn, hgT, R_hg, sq, R_sq, rx[:, 4 * j:4 * j + 4], R_rx, 7)
        for bl in range(4):
            bg = 4 * j + bl
            tok = slice(bl * 128, (bl + 1) * 128)
            gt = slice(bg * 128, (bg + 1) * 128)
            for n in range(4):
                for ck in range(8):
                    k.MM(k.ps(n)[:, 0:448], hgT[:, ck, tok], Win[:, ck, n * 448:(n + 1) * 448], ck == 0, ck == 7, [R_hg, R_w], [RPS[n]])
            rxa = rx[:, bg:bg + 1]
            for n in range(3):
                k.ACT(uv[:, n * 448:(n + 1) * 448], k.ps(n)[:, 0:448], AF.Gelu, [RPS[n], R_rx], [R_uv], scale=rxa)
            k.ACT(uv[:, 1344:1536], k.ps(3)[:, 0:192], AF.Gelu, [RPS[3], R_rx], [R_uv], scale=rxa)
            k.ACT(zm, k.ps(3)[:, 192:448], AF.Identity, [RPS[3], R_rx], [R_zm], scale=rxa)
            v = uv[:, 768:1536]
            bst3 = bst.rearrange("p (a s) -> p a s", a=2)
            for a in range(2):
                k.S.op("dve", lambda e, a=a: e.bn_stats(out=bst3[:, a, :], in_=v[:, a * 384:(a + 1) * 384]), [R_uv], [R_bn])
            k.S.op("dve", lambda e: e.bn_aggr(out=mv[:, 0:2], in_=bst), [R_bn], [R_bn])
            k.rsqrt(mv[:, 2:3], mv[:, 1:2], 1.0, EPS, 1, [R_bn], [R_bn])
            k.TS("dve", vt, v, mv[:, 0:1], mv[:, 2:3], ALU.subtract, ALU.mult, [R_uv, R_bn], [R_vt])
            k.TT("pool", vt2, vt, lng, ALU.mult, [R_vt, k.Rg], [R_vt])
            k.TT("pool", vn, vt2, lnb, ALU.add, [R_vt, k.Rg], [R_vn])
            for g_ in range(8):
                b = 4 + g_ // 4
                c0 = (g_ % 4) * 96
                k.MM(k.ps(b)[:, c0:c0 + 96], wsp[:, g_, :], vn[:, g_ * 96:(g_ + 1) * 96], True, True, [R_w, R_vn], [RPS[b]])
            for g_ in range(8):
                b = 4 + g_ // 4
                c0 = (g_ % 4) * 96
                k.STT(y[:, g_ * 96:(g_ + 1) * 96], k.ps(b)[:, c0:c0 + 96], bsp[:, g_:g_ + 1], uv[:, g_ * 96:(g_ + 1) * 96], ALU.add, ALU.mult,
                      [RPS[b], k.Rg, R_uv], [R_y])
            memq_norm(k, zm, R_zm, gmq, qmn, R_qmn, tmpm, R_tmpm, stt, R_st)
            pb = k.ps_bf(6)
            for c in range(6):
                k.TR(pb[:, c * 128:(c + 1) * 128], y[:, c * 128:(c + 1) * 128], [R_y, k.Rc], [RPS[6]])
            for c in range(2):
                k.TR(pb[:, (6 + c) * 128:(7 + c) * 128], qmn[:, c * 128:(c + 1) * 128], [R_qmn, k.Rc], [RPS[6]])
            for c in range(6):
                k.CP("act", mixT[:, c, gt], pb[:, c * 128:(c + 1) * 128], [RPS[6]], [R_mix[c][j]])
            k.CP("act", qmT[:, :, gt], pb[:, 768:1024].rearrange("p (c t) -> p c t", c=2), [RPS[6]], [R_qmT])
    k.release(m)


class RopeTab:
    def __init__(self, k, nb):
        self.k = k
        self.nb = nb
        n = nb * 16
        self.posf = k.alloc(nb)
        self.ang = k.alloc(n)
        self.kq = k.alloc(n)
        self.ki = k.alloc(n, I32)
        self.y = k.alloc(n)
        self.m = k.alloc(n)
        self.cs2 = k.alloc(nb * 32).rearrange("p (b i) -> p b i", b=nb)
        self.sn2 = k.alloc(nb * 32).rearrange("p (b i) -> p b i", b=nb)
        self.R = Res("rope")

    def compute(self, posi, R_pos, invf):
        k, nb, R = self.k, self.nb, self.R
        ang, kq, ki, y, m, cs2, sn2 = self.ang, self.kq, self.ki, self.y, self.m, self.cs2, self.sn2
        ang3 = ang.rearrange("p (b i) -> p b i", b=nb)
        k.CP("dve", self.posf, posi, [R_pos], [R])
        k.TT("dve", ang3, self.posf.unsqueeze(2).broadcast_to([128, nb, 16]), invf.unsqueeze(1).broadcast_to([128, nb, 16]), ALU.mult,
             [R, k.Rg], [R])
        TWO_PI = 2.0 * np.pi
        C1 = 6.28125
        C2 = float(np.float32(TWO_PI - C1))
        k.TS("dve", kq, ang, float(1.0 / TWO_PI), None, ALU.mult, None, [R], [R])
        k.CP("dve", ki, kq, [R], [R])
        k.CP("dve", kq, ki, [R], [R])
        k.STT(y, kq, -C1, ang, ALU.mult, ALU.add, [R], [R])
        k.STT(y, kq, -C2, y, ALU.mult, ALU.add, [R], [R])

        def wrap(t):
            k.TS("dve", m, t, float(np.pi), None, ALU.is_gt, None, [R], [R])
            k.STT(t, m, -TWO_PI, t, ALU.mult, ALU.add, [R], [R])
            k.TS("dve", m, t, float(-np.pi), None, ALU.is_lt, None, [R], [R])
            k.STT(t, m, TWO_PI, t, ALU.mult, ALU.add, [R], [R])

        wrap(y)
        y3 = y.rearrange("p (b i) -> p b i", b=nb)
        k.ACT(sn2[:, :, 16:32], y3, AF.Sin, [R], [R])
        k.TS("dve", sn2[:, :, 0:16], sn2[:, :, 16:32], -1.0, None, ALU.mult, None, [R], [R])
        k.TS("dve", y, y, float(np.pi / 2), None, ALU.add, None, [R], [R])
        wrap(y)
        k.ACT(cs2[:, :, 0:16], y3, AF.Sin, [R], [R])
        k.CP("dve", cs2[:, :, 16:32], cs2[:, :, 0:16], [R], [R])


def phase_KV(k, c):
    RPS = k.RPS
    m0 = k.mark()
    R_w = Res("kvw")
    Win = k.alloc(8 * 288, BF16).rearrange("p (c n) -> p c n", c=8)
    w_in3 = chunked(c["w_in0"])
    for ck in range(8):
        k.DMA("pool", Win[:, ck, :], w_in3[:, ck, 384:672], (), [R_w])
    stage = k.alloc(768)
    R_stage = Res("kvstage")
    Wuk = load_w_scaled_bf16(k, c["w_uk"], 2, 768, c["g_kvlat"], R_w, stage, R_stage)
    Wuv = load_w_scaled_bf16(k, c["w_uv"], 2, 768, c["g_kvlat"], R_w, stage, R_stage)
    rt = c["rt"]
    xs2 = [k.alloc(8 * 512).rearrange("p (c t) -> p c t", c=8) for _ in range(2)]
    R_xs2 = [Res("kxs0"), Res("kxs1")]
    hg = k.alloc(8 * 512, BF16).rearrange("p (c t) -> p c t", c=8)
    sq = k.alloc(8 * 512, BF16).rearrange("p (c t) -> p c t", c=8)
    R_hg, R_sq = Res("khg"), Res("ksq")
    rx = k.alloc(4)
    valid = k.alloc(4)
    R_rx = Res("krx")
    z = k.alloc(4 * 288).rearrange("p (b n) -> p b n", b=4)
    sqz = k.alloc(4 * 288).rearrange("p (b n) -> p b n", b=4)
    R_z, R_sqz = Res("kz"), Res("ksqz")
    st = k.alloc(16)
    R_st = Res("kst")
    ckvn = k.alloc(4 * 256, BF16).rearrange("p (b n) -> p b n", b=4)
    R_ckvn = Res("kckvn")
    krg = k.alloc(4 * 32).rearrange("p (b n) -> p b n", b=4)
    krt = k.alloc(4 * 32).rearrange("p (b n) -> p b n", b=4)
    kru = k.alloc(4 * 32).rearrange("p (b n) -> p b n", b=4)
    krr = k.alloc(4 * 32, BF16).rearrange("p (b n) -> p b n", b=4)
    R_kr, R_krr = Res("kkr"), Res("kkrr")
    ckvT = k.alloc(2 * 512, BF16).rearrange("p (c t) -> p c t", c=2)
    R_ckvT = Res("kckvT")
    KTr_t = [k.alloc(512, BF16) for _ in range(2)]
    R_KTr = [Res("kKTr0"), Res("kKTr1")]
    KTn_t = [k.alloc(6 * 512, BF16).rearrange("p (c t) -> p c t", c=6) for _ in range(2)]
    R_KTn = [Res("kKTn0"), Res("kKTn1")]
    Vx_t = [k.alloc(12 * 4 * 65, BF16).rearrange("p (h b c) -> p h b c", h=12, b=4) for _ in range(2)]
    R_Vx = [Res("kVx0"), Res("kVx1")]
    sqk = k.alloc(768)
    R_sqk = Res("ksqk")
    ssk = k.alloc(32)
    R_ssk = Res("kssk")
    rk3 = c["rk"].rearrange("p (b h) -> p b h", h=12)
    xw3 = chunked(c["xw"])
    KTn_w3 = c["KTn_w"].rearrange("(c p) t -> p c t", p=128)
    Vx_w4 = c["Vx_w"].rearrange("h p (b c) -> p h b c", c=65)
    gk = c["gk"]
    posw = c["posw"]
    ckvn2 = [ckvn, k.alloc(4 * 256, BF16).rearrange("p (b n) -> p b n", b=4)]
    R_ckvn2 = [R_ckvn, Res("kckvn1")]
    krr2 = [krr, k.alloc(4 * 32, BF16).rearrange("p (b n) -> p b n", b=4)]
    R_krr2 = [R_krr, Res("kkrr1")]
    st2 = [st, k.alloc(16)]
    R_st2 = [R_st, Res("kst1")]
    valid2 = [valid, k.alloc(4)]
    R_val2 = [Res("kval0"), Res("kval1")]

    def xload(t):
        k.DMA("sp", xs2[t % 2], xw3[:, :, t * 512:(t + 1) * 512], (), [R_xs2[t % 2]])

    def s1a(t):
        xs, R_xs = xs2[t % 2], R_xs2[t % 2]
        if t + 1 < 32:
            xload(t + 1)
        k.TT("dve", hg[:, 0:5, :], xs[:, 0:5, :], c["g_attn"][:, 0:5].unsqueeze(2).broadcast_to([128, 5, 512]), ALU.mult, [R_xs, k.Rg], [R_hg])
        for ck in range(5, 8):
            k.ACT(hg[:, ck, :], xs[:, ck, :], AF.Identity, [R_xs, k.Rg], [R_hg], scale=c["g_attn"][:, ck:ck + 1])
        k.ACT(sq, xs, AF.Square, [R_xs], [R_sq])
        pst = k.ps(3)[:, 508:512]
        for bl in range(4):
            for ck in range(8):
                k.MM(pst[:, bl:bl + 1], sq[:, ck, bl * 128:(bl + 1) * 128], k.ones_bf[:, 0:1], ck == 0, ck == 7, [R_sq, k.Rc], [RPS[3]])
        k.TS("dve", valid2[t % 2], pst, 0.0, None, ALU.is_gt, None, [RPS[3]], [R_val2[t % 2]])
        k.rsqrt(rx, pst, 1.0 / D, EPS, 4, [RPS[3]], [R_rx])

    def s1z(t, bl):
        for ck in range(8):
            k.MM(k.ps(bl)[:, 0:288], hg[:, ck, bl * 128:(bl + 1) * 128], Win[:, ck, :], ck == 0, ck == 7, [R_hg, R_w], [RPS[bl]])
        k.ACT(z[:, bl, :], k.ps(bl)[:, 0:288], AF.Identity, [RPS[bl], R_rx], [R_z], scale=rx[:, bl:bl + 1])

    def s1b(t):
        q = t % 2
        sT = st2[q]
        k.ACT(sqz, z, AF.Square, [R_z], [R_sqz])
        k.S.op("dve", lambda e: e.tensor_reduce(out=sT[:, 0:4], in_=sqz[:, :, 0:256], axis=AX.X, op=ALU.add), [R_sqz], [R_st2[q]])
        k.S.op("dve", lambda e: e.tensor_reduce(out=sT[:, 4:8], in_=sqz[:, :, 256:288], axis=AX.X, op=ALU.add), [R_sqz], [R_st2[q]])
        k.rsqrt(sT[:, 8:12], sT[:, 0:4], 1.0 / 256, EPS, 4, [R_st2[q]], [R_st2[q]])
        k.TT("dve", ckvn2[q], z[:, :, 0:256], sT[:, 8:12].unsqueeze(2).broadcast_to([128, 4, 256]), ALU.mult, [R_z, R_st2[q]], [R_ckvn2[q]])
        k.TT("pool", krg, z[:, :, 256:288], gk[:, 64:96].unsqueeze(1).broadcast_to([128, 4, 32]), ALU.mult, [R_z, k.Rg], [R_kr])
        tb4 = slice(4 * t, 4 * t + 4)
        k.TT("pool", krt, krg, rt.cs2[:, tb4, :], ALU.mult, [R_kr, rt.R], [R_kr])
        k.TT("pool", kru[:, :, 0:16], krg[:, :, 16:32], rt.sn2[:, tb4, 0:16], ALU.mult, [R_kr, rt.R], [R_kr])
        k.TT("pool", kru[:, :, 16:32], krg[:, :, 0:16], rt.sn2[:, tb4, 16:32], ALU.mult, [R_kr, rt.R], [R_kr])
        k.TT("pool", krr2[q], krt, kru, ALU.add, [R_kr], [R_krr2[q]])

    def s2tr(t):
        q = t % 2
        tcols = slice(t * 512, (t + 1) * 512)
        pb4, pb5 = k.ps_bf(4), k.ps_bf(5)
        for ck in range(2):
            for bl in range(4):
                k.TR(pb4[:, ck * 512 + bl * 128:ck * 512 + (bl + 1) * 128], ckvn2[q][:, bl, ck * 128:(ck + 1) * 128], [R_ckvn2[q], k.Rc], [RPS[4]])
        for bl in range(4):
            k.TR(pb5[0:32, bl * 128:(bl + 1) * 128], krr2[q][:, bl, :], [R_krr2[q], k.Rc], [RPS[5]])
        k.CP("act", ckvT, pb4.rearrange("p (c t) -> p c t", c=2), [RPS[4]], [R_ckvT])
        k.CP("act", KTr_t[q][0:32, :], pb5[0:32, 0:512], [RPS[5]], [R_KTr[q]])
        k.DMA("sp", c["KTr_w"][:, tcols], KTr_t[q][0:32, :], [R_KTr[q]], [c["R_KTr_w"]])

    def s2kv(t, bl):
        q = t % 2
        bk = [4, 5, 6, 7]
        tok = slice(bl * 128, (bl + 1) * 128)
        for n in range(2):
            for ck in range(2):
                k.MM(k.ps(bk[n])[:, 0:384], ckvT[:, ck, tok], Wuk[:, ck, n * 384:(n + 1) * 384], ck == 0, ck == 1, [R_ckvT, R_w], [RPS[bk[n]]])
            k.ACT(sqk[:, n * 384:(n + 1) * 384], k.ps(bk[n])[:, 0:384], AF.Square, [RPS[bk[n]]], [R_sqk])
        for n in range(2):
            for ck in range(2):
                k.MM(k.ps(bk[2 + n])[:, 0:384], ckvT[:, ck, tok], Wuv[:, ck, n * 384:(n + 1) * 384], ck == 0, ck == 1, [R_ckvT, R_w], [RPS[bk[2 + n]]])
            k.CP("dve" if n == 0 else "act", Vx_t[q][:, 6 * n:6 * n + 6, bl, 0:64], k.ps(bk[2 + n])[:, 0:384].rearrange("p (h d) -> p h d", h=6),
                 [RPS[bk[2 + n]]], [R_Vx[q]])
        k.S.op("dve", lambda e: e.tensor_reduce(out=ssk[:, 0:12], in_=sqk.rearrange("p (h d) -> p h d", h=12), axis=AX.X, op=ALU.add),
               [R_sqk], [R_ssk])
        k.TS("dve", ssk[:, 16:28], ssk[:, 0:12], st2[q][:, 4 + bl:5 + bl], None, ALU.add, None, [R_ssk, R_st2[q]], [R_ssk])
        k.rsqrt(rk3[:, 4 * t + bl, :], ssk[:, 16:28], 1.0, 96 * EPS, 12, [R_ssk], [c["R_rk"]])

    def s2kt(t):
        q = t % 2
        tcols = slice(t * 512, (t + 1) * 512)
        k.CP("pool", Vx_t[q][:, :, :, 64], valid2[q].unsqueeze(1).broadcast_to([128, 12, 4]), [R_val2[q]], [R_Vx[q]])
        for i in range(6):
            b = 4 + (i % 4)
            for ck in range(2):
                k.MM(k.ps(b), Wuk[:, ck, i * 128:(i + 1) * 128], ckvT[:, ck, :], ck == 0, ck == 1, [R_w, R_ckvT], [RPS[b]])
            k.CP("act" if i % 2 == 0 else "dve", KTn_t[q][:, i, :], k.ps(b), [RPS[b]], [R_KTn[q]])
        k.DMA("sp", KTn_w3[:, :, tcols], KTn_t[q], [R_KTn[q]], [c["R_KTn_w"]])
        for h0 in range(0, 12, 4):
            k.DMA("sp", Vx_w4[:, h0:h0 + 4, 4 * t:4 * t + 4, :], Vx_t[q][:, h0:h0 + 4, :, :], [R_Vx[q]], [c["R_Vx_w"]])

    NT = 32
    xload(0)
    for t in range(NT + 1):
        if t < NT:
            s1a(t)
        if t > 0:
            s2tr(t - 1)
        for bl in range(4):
            if t < NT:
                s1z(t, bl)
            if t > 0:
                s2kv(t - 1, bl)
        if t < NT:
            s1b(t)
        if t > 0:
            s2kt(t - 1)
    k.release(m0)


def phase_A_fused(k, c):
    RPS = k.RPS
    m0 = k.mark()
    R_w = Res("aw")
    Win = load_w_bf16(k, c["w_in0"], 8, 928, R_w)
    gq, gk, gmq = c["gq"], c["gk"], c["gmq"]
    Gq = k.alloc(96)
    k.CP("dve", Gq, gq, [k.Rg], [k.Rg])
    k.TT("dve", Gq[:, 0:64], gq[:, 0:64], gk[:, 0:64], ALU.mult, [k.Rg], [k.Rg])
    qT, qmT, R_qT, R_qmT = c["qT"], c["qmT"], c["R_qT"], c["R_qmT"]
    rt = c["rt"]
    xs = k.alloc(8 * 512).rearrange("p (c t) -> p c t", c=8)
    hgT = k.alloc(8 * 512, BF16).rearrange("p (c t) -> p c t", c=8)
    sq = k.alloc(8 * 512, BF16).rearrange("p (c t) -> p c t", c=8)
    R_xs, R_hg, R_sq = Res("axs"), Res("ahg"), Res("asq")
    rx = k.alloc(4)
    R_rx = Res("arx")

    class T:
        pass

    tl = []
    for p in range(2):
        t = T()
        t.z, t.R_z = k.alloc(928), Res(f"az{p}")
        t.stt, t.R_st = k.alloc(16), Res(f"ast{p}")
        t.qln, t.qmn, t.R_tm = k.alloc(384, BF16), k.alloc(256, BF16), Res(f"atm{p}")
        t.tmpm, t.R_tmpm = k.alloc(256), Res(f"atmpm{p}")
        t.qlT, t.R_qlT = k.alloc(3 * 128, BF16).rearrange("p (c t) -> p c t", c=3), Res(f"aqlT{p}")
        t.sqq, t.R_sqq = k.alloc(1152), Res(f"asqq{p}")
        t.ssq, t.R_ssq = k.alloc(32), Res(f"assq{p}")
        t.qn, t.R_qn = k.alloc(1152), Res(f"aqn{p}")
        t.qg, t.R_qg = k.alloc(1152), Res(f"aqg{p}")
        t.qt, t.qu, t.R_qtu = k.alloc(12 * 32), k.alloc(12 * 32), Res(f"aqtu{p}")
        t.qfin, t.R_qfin = k.alloc(12 * 96, BF16).rearrange("p (h d) -> p h d", h=12), Res(f"aqfin{p}")
        t.B = [4 * p, 4 * p + 1, 4 * p + 2, 4 * p + 3]
        tl.append(t)
    xw3 = chunked(c["xw"])
    R_stage = Res("astage")
    Wuq = load_w_scaled_bf16(k, c["w_uq"], 3, 1152, c["g_qlat"], R_w, tl[1].qn, R_stage)

    def block(j, bl, t):
        wt = 8 * j + 7
        bg = 4 * j + bl
        wb = 4 * wt + bl
        tok = slice(bl * 128, (bl + 1) * 128)
        B = t.B
        z, R_z, stt, R_st = t.z, t.R_z, t.stt, t.R_st
        sqq, R_sqq, ssq, R_ssq = t.sqq, t.R_sqq, t.ssq, t.R_ssq
        for n, (c0, c1) in enumerate([(0, 512), (512, 928)]):
            for ck in range(8):
                k.MM(k.ps(B[n])[:, 0:c1 - c0], hgT[:, ck, tok], Win[:, ck, c0:c1], ck == 0, ck == 7, [R_hg, R_w], [RPS[B[n]]])
            k.ACT(z[:, c0:c1], k.ps(B[n])[:, 0:c1 - c0], AF.Identity, [RPS[B[n]], R_rx], [R_z], scale=rx[:, bl:bl + 1])
        yield
        k.ACT(sqq[:, 0:384], z[:, 0:384], AF.Square, [R_z], [R_sqq])
        k.S.op("dve", lambda e: e.tensor_reduce(out=stt[:, 0:1], in_=sqq[:, 0:384], axis=AX.X, op=ALU.add), [R_sqq], [R_st])
        k.rsqrt(stt[:, 4:5], stt[:, 0:1], 1.0 / 384, EPS, 1, [R_st], [R_st])
        yield
        k.TS("dve", t.qln, z[:, 0:384], stt[:, 4:5], None, ALU.mult, None, [R_z, R_st], [t.R_tm])
        memq_norm(k, z[:, 672:928], R_z, gmq, t.qmn, t.R_tm, t.tmpm, t.R_tmpm, stt[:, 8:16], R_st)
        yield
        pb = k.ps_bf(B[2])
        for cc in range(3):
            k.TR(pb[:, cc * 128:(cc + 1) * 128], t.qln[:, cc * 128:(cc + 1) * 128], [t.R_tm, k.Rc], [RPS[B[2]]])
        for cc in range(2):
            k.TR(pb[:, (3 + cc) * 128:(4 + cc) * 128], t.qmn[:, cc * 128:(cc + 1) * 128], [t.R_tm, k.Rc], [RPS[B[2]]])
        k.CP("act", t.qlT, pb[:, 0:384].rearrange("p (c t) -> p c t", c=3), [RPS[B[2]]], [t.R_qlT])
        k.CP("act", qmT[:, :, bg * 128:(bg + 1) * 128], pb[:, 384:640].rearrange("p (c t) -> p c t", c=2), [RPS[B[2]]], [R_qmT])
        yield
        QB = [B[0], B[1], B[3]]
        for n in range(3):
            for ck in range(3):
                k.MM(k.ps(QB[n])[:, 0:384], t.qlT[:, ck, :], Wuq[:, ck, n * 384:(n + 1) * 384], ck == 0, ck == 2, [t.R_qlT, R_w], [RPS[QB[n]]])
            k.ACT(sqq[:, n * 384:(n + 1) * 384], k.ps(QB[n])[:, 0:384], AF.Square, [RPS[QB[n]]], [R_sqq])
        yield
        k.S.op("dve", lambda e: e.tensor_reduce(out=ssq[:, 0:12], in_=sqq.rearrange("p (h d) -> p h d", h=12), axis=AX.X, op=ALU.add),
               [R_sqq], [R_ssq])
        k.rsqrt(ssq[:, 16:28], ssq[:, 0:12], 1.0 / 96, EPS, 12, [R_ssq], [R_ssq])
        yield
        for n in range(3):
            k.TT("dve", t.qn[:, n * 384:(n + 1) * 384].rearrange("p (h d) -> p h d", h=4),
                 k.ps(QB[n])[:, 0:384].rearrange("p (h d) -> p h d", h=4),
                 ssq[:, 16 + 4 * n:20 + 4 * n].unsqueeze(2).broadcast_to([128, 4, 96]), ALU.mult, [RPS[QB[n]], R_ssq], [t.R_qn])
        yield
        qn3 = t.qn.rearrange("p (h d) -> p h d", h=12)
        qg3 = t.qg.rearrange("p (h d) -> p h d", h=12)
        k.TT("pool", qg3, qn3, Gq.unsqueeze(1).broadcast_to([128, 12, 96]), ALU.mult, [t.R_qn, k.Rg], [t.R_qg])
        qt3 = t.qt.rearrange("p (h d) -> p h d", h=12)
        qu3 = t.qu.rearrange("p (h d) -> p h d", h=12)
        k.TT("pool", qt3, qg3[:, :, 64:96], rt.cs2[:, wb, :].unsqueeze(1).broadcast_to([128, 12, 32]), ALU.mult, [t.R_qg, rt.R], [t.R_qtu])
        yield
        k.TT("pool", qu3[:, :, 0:16], qg3[:, :, 80:96], rt.sn2[:, wb, 0:16].unsqueeze(1).broadcast_to([128, 12, 16]), ALU.mult,
             [t.R_qg, rt.R], [t.R_qtu])
        k.TT("pool", qu3[:, :, 16:32], qg3[:, :, 64:80], rt.sn2[:, wb, 16:32].unsqueeze(1).broadcast_to([128, 12, 16]), ALU.mult,
             [t.R_qg, rt.R], [t.R_qtu])
        yield
        k.TT("dve", t.qfin[:, :, 64:96], qt3, qu3, ALU.add, [t.R_qtu], [t.R_qfin])
        k.CP("dve", t.qfin[:, :, 0:64], qg3[:, :, 0:64], [t.R_qg], [t.R_qfin])
        yield
        TB = [B[2], B[3]]
        for h in range(12):
            bk = TB[0] if h < 8 else TB[1]
            hh = h % 8
            k.TR(k.ps_bf(bk)[0:96, hh * 128:(hh + 1) * 128], t.qfin[:, h, :], [t.R_qfin, k.Rc], [RPS[bk]])
        k.CP("act", qT[0:96, 0:8, bg * 128:(bg + 1) * 128], k.ps_bf(TB[0])[0:96, :].rearrange("p (h t) -> p h t", h=8), [RPS[TB[0]]], [R_qT])
        k.CP("act", qT[0:96, 8:12, bg * 128:(bg + 1) * 128], k.ps_bf(TB[1])[0:96, 0:512].rearrange("p (h t) -> p h t", h=4), [RPS[TB[1]]], [R_qT])
        yield

    for j in range(NSLOT):
        wt = 8 * j + 7
        k.DMA("sp", xs, xw3[:, :, wt * 512:(wt + 1) * 512], (), [R_xs])
        norm_prep_slot(k, xs, R_xs, c["g_attn"], hgT, R_hg, sq, R_sq, rx, R_rx, 7)
        for pair in range(2):
            gens = [block(j, 2 * pair + p, tl[p]) for p in range(2)]
            live = list(gens)
            while live:
                for g in list(live):
                    try:
                        next(g)
                    except StopIteration:
                        live.remove(g)
    k.release(m0)


XRES_WORDS = 8 * TOWN


def phase_rest(k, fz=None):
    S = k.S
    k.final_dmas = []
    d = {}
    if fz is None:
        k.Rg = Res("gains")
        d["xT"] = k.inp("xT", [D, TOWN])
        KTn_w = k.inp("KTn_w", [768, SEQ], BF16)
        KTr_w = k.inp("KTr_w", [32, SEQ], BF16)
        Vx_w = k.inp("Vx_w", [12, 128, 128 * 65], BF16)
        rk_w = k.inp("rk_w", [128, 128 * 12])
        qT_i = k.inp("qT_i", [96, 12 * TOWN], BF16)
        qmT_i = k.inp("qmT_i", [128, 2 * TOWN], BF16)
        xT3_ = chunked(d["xT"])
        xsrc = lambda j: xT3_[:, :, j * 512:(j + 1) * 512]
    else:
        KTn_w, KTr_w, Vx_w = fz["KTn_w"], fz["KTr_w"], fz["Vx_w"]
        xw3_ = chunked(fz["xw"])
        xsrc = lambda j: xw3_[:, :, (8 * j + 7) * 512:(8 * j + 8) * 512]
    d["memT"] = k.inp("memT", [D, 256])
    for L in (0, 1):
        d[f"g_mem{L}"] = k.inp(f"g_mem{L}", [128, 8])
        d[f"w_mkv{L}"] = k.inp(f"w_mkv{L}", [D, 512])
        d[f"g_mk{L}"] = k.inp(f"g_mk{L}", [128, 64])
        d[f"w_out{L}"] = k.inp(f"w_out{L}", [D, D])
        d[f"g_ffn{L}"] = k.inp(f"g_ffn{L}", [128, 8])
        d[f"w_gate{L}"] = k.inp(f"w_gate{L}", [D, DFF])
        d[f"w_up{L}"] = k.inp(f"w_up{L}", [D, DFF])
        d[f"w_down{L}"] = k.inp(f"w_down{L}", [DFF, D])
    d["g_attn1"] = k.inp("g_attn1", [128, 8])
    d["w_in1"] = k.inp("w_in1", [D, 1792])
    d["ln_g"] = k.inp("ln_g", [128, 768])
    d["ln_b"] = k.inp("ln_b", [128, 768])
    d["w_spT"] = k.inp("w_spT", [128, 8 * 128])
    d["b_sp"] = k.inp("b_sp", [128, 8])
    d["g_mq1"] = k.inp("g_mq1", [128, 64])
    yT_o = k.outp("yT", [D, TOWN])

    xres = k.arena[:, ARENA_WORDS - XRES_WORDS:ARENA_WORDS].rearrange("p (c t) -> p c t", c=8)
    R_x = [Res(f"x{j}") for j in range(NSLOT)]
    LIMIT_FREE = ARENA_WORDS
    LIMIT_X = ARENA_WORDS - XRES_WORDS
    base = k.mark() if fz is None else fz["base0"]

    mixT = k.alloc(8 * TOWN, BF16).rearrange("p (c t) -> p c t", c=8)
    R_mix = [[Res(f"mix{c}_{j}") for j in range(NSLOT)] for c in range(8)]
    KmT = k.alloc(2 * 256, BF16).rearrange("p (c t) -> p c t", c=2)
    Vmx = k.alloc(2 * 4 * 65, BF16).rearrange("p (m h c) -> p m h c", m=2, h=4)
    R_KmT, R_Vmx = Res("KmT"), Res("Vmx")
    mem_kv_prep(k, "m0", d["memT"], d["g_mem0"], d["w_mkv0"], d["g_mk0"], KmT, R_KmT, Vmx, R_Vmx)
    ckpt(k, "memkv0")
    m_at = k.mark()
    if fz is None:
        qT = k.alloc(12 * TOWN, BF16).rearrange("p (h t) -> p h t", h=12)
        qmT = k.alloc(2 * TOWN, BF16).rearrange("p (c t) -> p c t", c=2)
        rk = k.alloc(128 * 12)
        R_qT, R_qmT, R_rk = Res("qT"), Res("qmT"), Res("rk")
        k.DMA("sp", qT[0:96].rearrange("p h t -> p (h t)"), qT_i, (), [R_qT])
        k.DMA("sp", qmT.rearrange("p c t -> p (c t)"), qmT_i, (), [R_qmT])
        k.DMA("sp", rk, rk_w, (), [R_rk])
    else:
        qT, qmT, rk = fz["qT"], fz["qmT"], fz["rk"]
        R_qT, R_qmT, R_rk = fz["R_qT"], fz["R_qmT"], fz["R_rk"]
    at = Attn(k, mixT, R_mix)
    at.mem_attention(qmT, R_qmT, KmT, R_KmT, Vmx, R_Vmx)
    ckpt(k, "memattn0")
    at.causal_attention(qT, R_qT, rk, R_rk, KTn_w, KTr_w, Vx_w)
    assert k.off <= LIMIT_FREE
    ckpt(k, "attn0")
    k.release(m_at)
    wout_residual(k, "l0", d["w_out0"], mixT, R_mix, xres, R_x, xsrc)
    assert k.off <= LIMIT_X
    ckpt(k, "wout0")
    k.release(base)
    ffn(k, "f0", xres, R_x, d["g_ffn0"], d["w_gate0"], d["w_up0"], d["w_down0"])
    ckpt(k, "ffn0")
    mixT = k.alloc(8 * TOWN, BF16).rearrange("p (c t) -> p c t", c=8)
    R_mix = [[Res(f"mixb{c}_{j}") for j in range(NSLOT)] for c in range(8)]
    qmT = k.alloc(2 * TOWN, BF16).rearrange("p (c t) -> p c t", c=2)
    R_qmT = Res("qmT1")
    KmT = k.alloc(2 * 256, BF16).rearrange("p (c t) -> p c t", c=2)
    Vmx = k.alloc(2 * 4 * 65, BF16).rearrange("p (m h c) -> p m h c", m=2, h=4)
    R_KmT, R_Vmx = Res("KmT1"), Res("Vmx1")
    mem_kv_prep(k, "m1", d["memT"], d["g_mem1"], d["w_mkv1"], d["g_mk1"], KmT, R_KmT, Vmx, R_Vmx)
    sgu_layer1(k, xres, R_x, mixT, R_mix, qmT, R_qmT, d)
    ckpt(k, "sgu")
    m_at = k.mark()
    at = Attn(k, mixT, R_mix)
    at.mem_attention(qmT, R_qmT, KmT, R_KmT, Vmx, R_Vmx)
    k.release(m_at)
    wout_residual(k, "l1", d["w_out1"], mixT, R_mix, xres, R_x, None)
    ckpt(k, "wout1")
    k.release(base)
    ffn(k, "f1", xres, R_x, d["g_ffn1"], d["w_gate1"], d["w_up1"], d["w_down1"])
    ckpt(k, "ffn1")
    yT3 = chunked(yT_o)
    for j in range(NSLOT):
        cols = slice(j * 512, (j + 1) * 512)
        k.final_dmas.append(k.DMA("sp", yT3[:, :, cols], xres[:, :, cols], [R_x[j]], ()))


def build_fused_body(k):
    nc = k.nc
    k.final_dmas = []
    k.Rg = Res("gains")
    c = {}
    c["xw"] = k.inp("xw", [D, SEQ])
    posw_d = k.inp("posw", [128, 128], I32)
    invf_d = k.inp("invf", [128, 16])
    c["w_in0"] = k.inp("w_in0", [D, 928])
    c["w_uq"] = k.inp("w_uq", [384, 1152])
    c["w_uk"] = k.inp("w_uk", [256, 768])
    c["w_uv"] = k.inp("w_uv", [256, 768])
    small = {}
    for name, w in [("g_attn0", 8), ("g_qlat", 3), ("g_kvlat", 2), ("gq_tile", 96), ("gk_tile", 96), ("g_mq0", 64)]:
        dd = k.inp(name, [128, w])
        t = k.alloc(w)
        k.DMA("sp", t, dd, (), [k.Rg])
        small[name] = t
    c["g_attn"], c["g_qlat"], c["g_kvlat"] = small["g_attn0"], small["g_qlat"], small["g_kvlat"]
    c["gq"], c["gk"], c["gmq"] = small["gq_tile"], small["gk_tile"], small["g_mq0"]
    c["invf"] = k.alloc(16)
    k.DMA("sp", c["invf"], invf_d, (), [k.Rg])
    c["posw"] = k.alloc(128, I32)
    c["R_pos"] = Res("posw")
    k.DMA("sp", c["posw"], posw_d, (), [c["R_pos"]])
    c["KTn_w"] = nc.dram_tensor("KTn_scr", [768, SEQ], BF16).ap()
    c["KTr_w"] = nc.dram_tensor("KTr_scr", [32, SEQ], BF16).ap()
    c["Vx_w"] = nc.dram_tensor("Vx_scr", [12, 128, 128 * 65], BF16).ap()
    c["R_KTn_w"], c["R_KTr_w"], c["R_Vx_w"] = Res("KTn_w"), Res("KTr_w"), Res("Vx_w")
    c["base0"] = k.mark()
    c["qT"] = k.alloc(12 * TOWN, BF16).rearrange("p (h t) -> p h t", h=12)
    c["qmT"] = k.alloc(2 * TOWN, BF16).rearrange("p (c t) -> p c t", c=2)
    c["rk"] = k.alloc(128 * 12)
    c["R_qT"], c["R_qmT"], c["R_rk"] = Res("qT"), Res("qmT"), Res("rk")
    m_rt = k.mark()
    cs2 = k.alloc(128 * 32).rearrange("p (b i) -> p b i", b=128)
    sn2 = k.alloc(128 * 32).rearrange("p (b i) -> p b i", b=128)
    m_tmp = k.mark()
    rt = RopeTab(k, 128)
    rt.compute(c["posw"], c["R_pos"], c["invf"])
    k.CP("pool", cs2, rt.cs2, [rt.R], [rt.R])
    k.CP("pool", sn2, rt.sn2, [rt.R], [rt.R])
    k.release(m_tmp)
    rt.cs2, rt.sn2 = cs2, sn2
    c["rt"] = rt
    phase_KV(k, c)
    ckpt(k, "kv")
    phase_A_fused(k, c)
    ckpt(k, "afused")
    k.release(m_rt)
    phase_rest(k, fz=c)


_CACHE = {}


def _get(mode):
    if mode not in _CACHE:
        _CACHE[mode] = build(mode)
    return _CACHE[mode]


def own_tokens(c):
    idx = []
    for j in range(NSLOT):
        g = 8 * j + c
        idx.append(np.arange(g * 512, (g + 1) * 512))
    return np.concatenate(idx)


def rep128(v):
    return np.ascontiguousarray(np.broadcast_to(np.asarray(v, np.float32)[None, :], (128, len(v))))


def pchunk(v):
    v = np.asarray(v, np.float32)
    return np.ascontiguousarray(v.reshape(-1, 128).T)


def l1_inputs(inp, c):
    tok = own_tokens(c)
    x = inp["x"][0]
    pos = inp["positions"][0][tok].astype(np.int32)
    inv_freq = (10000.0 ** (-np.arange(0, 32, 2, dtype=np.float32) / 32)).astype(np.float32)
    w_ukv = inp["l0_w_ukv"]
    return {
        "xT": np.ascontiguousarray(x[tok].T),
        "pos": np.ascontiguousarray(pos.reshape(16, 128).T),
        "invf": rep128(inv_freq),
        "w_in0": inp["l0_w_in"],
        "g_attn0": pchunk(inp["l0_attn_norm"]),
        "w_uq": np.ascontiguousarray(inp["l0_w_uq"].reshape(384, 1152)),
        "g_qlat": pchunk(inp["l0_q_lat_norm"]),
        "w_uk": np.ascontiguousarray(w_ukv[:, :, 0:64].reshape(256, 768)),
        "w_uv": np.ascontiguousarray(w_ukv[:, :, 64:128].reshape(256, 768)),
        "g_kvlat": pchunk(inp["l0_kv_lat_norm"]),
        "gq_tile": rep128(inp["l0_q_norm"]),
        "gk_tile": rep128(inp["l0_k_norm"]),
        "g_mq0": rep128(inp["l0_mq_norm"]),
    }


def l2_weights(inp):
    w = {"memT": np.ascontiguousarray(inp["mem"][0].T)}
    for L in (0, 1):
        p = f"l{L}_"
        w[f"g_mem{L}"] = pchunk(inp[p + "mem_norm"])
        w[f"w_mkv{L}"] = inp[p + "w_mem_kv"]
        w[f"g_mk{L}"] = rep128(inp[p + "mk_norm"])
        w[f"w_out{L}"] = inp[p + "w_out"]
        w[f"g_ffn{L}"] = pchunk(inp[p + "ffn_norm"])
        w[f"w_gate{L}"] = inp[p + "w_gate"]
        w[f"w_up{L}"] = inp[p + "w_up"]
        w[f"w_down{L}"] = inp[p + "w_down"]
    w["g_attn1"] = pchunk(inp["l1_attn_norm"])
    w["w_in1"] = inp["l1_w_in"]
    w["ln_g"] = rep128(inp["l1_sgu_ln_g"])
    w["ln_b"] = rep128(inp["l1_sgu_ln_b"])
    w["w_spT"] = np.ascontiguousarray(inp["l1_w_spatial"].transpose(2, 0, 1).reshape(128, 8 * 128))
    w["b_sp"] = np.ascontiguousarray(inp["l1_b_spatial"].T)
    w["g_mq1"] = rep128(inp["l1_mq_norm"])
    return {k_: np.ascontiguousarray(np.asarray(v, np.float32)) for k_, v in w.items()}


def gather_payload(res):
    NCH = 7 + 32
    bf = ml_dtypes.bfloat16
    KTn = np.zeros((768, NCH * 512), bf)
    KTr = np.zeros((32, NCH * 512), bf)
    Vx = np.zeros((12, 128, NCH * 4, 65), bf)
    rk = np.zeros((128, NCH * 4, 12), np.float32)
    for r in range(NCORES):
        a = np.asarray(res[r]["KTn_o"])
        b = np.asarray(res[r]["KTr_o"])
        v = np.asarray(res[r]["Vx_o"]).reshape(12, 128, 16, 65)
        q = np.asarray(res[r]["rk_o"]).reshape(128, 16, 12)
        for j in range(NSLOT):
            g = 8 * j + r
            KTn[:, (7 + g) * 512:(8 + g) * 512] = a[:, j * 512:(j + 1) * 512]
            KTr[:, (7 + g) * 512:(8 + g) * 512] = b[:, j * 512:(j + 1) * 512]
            Vx[:, :, (7 + g) * 4:(8 + g) * 4, :] = v[:, :, 4 * j:4 * j + 4, :]
            rk[:, (7 + g) * 4:(8 + g) * 4, :] = q[:, 4 * j:4 * j + 4, :]
    return KTn, KTr, Vx, rk


def kernel_unfused(**inp):
    inp = {k_: np.asarray(v) for k_, v in inp.items()}
    k1 = _get("L1")
    maps = [l1_inputs(inp, c) for c in range(NCORES)]
    r1 = run_bass_kernel_spmd(k1.nc, maps, core_ids=list(range(NCORES))).results
    KTn, KTr, Vx, rk = gather_payload(r1)
    k2 = _get("L2")
    w = l2_weights(inp)
    maps2 = []
    for c in range(NCORES):
        m = dict(w)
        m["xT"] = maps[c]["xT"]
        m["KTn_w"] = np.ascontiguousarray(KTn[:, c * 512:(c + 32) * 512])
        m["KTr_w"] = np.ascontiguousarray(KTr[:, c * 512:(c + 32) * 512])
        m["Vx_w"] = np.ascontiguousarray(Vx[:, :, c * 4:(c + 32) * 4, :]).reshape(12, 128, 128 * 65)
        m["rk_w"] = np.ascontiguousarray(rk[:, c * 4:(c + 32) * 4, :]).reshape(128, 128 * 12)
        m["qT_i"] = np.asarray(r1[c]["qT_o"])
        m["qmT_i"] = np.asarray(r1[c]["qmT_o"])
        maps2.append(m)
    r2 = run_bass_kernel_spmd(k2.nc, maps2, core_ids=list(range(NCORES))).results
    out = np.zeros((1, SEQ, D), np.float32)
    for c in range(NCORES):
        out[0, own_tokens(c), :] = np.asarray(r2[c]["yT"]).T
    return out


def fused_inputs(inp, c, w):
    x = inp["x"][0]
    pos = inp["positions"][0].astype(np.int32)
    xw = np.zeros((D, SEQ), np.float32)
    pw = np.zeros((SEQ,), np.int32)
    g0 = c - 7
    lo = max(0, g0)
    n = (g0 + 32 - lo) * 512
    dst = (lo - g0) * 512
    xw[:, dst:dst + n] = x[lo * 512:lo * 512 + n].T
    pw[dst:dst + n] = pos[lo * 512:lo * 512 + n]
    m = dict(w)
    m["xw"] = xw
    m["posw"] = np.ascontiguousarray(pw.reshape(128, 128).T)
    return m


def kernel(**inp):
    inp = {k_: np.asarray(v) for k_, v in inp.items()}
    kf = _get("fused")
    w = l2_weights(inp)
    l1 = l1_inputs(inp, 0)
    for name in ("invf", "w_in0", "g_attn0", "w_uq", "g_qlat", "w_uk", "w_uv", "g_kvlat", "gq_tile", "gk_tile", "g_mq0"):
        w[name] = l1[name]
    maps = [fused_inputs(inp, c, w) for c in range(NCORES)]
    r = run_bass_kernel_spmd(kf.nc, maps, core_ids=list(range(NCORES))).results
    out = np.zeros((1, SEQ, D), np.float32)
    for c in range(NCORES):
        out[0, own_tokens(c), :] = np.asarray(r[c]["yT"]).T
    return out
```

```python
import contextlib
import numpy as np
import ml_dtypes
import concourse.bass as bass
import concourse.mybir as mybir
from concourse.bass_utils import run_bass_kernel_spmd

F32 = mybir.dt.float32
BF16 = mybir.dt.bfloat16
I32 = mybir.dt.int32
AF = mybir.ActivationFunctionType
ALU = mybir.AluOpType
AX = mybir.AxisListType

NCORES = 8
SEQ = 16384
D = 1024
TOWN = 2048
NSLOT = 4
DFF = 2816
NF = 22
EPS = 1e-6
MASKNEG = -30000.0


class Res:
    __slots__ = ("name", "last_w", "readers", "dma_w")

    def __init__(self, name=""):
        self.name = name
        self.last_w = None
        self.readers = []
        self.dma_w = []


class Op:
    __slots__ = ("eng", "fn", "deps", "is_dma", "needed", "token")

    def __init__(self, eng, fn, is_dma):
        self.eng = eng
        self.fn = fn
        self.deps = []
        self.is_dma = is_dma
        self.needed = False
        self.token = None


class Sched:
    ENGS = ("pe", "act", "dve", "pool", "sp")
    NHW = 24
    NSW = 12
    NDMA = NHW + NSW

    def __init__(self, nc):
        self.nc = nc
        self.ops = {e: [] for e in self.ENGS}
        self.all_ops = []
        self.dma_ops = []
        self.dma_hw = []
        self.dma_sw = []
        self.cc_ops = []
        self.fence = []

    def _collect(self, op, reads, writes):
        deps = []
        for r in reads:
            if r.last_w is not None:
                deps.append(r.last_w)
            deps.extend(r.dma_w)
        for w in writes:
            if w.last_w is not None:
                deps.append(w.last_w)
            deps.extend(w.readers)
            if not op.is_dma:
                deps.extend(w.dma_w)
        deps.extend(self.fence)
        out = []
        seen = set()
        for d in deps:
            if id(d) in seen or d is op:
                continue
            seen.add(id(d))
            if (not d.is_dma) and (not op.is_dma) and d.eng == "pe" and op.eng == "pe":
                continue
            out.append(d)
        op.deps = out
        for r in reads:
            r.readers.append(op)
        for w in writes:
            if op.is_dma:
                if w.readers:
                    w.dma_w = [op]
                else:
                    w.dma_w.append(op)
                w.readers = []
            else:
                w.last_w = op
                w.dma_w = []
                w.readers = []

    def op(self, eng, fn, reads=(), writes=()):
        o = Op(eng, fn, False)
        self._collect(o, reads, writes)
        self.ops[eng].append(o)
        self.all_ops.append(o)
        return o

    def dma(self, eng, fn, reads=(), writes=()):
        o = Op(eng, fn, True)
        self._collect(o, reads, writes)
        sw = (eng == "pool")
        lst = self.dma_sw if sw else self.dma_hw
        n = self.NSW if sw else self.NHW
        base = self.NHW if sw else 0
        kk = len(lst)
        if kk >= n:
            prev = lst[kk - n]
            if prev not in o.deps:
                o.deps.append(prev)
        o.token = ("dma", base + kk % n, 16 * (kk // n + 1))
        o.needed = True
        lst.append(o)
        self.dma_ops.append(o)
        self.ops[eng].append(o)
        self.all_ops.append(o)
        return o

    def cc(self, fn, reads=(), writes=()):
        o = Op("pool", fn, True)
        self._collect(o, reads, writes)
        o.token = ("cc", len(self.cc_ops), 16)
        o.needed = True
        self.cc_ops.append(o)
        self.ops["pool"].append(o)
        self.all_ops.append(o)
        return o

    def barrier(self):
        fence = []
        for e in self.ENGS:
            comp = [o for o in self.ops[e] if not o.is_dma]
            if comp:
                fence.append(comp[-1])
        fence.extend(self.dma_hw[-self.NHW:])
        fence.extend(self.dma_sw[-self.NSW:])
        self.fence = fence

    def emit(self, final_waits=()):
        nc = self.nc
        for o in self.all_ops:
            for d in o.deps:
                d.needed = True
        for o in final_waits:
            o.needed = True
        for e in self.ENGS:
            cnt = 0
            for o in self.ops[e]:
                if o.is_dma:
                    continue
                if o.needed:
                    cnt += 1
                    o.token = ("eng", e, cnt)
        with contextlib.ExitStack() as st:
            esem = {e: st.enter_context(nc.semaphore(f"s_{e}")) for e in self.ENGS}
            dsem = [st.enter_context(nc.semaphore(f"s_dma{i}")) for i in range(self.NDMA)]
            csem = [st.enter_context(nc.semaphore(f"s_cc{i}")) for i in range(len(self.cc_ops))]
            block = st.enter_context(nc.Block())

            def semof(tok):
                if tok[0] == "eng":
                    return esem[tok[1]], tok[2], ("eng", tok[1])
                if tok[0] == "cc":
                    return csem[tok[1]], tok[2], ("cc", tok[1])
                return dsem[tok[1]], tok[2], ("dma", tok[1])

            def run(e, eh, extra_final=False):
                waited = {}
                for o in self.ops[e]:
                    for d in o.deps:
                        sem, val, key = semof(d.token)
                        if waited.get(key, 0) >= val:
                            continue
                        waited[key] = val
                        eh.wait_ge(sem, val)
                    inst = o.fn(eh)
                    if o.is_dma and o.token[0] == "cc":
                        inst.then_inc(csem[o.token[1]], 16)
                    elif o.is_dma:
                        inst.then_inc(dsem[o.token[1]], 16)
                    elif o.needed:
                        inst.then_inc(esem[e], 1)
                if extra_final:
                    for o in final_waits:
                        sem, val, key = semof(o.token)
                        if waited.get(key, 0) >= val:
                            continue
                        waited[key] = val
                        eh.wait_ge(sem, val)

            @block.tensor
            def _(eh):
                run("pe", eh)

            @block.scalar
            def _(eh):
                run("act", eh)

            @block.vector
            def _(eh):
                run("dve", eh)

            @block.gpsimd
            def _(eh):
                run("pool", eh)

            @block.sync
            def _(eh):
                run("sp", eh, extra_final=True)


ARENA_WORDS = 53000


class K:
    def __init__(self, mode):
        self.mode = mode
        self.nc = bass.Bass("TRN2", target_bir_lowering=False)
        self.S = Sched(self.nc)
        self.din = {}
        self.dout = {}
        self.off = 0
        self.rot = {}

    def inp(self, name, shape, dt=F32):
        ap = self.nc.dram_tensor(name, list(shape), dt, kind="ExternalInput").ap()
        self.din[name] = ap
        return ap

    def outp(self, name, shape, dt=F32):
        ap = self.nc.dram_tensor(name, list(shape), dt, kind="ExternalOutput").ap()
        self.dout[name] = ap
        return ap

    def alloc(self, cols, dt=F32):
        w = cols if dt in (F32, I32) else (cols + 1) // 2
        assert self.off + w <= ARENA_WORDS, f"arena overflow {self.off + w}"
        ap = self.arena[:, self.off:self.off + w]
        self.off += w
        if dt != F32:
            ap = ap.bitcast(dt)
            if ap.shape[1] != cols:
                ap = ap[:, 0:cols]
        return ap

    def mark(self):
        return self.off

    def release(self, m):
        self.S.barrier()
        self.off = m

    def ps(self, b, n=512):
        return self.psum[:, 512 * b:512 * b + n]

    def ps_bf(self, b):
        return self.psum[:, 512 * b:512 * (b + 1)].bitcast(BF16)

    def MM(self, out, lhsT, rhs, start, stop, rd, wr):
        return self.S.op("pe", lambda e: e.matmul(out, lhsT=lhsT, rhs=rhs, start=start, stop=stop), rd, wr)

    def TR(self, out, in_, rd, wr):
        ident = self.ident
        return self.S.op("pe", lambda e: e.transpose(out=out, in_=in_, identity=ident), rd, wr)

    def ACT(self, out, in_, func, rd, wr, scale=None, bias=None):
        kw = {}
        if scale is not None:
            kw["scale"] = scale
        if bias is not None:
            kw["bias"] = bias
        return self.S.op("act", lambda e: e.activation(out=out, in_=in_, func=func, **kw), rd, wr)

    def TT(self, eng, out, in0, in1, op, rd, wr):
        return self.S.op(eng, lambda e: e.tensor_tensor(out=out, in0=in0, in1=in1, op=op), rd, wr)

    def TS(self, eng, out, in0, s1, s2, op0, op1, rd, wr):
        if op1 is None:
            return self.S.op(eng, lambda e: e.tensor_scalar(out=out, in0=in0, scalar1=s1, scalar2=None, op0=op0), rd, wr)
        return self.S.op(eng, lambda e: e.tensor_scalar(out=out, in0=in0, scalar1=s1, scalar2=s2, op0=op0, op1=op1), rd, wr)

    def STT(self, out, in0, scalar, in1, op0, op1, rd, wr):
        return self.S.op("dve", lambda e: e.scalar_tensor_tensor(out=out, in0=in0, scalar=scalar, in1=in1, op0=op0, op1=op1), rd, wr)

    def CP(self, eng, out, in_, rd, wr):
        if eng == "act":
            return self.S.op("act", lambda e: e.copy(out=out, in_=in_), rd, wr)
        return self.S.op(eng, lambda e: e.tensor_copy(out=out, in_=in_), rd, wr)

    def MEMSET(self, eng, ap, val, wr):
        return self.S.op(eng, lambda e: e.memset(ap, val), (), wr)

    def DMA(self, q, out, in_, rd, wr):
        return self.S.dma(q, lambda e: e.dma_start(out=out, in_=in_), rd, wr)

    def rsqrt(self, out, in_, mul, add, w, rd, wr):
        k = self.rot.get("rs", 0)
        self.rot["rs"] = k + 1
        ta, ra = self.rs_tmp[k % 4]
        tb, rb = self.rs_tmp2[k % 4]
        if getattr(self, "pool_rsqrt", False) and w <= 16:
            self.TS("dve", ta[:, 0:w], in_, mul, add, ALU.mult, ALU.add, rd, [ra])
            self.TT("pool", out, ta[:, 0:w], self.negh[:, 0:w], ALU.pow, [ra, self.Rc], wr)
            return
        self.TS("dve", ta[:, 0:w], in_, mul, add, ALU.mult, ALU.add, rd, [ra])
        self.ACT(tb[:, 0:w], ta[:, 0:w], AF.Sqrt, [ra], [rb])
        self.S.op("dve", lambda e: e.reciprocal(out=out, in_=tb[:, 0:w]), [rb], wr)


class Stop(Exception):
    pass


import os
STOP_AT = os.environ.get("K_STOP", "")


def ckpt(k, name):
    if STOP_AT and name == STOP_AT:
        raise Stop()


def chunked(ap, p=128):
    return ap.rearrange("(c p) n -> p c n", p=p)


def build(mode):
    k = K(mode)
    nc, S = k.nc, k.S
    with contextlib.ExitStack() as st:
        k.arena = st.enter_context(nc.sbuf_tensor("arena", [128, ARENA_WORDS], F32))
        k.psum = st.enter_context(nc.psum_tensor("psum", [128, 4096], F32))
        RPS = [Res(f"ps{i}") for i in range(8)]
        k.RPS = RPS

        k.ident = k.alloc(128, BF16)
        R_const = Res("const")
        iop = k.alloc(128)
        ioc = k.alloc(128)
        tmpf = k.alloc(128)
        k.ones_bf = k.alloc(128, BF16)
        k.ones_f = k.alloc(128)
        k.maskneg = k.alloc(128, BF16)
        k.rs_tmp = [(k.alloc(16), Res("rsa")) for _ in range(4)]
        k.rs_tmp2 = [(k.alloc(16), Res("rsb")) for _ in range(4)]
        S.op("pool", lambda e: e.iota(iop, pattern=[[0, 128]], base=0, channel_multiplier=1, allow_small_or_imprecise_dtypes=True), (), [R_const])
        S.op("pool", lambda e: e.iota(ioc, pattern=[[1, 128]], base=0, channel_multiplier=0, allow_small_or_imprecise_dtypes=True), (), [R_const])
        k.TT("dve", tmpf, iop, ioc, ALU.is_equal, [R_const], [R_const])
        k.CP("dve", k.ident, tmpf, [R_const], [R_const])
        k.TT("dve", tmpf, iop, ioc, ALU.is_gt, [R_const], [R_const])
        k.TS("dve", k.maskneg, tmpf, MASKNEG, None, ALU.mult, None, [R_const], [R_const])
        k.MEMSET("dve", k.ones_bf, 1.0, [R_const])
        k.MEMSET("dve", k.ones_f, 1.0, [R_const])
        k.negh = k.alloc(16)
        k.MEMSET("dve", k.negh, -0.5, [R_const])
        k.pool_rsqrt = (mode == "fused")
        k.Rc = R_const
        k.iop, k.ioc = iop, ioc

        try:
            if mode == "L1":
                phase_A_layer0(k)
            elif mode == "L2":
                phase_rest(k)
            else:
                build_fused_body(k)
        except Stop:
            pass
        finals = list(k.final_dmas)
        S.emit(final_waits=finals)
    return k


def load_w_bf16(k, dram_ap, nck, cols, res, colsplit=None):
    t = k.alloc(nck * cols, BF16).rearrange("p (c n) -> p c n", c=nck)
    src = chunked(dram_ap)
    step = colsplit or cols
    for ck in range(nck):
        for c0 in range(0, cols, step):
            c1 = min(cols, c0 + step)
            k.DMA("pool", t[:, ck, c0:c1], src[:, ck, c0:c1], (), [res])
    return t


def load_w_scaled_bf16(k, dram_ap, nck, cols, g_ap, res, stage, stage_res):
    t = k.alloc(nck * cols, BF16).rearrange("p (c n) -> p c n", c=nck)
    src = chunked(dram_ap)
    for ck in range(nck):
        k.DMA("sp", stage[:, 0:cols], src[:, ck, :], (), [stage_res])
        k.TS("pool", t[:, ck, :], stage[:, 0:cols], g_ap[:, ck:ck + 1], None, ALU.mult, None, [stage_res, k.Rg], [res])
    return t


def norm_prep_slot(k, xsrc, R_x, g_ap, hgT, R_hg, sq, R_sq, rx_out, R_rx, psb):
    for ck in range(8):
        k.ACT(hgT[:, ck, :], xsrc[:, ck, :], AF.Identity, [R_x, k.Rg], [R_hg], scale=g_ap[:, ck:ck + 1])
    k.ACT(sq, xsrc, AF.Square, [R_x], [R_sq])
    pst = k.ps(psb)
    for bl in range(4):
        for ck in range(8):
            k.MM(pst[:, bl:bl + 1], sq[:, ck, bl * 128:(bl + 1) * 128], k.ones_bf[:, 0:1], ck == 0, ck == 7,
                 [R_sq, k.Rc], [k.RPS[psb]])
    k.rsqrt(rx_out, pst[:, 0:4], 1.0 / D, EPS, 4, [k.RPS[psb]], [R_rx])


def memq_norm(k, zm, R_z, gmq_tile, qmn, R_qmn, tmp, R_tmp, st, R_st):
    zm3 = zm.rearrange("p (h d) -> p h d", h=4)
    t3 = tmp.rearrange("p (h d) -> p h d", h=4)
    k.TT("pool", tmp, zm, zm, ALU.mult, [R_z], [R_tmp])
    S = k.S
    S.op("dve", lambda e: e.tensor_reduce(out=st[:, 0:4], in_=t3, axis=AX.X, op=ALU.add), [R_tmp], [R_st])
    k.rsqrt(st[:, 4:8], st[:, 0:4], 1.0 / 64, EPS, 4, [R_st], [R_st])
    k.TT("dve", t3, zm3, st[:, 4:8].unsqueeze(2).broadcast_to([128, 4, 64]), ALU.mult, [R_z, R_st], [R_tmp])
    k.TT("pool", qmn.rearrange("p (h d) -> p h d", h=4), t3, gmq_tile.unsqueeze(1).broadcast_to([128, 4, 64]), ALU.mult,
         [R_tmp, k.Rg], [R_qmn])


def phase_A_layer0(k):
    S = k.S
    k.final_dmas = []
    xT_d = k.inp("xT", [D, TOWN])
    pos_d = k.inp("pos", [128, 16], I32)
    invf_d = k.inp("invf", [128, 16])
    w_in_d = k.inp("w_in0", [D, 928])
    g_attn_d = k.inp("g_attn0", [128, 8])
    w_uq_d = k.inp("w_uq", [384, 1152])
    g_qlat_d = k.inp("g_qlat", [128, 3])
    w_uk_d = k.inp("w_uk", [256, 768])
    w_uv_d = k.inp("w_uv", [256, 768])
    g_kvlat_d = k.inp("g_kvlat", [128, 2])
    gq_d = k.inp("gq_tile", [128, 96])
    gk_d = k.inp("gk_tile", [128, 96])
    gmq_d = k.inp("g_mq0", [128, 64])
    qT_o = k.outp("qT_o", [96, 12 * TOWN], BF16)
    qmT_o = k.outp("qmT_o", [128, 2 * TOWN], BF16)
    KTn_o = k.outp("KTn_o", [768, TOWN], BF16)
    KTr_o = k.outp("KTr_o", [32, TOWN], BF16)
    Vx_o = k.outp("Vx_o", [12, 128, 16 * 65], BF16)
    rk_o = k.outp("rk_o", [128, 16 * 12])

    k.Rg = Res("gains")
    R_w = Res("weights")
    g_attn = k.alloc(8)
    g_qlat = k.alloc(3)
    g_kvlat = k.alloc(2)
    gq = k.alloc(96)
    gk = k.alloc(96)
    gmq = k.alloc(64)
    invf = k.alloc(16)
    posi = k.alloc(16, I32)
    for t, d in [(g_attn, g_attn_d), (g_qlat, g_qlat_d), (g_kvlat, g_kvlat_d), (gq, gq_d), (gk, gk_d), (gmq, gmq_d), (invf, invf_d)]:
        k.DMA("sp", t, d, (), [k.Rg])
    k.DMA("sp", posi, pos_d, (), [k.Rg])

    Win = load_w_bf16(k, w_in_d, 8, 928, R_w)
    stage = k.alloc(1152)
    R_stage = Res("stage")
    Wuq = load_w_scaled_bf16(k, w_uq_d, 3, 1152, g_qlat, R_w, stage, R_stage)
    Wuk = load_w_scaled_bf16(k, w_uk_d, 2, 768, g_kvlat, R_w, stage, R_stage)
    Wuv = load_w_scaled_bf16(k, w_uv_d, 2, 768, g_kvlat, R_w, stage, R_stage)

    ckpt(k, "weights")
    R_rope = Res("rope")
    posf = k.alloc(16)
    ang = k.alloc(256)
    kq = k.alloc(256)
    ki = k.alloc(256, I32)
    y = k.alloc(256)
    m = k.alloc(256)
    cs2 = k.alloc(16 * 32).rearrange("p (b i) -> p b i", b=16)
    sn2 = k.alloc(16 * 32).rearrange("p (b i) -> p b i", b=16)
    ang3 = ang.rearrange("p (b i) -> p b i", b=16)
    k.CP("dve", posf, posi, [k.Rg], [R_rope])
    k.TT("dve", ang3, posf.unsqueeze(2).broadcast_to([128, 16, 16]), invf.unsqueeze(1).broadcast_to([128, 16, 16]), ALU.mult,
         [R_rope, k.Rg], [R_rope])
    TWO_PI = 2.0 * np.pi
    C1 = 6.28125
    C2 = float(np.float32(TWO_PI - C1))
    k.TS("dve", kq, ang, float(1.0 / TWO_PI), None, ALU.mult, None, [R_rope], [R_rope])
    k.CP("dve", ki, kq, [R_rope], [R_rope])
    k.CP("dve", kq, ki, [R_rope], [R_rope])
    k.STT(y, kq, -C1, ang, ALU.mult, ALU.add, [R_rope], [R_rope])
    k.STT(y, kq, -C2, y, ALU.mult, ALU.add, [R_rope], [R_rope])

    def wrap(t):
        k.TS("dve", m, t, float(np.pi), None, ALU.is_gt, None, [R_rope], [R_rope])
        k.STT(t, m, -TWO_PI, t, ALU.mult, ALU.add, [R_rope], [R_rope])
        k.TS("dve", m, t, float(-np.pi), None, ALU.is_lt, None, [R_rope], [R_rope])
        k.STT(t, m, TWO_PI, t, ALU.mult, ALU.add, [R_rope], [R_rope])

    wrap(y)
    y3 = y.rearrange("p (b i) -> p b i", b=16)
    k.ACT(sn2[:, :, 16:32], y3, AF.Sin, [R_rope], [R_rope])
    k.TS("dve", sn2[:, :, 0:16], sn2[:, :, 16:32], -1.0, None, ALU.mult, None, [R_rope], [R_rope])
    k.TS("dve", y, y, float(np.pi / 2), None, ALU.add, None, [R_rope], [R_rope])
    wrap(y)
    k.ACT(cs2[:, :, 0:16], y3, AF.Sin, [R_rope], [R_rope])
    k.CP("dve", cs2[:, :, 16:32], cs2[:, :, 0:16], [R_rope], [R_rope])

    ckpt(k, "rope")
    Gq = k.alloc(96)
    k.CP("dve", Gq, gq, [k.Rg], [k.Rg])
    k.TT("dve", Gq[:, 0:64], gq[:, 0:64], gk[:, 0:64], ALU.mult, [k.Rg], [k.Rg])

    qT = k.alloc(12 * TOWN, BF16).rearrange("p (h t) -> p h t", h=12)
    qmT = k.alloc(2 * TOWN, BF16).rearrange("p (c t) -> p c t", c=2)
    rk_own = k.alloc(16 * 12).rearrange("p (b h) -> p b h", b=16)
    R_qT = Res("qT")
    R_qmT = Res("qmT")
    R_rk = Res("rk")
    xs0 = k.alloc(8 * 512).rearrange("p (c t) -> p c t", c=8)
    xs = [xs0, xs0]
    R_xs0 = Res("xs0")
    R_xs = [R_xs0, R_xs0]
    hg0 = k.alloc(8 * 512, BF16).rearrange("p (c t) -> p c t", c=8)
    hgT = [hg0, hg0]
    R_hg0 = Res("hg0")
    R_hg = [R_hg0, R_hg0]
    sq = k.alloc(8 * 512, BF16).rearrange("p (c t) -> p c t", c=8)
    R_sq = Res("sq")
    rx = k.alloc(16)
    R_rx = Res("rx")
    ckvT = [k.alloc(2 * 512, BF16).rearrange("p (c t) -> p c t", c=2) for _ in range(2)]
    R_ckvT = [Res("ckvT0"), Res("ckvT1")]
    KTn = [k.alloc(6 * 512, BF16).rearrange("p (c t) -> p c t", c=6) for _ in range(2)]
    R_KTn = [Res("KTn0"), Res("KTn1")]
    KTr = [k.alloc(512, BF16) for _ in range(2)]
    R_KTr = [Res("KTr0"), Res("KTr1")]
    Vx = [k.alloc(12 * 4 * 65, BF16).rearrange("p (h b c) -> p h b c", h=12, b=4) for _ in range(2)]
    R_Vx = [Res("Vx0"), Res("Vx1")]
    for i in range(2):
        k.MEMSET("pool", Vx[i][:, :, :, 64:65], 1.0, [R_Vx[i]])
    z = k.alloc(928)
    R_z = Res("z")
    stt = k.alloc(16)
    R_st = Res("st")
    qln = k.alloc(384, BF16)
    ckvn = k.alloc(256, BF16)
    qmn = k.alloc(256, BF16)
    krr = k.alloc(32, BF16)
    R_tm = Res("tm_bf")
    tmpm = k.alloc(256)
    R_tmpm = Res("tmpm")
    krg = k.alloc(32)
    krt = k.alloc(32)
    kru = k.alloc(32)
    R_kr = Res("kr")
    qlT = k.alloc(3 * 128, BF16).rearrange("p (c t) -> p c t", c=3)
    R_qlT = Res("qlT")
    sqq = k.alloc(1152)
    R_sqq = Res("sqq")
    ssq = k.alloc(32)
    R_ssq = Res("ssq")
    qn = k.alloc(1152)
    R_qn = Res("qn")
    qg = k.alloc(1152)
    R_qg = Res("qg")
    qt = k.alloc(12 * 32)
    qu = k.alloc(12 * 32)
    R_qtu = Res("qtu")
    qfin = k.alloc(12 * 96, BF16).rearrange("p (h d) -> p h d", h=12)
    R_qfin = Res("qfin")
    sqk = k.alloc(768)
    R_sqk = Res("sqk")
    ssk = k.alloc(32)
    R_ssk = Res("ssk")
    RPS = k.RPS

    xT3 = chunked(xT_d)
    for j in range(NSLOT):
        sb = j % 2
        k.DMA("sp", xs[sb], xT3[:, :, j * 512:(j + 1) * 512], (), [R_xs[sb]])
        norm_prep_slot(k, xs[sb], R_xs[sb], g_attn, hgT[sb], R_hg[sb], sq, R_sq, rx[:, 4 * j:4 * j + 4], R_rx, 7)
        ckpt(k, "norm")
        for bl in range(4):
            bg = 4 * j + bl
            tok = slice(bl * 128, (bl + 1) * 128)
            for n, (c0, c1) in enumerate([(0, 512), (512, 928)]):
                for ck in range(8):
                    k.MM(k.ps(n)[:, 0:c1 - c0], hgT[sb][:, ck, tok], Win[:, ck, c0:c1], ck == 0, ck == 7,
                         [R_hg[sb], R_w], [RPS[n]])
                k.ACT(z[:, c0:c1], k.ps(n)[:, 0:c1 - c0], AF.Identity, [RPS[n], R_rx], [R_z], scale=rx[:, bg:bg + 1])
            ckpt(k, "z")
            k.ACT(sqq[:, 0:672], z[:, 0:672], AF.Square, [R_z], [R_sqq])
            k.S.op("dve", lambda e: e.tensor_reduce(out=stt[:, 0:1], in_=sqq[:, 0:384], axis=AX.X, op=ALU.add), [R_sqq], [R_st])
            k.S.op("dve", lambda e: e.tensor_reduce(out=stt[:, 1:2], in_=sqq[:, 384:640], axis=AX.X, op=ALU.add), [R_sqq], [R_st])
            k.S.op("dve", lambda e: e.tensor_reduce(out=stt[:, 2:3], in_=sqq[:, 640:672], axis=AX.X, op=ALU.add), [R_sqq], [R_st])
            k.rsqrt(stt[:, 4:5], stt[:, 0:1], 1.0 / 384, EPS, 1, [R_st], [R_st])
            k.rsqrt(stt[:, 5:6], stt[:, 1:2], 1.0 / 256, EPS, 1, [R_st], [R_st])
            k.TS("dve", qln, z[:, 0:384], stt[:, 4:5], None, ALU.mult, None, [R_z, R_st], [R_tm])
            k.TS("dve", ckvn, z[:, 384:640], stt[:, 5:6], None, ALU.mult, None, [R_z, R_st], [R_tm])
            memq_norm(k, z[:, 672:928], R_z, gmq, qmn, R_tm, tmpm, R_tmpm, stt[:, 8:16], R_st)
            k.TT("pool", krg, z[:, 640:672], gk[:, 64:96], ALU.mult, [R_z, k.Rg], [R_kr])
            k.TT("pool", krt, krg, cs2[:, bg, :], ALU.mult, [R_kr, R_rope], [R_kr])
            k.TT("pool", kru[:, 0:16], krg[:, 16:32], sn2[:, bg, 0:16], ALU.mult, [R_kr, R_rope], [R_kr])
            k.TT("pool", kru[:, 16:32], krg[:, 0:16], sn2[:, bg, 16:32], ALU.mult, [R_kr, R_rope], [R_kr])
            k.TT("pool", krr, krt, kru, ALU.add, [R_kr], [R_tm])
            ckpt(k, "tm")
            pb = k.ps_bf(2)
            for c in range(3):
                k.TR(pb[:, c * 128:(c + 1) * 128], qln[:, c * 128:(c + 1) * 128], [R_tm, k.Rc], [RPS[2]])
            ckpt(k, "tr1")
            for c in range(2):
                k.TR(pb[:, (3 + c) * 128:(4 + c) * 128], ckvn[:, c * 128:(c + 1) * 128], [R_tm, k.Rc], [RPS[2]])
            for c in range(2):
                k.TR(pb[:, (5 + c) * 128:(6 + c) * 128], qmn[:, c * 128:(c + 1) * 128], [R_tm, k.Rc], [RPS[2]])
            ckpt(k, "tr3")
            k.TR(pb[0:32, 7 * 128:8 * 128], krr, [R_tm, k.Rc], [RPS[2]])
            ckpt(k, "tr4")
            k.CP("act", qlT, pb[:, 0:384].rearrange("p (c t) -> p c t", c=3), [RPS[2]], [R_qlT])
            ckpt(k, "cp1")
            k.CP("act", ckvT[sb][:, :, tok], pb[:, 384:640].rearrange("p (c t) -> p c t", c=2), [RPS[2]], [R_ckvT[sb]])
            ckpt(k, "cp2")
            k.CP("act", qmT[:, :, bg * 128:(bg + 1) * 128], pb[:, 640:896].rearrange("p (c t) -> p c t", c=2), [RPS[2]], [R_qmT])
            ckpt(k, "cp3")
            k.CP("act", KTr[sb][0:32, tok], pb[0:32, 896:1024], [RPS[2]], [R_KTr[sb]])
            ckpt(k, "tr")
            for n in range(3):
                for ck in range(3):
                    k.MM(k.ps(3 + n)[:, 0:384], qlT[:, ck, :], Wuq[:, ck, n * 384:(n + 1) * 384], ck == 0, ck == 2,
                         [R_qlT, R_w], [RPS[3 + n]])
                k.ACT(sqq[:, n * 384:(n + 1) * 384], k.ps(3 + n)[:, 0:384], AF.Square, [RPS[3 + n]], [R_sqq])
            k.S.op("dve", lambda e: e.tensor_reduce(out=ssq[:, 0:12], in_=sqq.rearrange("p (h d) -> p h d", h=12), axis=AX.X, op=ALU.add),
                   [R_sqq], [R_ssq])
            k.rsqrt(ssq[:, 16:28], ssq[:, 0:12], 1.0 / 96, EPS, 12, [R_ssq], [R_ssq])
            for n in range(3):
                k.TT("dve", qn[:, n * 384:(n + 1) * 384].rearrange("p (h d) -> p h d", h=4),
                     k.ps(3 + n)[:, 0:384].rearrange("p (h d) -> p h d", h=4),
                     ssq[:, 16 + 4 * n:20 + 4 * n].unsqueeze(2).broadcast_to([128, 4, 96]), ALU.mult,
                     [RPS[3 + n], R_ssq], [R_qn])
            qn3 = qn.rearrange("p (h d) -> p h d", h=12)
            qg3 = qg.rearrange("p (h d) -> p h d", h=12)
            k.TT("pool", qg3, qn3, Gq.unsqueeze(1).broadcast_to([128, 12, 96]), ALU.mult, [R_qn, k.Rg], [R_qg])
            qt3 = qt.rearrange("p (h d) -> p h d", h=12)
            qu3 = qu.rearrange("p (h d) -> p h d", h=12)
            k.TT("pool", qt3, qg3[:, :, 64:96], cs2[:, bg, :].unsqueeze(1).broadcast_to([128, 12, 32]), ALU.mult, [R_qg, R_rope], [R_qtu])
            k.TT("pool", qu3[:, :, 0:16], qg3[:, :, 80:96], sn2[:, bg, 0:16].unsqueeze(1).broadcast_to([128, 12, 16]), ALU.mult,
                 [R_qg, R_rope], [R_qtu])
            k.TT("pool", qu3[:, :, 16:32], qg3[:, :, 64:80], sn2[:, bg, 16:32].unsqueeze(1).broadcast_to([128, 12, 16]), ALU.mult,
                 [R_qg, R_rope], [R_qtu])
            k.TT("dve", qfin[:, :, 64:96], qt3, qu3, ALU.add, [R_qtu], [R_qfin])
            k.CP("dve", qfin[:, :, 0:64], qg3[:, :, 0:64], [R_qg], [R_qfin])
            for h in range(12):
                bk = 6 if h < 8 else 7
                hh = h % 8
                k.TR(k.ps_bf(bk)[0:96, hh * 128:(hh + 1) * 128], qfin[:, h, :], [R_qfin, k.Rc], [RPS[bk]])
            k.CP("act", qT[0:96, 0:8, bg * 128:(bg + 1) * 128], k.ps_bf(6)[0:96, :].rearrange("p (h t) -> p h t", h=8), [RPS[6]], [R_qT])
            k.CP("act", qT[0:96, 8:12, bg * 128:(bg + 1) * 128], k.ps_bf(7)[0:96, 0:512].rearrange("p (h t) -> p h t", h=4), [RPS[7]], [R_qT])
            ckpt(k, "q")
            for n in range(2):
                for ck in range(2):
                    k.MM(k.ps(n)[:, 0:384], ckvT[sb][:, ck, tok], Wuk[:, ck, n * 384:(n + 1) * 384], ck == 0, ck == 1,
                         [R_ckvT[sb], R_w], [RPS[n]])
                k.ACT(sqk[:, n * 384:(n + 1) * 384], k.ps(n)[:, 0:384], AF.Square, [RPS[n]], [R_sqk])
            for n in range(2):
                for ck in range(2):
                    k.MM(k.ps(3 + n)[:, 0:384], ckvT[sb][:, ck, tok], Wuv[:, ck, n * 384:(n + 1) * 384], ck == 0, ck == 1,
                         [R_ckvT[sb], R_w], [RPS[3 + n]])
                k.CP("act", Vx[sb][:, 6 * n:6 * n + 6, bl, 0:64], k.ps(3 + n)[:, 0:384].rearrange("p (h d) -> p h d", h=6),
                     [RPS[3 + n]], [R_Vx[sb]])
            k.S.op("dve", lambda e: e.tensor_reduce(out=ssk[:, 0:12], in_=sqk.rearrange("p (h d) -> p h d", h=12), axis=AX.X, op=ALU.add),
                   [R_sqk], [R_ssk])
            k.TS("dve", ssk[:, 16:28], ssk[:, 0:12], stt[:, 2:3], None, ALU.add, None, [R_ssk, R_st], [R_ssk])
            k.rsqrt(rk_own[:, bg, :], ssk[:, 16:28], 1.0, 96 * EPS, 12, [R_ssk], [R_rk])
        ckpt(k, "blocks")
        for i in range(6):
            for ck in range(2):
                k.MM(k.ps(2), Wuk[:, ck, i * 128:(i + 1) * 128], ckvT[sb][:, ck, :], ck == 0, ck == 1, [R_w, R_ckvT[sb]], [RPS[2]])
            k.CP("act" if i % 2 == 0 else "dve", KTn[sb][:, i, :], k.ps(2), [RPS[2]], [R_KTn[sb]])
        ckpt(k, "ktn")
        KTn_o3 = KTn_o.rearrange("(c p) t -> p c t", p=128)
        k.final_dmas.append(k.DMA("sp", KTn_o3[:, :, j * 512:(j + 1) * 512], KTn[sb], [R_KTn[sb]], ()))
        k.final_dmas.append(k.DMA("sp", KTr_o[:, j * 512:(j + 1) * 512], KTr[sb][0:32, :], [R_KTr[sb]], ()))
        Vx_o4 = Vx_o.rearrange("h p (b c) -> p h b c", b=16)
        for h0 in range(0, 12, 4):
            k.final_dmas.append(k.DMA("sp", Vx_o4[:, h0:h0 + 4, 4 * j:4 * j + 4, :], Vx[sb][:, h0:h0 + 4, :, :], [R_Vx[sb]], ()))
    ckpt(k, "slots")
    k.final_dmas.append(k.DMA("sp", rk_o, rk_own.rearrange("p b h -> p (b h)"), [R_rk], ()))
    k.final_dmas.append(k.DMA("sp", qT_o, qT[0:96].rearrange("p h t -> p (h t)"), [R_qT], ()))
    k.final_dmas.append(k.DMA("sp", qmT_o, qmT.rearrange("p c t -> p (c t)"), [R_qmT], ()))


def mem_kv_prep(k, pre, memT_d, gmem_d, wmkv_d, gmk_d, KmT, R_KmT, Vmx, R_Vmx):
    RPS = k.RPS
    m = k.mark()
    memx = k.alloc(8 * 256).rearrange("p (c t) -> p c t", c=8)
    mg = k.alloc(8 * 256, BF16).rearrange("p (c t) -> p c t", c=8)
    sqm = k.alloc(8 * 256, BF16).rearrange("p (c t) -> p c t", c=8)
    gmem = k.alloc(8)
    gmk = k.alloc(64)
    rmem = k.alloc(4)
    kvm = k.alloc(512)
    kn = k.alloc(256, BF16)
    tmp = k.alloc(256)
    st = k.alloc(8)
    R_a, R_b, R_c, R_d, R_e, R_f, R_w = (Res(pre + n) for n in "abcdefw")
    k.DMA("sp", memx, chunked(memT_d), (), [R_a])
    k.DMA("sp", gmem, gmem_d, (), [k.Rg])
    k.DMA("sp", gmk, gmk_d, (), [k.Rg])
    Wm = load_w_bf16(k, wmkv_d, 8, 512, R_w)
    for ck in range(8):
        k.ACT(mg[:, ck, :], memx[:, ck, :], AF.Identity, [R_a, k.Rg], [R_b], scale=gmem[:, ck:ck + 1])
    k.ACT(sqm, memx, AF.Square, [R_a], [R_c])
    for mb in range(2):
        for ck in range(8):
            k.MM(k.ps(7)[:, mb:mb + 1], sqm[:, ck, mb * 128:(mb + 1) * 128], k.ones_bf[:, 0:1], ck == 0, ck == 7, [R_c, k.Rc], [RPS[7]])
    k.rsqrt(rmem[:, 0:2], k.ps(7)[:, 0:2], 1.0 / D, EPS, 2, [RPS[7]], [R_d])
    k.MEMSET("pool", Vmx[:, :, :, 64:65], 1.0, [R_Vmx])
    for mb in range(2):
        for ck in range(8):
            k.MM(k.ps(0), mg[:, ck, mb * 128:(mb + 1) * 128], Wm[:, ck, :], ck == 0, ck == 7, [R_b, R_w], [RPS[0]])
        k.ACT(kvm, k.ps(0), AF.Identity, [RPS[0], R_d], [R_e], scale=rmem[:, mb:mb + 1])
        memq_norm(k, kvm[:, 0:256], R_e, gmk, kn, R_f, tmp, R_f, st, R_f)
        k.CP("dve", Vmx[:, mb, :, 0:64], kvm[:, 256:512].rearrange("p (h d) -> p h d", h=4), [R_e], [R_Vmx])
        pb = k.ps_bf(2)
        for c in range(2):
            k.TR(pb[:, c * 128:(c + 1) * 128], kn[:, c * 128:(c + 1) * 128], [R_f, k.Rc], [RPS[2]])
        k.CP("act", KmT[:, :, mb * 128:(mb + 1) * 128], pb[:, 0:256].rearrange("p (c t) -> p c t", c=2), [RPS[2]], [R_KmT])
    k.release(m)


class Attn:
    def __init__(self, k, mixT, R_mix):
        self.k = k
        self.mixT = mixT
        self.R_mix = R_mix
        self.P = [k.alloc(1024, BF16) for _ in range(3)]
        self.R_P = [Res("P0"), Res("P1"), Res("P2")]
        self.R_Sb = [Res(f"Sb{i}") for i in range(4)]
        self.rec = k.alloc(512)
        self.R_rec = Res("rec")
        self.bsb = k.alloc(512)
        self.R_bsb = Res("bsb")
        self.ost = [k.alloc(512, BF16) for _ in range(2)]
        self.R_ost = [Res("ost0"), Res("ost1")]
        self.nS = 0
        self.nO = 0

    def sbuf(self):
        i = self.nS % 2
        self.nS += 1
        return i, self.k.psum[:, (4 + 2 * i) * 512:(6 + 2 * i) * 512], [self.R_Sb[2 * i], self.R_Sb[2 * i + 1]]

    def finalize(self, j, chunk, odd, si=None):
        k = self.k
        acc = k.ps(j)
        Racc = k.RPS[j]
        k.CP("act", self.rec[64:65, :], acc[64:65, :], [Racc], [self.R_rec])
        if si is None:
            i, Sps, RS = self.sbuf()
        else:
            Sps, RS = k.psum[:, (4 + si) * 512:(5 + si) * 512], [self.R_Sb[si]]
        k.MM(Sps[0:64, 0:512], k.ones_f[64:65, 0:64], self.rec[64:65, :], True, True, [self.R_rec, k.Rc], RS)
        k.S.op("dve", lambda e: e.reciprocal(out=self.bsb[0:64, :], in_=Sps[0:64, 0:512]), RS, [self.R_bsb])
        cols = slice(j * 512, (j + 1) * 512)
        if not odd:
            k.TT("dve", self.mixT[0:64, chunk, cols], acc[0:64, :], self.bsb[0:64, :], ALU.mult, [Racc, self.R_bsb], [self.R_mix[chunk][j]])
        else:
            t = self.nO % 2
            self.nO += 1
            k.TT("dve", self.ost[t][0:64, :], acc[0:64, :], self.bsb[0:64, :], ALU.mult, [Racc, self.R_bsb], [self.R_ost[t]])
            k.DMA("pool", self.mixT[64:128, chunk, cols], self.ost[t][0:64, :], [self.R_ost[t]], [self.R_mix[chunk][j]])

    def mem_attention(self, qmT, R_qmT, KmT, R_KmT, Vmx, R_Vmx):
        k = self.k
        for hm in range(4):
            pr = slice((hm % 2) * 64, (hm % 2) * 64 + 64)
            for j in range(4):
                i, Sps, RS = self.sbuf()
                for mb in range(2):
                    k.MM(Sps[:, mb * 512:(mb + 1) * 512], KmT[pr, hm // 2, mb * 128:(mb + 1) * 128], qmT[pr, hm // 2, j * 512:(j + 1) * 512],
                         True, True, [R_KmT, R_qmT], RS)
                k.ACT(self.P[i], Sps, AF.Exp, RS, [self.R_P[i]], scale=0.125)
                for mb in range(2):
                    k.MM(k.ps(j)[0:65, :], Vmx[:, mb, hm, :], self.P[i][:, mb * 512:(mb + 1) * 512], mb == 0, mb == 1,
                         [R_Vmx, self.R_P[i]], [k.RPS[j]])
                self.finalize(j, 6 + hm // 2, hm % 2)

    def causal_attention(self, qT, R_qT, rk, R_rk, KTn_w, KTr_w, Vx_w):
        k = self.k
        KT = [k.alloc(4096, BF16) for _ in range(2)]
        Vb = [k.alloc(32 * 128, BF16).rearrange("p (b c) -> p b c", c=128) for _ in range(2)]
        R_KT = [Res("KT0"), Res("KT1")]
        R_V = [Res("V0"), Res("V1")]
        for i_ in range(2):
            k.MEMSET("pool", Vb[i_], 0.0, [R_V[i_]])
        Vx4 = Vx_w.rearrange("h p (b c) -> h p b c", c=65)

        def load(n):
            h, grp = divmod(n, 4)
            b = n % 2
            ts = slice(4096 * grp, 4096 * (grp + 1))
            k.DMA("sp", KT[b][0:64, :], KTn_w[64 * h:64 * h + 64, ts], (), [R_KT[b]])
            k.DMA("sp", KT[b][64:96, :], KTr_w[:, ts], (), [R_KT[b]])
            k.DMA("sp", Vb[b][:, :, 0:65], Vx4[h, :, 32 * grp:32 * grp + 32, :], (), [R_V[b]])

        descs = []
        for n in range(48):
            h, grp = divmod(n, 4)
            b = n % 2
            js = [j for j in range(4) if j >= grp]
            batches = [[j_] for j_ in js]
            for il in range(8):
                i = 8 * grp + il
                for kb in range(4):
                    for bi, batch in enumerate(batches):
                        last = (il == 7 and kb == 3 and bi == len(batches) - 1)
                        descs.append((n, h, grp, b, il, i, kb, batch, last))

        def emit_qk(dsc):
            n, h, grp, b, il, i, kb, batch, last = dsc
            Bw = 4 * i + kb
            col = (il * 4 + kb) * 128
            lk = KT[b][0:96, col:col + 128]
            si = self.nbatch % 4
            pi = self.nbatch % 5
            self.nbatch += 1
            Sps, RS = k.psum[:, (4 + si) * 512:(5 + si) * 512], self.R_Sb[si]
            dg = (il == 7 and batch[0] == grp)
            n0 = 128 * kb if dg else 0
            ncols = 512 * len(batch)
            for u, j in enumerate(batch):
                off = n0 if u == 0 else 0
                msk = dg and u == 0
                k.MM(Sps[:, u * 512 + off:(u + 1) * 512], lk, qT[0:96, h, j * 512 + off:(j + 1) * 512], True, not msk,
                     [R_KT[b], R_qT], [RS])
                if msk:
                    k.MM(Sps[:, off:off + 128], k.ident, k.maskneg, False, True, [k.Rc], [RS])
            k.ACT(Ph[pi][:, n0:ncols], Sps[:, n0:ncols], AF.Exp, [RS, R_rk], [R_Ph[pi]], scale=rk[:, Bw * 12 + h:Bw * 12 + h + 1])
            return si, n0, pi

        def emit_pv(dsc, si, n0, pi):
            n, h, grp, b, il, i, kb, batch, last = dsc
            Bw = 4 * i + kb
            for u, j in enumerate(batch):
                off = n0 if u == 0 else 0
                k.MM(k.ps(j)[:, off:512], Vb[b][:, il * 4 + kb, :], Ph[pi][:, u * 512 + off:(u + 1) * 512],
                     Bw == 0, (i == 8 * j + 7 and kb == 3), [R_V[b], R_Ph[pi]], [k.RPS[j]])
            if last:
                self.finalize(grp, h // 2, h % 2, si=si)
                if n + 2 < 48:
                    load(n + 2)

        self.nbatch = 0
        Ph = [self.P[0][:, 0:512], self.P[0][:, 512:1024], self.P[1][:, 0:512], self.P[1][:, 512:1024], self.P[2][:, 0:512]]
        R_Ph = [Res(f"Ph{i}") for i in range(5)]
        k.S.barrier()
        load(0)
        load(1)
        pend = []
        for dsc in descs:
            cur = emit_qk(dsc)
            pend.append((dsc, cur))
            if len(pend) > 3:
                d0, c0 = pend.pop(0)
                emit_pv(d0, *c0)
        for d0, c0 in pend:
            emit_pv(d0, *c0)


def wout_residual(k, pre, wout_d, mixT, R_mix, xres, R_x, xsrc):
    RPS = k.RPS
    m = k.mark()
    R_w = Res(pre + "wout")
    Wout = load_w_bf16(k, wout_d, 8, 1024, R_w)
    nb = 0
    for j in range(NSLOT):
        cols = slice(j * 512, (j + 1) * 512)
        if xsrc is not None:
            k.DMA("sp", xres[:, :, cols], xsrc(j), (), [R_x[j]])
        for dch in range(8):
            b = nb % 8
            nb += 1
            for mc in range(8):
                k.MM(k.ps(b), Wout[:, mc, dch * 128:(dch + 1) * 128], mixT[:, mc, cols], mc == 0, mc == 7, [R_w, R_mix[mc][j]], [RPS[b]])
            k.TT("dve", xres[:, dch, cols], k.ps(b), xres[:, dch, cols], ALU.add, [RPS[b], R_x[j]], [R_x[j]])
    k.release(m)


def ffn(k, pre, xres, R_x, gffn_d, wg_d, wu_d, wd_d):
    RPS = k.RPS
    m = k.mark()
    g = k.alloc(8)
    k.DMA("sp", g, gffn_d, (), [k.Rg])
    hT = k.alloc(8 * 1024, BF16).rearrange("p (c t) -> p c t", c=8)
    actT = k.alloc(NF * 1024, BF16).rearrange("p (f t) -> p f t", f=NF)
    sq = k.alloc(8 * 512, BF16).rearrange("p (c t) -> p c t", c=8)
    rt1 = k.alloc(512)
    rt2 = k.alloc(512)
    rbc = k.alloc(512)
    sg = [k.alloc(512) for _ in range(2)]
    Wg_r = [k.alloc(8 * 256, BF16).rearrange("p (c n) -> p c n", c=8) for _ in range(3)]
    Wu_r = [k.alloc(8 * 256, BF16).rearrange("p (c n) -> p c n", c=8) for _ in range(3)]
    Wd_r = [k.alloc(NF * 128, BF16).rearrange("p (f n) -> p f n", f=NF) for _ in range(3)]
    R_hT, R_sq, R_rt, R_rbc = Res(pre + "hT"), Res(pre + "sq"), Res(pre + "rt"), Res(pre + "rbc")
    R_act = [[Res(f"{pre}act{f}_{t}") for t in range(2)] for f in range(NF)]
    R_sg = [Res(pre + "sg0"), Res(pre + "sg1")]
    R_Wgu = [Res(f"{pre}wgu{i}") for i in range(3)]
    R_Wd = [Res(f"{pre}wd{i}") for i in range(3)]
    wg3, wu3 = chunked(wg_d), chunked(wu_d)
    wd3 = chunked(wd_d)
    NFG = NF // 2

    def load_gu(fg):
        s = fg % 3
        k.DMA("pool", Wg_r[s], wg3[:, :, fg * 256:(fg + 1) * 256], (), [R_Wgu[s]])
        k.DMA("pool", Wu_r[s], wu3[:, :, fg * 256:(fg + 1) * 256], (), [R_Wgu[s]])

    def load_d(dch):
        s = dch % 3
        k.DMA("pool", Wd_r[s], wd3[:, :, dch * 128:(dch + 1) * 128], (), [R_Wd[s]])

    for half in range(2):
        hc = half * 1024
        load_gu(0)
        load_gu(1)
        for t in range(2):
            cols = slice(hc + t * 512, hc + (t + 1) * 512)
            j = (hc + t * 512) // 512
            k.ACT(sq, xres[:, :, cols], AF.Square, [R_x[j]], [R_sq])
            for ck in range(8):
                k.MM(k.ps(7), k.ones_bf, sq[:, ck, :], ck == 0, ck == 7, [k.Rc, R_sq], [RPS[7]])
            k.TS("dve", rt1, k.ps(7), 1.0 / D, EPS, ALU.mult, ALU.add, [RPS[7]], [R_rt])
            k.ACT(rt2, rt1, AF.Sqrt, [R_rt], [R_rt])
            k.S.op("dve", lambda e: e.reciprocal(out=rbc, in_=rt2), [R_rt], [R_rbc])
            for ck in range(8):
                k.STT(hT[:, ck, t * 512:(t + 1) * 512], xres[:, ck, cols], g[:, ck:ck + 1], rbc, ALU.mult, ALU.mult,
                      [R_x[j], k.Rg, R_rbc], [R_hT])
        nsg = 0
        for fg in range(NFG):
            if fg + 2 < NFG:
                load_gu(fg + 2)
            if fg == NFG - 2:
                load_d(0)
            if fg == NFG - 1:
                load_d(1)
            s = fg % 3
            for fl in range(2):
                f = 2 * fg + fl
                gb = [0, 1] if f % 2 == 0 else [4, 5]
                ub = [2, 3] if f % 2 == 0 else [6, 7]
                for W_r, banks in ((Wg_r, gb), (Wu_r, ub)):
                    for ck in range(8):
                        for t in range(2):
                            k.MM(k.ps(banks[t]), W_r[s][:, ck, fl * 128:(fl + 1) * 128], hT[:, ck, t * 512:(t + 1) * 512], ck == 0, ck == 7,
                                 [R_Wgu[s], R_hT], [RPS[banks[t]]])
                for t in range(2):
                    q = nsg % 2
                    nsg += 1
                    k.ACT(sg[q], k.ps(gb[t]), AF.Silu, [RPS[gb[t]]], [R_sg[q]])
                    k.TT("dve", actT[:, f, t * 512:(t + 1) * 512], k.ps(ub[t]), sg[q], ALU.mult, [RPS[ub[t]], R_sg[q]], [R_act[f][t]])
        nb = 0
        for dch in range(8):
            if dch + 2 < 8:
                load_d(dch + 2)
            s = dch % 3
            for t in range(2):
                b = nb % 8
                nb += 1
                cols = slice(hc + t * 512, hc + (t + 1) * 512)
                j = (hc + t * 512) // 512
                for f in range(NF):
                    k.MM(k.ps(b), Wd_r[s][:, f, :], actT[:, f, t * 512:(t + 1) * 512], f == 0, f == NF - 1, [R_Wd[s], R_act[f][t]], [RPS[b]])
                k.TT("dve", xres[:, dch, cols], k.ps(b), xres[:, dch, cols], ALU.add, [RPS[b], R_x[j]], [R_x[j]])
    k.release(m)


def sgu_layer1(k, xres, R_x, mixT, R_mix, qmT, R_qmT, d):
    RPS = k.RPS
    m = k.mark()
    R_w = Res("l1w")
    g_attn = k.alloc(8)
    gmq = k.alloc(64)
    lng = k.alloc(768)
    lnb = k.alloc(768)
    bsp = k.alloc(8)
    for t_, d_ in [(g_attn, d["g_attn1"]), (gmq, d["g_mq1"]), (lng, d["ln_g"]), (lnb, d["ln_b"]), (bsp, d["b_sp"])]:
        k.DMA("sp", t_, d_, (), [k.Rg])
    Win = load_w_bf16(k, d["w_in1"], 8, 1792, R_w, colsplit=896)
    wsp = k.alloc(8 * 128, BF16).rearrange("p (g t) -> p g t", g=8)
    wst = k.alloc(8 * 128).rearrange("p (g t) -> p g t", g=8)
    msk = k.alloc(128)
    R_ws = Res("wsp")
    k.DMA("sp", wst, d["w_spT"].rearrange("p (g t) -> p g t", g=8), (), [R_ws])
    k.TT("dve", msk, k.iop, k.ioc, ALU.is_le, [k.Rc], [R_ws])
    k.TT("dve", wsp, wst, msk.unsqueeze(1).broadcast_to([128, 8, 128]), ALU.mult, [R_ws], [R_w])
    hgT = k.alloc(8 * 512, BF16).rearrange("p (c t) -> p c t", c=8)
    sq = k.alloc(8 * 512, BF16).rearrange("p (c t) -> p c t", c=8)
    rx = k.alloc(16)
    R_hg, R_sq, R_rx = Res("l1hg"), Res("l1sq"), Res("l1rx")
    uv2 = [k.alloc(1536) for _ in range(2)]
    zm2 = [k.alloc(256) for _ in range(2)]
    R_uv2, R_zm2 = [Res("uv0"), Res("uv1")], [Res("zm0"), Res("zm1")]
    bst = k.alloc(12)
    mv = k.alloc(4)
    R_bn = Res("bn")
    vt = k.alloc(768)
    vt2 = k.alloc(768)
    R_vt = Res("vt")
    vn = k.alloc(768, BF16)
    R_vn = Res("vn")
    y = k.alloc(768, BF16)
    R_y = Res("y")
    qmn = k.alloc(256, BF16)
    R_qmn = Res("qmn1")
    tmpm = k.alloc(256)
    R_tmpm = Res("tmpm1")
    stt = k.alloc(8)
    R_st = Res("st1")
    hg2 = [hgT, k.alloc(8 * 512, BF16).rearrange("p (c t) -> p c t", c=8)]
    R_hg2 = [R_hg, Res("l1hg1")]

    def stage1(j, bl):
        bg = 4 * j + bl
        q = bg % 2
        tok = slice(bl * 128, (bl + 1) * 128)
        uv, zm = uv2[q], zm2[q]
        for n in range(4):
            for ck in range(8):
                k.MM(k.ps(n)[:, 0:448], hg2[j % 2][:, ck, tok], Win[:, ck, n * 448:(n + 1) * 448], ck == 0, ck == 7, [R_hg2[j % 2], R_w], [RPS[n]])
        rxa = rx[:, bg:bg + 1]
        for n in range(3):
            k.ACT(uv[:, n * 448:(n + 1) * 448], k.ps(n)[:, 0:448], AF.Gelu, [RPS[n], R_rx], [R_uv2[q]], scale=rxa)
        k.ACT(uv[:, 1344:1536], k.ps(3)[:, 0:192], AF.Gelu, [RPS[3], R_rx], [R_uv2[q]], scale=rxa)
        k.ACT(zm, k.ps(3)[:, 192:448], AF.Identity, [RPS[3], R_rx], [R_zm2[q]], scale=rxa)

    def stage2(j, bl):
        bg = 4 * j + bl
        q = bg % 2
        gt = slice(bg * 128, (bg + 1) * 128)
        uv, zm, R_uv, R_zm = uv2[q], zm2[q], R_uv2[q], R_zm2[q]
        v = uv[:, 768:1536]
        bst3 = bst.rearrange("p (a s) -> p a s", a=2)
        for a_ in range(2):
            k.S.op("dve", lambda e, a_=a_: e.bn_stats(out=bst3[:, a_, :], in_=v[:, a_ * 384:(a_ + 1) * 384]), [R_uv], [R_bn])
        k.S.op("dve", lambda e: e.bn_aggr(out=mv[:, 0:2], in_=bst), [R_bn], [R_bn])
        k.rsqrt(mv[:, 2:3], mv[:, 1:2], 1.0, EPS, 1, [R_bn], [R_bn])
        k.TS("dve", vt, v, mv[:, 0:1], mv[:, 2:3], ALU.subtract, ALU.mult, [R_uv, R_bn], [R_vt])
        k.TT("pool", vt2, vt, lng, ALU.mult, [R_vt, k.Rg], [R_vt])
        k.TT("pool", vn, vt2, lnb, ALU.add, [R_vt, k.Rg], [R_vn])
        for g_ in range(8):
            b_ = 4 + g_ // 4
            c0 = (g_ % 4) * 96
            k.MM(k.ps(b_)[:, c0:c0 + 96], wsp[:, g_, :], vn[:, g_ * 96:(g_ + 1) * 96], True, True, [R_w, R_vn], [RPS[b_]])
        for g_ in range(8):
            b_ = 4 + g_ // 4
            c0 = (g_ % 4) * 96
            k.STT(y[:, g_ * 96:(g_ + 1) * 96], k.ps(b_)[:, c0:c0 + 96], bsp[:, g_:g_ + 1], uv[:, g_ * 96:(g_ + 1) * 96], ALU.add, ALU.mult,
                  [RPS[b_], k.Rg, R_uv], [R_y])
        memq_norm(k, zm, R_zm, gmq, qmn, R_qmn, tmpm, R_tmpm, stt, R_st)
        pb = k.ps_bf(6)
        for c_ in range(6):
            k.TR(pb[:, c_ * 128:(c_ + 1) * 128], y[:, c_ * 128:(c_ + 1) * 128], [R_y, k.Rc], [RPS[6]])
        for c_ in range(2):
            k.TR(pb[:, (6 + c_) * 128:(7 + c_) * 128], qmn[:, c_ * 128:(c_ + 1) * 128], [R_qmn, k.Rc], [RPS[6]])
        for c_ in range(6):
            k.CP("act", mixT[:, c_, gt], pb[:, c_ * 128:(c_ + 1) * 128], [RPS[6]], [R_mix[c_][j]])
        k.CP("act", qmT[:, :, gt], pb[:, 768:1024].rearrange("p (c t) -> p c t", c=2), [RPS[6]], [R_qmT])

    blocks = [(j, bl) for j in range(NSLOT) for bl in range(4)]
    for idx, (j, bl) in enumerate(blocks):
        if bl == 0:
            norm_prep_slot(k, xres[:, :, j * 512:(j + 1) * 512], R_x[j], g_attn, hg2[j % 2], R_hg2[j % 2], sq, R_sq, rx[:, 4 * j:4 * j + 4], R_rx, 7)
        stage1(j, bl)
        if idx > 0:
            stage2(*blocks[idx - 1])
    stage2(*blocks[-1])
    k.release(m)


class RopeTab:
    def __init__(self, k, nb):
        self.k = k
        self.nb = nb
        n = nb * 16
        self.posf = k.alloc(nb)
        self.ang = k.alloc(n)
        self.kq = k.alloc(n)
        self.ki = k.alloc(n, I32)
        self.y = k.alloc(n)
        self.m = k.alloc(n)
        self.cs2 = k.alloc(nb * 32).rearrange("p (b i) -> p b i", b=nb)
        self.sn2 = k.alloc(nb * 32).rearrange("p (b i) -> p b i", b=nb)
        self.R = Res("rope")

    def compute(self, posi, R_pos, invf):
        k, nb, R = self.k, self.nb, self.R
        ang, kq, ki, y, m, cs2, sn2 = self.ang, self.kq, self.ki, self.y, self.m, self.cs2, self.sn2
        ang3 = ang.rearrange("p (b i) -> p b i", b=nb)
        k.CP("dve", self.posf, posi, [R_pos], [R])
        k.TT("dve", ang3, self.posf.unsqueeze(2).broadcast_to([128, nb, 16]), invf.unsqueeze(1).broadcast_to([128, nb, 16]), ALU.mult,
             [R, k.Rg], [R])
        TWO_PI = 2.0 * np.pi
        C1 = 6.28125
        C2 = float(np.float32(TWO_PI - C1))
        k.TS("dve", kq, ang, float(1.0 / TWO_PI), None, ALU.mult, None, [R], [R])
        k.CP("dve", ki, kq, [R], [R])
        k.CP("dve", kq, ki, [R], [R])
        k.STT(y, kq, -C1, ang, ALU.mult, ALU.add, [R], [R])
        k.STT(y, kq, -C2, y, ALU.mult, ALU.add, [R], [R])

        def wrap(t):
            k.TS("dve", m, t, float(np.pi), None, ALU.is_gt, None, [R], [R])
            k.STT(t, m, -TWO_PI, t, ALU.mult, ALU.add, [R], [R])
            k.TS("dve", m, t, float(-np.pi), None, ALU.is_lt, None, [R], [R])
            k.STT(t, m, TWO_PI, t, ALU.mult, ALU.add, [R], [R])

        wrap(y)
        y3 = y.rearrange("p (b i) -> p b i", b=nb)
        k.ACT(sn2[:, :, 16:32], y3, AF.Sin, [R], [R])
        k.TS("dve", sn2[:, :, 0:16], sn2[:, :, 16:32], -1.0, None, ALU.mult, None, [R], [R])
        k.TS("dve", y, y, float(np.pi / 2), None, ALU.add, None, [R], [R])
        wrap(y)
        k.ACT(cs2[:, :, 0:16], y3, AF.Sin, [R], [R])
        k.CP("dve", cs2[:, :, 16:32], cs2[:, :, 0:16], [R], [R])


def phase_KV(k, c):
    RPS = k.RPS
    m0 = k.mark()
    R_w = Res("kvw")
    Win = k.alloc(8 * 288, BF16).rearrange("p (c n) -> p c n", c=8)
    w_in3 = chunked(c["w_in0"])
    for ck in range(8):
        k.DMA("pool", Win[:, ck, :], w_in3[:, ck, 384:672], (), [R_w])
    stage = k.alloc(768)
    R_stage = Res("kvstage")
    Wuk = load_w_scaled_bf16(k, c["w_uk"], 2, 768, c["g_kvlat"], R_w, stage, R_stage)
    Wuv = load_w_scaled_bf16(k, c["w_uv"], 2, 768, c["g_kvlat"], R_w, stage, R_stage)
    rt = c["rt"]
    xs2 = [k.alloc(8 * 512).rearrange("p (c t) -> p c t", c=8) for _ in range(2)]
    R_xs2 = [Res("kxs0"), Res("kxs1")]
    hg = k.alloc(8 * 512, BF16).rearrange("p (c t) -> p c t", c=8)
    sq = k.alloc(8 * 512, BF16).rearrange("p (c t) -> p c t", c=8)
    R_hg, R_sq = Res("khg"), Res("ksq")
    rx = k.alloc(4)
    valid = k.alloc(4)
    R_rx = Res("krx")
    z = k.alloc(4 * 288).rearrange("p (b n) -> p b n", b=4)
    sqz = k.alloc(4 * 288).rearrange("p (b n) -> p b n", b=4)
    R_z, R_sqz = Res("kz"), Res("ksqz")
    st = k.alloc(16)
    R_st = Res("kst")
    ckvn = k.alloc(4 * 256, BF16).rearrange("p (b n) -> p b n", b=4)
    R_ckvn = Res("kckvn")
    krg = k.alloc(4 * 32).rearrange("p (b n) -> p b n", b=4)
    krt = k.alloc(4 * 32).rearrange("p (b n) -> p b n", b=4)
    kru = k.alloc(4 * 32).rearrange("p (b n) -> p b n", b=4)
    krr = k.alloc(4 * 32, BF16).rearrange("p (b n) -> p b n", b=4)
    R_kr, R_krr = Res("kkr"), Res("kkrr")
    ckvT = k.alloc(2 * 512, BF16).rearrange("p (c t) -> p c t", c=2)
    R_ckvT = Res("kckvT")
    KTr_t = [k.alloc(512, BF16) for _ in range(2)]
    R_KTr = [Res("kKTr0"), Res("kKTr1")]
    KTn_t = [k.alloc(6 * 512, BF16).rearrange("p (c t) -> p c t", c=6) for _ in range(2)]
    R_KTn = [Res("kKTn0"), Res("kKTn1")]
    Vx_t = [k.alloc(12 * 4 * 65, BF16).rearrange("p (h b c) -> p h b c", h=12, b=4) for _ in range(2)]
    R_Vx = [Res("kVx0"), Res("kVx1")]
    sqk = k.alloc(768)
    R_sqk = Res("ksqk")
    ssk = k.alloc(32)
    R_ssk = Res("kssk")
    rk3 = c["rk"].rearrange("p (b h) -> p b h", h=12)
    xw3 = chunked(c["xw"])
    KTn_w3 = c["KTn_w"].rearrange("(c p) t -> p c t", p=128)
    Vx_w4 = c["Vx_w"].rearrange("h p (b c) -> p h b c", c=65)
    gk = c["gk"]
    posw = c["posw"]
    ckvn2 = [ckvn, k.alloc(4 * 256, BF16).rearrange("p (b n) -> p b n", b=4)]
    R_ckvn2 = [R_ckvn, Res("kckvn1")]
    krr2 = [krr, k.alloc(4 * 32, BF16).rearrange("p (b n) -> p b n", b=4)]
    R_krr2 = [R_krr, Res("kkrr1")]
    st2 = [st, k.alloc(16)]
    R_st2 = [R_st, Res("kst1")]
    valid2 = [valid, k.alloc(4)]
    R_val2 = [Res("kval0"), Res("kval1")]

    def xload(t):
        k.DMA("sp", xs2[t % 2], xw3[:, :, t * 512:(t + 1) * 512], (), [R_xs2[t % 2]])

    def s1a(t):
        xs, R_xs = xs2[t % 2], R_xs2[t % 2]
        if t + 1 < 32:
            xload(t + 1)
        k.TT("dve", hg[:, 0:5, :], xs[:, 0:5, :], c["g_attn"][:, 0:5].unsqueeze(2).broadcast_to([128, 5, 512]), ALU.mult, [R_xs, k.Rg], [R_hg])
        for ck in range(5, 8):
            k.ACT(hg[:, ck, :], xs[:, ck, :], AF.Identity, [R_xs, k.Rg], [R_hg], scale=c["g_attn"][:, ck:ck + 1])
        k.ACT(sq, xs, AF.Square, [R_xs], [R_sq])
        pst = k.ps(3)[:, 508:512]
        for bl in range(4):
            for ck in range(8):
                k.MM(pst[:, bl:bl + 1], sq[:, ck, bl * 128:(bl + 1) * 128], k.ones_bf[:, 0:1], ck == 0, ck == 7, [R_sq, k.Rc], [RPS[3]])
        k.TS("dve", valid2[t % 2], pst, 0.0, None, ALU.is_gt, None, [RPS[3]], [R_val2[t % 2]])
        k.rsqrt(rx, pst, 1.0 / D, EPS, 4, [RPS[3]], [R_rx])

    def s1z(t, bl):
        for ck in range(8):
            k.MM(k.ps(bl)[:, 0:288], hg[:, ck, bl * 128:(bl + 1) * 128], Win[:, ck, :], ck == 0, ck == 7, [R_hg, R_w], [RPS[bl]])
        k.ACT(z[:, bl, :], k.ps(bl)[:, 0:288], AF.Identity, [RPS[bl], R_rx], [R_z], scale=rx[:, bl:bl + 1])

    def s1b(t):
        q = t % 2
        sT = st2[q]
        k.ACT(sqz, z, AF.Square, [R_z], [R_sqz])
        k.S.op("dve", lambda e: e.tensor_reduce(out=sT[:, 0:4], in_=sqz[:, :, 0:256], axis=AX.X, op=ALU.add), [R_sqz], [R_st2[q]])
        k.S.op("dve", lambda e: e.tensor_reduce(out=sT[:, 4:8], in_=sqz[:, :, 256:288], axis=AX.X, op=ALU.add), [R_sqz], [R_st2[q]])
        k.rsqrt(sT[:, 8:12], sT[:, 0:4], 1.0 / 256, EPS, 4, [R_st2[q]], [R_st2[q]])
        k.TT("dve", ckvn2[q], z[:, :, 0:256], sT[:, 8:12].unsqueeze(2).broadcast_to([128, 4, 256]), ALU.mult, [R_z, R_st2[q]], [R_ckvn2[q]])
        k.TT("pool", krg, z[:, :, 256:288], gk[:, 64:96].unsqueeze(1).broadcast_to([128, 4, 32]), ALU.mult, [R_z, k.Rg], [R_kr])
        tb4 = slice(4 * t, 4 * t + 4)
        k.TT("pool", krt, krg, rt.cs2[:, tb4, :], ALU.mult, [R_kr, rt.R], [R_kr])
        k.TT("pool", kru[:, :, 0:16], krg[:, :, 16:32], rt.sn2[:, tb4, 0:16], ALU.mult, [R_kr, rt.R], [R_kr])
        k.TT("pool", kru[:, :, 16:32], krg[:, :, 0:16], rt.sn2[:, tb4, 16:32], ALU.mult, [R_kr, rt.R], [R_kr])
        k.TT("pool", krr2[q], krt, kru, ALU.add, [R_kr], [R_krr2[q]])

    def s2tr(t):
        q = t % 2
        tcols = slice(t * 512, (t + 1) * 512)
        pb4, pb5 = k.ps_bf(4), k.ps_bf(5)
        for ck in range(2):
            for bl in range(4):
                k.TR(pb4[:, ck * 512 + bl * 128:ck * 512 + (bl + 1) * 128], ckvn2[q][:, bl, ck * 128:(ck + 1) * 128], [R_ckvn2[q], k.Rc], [RPS[4]])
        for bl in range(4):
            k.TR(pb5[0:32, bl * 128:(bl + 1) * 128], krr2[q][:, bl, :], [R_krr2[q], k.Rc], [RPS[5]])
        k.CP("act", ckvT, pb4.rearrange("p (c t) -> p c t", c=2), [RPS[4]], [R_ckvT])
        k.CP("act", KTr_t[q][0:32, :], pb5[0:32, 0:512], [RPS[5]], [R_KTr[q]])
        k.DMA("sp", c["KTr_w"][:, tcols], KTr_t[q][0:32, :], [R_KTr[q]], [c["R_KTr_w"]])

    def s2kv(t, bl):
        q = t % 2
        bk = [4, 5, 6, 7]
        tok = slice(bl * 128, (bl + 1) * 128)
        for n in range(2):
            for ck in range(2):
                k.MM(k.ps(bk[n])[:, 0:384], ckvT[:, ck, tok], Wuk[:, ck, n * 384:(n + 1) * 384], ck == 0, ck == 1, [R_ckvT, R_w], [RPS[bk[n]]])
            k.ACT(sqk[:, n * 384:(n + 1) * 384], k.ps(bk[n])[:, 0:384], AF.Square, [RPS[bk[n]]], [R_sqk])
        for n in range(2):
            for ck in range(2):
                k.MM(k.ps(bk[2 + n])[:, 0:384], ckvT[:, ck, tok], Wuv[:, ck, n * 384:(n + 1) * 384], ck == 0, ck == 1, [R_ckvT, R_w], [RPS[bk[2 + n]]])
            k.CP("dve" if n == 0 else "act", Vx_t[q][:, 6 * n:6 * n + 6, bl, 0:64], k.ps(bk[2 + n])[:, 0:384].rearrange("p (h d) -> p h d", h=6),
                 [RPS[bk[2 + n]]], [R_Vx[q]])
        k.S.op("dve", lambda e: e.tensor_reduce(out=ssk[:, 0:12], in_=sqk.rearrange("p (h d) -> p h d", h=12), axis=AX.X, op=ALU.add),
               [R_sqk], [R_ssk])
        k.TS("dve", ssk[:, 16:28], ssk[:, 0:12], st2[q][:, 4 + bl:5 + bl], None, ALU.add, None, [R_ssk, R_st2[q]], [R_ssk])
        k.rsqrt(rk3[:, 4 * t + bl, :], ssk[:, 16:28], 1.0, 96 * EPS, 12, [R_ssk], [c["R_rk"]])

    def s2kt(t):
        q = t % 2
        tcols = slice(t * 512, (t + 1) * 512)
        k.CP("pool", Vx_t[q][:, :, :, 64], valid2[q].unsqueeze(1).broadcast_to([128, 12, 4]), [R_val2[q]], [R_Vx[q]])
        for i in range(6):
            b = 4 + (i % 4)
            for ck in range(2):
                k.MM(k.ps(b), Wuk[:, ck, i * 128:(i + 1) * 128], ckvT[:, ck, :], ck == 0, ck == 1, [R_w, R_ckvT], [RPS[b]])
            k.CP("act" if i % 2 == 0 else "dve", KTn_t[q][:, i, :], k.ps(b), [RPS[b]], [R_KTn[q]])
        k.DMA("sp", KTn_w3[:, :, tcols], KTn_t[q], [R_KTn[q]], [c["R_KTn_w"]])
        for h0 in range(0, 12, 4):
            k.DMA("sp", Vx_w4[:, h0:h0 + 4, 4 * t:4 * t + 4, :], Vx_t[q][:, h0:h0 + 4, :, :], [R_Vx[q]], [c["R_Vx_w"]])

    NT = 32
    xload(0)
    for t in range(NT + 1):
        if t < NT:
            s1a(t)
        if t > 0:
            s2tr(t - 1)
        for bl in range(4):
            if t < NT:
                s1z(t, bl)
            if t > 0:
                s2kv(t - 1, bl)
        if t < NT:
            s1b(t)
        if t > 0:
            s2kt(t - 1)
    k.release(m0)


def phase_A_fused(k, c):
    RPS = k.RPS
    m0 = k.mark()
    R_w = Res("aw")
    Win = load_w_bf16(k, c["w_in0"], 8, 928, R_w)
    gq, gk, gmq = c["gq"], c["gk"], c["gmq"]
    Gq = k.alloc(96)
    k.CP("dve", Gq, gq, [k.Rg], [k.Rg])
    k.TT("dve", Gq[:, 0:64], gq[:, 0:64], gk[:, 0:64], ALU.mult, [k.Rg], [k.Rg])
    qT, qmT, R_qT, R_qmT = c["qT"], c["qmT"], c["R_qT"], c["R_qmT"]
    rt = c["rt"]
    xs = k.alloc(8 * 512).rearrange("p (c t) -> p c t", c=8)
    hgT = k.alloc(8 * 512, BF16).rearrange("p (c t) -> p c t", c=8)
    sq = k.alloc(8 * 512, BF16).rearrange("p (c t) -> p c t", c=8)
    R_xs, R_hg, R_sq = Res("axs"), Res("ahg"), Res("asq")
    rx = k.alloc(4)
    R_rx = Res("arx")

    class T:
        pass

    tl = []
    for p in range(2):
        t = T()
        t.z, t.R_z = k.alloc(928), Res(f"az{p}")
        t.stt, t.R_st = k.alloc(16), Res(f"ast{p}")
        t.qln, t.qmn, t.R_tm = k.alloc(384, BF16), k.alloc(256, BF16), Res(f"atm{p}")
        t.tmpm, t.R_tmpm = k.alloc(256), Res(f"atmpm{p}")
        t.qlT, t.R_qlT = k.alloc(3 * 128, BF16).rearrange("p (c t) -> p c t", c=3), Res(f"aqlT{p}")
        t.sqq, t.R_sqq = k.alloc(1152), Res(f"asqq{p}")
        t.ssq, t.R_ssq = k.alloc(32), Res(f"assq{p}")
        t.qn, t.R_qn = k.alloc(1152), Res(f"aqn{p}")
        t.qg, t.R_qg = k.alloc(1152), Res(f"aqg{p}")
        t.qt, t.qu, t.R_qtu = k.alloc(12 * 32), k.alloc(12 * 32), Res(f"aqtu{p}")
        t.qfin, t.R_qfin = k.alloc(12 * 96, BF16).rearrange("p (h d) -> p h d", h=12), Res(f"aqfin{p}")
        t.B = [4 * p, 4 * p + 1, 4 * p + 2, 4 * p + 3]
        tl.append(t)
    xw3 = chunked(c["xw"])
    R_stage = Res("astage")
    Wuq = load_w_scaled_bf16(k, c["w_uq"], 3, 1152, c["g_qlat"], R_w, tl[1].qn, R_stage)

    def block(j, bl, t):
        wt = 8 * j + 7
        bg = 4 * j + bl
        wb = 4 * wt + bl
        tok = slice(bl * 128, (bl + 1) * 128)
        B = t.B
        z, R_z, stt, R_st = t.z, t.R_z, t.stt, t.R_st
        sqq, R_sqq, ssq, R_ssq = t.sqq, t.R_sqq, t.ssq, t.R_ssq
        for n, (c0, c1) in enumerate([(0, 512), (512, 928)]):
            for ck in range(8):
                k.MM(k.ps(B[n])[:, 0:c1 - c0], hgT[:, ck, tok], Win[:, ck, c0:c1], ck == 0, ck == 7, [R_hg, R_w], [RPS[B[n]]])
            k.ACT(z[:, c0:c1], k.ps(B[n])[:, 0:c1 - c0], AF.Identity, [RPS[B[n]], R_rx], [R_z], scale=rx[:, bl:bl + 1])
        yield
        k.ACT(sqq[:, 0:384], z[:, 0:384], AF.Square, [R_z], [R_sqq])
        k.S.op("dve", lambda e: e.tensor_reduce(out=stt[:, 0:1], in_=sqq[:, 0:384], axis=AX.X, op=ALU.add), [R_sqq], [R_st])
        k.rsqrt(stt[:, 4:5], stt[:, 0:1], 1.0 / 384, EPS, 1, [R_st], [R_st])
        yield
        k.TS("dve", t.qln, z[:, 0:384], stt[:, 4:5], None, ALU.mult, None, [R_z, R_st], [t.R_tm])
        memq_norm(k, z[:, 672:928], R_z, gmq, t.qmn, t.R_tm, t.tmpm, t.R_tmpm, stt[:, 8:16], R_st)
        yield
        pb = k.ps_bf(B[2])
        for cc in range(3):
            k.TR(pb[:, cc * 128:(cc + 1) * 128], t.qln[:, cc * 128:(cc + 1) * 128], [t.R_tm, k.Rc], [RPS[B[2]]])
        for cc in range(2):
            k.TR(pb[:, (3 + cc) * 128:(4 + cc) * 128], t.qmn[:, cc * 128:(cc + 1) * 128], [t.R_tm, k.Rc], [RPS[B[2]]])
        k.CP("act", t.qlT, pb[:, 0:384].rearrange("p (c t) -> p c t", c=3), [RPS[B[2]]], [t.R_qlT])
        k.CP("act", qmT[:, :, bg * 128:(bg + 1) * 128], pb[:, 384:640].rearrange("p (c t) -> p c t", c=2), [RPS[B[2]]], [R_qmT])
        yield
        QB = [B[0], B[1], B[3]]
        for n in range(3):
            for ck in range(3):
                k.MM(k.ps(QB[n])[:, 0:384], t.qlT[:, ck, :], Wuq[:, ck, n * 384:(n + 1) * 384], ck == 0, ck == 2, [t.R_qlT, R_w], [RPS[QB[n]]])
            k.ACT(sqq[:, n * 384:(n + 1) * 384], k.ps(QB[n])[:, 0:384], AF.Square, [RPS[QB[n]]], [R_sqq])
        yield
        k.S.op("dve", lambda e: e.tensor_reduce(out=ssq[:, 0:12], in_=sqq.rearrange("p (h d) -> p h d", h=12), axis=AX.X, op=ALU.add),
               [R_sqq], [R_ssq])
        k.rsqrt(ssq[:, 16:28], ssq[:, 0:12], 1.0 / 96, EPS, 12, [R_ssq], [R_ssq])
        yield
        for n in range(3):
            k.TT("dve", t.qn[:, n * 384:(n + 1) * 384].rearrange("p (h d) -> p h d", h=4),
                 k.ps(QB[n])[:, 0:384].rearrange("p (h d) -> p h d", h=4),
                 ssq[:, 16 + 4 * n:20 + 4 * n].unsqueeze(2).broadcast_to([128, 4, 96]), ALU.mult, [RPS[QB[n]], R_ssq], [t.R_qn])
        yield
        qn3 = t.qn.rearrange("p (h d) -> p h d", h=12)
        qg3 = t.qg.rearrange("p (h d) -> p h d", h=12)
        k.TT("pool", qg3, qn3, Gq.unsqueeze(1).broadcast_to([128, 12, 96]), ALU.mult, [t.R_qn, k.Rg], [t.R_qg])
        qt3 = t.qt.rearrange("p (h d) -> p h d", h=12)
        qu3 = t.qu.rearrange("p (h d) -> p h d", h=12)
        k.TT("pool", qt3, qg3[:, :, 64:96], rt.cs2[:, wb, :].unsqueeze(1).broadcast_to([128, 12, 32]), ALU.mult, [t.R_qg, rt.R], [t.R_qtu])
        yield
        k.TT("pool", qu3[:, :, 0:16], qg3[:, :, 80:96], rt.sn2[:, wb, 0:16].unsqueeze(1).broadcast_to([128, 12, 16]), ALU.mult,
             [t.R_qg, rt.R], [t.R_qtu])
        k.TT("pool", qu3[:, :, 16:32], qg3[:, :, 64:80], rt.sn2[:, wb, 16:32].unsqueeze(1).broadcast_to([128, 12, 16]), ALU.mult,
             [t.R_qg, rt.R], [t.R_qtu])
        yield
        k.TT("dve", t.qfin[:, :, 64:96], qt3, qu3, ALU.add, [t.R_qtu], [t.R_qfin])
        k.CP("dve", t.qfin[:, :, 0:64], qg3[:, :, 0:64], [t.R_qg], [t.R_qfin])
        yield
        TB = [B[2], B[3]]
        for h in range(12):
            bk = TB[0] if h < 8 else TB[1]
            hh = h % 8
            k.TR(k.ps_bf(bk)[0:96, hh * 128:(hh + 1) * 128], t.qfin[:, h, :], [t.R_qfin, k.Rc], [RPS[bk]])
        k.CP("act", qT[0:96, 0:8, bg * 128:(bg + 1) * 128], k.ps_bf(TB[0])[0:96, :].rearrange("p (h t) -> p h t", h=8), [RPS[TB[0]]], [R_qT])
        k.CP("act", qT[0:96, 8:12, bg * 128:(bg + 1) * 128], k.ps_bf(TB[1])[0:96, 0:512].rearrange("p (h t) -> p h t", h=4), [RPS[TB[1]]], [R_qT])
        yield

    for j in range(NSLOT):
        wt = 8 * j + 7
        k.DMA("sp", xs, xw3[:, :, wt * 512:(wt + 1) * 512], (), [R_xs])
        norm_prep_slot(k, xs, R_xs, c["g_attn"], hgT, R_hg, sq, R_sq, rx, R_rx, 7)
        for pair in range(2):
            gens = [block(j, 2 * pair + p, tl[p]) for p in range(2)]
            live = list(gens)
            while live:
                for g in list(live):
                    try:
                        next(g)
                    except StopIteration:
                        live.remove(g)
    k.release(m0)


XRES_WORDS = 8 * TOWN


def phase_rest(k, fz=None):
    S = k.S
    k.final_dmas = []
    d = {}
    if fz is None:
        k.Rg = Res("gains")
        d["xT"] = k.inp("xT", [D, TOWN])
        KTn_w = k.inp("KTn_w", [768, SEQ], BF16)
        KTr_w = k.inp("KTr_w", [32, SEQ], BF16)
        Vx_w = k.inp("Vx_w", [12, 128, 128 * 65], BF16)
        rk_w = k.inp("rk_w", [128, 128 * 12])
        qT_i = k.inp("qT_i", [96, 12 * TOWN], BF16)
        qmT_i = k.inp("qmT_i", [128, 2 * TOWN], BF16)
        xT3_ = chunked(d["xT"])
        xsrc = lambda j: xT3_[:, :, j * 512:(j + 1) * 512]
    else:
        KTn_w, KTr_w, Vx_w = fz["KTn_w"], fz["KTr_w"], fz["Vx_w"]
        xw3_ = chunked(fz["xw"])
        xsrc = lambda j: xw3_[:, :, (8 * j + 7) * 512:(8 * j + 8) * 512]
    d["memT"] = k.inp("memT", [D, 256])
    for L in (0, 1):
        d[f"g_mem{L}"] = k.inp(f"g_mem{L}", [128, 8])
        d[f"w_mkv{L}"] = k.inp(f"w_mkv{L}", [D, 512])
        d[f"g_mk{L}"] = k.inp(f"g_mk{L}", [128, 64])
        d[f"w_out{L}"] = k.inp(f"w_out{L}", [D, D])
        d[f"g_ffn{L}"] = k.inp(f"g_ffn{L}", [128, 8])
        d[f"w_gate{L}"] = k.inp(f"w_gate{L}", [D, DFF])
        d[f"w_up{L}"] = k.inp(f"w_up{L}", [D, DFF])
        d[f"w_down{L}"] = k.inp(f"w_down{L}", [DFF, D])
    d["g_attn1"] = k.inp("g_attn1", [128, 8])
    d["w_in1"] = k.inp("w_in1", [D, 1792])
    d["ln_g"] = k.inp("ln_g", [128, 768])
    d["ln_b"] = k.inp("ln_b", [128, 768])
    d["w_spT"] = k.inp("w_spT", [128, 8 * 128])
    d["b_sp"] = k.inp("b_sp", [128, 8])
    d["g_mq1"] = k.inp("g_mq1", [128, 64])
    yT_o = k.outp("yT", [D, TOWN])

    xres = k.arena[:, ARENA_WORDS - XRES_WORDS:ARENA_WORDS].rearrange("p (c t) -> p c t", c=8)
    R_x = [Res(f"x{j}") for j in range(NSLOT)]
    LIMIT_FREE = ARENA_WORDS
    LIMIT_X = ARENA_WORDS - XRES_WORDS
    base = k.mark() if fz is None else fz["base0"]

    mixT = k.alloc(8 * TOWN, BF16).rearrange("p (c t) -> p c t", c=8)
    R_mix = [[Res(f"mix{c}_{j}") for j in range(NSLOT)] for c in range(8)]
    KmT = k.alloc(2 * 256, BF16).rearrange("p (c t) -> p c t", c=2)
    Vmx = k.alloc(2 * 4 * 65, BF16).rearrange("p (m h c) -> p m h c", m=2, h=4)
    R_KmT, R_Vmx = Res("KmT"), Res("Vmx")
    mem_kv_prep(k, "m0", d["memT"], d["g_mem0"], d["w_mkv0"], d["g_mk0"], KmT, R_KmT, Vmx, R_Vmx)
    ckpt(k, "memkv0")
    m_at = k.mark()
    if fz is None:
        qT = k.alloc(12 * TOWN, BF16).rearrange("p (h t) -> p h t", h=12)
        qmT = k.alloc(2 * TOWN, BF16).rearrange("p (c t) -> p c t", c=2)
        rk = k.alloc(128 * 12)
        R_qT, R_qmT, R_rk = Res("qT"), Res("qmT"), Res("rk")
        k.DMA("sp", qT[0:96].rearrange("p h t -> p (h t)"), qT_i, (), [R_qT])
        k.DMA("sp", qmT.rearrange("p c t -> p (c t)"), qmT_i, (), [R_qmT])
        k.DMA("sp", rk, rk_w, (), [R_rk])
    else:
        qT, qmT, rk = fz["qT"], fz["qmT"], fz["rk"]
        R_qT, R_qmT, R_rk = fz["R_qT"], fz["R_qmT"], fz["R_rk"]
    at = Attn(k, mixT, R_mix)
    at.mem_attention(qmT, R_qmT, KmT, R_KmT, Vmx, R_Vmx)
    ckpt(k, "memattn0")
    at.causal_attention(qT, R_qT, rk, R_rk, KTn_w, KTr_w, Vx_w)
    assert k.off <= LIMIT_FREE
    ckpt(k, "attn0")
    k.release(m_at)
    wout_residual(k, "l0", d["w_out0"], mixT, R_mix, xres, R_x, xsrc)
    assert k.off <= LIMIT_X
    ckpt(k, "wout0")
    k.release(base)
    ffn(k, "f0", xres, R_x, d["g_ffn0"], d["w_gate0"], d["w_up0"], d["w_down0"])
    ckpt(k, "ffn0")
    mixT = k.alloc(8 * TOWN, BF16).rearrange("p (c t) -> p c t", c=8)
    R_mix = [[Res(f"mixb{c}_{j}") for j in range(NSLOT)] for c in range(8)]
    qmT = k.alloc(2 * TOWN, BF16).rearrange("p (c t) -> p c t", c=2)
    R_qmT = Res("qmT1")
    KmT = k.alloc(2 * 256, BF16).rearrange("p (c t) -> p c t", c=2)
    Vmx = k.alloc(2 * 4 * 65, BF16).rearrange("p (m h c) -> p m h c", m=2, h=4)
    R_KmT, R_Vmx = Res("KmT1"), Res("Vmx1")
    mem_kv_prep(k, "m1", d["memT"], d["g_mem1"], d["w_mkv1"], d["g_mk1"], KmT, R_KmT, Vmx, R_Vmx)
    sgu_layer1(k, xres, R_x, mixT, R_mix, qmT, R_qmT, d)
    ckpt(k, "sgu")
    m_at = k.mark()
    at = Attn(k, mixT, R_mix)
    at.mem_attention(qmT, R_qmT, KmT, R_KmT, Vmx, R_Vmx)
    k.release(m_at)
    wout_residual(k, "l1", d["w_out1"], mixT, R_mix, xres, R_x, None)
    ckpt(k, "wout1")
    k.release(base)
    ffn(k, "f1", xres, R_x, d["g_ffn1"], d["w_gate1"], d["w_up1"], d["w_down1"])
    ckpt(k, "ffn1")
    yT3 = chunked(yT_o)
    for j in range(NSLOT):
        cols = slice(j * 512, (j + 1) * 512)
        k.final_dmas.append(k.DMA("sp", yT3[:, :, cols], xres[:, :, cols], [R_x[j]], ()))


def build_fused_body(k):
    nc = k.nc
    k.final_dmas = []
    k.Rg = Res("gains")
    c = {}
    c["xw"] = k.inp("xw", [D, SEQ])
    posw_d = k.inp("posw", [128, 128], I32)
    invf_d = k.inp("invf", [128, 16])
    c["w_in0"] = k.inp("w_in0", [D, 928])
    c["w_uq"] = k.inp("w_uq", [384, 1152])
    c["w_uk"] = k.inp("w_uk", [256, 768])
    c["w_uv"] = k.inp("w_uv", [256, 768])
    small = {}
    for name, w in [("g_attn0", 8), ("g_qlat", 3), ("g_kvlat", 2), ("gq_tile", 96), ("gk_tile", 96), ("g_mq0", 64)]:
        dd = k.inp(name, [128, w])
        t = k.alloc(w)
        k.DMA("sp", t, dd, (), [k.Rg])
        small[name] = t
    c["g_attn"], c["g_qlat"], c["g_kvlat"] = small["g_attn0"], small["g_qlat"], small["g_kvlat"]
    c["gq"], c["gk"], c["gmq"] = small["gq_tile"], small["gk_tile"], small["g_mq0"]
    c["invf"] = k.alloc(16)
    k.DMA("sp", c["invf"], invf_d, (), [k.Rg])
    c["posw"] = k.alloc(128, I32)
    c["R_pos"] = Res("posw")
    k.DMA("sp", c["posw"], posw_d, (), [c["R_pos"]])
    c["KTn_w"] = nc.dram_tensor("KTn_scr", [768, SEQ], BF16).ap()
    c["KTr_w"] = nc.dram_tensor("KTr_scr", [32, SEQ], BF16).ap()
    c["Vx_w"] = nc.dram_tensor("Vx_scr", [12, 128, 128 * 65], BF16).ap()
    c["R_KTn_w"], c["R_KTr_w"], c["R_Vx_w"] = Res("KTn_w"), Res("KTr_w"), Res("Vx_w")
    c["base0"] = k.mark()
    c["qT"] = k.alloc(12 * TOWN, BF16).rearrange("p (h t) -> p h t", h=12)
    c["qmT"] = k.alloc(2 * TOWN, BF16).rearrange("p (c t) -> p c t", c=2)
    c["rk"] = k.alloc(128 * 12)
    c["R_qT"], c["R_qmT"], c["R_rk"] = Res("qT"), Res("qmT"), Res("rk")
    m_rt = k.mark()
    cs2 = k.alloc(128 * 32).rearrange("p (b i) -> p b i", b=128)
    sn2 = k.alloc(128 * 32).rearrange("p (b i) -> p b i", b=128)
    m_tmp = k.mark()
    rt = RopeTab(k, 128)
    rt.compute(c["posw"], c["R_pos"], c["invf"])
    k.CP("pool", cs2, rt.cs2, [rt.R], [rt.R])
    k.CP("pool", sn2, rt.sn2, [rt.R], [rt.R])
    k.release(m_tmp)
    rt.cs2, rt.sn2 = cs2, sn2
    c["rt"] = rt
    phase_KV(k, c)
    ckpt(k, "kv")
    phase_A_fused(k, c)
    ckpt(k, "afused")
    k.release(m_rt)
    phase_rest(k, fz=c)


_CACHE = {}


def _get(mode):
    if mode not in _CACHE:
        _CACHE[mode] = build(mode)
    return _CACHE[mode]


def own_tokens(c):
    idx = []
    for j in range(NSLOT):
        g = 8 * j + c
        idx.append(np.arange(g * 512, (g + 1) * 512))
    return np.concatenate(idx)


def rep128(v):
    return np.ascontiguousarray(np.broadcast_to(np.asarray(v, np.float32)[None, :], (128, len(v))))


def pchunk(v):
    v = np.asarray(v, np.float32)
    return np.ascontiguousarray(v.reshape(-1, 128).T)


def l1_inputs(inp, c):
    tok = own_tokens(c)
    x = inp["x"][0]
    pos = inp["positions"][0][tok].astype(np.int32)
    inv_freq = (10000.0 ** (-np.arange(0, 32, 2, dtype=np.float32) / 32)).astype(np.float32)
    w_ukv = inp["l0_w_ukv"]
    return {
        "xT": np.ascontiguousarray(x[tok].T),
        "pos": np.ascontiguousarray(pos.reshape(16, 128).T),
        "invf": rep128(inv_freq),
        "w_in0": inp["l0_w_in"],
        "g_attn0": pchunk(inp["l0_attn_norm"]),
        "w_uq": np.ascontiguousarray(inp["l0_w_uq"].reshape(384, 1152)),
        "g_qlat": pchunk(inp["l0_q_lat_norm"]),
        "w_uk": np.ascontiguousarray(w_ukv[:, :, 0:64].reshape(256, 768)),
        "w_uv": np.ascontiguousarray(w_ukv[:, :, 64:128].reshape(256, 768)),
        "g_kvlat": pchunk(inp["l0_kv_lat_norm"]),
        "gq_tile": rep128(inp["l0_q_norm"]),
        "gk_tile": rep128(inp["l0_k_norm"]),
        "g_mq0": rep128(inp["l0_mq_norm"]),
    }


def l2_weights(inp):
    w = {"memT": np.ascontiguousarray(inp["mem"][0].T)}
    for L in (0, 1):
        p = f"l{L}_"
        w[f"g_mem{L}"] = pchunk(inp[p + "mem_norm"])
        w[f"w_mkv{L}"] = inp[p + "w_mem_kv"]
        w[f"g_mk{L}"] = rep128(inp[p + "mk_norm"])
        w[f"w_out{L}"] = inp[p + "w_out"]
        w[f"g_ffn{L}"] = pchunk(inp[p + "ffn_norm"])
        w[f"w_gate{L}"] = inp[p + "w_gate"]
        w[f"w_up{L}"] = inp[p + "w_up"]
        w[f"w_down{L}"] = inp[p + "w_down"]
    w["g_attn1"] = pchunk(inp["l1_attn_norm"])
    w["w_in1"] = inp["l1_w_in"]
    w["ln_g"] = rep128(inp["l1_sgu_ln_g"])
    w["ln_b"] = rep128(inp["l1_sgu_ln_b"])
    w["w_spT"] = np.ascontiguousarray(inp["l1_w_spatial"].transpose(2, 0, 1).reshape(128, 8 * 128))
    w["b_sp"] = np.ascontiguousarray(inp["l1_b_spatial"].T)
    w["g_mq1"] = rep128(inp["l1_mq_norm"])
    return {k_: np.ascontiguousarray(np.asarray(v, np.float32)) for k_, v in w.items()}


def gather_payload(res):
    NCH = 7 + 32
    bf = ml_dtypes.bfloat16
    KTn = np.zeros((768, NCH * 512), bf)
    KTr = np.zeros((32, NCH * 512), bf)
    Vx = np.zeros((12, 128, NCH * 4, 65), bf)
    rk = np.zeros((128, NCH * 4, 12), np.float32)
    for r in range(NCORES):
        a = np.asarray(res[r]["KTn_o"])
        b = np.asarray(res[r]["KTr_o"])
        v = np.asarray(res[r]["Vx_o"]).reshape(12, 128, 16, 65)
        q = np.asarray(res[r]["rk_o"]).reshape(128, 16, 12)
        for j in range(NSLOT):
            g = 8 * j + r
            KTn[:, (7 + g) * 512:(8 + g) * 512] = a[:, j * 512:(j + 1) * 512]
            KTr[:, (7 + g) * 512:(8 + g) * 512] = b[:, j * 512:(j + 1) * 512]
            Vx[:, :, (7 + g) * 4:(8 + g) * 4, :] = v[:, :, 4 * j:4 * j + 4, :]
            rk[:, (7 + g) * 4:(8 + g) * 4, :] = q[:, 4 * j:4 * j + 4, :]
    return KTn, KTr, Vx, rk


def kernel_unfused(**inp):
    inp = {k_: np.asarray(v) for k_, v in inp.items()}
    k1 = _get("L1")
    maps = [l1_inputs(inp, c) for c in range(NCORES)]
    r1 = run_bass_kernel_spmd(k1.nc, maps, core_ids=list(range(NCORES))).results
    KTn, KTr, Vx, rk = gather_payload(r1)
    k2 = _get("L2")
    w = l2_weights(inp)
    maps2 = []
    for c in range(NCORES):
        m = dict(w)
        m["xT"] = maps[c]["xT"]
        m["KTn_w"] = np.ascontiguousarray(KTn[:, c * 512:(c + 32) * 512])
        m["KTr_w"] = np.ascontiguousarray(KTr[:, c * 512:(c + 32) * 512])
        m["Vx_w"] = np.ascontiguousarray(Vx[:, :, c * 4:(c + 32) * 4, :]).reshape(12, 128, 128 * 65)
        m["rk_w"] = np.ascontiguousarray(rk[:, c * 4:(c + 32) * 4, :]).reshape(128, 128 * 12)
        m["qT_i"] = np.asarray(r1[c]["qT_o"])
        m["qmT_i"] = np.asarray(r1[c]["qmT_o"])
        maps2.append(m)
    r2 = run_bass_kernel_spmd(k2.nc, maps2, core_ids=list(range(NCORES))).results
    out = np.zeros((1, SEQ, D), np.float32)
    for c in range(NCORES):
        out[0, own_tokens(c), :] = np.asarray(r2[c]["yT"]).T
    return out


def fused_inputs(inp, c, w):
    x = inp["x"][0]
    pos = inp["positions"][0].astype(np.int32)
    xw = np.zeros((D, SEQ), np.float32)
    pw = np.zeros((SEQ,), np.int32)
    g0 = c - 7
    lo = max(0, g0)
    n = (g0 + 32 - lo) * 512
    dst = (lo - g0) * 512
    xw[:, dst:dst + n] = x[lo * 512:lo * 512 + n].T
    pw[dst:dst + n] = pos[lo * 512:lo * 512 + n]
    m = dict(w)
    m["xw"] = xw
    m["posw"] = np.ascontiguousarray(pw.reshape(128, 128).T)
    return m


def kernel(**inp):
    inp = {k_: np.asarray(v) for k_, v in inp.items()}
    kf = _get("fused")
    w = l2_weights(inp)
    l1 = l1_inputs(inp, 0)
    for name in ("invf", "w_in0", "g_attn0", "w_uq", "g_qlat", "w_uk", "w_uv", "g_kvlat", "gq_tile", "gk_tile", "g_mq0"):
        w[name] = l1[name]
    maps = [fused_inputs(inp, c, w) for c in range(NCORES)]
    r = run_bass_kernel_spmd(kf.nc, maps, core_ids=list(range(NCORES))).results
    out = np.zeros((1, SEQ, D), np.float32)
    for c in range(NCORES):
        out[0, own_tokens(c), :] = np.asarray(r[c]["yT"]).T
    return out
```
